# Optimizing a Trainium2 kernel written in Bass

```python
import math
import jax
import jax.numpy as jnp
from jax import lax

D_MODEL = 1024
BATCH = 8
SEQ = 4096
DEPTH = 4

MEM_LEN = 256
MAX_POS_OFFSET = 4096

DN_HEADS = 4
DN_HEAD_DIM = 128
DN_KEY_DIM = DN_HEADS * DN_HEAD_DIM
DN_QKV_DIM = 3 * DN_KEY_DIM
DN_CONV = 4
DN_CHUNK = 64

SW_HEADS = 8
SW_HEAD_DIM = 64
SW_DIM = SW_HEADS * SW_HEAD_DIM
SW_BRANCHES = ((128, 1), (512, 4), (2048, 16))
SW_BLOCK = 128
ROPE_THETA = 10000.0

IN_SPLITS = (DN_QKV_DIM, DN_KEY_DIM, DN_HEADS, DN_HEADS, SW_DIM, SW_DIM, SW_DIM)
HYB_IN = sum(IN_SPLITS)
HYB_MIX = DN_KEY_DIM + SW_DIM

S5_GROUP = 16
S5_GROUPS = D_MODEL // S5_GROUP
S5_STATE = 64

X_HEADS = 4
X_HEAD_DIM = D_MODEL // X_HEADS

FFN_HIDDEN = -(-8 * D_MODEL // (3 * 256)) * 256

N_EVEN = (DEPTH + 1) // 2
N_ODD = DEPTH // 2
DEEPNORM_ALPHA = (2 * DEPTH) ** 0.25
DEEPNORM_BETA = (8 * DEPTH) ** -0.25
LN_EPS = 1e-5
RMS_EPS = 1e-6

kernel_name = 'hybrid_deltanet_dilated_s5_block'


def _split_points(sizes):
    pts, acc = [], 0
    for s in sizes[:-1]:
        acc += s
        pts.append(acc)
    return pts


def layer_norm(x, g, b):
    xf = x.astype(jnp.float32)
    mu = xf.mean(-1, keepdims=True)
    var = jnp.square(xf - mu).mean(-1, keepdims=True)
    return (xf - mu) * lax.rsqrt(var + LN_EPS) * g.astype(jnp.float32) + b.astype(jnp.float32)


def post_norm(h, sub, g, b):
    return layer_norm(DEEPNORM_ALPHA * h.astype(jnp.float32) + sub.astype(jnp.float32), g, b).astype(h.dtype)


def rms_norm(x, g):
    xf = x.astype(jnp.float32)
    return xf * lax.rsqrt(jnp.mean(xf * xf, -1, keepdims=True) + RMS_EPS) * g.astype(jnp.float32)


def l2_normalize(x):
    xf = x.astype(jnp.float32)
    return xf * lax.rsqrt(jnp.sum(xf * xf, -1, keepdims=True) + RMS_EPS)


def causal_depthwise_conv(x, w):
    k, c = w.shape
    return lax.conv_general_dilated(
        x, w.astype(x.dtype)[:, None, :], window_strides=(1,), padding=[(k - 1, 0)],
        dimension_numbers=('NWC', 'WIO', 'NWC'), feature_group_count=c)


def rope(x, pos):
    d = x.shape[-1]
    inv_freq = ROPE_THETA ** (-jnp.arange(0, d, 2, dtype=jnp.float32) / d)
    ang = pos.astype(jnp.float32)[..., None] * inv_freq
    cos, sin = jnp.cos(ang)[:, :, None, :], jnp.sin(ang)[:, :, None, :]
    xf = x.astype(jnp.float32)
    x1, x2 = xf[..., : d // 2], xf[..., d // 2:]
    return jnp.concatenate([x1 * cos - x2 * sin, x2 * cos + x1 * sin], -1).astype(x.dtype)


def gated_delta_rule(q, k, v, g, beta):
    bsz, s, h, dk = q.shape
    dv = v.shape[-1]
    c = DN_CHUNK
    n = s // c

    def chunk(t):
        return t.reshape(bsz, n, c, h, -1).transpose(0, 3, 1, 2, 4)

    q = chunk(q) * (dk ** -0.5)
    k = chunk(k)
    v = chunk(v)
    g = jnp.cumsum(chunk(g[..., None])[..., 0], axis=-1)
    beta = chunk(beta[..., None])[..., 0]
    causal = jnp.tril(jnp.ones((c, c), dtype=bool))
    strict = jnp.tril(jnp.ones((c, c), dtype=bool), -1)
    decay = jnp.exp(jnp.where(causal, g[..., :, None] - g[..., None, :], -jnp.inf))
    k_beta = k * beta[..., None]
    lower = jnp.where(strict, jnp.einsum('bhnid,bhnjd->bhnij', k_beta, k), 0.0) * decay
    eye = jnp.eye(c, dtype=jnp.float32)
    rhs = jnp.concatenate([v * beta[..., None], k_beta * jnp.exp(g)[..., None]], axis=-1)
    sol = lax.linalg.triangular_solve(lower + eye, rhs, left_side=True, lower=True,
                                      unit_diagonal=True)
    u, w = sol[..., :dv], sol[..., dv:]
    intra = jnp.einsum('bhnid,bhnjd->bhnij', q, k) * decay
    q_dec = q * jnp.exp(g)[..., None]
    g_last = g[..., -1]
    k_dec = k * jnp.exp(g_last[..., None] - g)[..., None]

    def step(state, inp):
        q_c, w_c, u_c, k_c, a_c, gl = inp
        v_new = u_c - jnp.einsum('bhcd,bhde->bhce', w_c, state)
        out = (jnp.einsum('bhcd,bhde->bhce', q_c, state)
               + jnp.einsum('bhij,bhje->bhie', a_c, v_new))
        state = state * jnp.exp(gl)[..., None, None] + jnp.einsum('bhcd,bhce->bhde', k_c, v_new)
        return state, out

    xs = tuple(jnp.moveaxis(t, 2, 0) for t in (q_dec, w, u, k_dec, intra, g_last))
    state0 = jnp.zeros((bsz, h, dk, dv), jnp.float32)
    _, out = lax.scan(step, state0, xs)
    return out.transpose(1, 0, 3, 2, 4).reshape(bsz, s, h, dv)


def gated_deltanet(qkv, z, b_logit, a_logit, conv_w, a_log, dt_bias, norm_g):
    bsz, s, _ = qkv.shape
    qkv = jax.nn.silu(causal_depthwise_conv(qkv, conv_w))
    q, k, v = jnp.split(qkv, [DN_KEY_DIM, 2 * DN_KEY_DIM], axis=-1)
    heads = lambda t: t.reshape(bsz, s, DN_HEADS, DN_HEAD_DIM)
    q = l2_normalize(heads(q))
    k = l2_normalize(heads(k))
    v = heads(v).astype(jnp.float32)
    beta = jax.nn.sigmoid(b_logit.astype(jnp.float32))
    g = -jnp.exp(a_log.astype(jnp.float32)) * jax.nn.softplus(
        a_logit.astype(jnp.float32) + dt_bias.astype(jnp.float32))
    o = gated_delta_rule(q, k, v, g, beta)
    o = rms_norm(o, norm_g) * jax.nn.silu(heads(z).astype(jnp.float32))
    return o.reshape(bsz, s, DN_KEY_DIM)


def dilated_branch(q, k, v, window, dilation):
    bsz, s, h, d = q.shape
    steps = window // dilation
    span = dilation * SW_BLOCK
    s_pad = -(-s // span) * span
    length = s_pad // dilation
    nb = length // SW_BLOCK

    def to_blocks(t):
        t = jnp.pad(t, ((0, 0), (0, s_pad - s), (0, 0), (0, 0)))
        t = t.reshape(bsz, length, dilation, h, -1).transpose(0, 2, 1, 3, 4)
        return t.reshape(bsz, dilation, nb, SW_BLOCK, h, -1)

    def with_prev(t):
        prev = jnp.pad(t, ((0, 0), (0, 0), (1, 0), (0, 0), (0, 0), (0, 0)))[:, :, :-1]
        return jnp.concatenate([prev, t], axis=3)

    def from_blocks(t):
        t = t.reshape(bsz, dilation, length, h, -1).transpose(0, 2, 1, 3, 4)
        return t.reshape(bsz, s_pad, h, -1)[:, :s]

    qb = to_blocks(q)
    kw = with_prev(to_blocks(k))
    vw = with_prev(to_blocks(v))
    scores = jnp.einsum('brnqhd,brnkhd->brnhqk', qb, kw,
                        preferred_element_type=jnp.float32) * (d ** -0.5)
    qi = jnp.arange(SW_BLOCK)[:, None] + SW_BLOCK
    kj = jnp.arange(2 * SW_BLOCK)[None, :]
    dist = qi - kj
    blk = jnp.arange(nb)[:, None, None]
    valid = (dist >= 0) & (dist <= steps) & (blk * SW_BLOCK + kj - SW_BLOCK >= 0)
    scores = jnp.where(valid[None, None, :, None], scores, -jnp.inf)
    m = scores.max(-1, keepdims=True)
    p = jnp.exp(scores - m)
    l = p.sum(-1)
    o = jnp.einsum('brnhqk,brnkhd->brnqhd', p, vw.astype(jnp.float32))
    o = o / jnp.swapaxes(l, -1, -2)[..., None]
    lse = jnp.swapaxes(m[..., 0] + jnp.log(l), -1, -2)
    return from_blocks(o), from_blocks(lse[..., None])[..., 0]


def dilated_attention(q, k, v, pos):
    q = rope(q, pos)
    k = rope(k, pos)
    outs, lses = [], []
    for window, dilation in SW_BRANCHES:
        o, lse = dilated_branch(q, k, v, window, dilation)
        outs.append(o)
        lses.append(lse)
    wts = jax.nn.softmax(jnp.stack(lses), axis=0)[..., None]
    return jnp.sum(jnp.stack(outs) * wts, axis=0)


def delta_dilated_mixer(h, positions, w_in, conv_w, a_log, dt_bias, norm_g, w_out):
    bsz, s, _ = h.shape
    proj = h @ w_in
    dn_qkv, dn_z, dn_b, dn_a, sw_q, sw_k, sw_v = jnp.split(proj, _split_points(IN_SPLITS), axis=-1)
    a_out = gated_deltanet(dn_qkv, dn_z, dn_b, dn_a, conv_w, a_log, dt_bias, norm_g)
    heads = lambda t: t.reshape(bsz, s, SW_HEADS, SW_HEAD_DIM)
    b_out = dilated_attention(heads(sw_q), heads(sw_k), heads(sw_v), positions).reshape(bsz, s, SW_DIM)
    mixed = jnp.concatenate([a_out, b_out], axis=-1).astype(h.dtype)
    return mixed @ w_out


def _lin_rec_combine(e1, e2):
    a1, b1 = e1
    a2, b2 = e2
    return a1 * a2, a2 * b1 + b2


def s5_mixer(u, a_re, a_im, log_dt, b_re, b_im, c_re, c_im, d_skip, w_o, w_g):
    bsz, s, d = u.shape
    uf = u.astype(jnp.float32)
    ug = uf.reshape(bsz, s, S5_GROUPS, S5_GROUP)
    f32 = jnp.float32
    a = lax.complex(a_re.astype(f32), a_im.astype(f32))
    dt = jnp.exp(log_dt.astype(f32))[:, None]
    a_bar = jnp.exp(a * dt)
    b_bar = ((a_bar - 1.0) / a)[..., None] * lax.complex(b_re.astype(f32), b_im.astype(f32))
    bu = jnp.einsum('gph,bsgh->bsgp', b_bar, ug.astype(jnp.complex64))
    a_seq = jnp.broadcast_to(a_bar, (1, s) + a_bar.shape)
    _, states = lax.associative_scan(_lin_rec_combine, (a_seq, bu), axis=1)
    y = (jnp.einsum('ghp,bsgp->bsgh', c_re.astype(f32), states.real)
         - jnp.einsum('ghp,bsgp->bsgh', c_im.astype(f32), states.imag))
    y = y.reshape(bsz, s, d) + d_skip.astype(f32) * uf
    hid = jax.nn.gelu(y).astype(u.dtype)
    return (hid @ w_o) * jax.nn.sigmoid(hid @ w_g)


def memory_cross_attention(h, mem, wq, wk, wv, wo):
    bsz, s, _ = h.shape
    m = mem.shape[1]
    q = (h @ wq).reshape(bsz, s, X_HEADS, X_HEAD_DIM)
    k = (mem @ wk).reshape(bsz, m, X_HEADS, X_HEAD_DIM)
    v = (mem @ wv).reshape(bsz, m, X_HEADS, X_HEAD_DIM)
    scores = jnp.einsum('bqhd,bkhd->bhqk', q, k, preferred_element_type=jnp.float32) * (X_HEAD_DIM ** -0.5)
    p = jax.nn.softmax(scores, axis=-1)
    o = jnp.einsum('bhqk,bkhd->bqhd', p, v.astype(jnp.float32)).reshape(bsz, s, D_MODEL)
    return o.astype(h.dtype) @ wo


def swiglu(h, wg, wu, wd):
    return (jax.nn.silu(h @ wg) * (h @ wu)) @ wd


def setup_inputs(seed: int = 0) -> dict:
    key = jax.random.key(seed)
    keys = iter(jax.random.split(key, 48))
    f32 = jnp.float32

    def nrm(shape, scale):
        return jax.random.normal(next(keys), shape, f32) * scale

    def uni(shape, lo, hi):
        return jax.random.uniform(next(keys), shape, f32, lo, hi)

    def gain(shape):
        return 1.0 + nrm(shape, 0.02)

    x = nrm((BATCH, SEQ, D_MODEL), 1.0)
    mem = nrm((BATCH, MEM_LEN, D_MODEL), 1.0)
    positions = (jax.random.randint(next(keys), (BATCH, 1), 0, MAX_POS_OFFSET, dtype=jnp.int32)
                 + jnp.arange(SEQ, dtype=jnp.int32)[None, :])

    hyb_w_in = nrm((N_EVEN, D_MODEL, HYB_IN), D_MODEL ** -0.5)
    dn_conv_w = nrm((N_EVEN, DN_CONV, DN_QKV_DIM), DN_CONV ** -0.5)
    dn_a_log = jnp.log(uni((N_EVEN, DN_HEADS), 1.0, 16.0))
    dt = jnp.exp(uni((N_EVEN, DN_HEADS), math.log(1e-3), math.log(1e-1)))
    dn_dt_bias = dt + jnp.log(-jnp.expm1(-dt))
    dn_norm_g = gain((N_EVEN, DN_HEAD_DIM))
    hyb_w_out = nrm((N_EVEN, HYB_MIX, D_MODEL), HYB_MIX ** -0.5 * DEEPNORM_BETA)

    s5_a_re = -0.5 + nrm((N_ODD, S5_GROUPS, S5_STATE), 0.01)
    s5_a_im = math.pi * jnp.arange(S5_STATE, dtype=f32) + nrm((N_ODD, S5_GROUPS, S5_STATE), 0.01)
    s5_log_dt = uni((N_ODD, S5_GROUPS), math.log(1e-3), math.log(1e-1))
    s5_b_re = nrm((N_ODD, S5_GROUPS, S5_STATE, S5_GROUP), (2 * S5_GROUP) ** -0.5)
    s5_b_im = nrm((N_ODD, S5_GROUPS, S5_STATE, S5_GROUP), (2 * S5_GROUP) ** -0.5)
    s5_c_re = nrm((N_ODD, S5_GROUPS, S5_GROUP, S5_STATE), 0.5 ** 0.5)
    s5_c_im = nrm((N_ODD, S5_GROUPS, S5_GROUP, S5_STATE), 0.5 ** 0.5)
    s5_d = nrm((N_ODD, D_MODEL), 1.0)
    s5_glu_wo = nrm((N_ODD, D_MODEL, D_MODEL), D_MODEL ** -0.5 * DEEPNORM_BETA)
    s5_glu_wg = nrm((N_ODD, D_MODEL, D_MODEL), D_MODEL ** -0.5)

    ln_mix_g = gain((DEPTH, D_MODEL))
    ln_mix_b = nrm((DEPTH, D_MODEL), 0.02)
    xq_w = nrm((DEPTH, D_MODEL, D_MODEL), D_MODEL ** -0.5)
    xk_w = nrm((DEPTH, D_MODEL, D_MODEL), D_MODEL ** -0.5)
    xv_w = nrm((DEPTH, D_MODEL, D_MODEL), D_MODEL ** -0.5)
    xo_w = nrm((DEPTH, D_MODEL, D_MODEL), D_MODEL ** -0.5 * DEEPNORM_BETA)
    ln_x_g = gain((DEPTH, D_MODEL))
    ln_x_b = nrm((DEPTH, D_MODEL), 0.02)
    ffn_wg = nrm((DEPTH, D_MODEL, FFN_HIDDEN), D_MODEL ** -0.5)
    ffn_wu = nrm((DEPTH, D_MODEL, FFN_HIDDEN), D_MODEL ** -0.5)
    ffn_wd = nrm((DEPTH, FFN_HIDDEN, D_MODEL), FFN_HIDDEN ** -0.5 * DEEPNORM_BETA)
    ln_ffn_g = gain((DEPTH, D_MODEL))
    ln_ffn_b = nrm((DEPTH, D_MODEL), 0.02)

    return {
        'x': x, 'mem': mem, 'positions': positions,
        'hyb_w_in': hyb_w_in, 'dn_conv_w': dn_conv_w, 'dn_a_log': dn_a_log,
        'dn_dt_bias': dn_dt_bias, 'dn_norm_g': dn_norm_g, 'hyb_w_out': hyb_w_out,
        's5_a_re': s5_a_re, 's5_a_im': s5_a_im, 's5_log_dt': s5_log_dt,
        's5_b_re': s5_b_re, 's5_b_im': s5_b_im, 's5_c_re': s5_c_re, 's5_c_im': s5_c_im,
        's5_d': s5_d, 's5_glu_wo': s5_glu_wo, 's5_glu_wg': s5_glu_wg,
        'ln_mix_g': ln_mix_g, 'ln_mix_b': ln_mix_b,
        'xq_w': xq_w, 'xk_w': xk_w, 'xv_w': xv_w, 'xo_w': xo_w,
        'ln_x_g': ln_x_g, 'ln_x_b': ln_x_b,
        'ffn_wg': ffn_wg, 'ffn_wu': ffn_wu, 'ffn_wd': ffn_wd,
        'ln_ffn_g': ln_ffn_g, 'ln_ffn_b': ln_ffn_b,
    }


def reference(x, mem, positions,
              hyb_w_in, dn_conv_w, dn_a_log, dn_dt_bias, dn_norm_g, hyb_w_out,
              s5_a_re, s5_a_im, s5_log_dt, s5_b_re, s5_b_im, s5_c_re, s5_c_im,
              s5_d, s5_glu_wo, s5_glu_wg,
              ln_mix_g, ln_mix_b,
              xq_w, xk_w, xv_w, xo_w, ln_x_g, ln_x_b,
              ffn_wg, ffn_wu, ffn_wd, ln_ffn_g, ln_ffn_b):
    h = x
    for layer in range(DEPTH):
        i = layer // 2
        if layer % 2 == 0:
            mix = delta_dilated_mixer(h, positions, hyb_w_in[i], dn_conv_w[i], dn_a_log[i],
                                      dn_dt_bias[i], dn_norm_g[i], hyb_w_out[i])
        else:
            mix = s5_mixer(h, s5_a_re[i], s5_a_im[i], s5_log_dt[i], s5_b_re[i], s5_b_im[i],
                           s5_c_re[i], s5_c_im[i], s5_d[i], s5_glu_wo[i], s5_glu_wg[i])
        h = post_norm(h, mix, ln_mix_g[layer], ln_mix_b[layer])
        h = post_norm(h, memory_cross_attention(h, mem, xq_w[layer], xk_w[layer], xv_w[layer], xo_w[layer]),
                      ln_x_g[layer], ln_x_b[layer])
        h = post_norm(h, swiglu(h, ffn_wg[layer], ffn_wu[layer], ffn_wd[layer]),
                      ln_ffn_g[layer], ln_ffn_b[layer])
    return h
```

```python
import numpy as np
from contextlib import ExitStack
import concourse.bass as bass
import concourse.mybir as mybir
from concourse.bass_utils import run_bass_kernel_spmd

F32 = mybir.dt.float32
BF16 = mybir.dt.bfloat16
I32 = mybir.dt.int32
AF = mybir.ActivationFunctionType
ALU = mybir.AluOpType
AX = mybir.AxisListType

NDMA = 16
NSDMA = 8


def is_dma(t):
    return t.startswith("dma") or t.startswith("sdma")


class Buf:
    __slots__ = ("name", "w", "r")

    def __init__(self, name=""):
        self.name = name
        self.w = None
        self.r = []


class V:
    def __init__(self, ap, bufs):
        self.ap = ap
        self.bufs = bufs if isinstance(bufs, tuple) else (bufs,)

    def __getitem__(self, k):
        return V(self.ap[k], self.bufs)

    def re(self, s, **kw):
        return V(self.ap.rearrange(s, **kw), self.bufs)

    def bc(self, dt):
        return V(self.ap.bitcast(dt), self.bufs)

    def bcast(self, shape):
        return V(self.ap.to_broadcast(shape), self.bufs)


class Op:
    __slots__ = ("eng", "fn", "deps", "track", "tidx", "inc", "val")


class Prog:
    ENG = ("pe", "act", "dve", "pool", "sp")
    BLK = {"pe": "tensor", "act": "scalar", "dve": "vector", "pool": "gpsimd", "sp": "sync"}

    def __init__(self, nc):
        self.nc = nc
        self.es = ExitStack()
        self.ops = {e: [] for e in self.ENG}
        self.tracks = {e: [] for e in self.ENG}
        for i in range(NDMA):
            self.tracks["dma%d" % i] = []
        for i in range(NSDMA):
            self.tracks["sdma%d" % i] = []
        self.dma_rr = 0
        self.sdma_rr = 0
        self.seen = {e: {} for e in self.ENG}
        self.nalloc = 0
        self.out_ops = []
        self.stack = [self.es]

    def sb(self, shape, dt, name=None):
        self.nalloc += 1
        name = name or ("t%d" % self.nalloc)
        t = self.stack[-1].enter_context(self.nc.sbuf_tensor(name + "_%d" % self.nalloc, list(shape), dt))
        return V(t.ap() if hasattr(t, "ap") and callable(t.ap) else t[:], Buf(name))

    def ps(self, shape, dt=F32, name=None):
        self.nalloc += 1
        name = name or ("p%d" % self.nalloc)
        t = self.stack[-1].enter_context(self.nc.psum_tensor(name + "_%d" % self.nalloc, list(shape), dt))
        return V(t.ap() if hasattr(t, "ap") and callable(t.ap) else t[:], Buf(name))

    def dram(self, name, shape, dt, kind="Internal"):
        t = self.nc.dram_tensor(name, list(shape), dt, kind=kind)
        return DT(t.ap(), name)


    def push(self):
        self.stack.append(ExitStack())

    def pop(self):
        self.barrier()
        self.stack.pop().close()

    def barrier(self):
        last = {t: len(l) - 1 for t, l in self.tracks.items() if l}
        for eng in self.ENG:
            op = Op()
            op.eng = eng
            op.fn = None
            op.inc = False
            op.val = None
            op.track = eng
            op.tidx = len(self.tracks[eng])
            op.deps = {}
            seen = self.seen[eng]
            for t, i in last.items():
                if seen.get(t, -1) < i:
                    seen[t] = i
                    op.deps[t] = i
                    self.tracks[t][i].inc = True
            self.ops[eng].append(op)

    def add(self, eng, fn, reads, writes, dma=False):
        op = Op()
        op.eng = eng
        op.fn = fn
        op.inc = dma
        op.val = None
        if dma and eng == "pool":
            tr = "sdma%d" % self.sdma_rr
            self.sdma_rr = (self.sdma_rr + 1) % NSDMA
        elif dma:
            tr = "dma%d" % self.dma_rr
            self.dma_rr = (self.dma_rr + 1) % NDMA
        else:
            tr = eng
        op.track = tr
        op.tidx = len(self.tracks[tr])
        deps = {}

        def need(t, i):
            if deps.get(t, -1) < i:
                deps[t] = i

        for v in reads:
            for b in v.bufs:
                if b.w is not None:
                    t, i = b.w
                    if t == tr and eng == "pe":
                        continue
                    need(t, i)
        for v in writes:
            for b in v.bufs:
                if b.w is not None:
                    t, i = b.w
                    if not (t == tr and eng == "pe"):
                        need(t, i)
                for (t, i) in b.r:
                    if not (t == tr and eng == "pe"):
                        need(t, i)
        if dma and op.tidx > 0:
            need(tr, op.tidx - 1)
        seen = self.seen[eng]
        op.deps = {}
        for t, i in deps.items():
            if seen.get(t, -1) < i:
                seen[t] = i
                op.deps[t] = i
                self.tracks[t][i].inc = True
        self.tracks[tr].append(op)
        self.ops[eng].append(op)
        for v in reads:
            for b in v.bufs:
                b.r.append((tr, op.tidx))
        for v in writes:
            for b in v.bufs:
                b.w = (tr, op.tidx)
                b.r = []
        return op

    def mm(self, out, lhsT, rhs, start=True, stop=True):
        return self.add("pe", lambda e: e.matmul(out.ap, lhsT.ap, rhs.ap, start=start, stop=stop),
                        [lhsT, rhs] + ([] if start else [out]), [out])

    def tr(self, out, in_, ident):
        return self.add("pe", lambda e: e.transpose(out.ap, in_.ap, ident.ap), [in_, ident], [out])

    def act(self, out, in_, func, bias=None, scale=None, accum=None, eng="act"):
        kw = {}
        rd = [in_]
        wr = [out]
        if bias is not None:
            if isinstance(bias, V):
                kw["bias"] = bias.ap
                rd.append(bias)
            else:
                kw["bias"] = bias
        if scale is not None:
            if isinstance(scale, V):
                kw["scale"] = scale.ap
                rd.append(scale)
            else:
                kw["scale"] = scale
        if accum is not None:
            kw["accum_out"] = accum.ap
            wr.append(accum)
        return self.add("act", lambda e: e.activation(out.ap, in_.ap, func, **kw), rd, wr)

    def tt(self, out, a, b, op, eng="dve"):
        return self.add(eng, lambda e: e.tensor_tensor(out.ap, a.ap, b.ap, op), [a, b], [out])

    def ts(self, out, a, s1, s2, op0, op1=None, accum=None, eng="dve"):
        rd = [a]
        wr = [out]
        s1a = s1.ap if isinstance(s1, V) else s1
        s2a = s2.ap if isinstance(s2, V) else s2
        if isinstance(s1, V):
            rd.append(s1)
        if isinstance(s2, V):
            rd.append(s2)
        kw = {}
        if op1 is not None:
            kw["op1"] = op1
        if accum is not None:
            kw["accum_out"] = accum.ap
            wr.append(accum)
        return self.add(eng, lambda e: e.tensor_scalar(out.ap, a.ap, s1a, s2a, op0, **kw), rd, wr)

    def stt(self, out, a, s, b, op0, op1, eng="dve"):
        rd = [a, b]
        sa = s.ap if isinstance(s, V) else s
        if isinstance(s, V):
            rd.append(s)
        return self.add(eng, lambda e: e.scalar_tensor_tensor(out.ap, a.ap, sa, b.ap, op0, op1), rd, [out])

    def copy(self, out, in_, eng="dve"):
        if eng == "act":
            return self.add("act", lambda e: e.copy(out.ap, in_.ap), [in_], [out])
        return self.add(eng, lambda e: e.tensor_copy(out.ap, in_.ap), [in_], [out])

    def memset(self, out, val, eng="pool"):
        return self.add(eng, lambda e: e.memset(out.ap, val), [], [out])

    def recip(self, out, in_):
        return self.add("dve", lambda e: e.reciprocal(out.ap, in_.ap), [in_], [out])

    def scan(self, out, d0, d1, init, op0, op1):
        rd = [d0, d1]
        ia = init.ap if isinstance(init, V) else init
        if isinstance(init, V):
            rd.append(init)
        return self.add("dve", lambda e: e.tensor_tensor_scan(out.ap, d0.ap, d1.ap, ia, op0, op1), rd, [out])

    def dma(self, out, in_, eng="sp", is_out=False, **kw):
        op = self.add(eng, lambda e: e.dma_start(out=out.ap, in_=in_.ap, **kw), [in_], [out], dma=True)
        if is_out:
            self.out_ops.append(op)
        return op

    def emit(self):
        nc = self.nc
        fin = Op()
        fin.eng = "sp"
        fin.fn = None
        fin.inc = False
        fin.track = "sp"
        fin.tidx = len(self.tracks["sp"])
        fin.deps = {}
        for tname, lst in self.tracks.items():
            if is_dma(tname) and lst:
                fin.deps[tname] = len(lst) - 1
        self.ops["sp"].append(fin)
        for tname, lst in self.tracks.items():
            c = 0
            for op in lst:
                if is_dma(tname):
                    c += 16
                    op.val = c
                else:
                    if op.inc:
                        c += 1
                    op.val = c
        sems = {}
        for tname, lst in self.tracks.items():
            if lst:
                sems[tname] = self.es.enter_context(nc.semaphore("s_" + tname))
        block = self.es.enter_context(nc.Block())
        tracks = self.tracks
        for eng in self.ENG:
            ops = self.ops[eng]
            if not ops:
                continue

            def body(e, ops=ops):
                for op in ops:
                    for t, i in op.deps.items():
                        e.wait_ge(sems[t], tracks[t][i].val)
                    if op.fn is None:
                        continue
                    ins = op.fn(e)
                    if is_dma(op.track):
                        ins.then_inc(sems[op.track], 16)
                    elif op.inc:
                        ins.then_inc(sems[op.track], 1)

            getattr(block, self.BLK[eng])(body)
        self.es.close()


class DT:
    def __init__(self, ap, name):
        self.ap = ap
        self.name = name
        self.bufs = {}

    def v(self, ap, keys=(0,)):
        if not isinstance(keys, (tuple, list)):
            keys = (keys,)
        bs = []
        for k in keys:
            if k not in self.bufs:
                self.bufs[k] = Buf("%s[%s]" % (self.name, k))
            bs.append(self.bufs[k])
        return V(ap, tuple(bs))


class Pool:
    def __init__(self, P, n, shape, dt, name, psum=False):
        self.tiles = [(P.ps if psum else P.sb)(shape, dt, name="%s%d" % (name, i)) for i in range(n)]
        self.i = 0

    def next(self):
        t = self.tiles[self.i]
        self.i = (self.i + 1) % len(self.tiles)
        return t


import math

S = 4096
D = 1024
NT = 32
FF = 2816
ALPHA = 8 ** 0.25
LN_EPS = 1e-5
RMS_EPS = 1e-6
TWO_PI = 2.0 * math.pi


def host_consts():
    c = {}
    i = np.arange(128)
    c["c_ident"] = np.eye(128, dtype=np.float32)
    c["c_tri"] = (i[:, None] <= i[None, :]).astype(np.float32)
    c["c_ones"] = np.ones((128, 128), np.float32)
    c["c_negU"] = np.where(i[None, :] >= i[:, None], 0.0, -30000.0).astype(np.float32)
    c["c_posL"] = np.where(i[None, :] <= i[:, None], 0.0, 30000.0).astype(np.float32)
    c["c_strU"] = (i[None, :] > i[:, None]).astype(np.float32)
    c["c_strL"] = (i[None, :] < i[:, None]).astype(np.float32)
    dm = np.zeros((128, 256), np.float32)
    dm[:, :128] = (i[:, None] <= i[None, :])
    dm[:, 128:] = (i[:, None] >= i[None, :])
    c["c_dmask"] = dm
    c["c_dneg"] = ((dm - 1.0) * 30000.0).astype(np.float32)
    invf = (10000.0 ** (-np.arange(0, 64, 2, dtype=np.float32) / 64)).astype(np.float32)
    r = np.arange(128)
    c["c_invf"] = (invf[r % 32] / np.float32(TWO_PI)).astype(np.float32).reshape(128, 1)
    c["c_sgn"] = np.where((r % 64) < 32, -1.0, 1.0).astype(np.float32).reshape(128, 1)
    c["c_iota"] = np.broadcast_to(np.arange(64, dtype=np.float32)[None, :], (128, 64)).copy()
    sel = np.zeros((128, 64), np.float32)
    sel[64 + np.arange(64), np.arange(64)] = 1.0
    c["c_sel"] = sel
    return c


def host_layer_inputs(inp):
    o = {}
    for l in range(2):
        a_re, a_im, ldt = inp["s5_a_re"][l], inp["s5_a_im"][l], inp["s5_log_dt"][l]
        col = np.stack([a_re, a_im, np.repeat(ldt[:, None], 64, 1)], 0)
        col = col.reshape(3, 32, 2, 64).transpose(2, 3, 0, 1).reshape(128, 3, 32)
        o["s5col%d" % l] = np.ascontiguousarray(col, np.float32)
        o["s5row%d" % l] = np.ascontiguousarray(np.stack([a_re.reshape(-1), a_im.reshape(-1), np.repeat(ldt, 64)], 0), np.float32)
        for nm, src in (("re", inp["s5_b_re"][l]), ("im", inp["s5_b_im"][l])):
            bp = np.zeros((4, 2, 16, 32, 2, 64), np.float32)
            sp = src.reshape(8, 4, 2, 64, 16)
            for q in range(4):
                for gl in range(2):
                    bp[q, gl, :, q::4, gl, :] = sp[:, q, gl].transpose(2, 0, 1)
            o["s5b%s%d" % (nm, l)] = bp.reshape(128, 32 * 128)
        for nm, src in (("re", inp["s5_c_re"][l]), ("im", inp["s5_c_im"][l])):
            cp = np.zeros((2, 64, 32, 4, 2, 16), np.float32)
            sp = src.reshape(8, 4, 2, 16, 64)
            for q in range(4):
                for gl in range(2):
                    cp[gl, :, q::4, q, gl, :] = sp[:, q, gl].transpose(2, 0, 1)
            o["s5c%s%d" % (nm, l)] = cp.reshape(128, 32 * 128)
        o["s5d%d" % l] = np.ascontiguousarray(inp["s5_d"][l].reshape(8, 128).T, np.float32)
        o["dncw%d" % l] = np.ascontiguousarray(inp["dn_conv_w"][l].T.reshape(12, 128, 4).transpose(1, 0, 2), np.float32)
    return o


class Ctx:
    pass


def load_w(P, dst, src_ap, KC, eng="pool"):
    for kc in range(KC):
        P.dma(dst[:, kc, :], src_ap(kc), eng=eng)


def load_h(P, C, tt, h_src):
    ht = C.p_h.next()
    P.dma(ht, h_src.v(h_src.ap[tt * 128:(tt + 1) * 128, :], tt))
    return ht


def ln_epilogue(P, C, lnp, tt, sub_halves, ht):
    z = C.p_z.next()
    st = C.p_st.next()
    for hf in range(2):
        P.stt(z[:, hf * 512:(hf + 1) * 512], ht[:, hf * 512:(hf + 1) * 512], ALPHA, sub_halves[hf], ALU.mult, ALU.add)
        zz = z[:, hf * 512:(hf + 1) * 512]
        P.add("dve", lambda e, o=st[:, hf, :], i=zz: e.bn_stats(o.ap, i.ap), [zz], [st])
    mv = C.p_mv.next()
    P.add("dve", lambda e, o=mv[:, 0:2], i=st.re("p a b -> p (a b)"): e.bn_aggr(o.ap, i.ap), [st], [mv])
    P.ts(mv[:, 2:3], mv[:, 1:2], LN_EPS, None, ALU.add)
    P.act(mv[:, 3:4], mv[:, 2:3], AF.Sqrt)
    P.recip(mv[:, 4:5], mv[:, 3:4])
    zn = C.p_zn.next()
    P.ts(zn, z, mv[:, 0:1], mv[:, 4:5], ALU.subtract, ALU.mult)
    hn = C.p_hn.next()
    P.tt(zn, zn, lnp[0], ALU.mult, eng="pool")
    P.tt(hn, zn, lnp[1], ALU.add, eng="pool")
    P.dma(C.out.v(C.out.ap[tt * 128:(tt + 1) * 128, :], tt), hn, is_out=True)
    hb = C.p_hb.next()
    P.copy(hb, hn, eng="act")
    return (tt, hb)


def ln_epilogue_b(P, C, pend):
    if pend is None:
        return
    tt, hb = pend
    pt = C.pst
    for c in range(8):
        P.tr(pt[:, c * 128:(c + 1) * 128], hb[:, c * 128:(c + 1) * 128], C.identb)
    hts = C.p_hts.next()
    P.copy(hts, pt.re("p (c t) -> p c t", c=8), eng="act")
    P.dma(C.hT.v(C.hT.ap.rearrange("(c p) t -> p c t", p=128)[:, :, tt * 128:(tt + 1) * 128], ("t", tt)), hts)


def ln_pools(P, C):
    C.p_h = Pool(P, 8, [128, 1024], F32, "h")
    C.p_z = Pool(P, 2, [128, 1024], F32, "z")
    C.p_zn = Pool(P, 2, [128, 1024], F32, "zn")
    C.p_hn = Pool(P, 2, [128, 1024], F32, "hn")
    C.p_hb = Pool(P, 3, [128, 1024], BF16, "hb")
    C.p_hts = Pool(P, 2, [128, 8, 128], BF16, "hts")
    C.p_st = Pool(P, 2, [128, 2, 6], F32, "st")
    C.p_mv = Pool(P, 2, [128, 8], F32, "mv")


def load_ln(P, C, g_ap, b_ap):
    G = P.sb([128, 1024], F32, "lng")
    B = P.sb([128, 1024], F32, "lnb")
    P.dma(G, g_ap)
    P.dma(B, b_ap)
    return (G, B)


def hT_keys(b):
    return tuple(("t", t) for t in range(b * 4, b * 4 + 4))


def phase_prep(P, C, I):
    import os
    MODE = os.environ.get("PREP_MODE", "all")
    P.push()
    xs = Pool(P, 2, [128, 1024], F32, "xs")
    xb = Pool(P, 2, [128, 1024], BF16, "xb")
    hts = Pool(P, 2, [128, 8, 128], BF16, "hts")
    for tt in range(NT if MODE in ('all', 'A') else 0):
        x_t = xs.next()
        P.dma(x_t, I["x"].v(I["x"].ap[tt * 128:(tt + 1) * 128, :], tt))
        b_t = xb.next()
        P.copy(b_t, x_t, eng="dve")
        for c in range(8):
            P.tr(C.pst[:, c * 128:(c + 1) * 128], b_t[:, c * 128:(c + 1) * 128], C.identb)
        h_t = hts.next()
        P.copy(h_t, C.pst.re("p (c t) -> p c t", c=8), eng="act")
        P.dma(C.hT.v(C.hT.ap.rearrange("(c p) t -> p c t", p=128)[:, :, tt * 128:(tt + 1) * 128], ("t", tt)), h_t)
    for mt in range(2 if MODE in ('all', 'A2') else 0):
        x_t = xs.next()
        P.dma(x_t, I["mem"].v(I["mem"].ap[mt * 128:(mt + 1) * 128, :], mt))
        b_t = xb.next()
        P.copy(b_t, x_t, eng="dve")
        for c in range(8):
            P.tr(C.pst[:, c * 128:(c + 1) * 128], b_t[:, c * 128:(c + 1) * 128], C.identb)
        h_t = hts.next()
        P.copy(h_t, C.pst.re("p (c t) -> p c t", c=8), eng="act")
        P.dma(C.memT.v(C.memT.ap.rearrange("(c p) t -> p c t", p=128)[:, :, mt * 128:(mt + 1) * 128], 0), h_t)
    invf = P.sb([128, 1], F32, "invf")
    sgn = P.sb([128, 1], F32, "sgn")
    P.dma(invf, I["c_invf"].v(I["c_invf"].ap))
    P.dma(sgn, I["c_sgn"].v(I["c_sgn"].ap))
    W = 1024
    pi_ = P.sb([128, W], I32, "posi")
    pf = P.sb([128, W], F32, "posf")
    tu = P.sb([128, W], F32, "tu")
    f = P.sb([128, W], F32, "f")
    tb = P.sb([128, W], F32, "tb")
    scr = frac_scratch(P, [128, W])
    for cq in range(S // W if MODE in ('all', 'B') else 0):
        P.dma(pi_, I["pos"].v(I["pos"].ap[:, cq * W:(cq + 1) * W].partition_broadcast(128)))
        P.copy(pf, pi_, eng="dve")
        P.ts(tu, pf, invf[:, 0:1], None, ALU.mult)
        for which, shift, dst in (("sin", 0.0, C.sinT), ("cos", 0.25, C.cosT)):
            frac_wrap(P, f, tu, shift, scr)
            P.act(tb, f, AF.Sin, scale=TWO_PI)
            if which == "sin":
                P.ts(tb, tb, sgn[:, 0:1], None, ALU.mult)
            P.dma(dst.v(dst.ap[:, cq * W:(cq + 1) * W], cq), tb)
    P.pop()


def frac_scratch(P, shape):
    return (P.sb(shape, I32, "fw_ki"), P.sb(shape, F32, "fw_kf"), P.sb(shape, F32, "fw_m"))


def frac_wrap(P, f, tu, shift, scr):
    ki, kf, m = scr
    if shift != 0.0:
        P.ts(f, tu, shift, None, ALU.add)
        src = f
    else:
        src = tu
    P.copy(ki, src, eng="dve")
    P.copy(kf, ki, eng="dve")
    P.tt(f, src, kf, ALU.subtract)
    P.ts(m, f, 0.5, None, ALU.is_gt)
    P.tt(f, f, m, ALU.subtract)
    P.ts(m, f, -0.5, None, ALU.is_lt)
    P.tt(f, f, m, ALU.add)


def phase_proj_ln(P, C, inT, K, W_ap, lnp_aps, h_src, gate_ap=None):
    P.push()
    KC = K // 128
    Wb = P.sb([128, KC, 1024], BF16, "Wb")
    load_w(P, Wb, W_ap, KC)
    Wg = None
    if gate_ap is not None:
        Wg = P.sb([128, KC, 1024], BF16, "Wg")
        load_w(P, Wg, gate_ap, KC)
    lnp = load_ln(P, C, *lnp_aps)
    ln_pools(P, C)
    blk = Pool(P, 2, [128, KC, 512], BF16, "inblk")
    sig = Pool(P, 2, [128, 1024], F32, "sig") if Wg is not None else None
    pend = None

    def prefetch(b):
        ib_ = blk.next()
        P.dma(ib_, inT.v(inT.ap.rearrange("(c p) t -> p c t", p=128)[:, :, b * 512:(b + 1) * 512], hT_keys(b)))
        return ib_, [load_h(P, C, b * 4 + t4, h_src) for t4 in range(4)]

    nxt = prefetch(0)
    for b in range(8):
        ib, hts_ = nxt
        if b < 7:
            nxt = prefetch(b + 1)
        for t4 in range(4):
            tt = b * 4 + t4
            par = (tt % 2) * 2 if Wg is None else 0
            sub = [C.PS[par], C.PS[par + 1]]
            for hf in range(2):
                for kc in range(KC):
                    P.mm(sub[hf], ib[:, kc, t4 * 128:(t4 + 1) * 128], Wb[:, kc, hf * 512:(hf + 1) * 512], start=(kc == 0), stop=(kc == KC - 1))
            if Wg is not None:
                gp = [C.PS[2], C.PS[3]]
                sg = sig.next()
                for hf in range(2):
                    for kc in range(KC):
                        P.mm(gp[hf], ib[:, kc, t4 * 128:(t4 + 1) * 128], Wg[:, kc, hf * 512:(hf + 1) * 512], start=(kc == 0), stop=(kc == KC - 1))
                    P.act(sg[:, hf * 512:(hf + 1) * 512], gp[hf], AF.Sigmoid)
                    P.tt(sg[:, hf * 512:(hf + 1) * 512], sub[hf], sg[:, hf * 512:(hf + 1) * 512], ALU.mult)
                sub = [sg[:, 0:512], sg[:, 512:1024]]
            ln_epilogue_b(P, C, pend)
            pend = ln_epilogue(P, C, lnp, tt, sub, hts_[t4])
    ln_epilogue_b(P, C, pend)
    P.pop()


def phase_ffn_up(P, C, wg_ap, wu_ap):
    P.push()
    Wg = P.sb([128, 8, FF], BF16, "Wg")
    Wu = P.sb([128, 8, FF], BF16, "Wu")
    load_w(P, Wg, wg_ap, 8)
    load_w(P, Wu, wu_ap, 8)
    blk = Pool(P, 2, [128, 8, 512], BF16, "hblk")
    ab = Pool(P, 2, [128, 22, 512], BF16, "ablk")
    sgp = Pool(P, 2, [128, 512], F32, "sg")
    def prefetch(b):
        ib_ = blk.next()
        P.dma(ib_, C.hT.v(C.hT.ap.rearrange("(c p) t -> p c t", p=128)[:, :, b * 512:(b + 1) * 512], hT_keys(b)))
        return ib_

    nxt = prefetch(0)
    for b in range(8):
        ib = nxt
        if b < 7:
            nxt = prefetch(b + 1)
        a = ab.next()
        for j in range(22):
            pg = C.PS[(j % 2) * 2]
            pu = C.PS[(j % 2) * 2 + 1]
            for kc in range(8):
                P.mm(pg, Wg[:, kc, j * 128:(j + 1) * 128], ib[:, kc, :], start=(kc == 0), stop=(kc == 7))
            for kc in range(8):
                P.mm(pu, Wu[:, kc, j * 128:(j + 1) * 128], ib[:, kc, :], start=(kc == 0), stop=(kc == 7))
            sg = sgp.next()
            P.act(sg, pg, AF.Silu)
            P.tt(a[:, j, :], pu, sg, ALU.mult)
        P.dma(C.aT.v(C.aT.ap.rearrange("(c p) t -> p c t", p=128)[:, :, b * 512:(b + 1) * 512], hT_keys(b)), a)
    P.pop()


def phase_cross(P, C, I, layer, lnp_aps, h_src):
    P.push()
    wq, wk, wv, wo = (I[n] for n in ("xq_w", "xk_w", "xv_w", "xo_w"))
    rows = lambda w: (lambda kc: w.v(w.ap[layer, kc * 128:(kc + 1) * 128, :]))
    Wq = P.sb([128, 8, 1024], BF16, "Wq")
    Wo = P.sb([128, 8, 1024], BF16, "Wo")
    Wk = P.sb([128, 8, 1024], BF16, "Wk")
    load_w(P, Wk, rows(wk), 8)
    mT = P.sb([128, 8, 256], BF16, "mT")
    P.dma(mT, C.memT.v(C.memT.ap.rearrange("(c p) t -> p c t", p=128), 0))
    kT = P.sb([128, 8, 256], BF16, "kT")
    import os
    if os.environ.get("CROSS_MODE", "all") == "pre0":
        dbg = P.sb([128, 1024], F32, "dbg")
        P.copy(dbg, Wk[:, 3, :])
        P.dma(C.out.v(C.out.ap[0:128, :], 0), dbg, is_out=True)
        P.pop()
        return
    for oc in range(8):
        ps = C.PS[oc % 2]
        for kc in range(8):
            P.mm(ps[:, 0:256], Wk[:, kc, oc * 128:(oc + 1) * 128], mT[:, kc, :], start=(kc == 0), stop=(kc == 7))
        P.copy(kT[:, oc, :], ps[:, 0:256], eng="act")
    load_w(P, Wk, rows(wv), 8)
    Vm = P.sb([128, 2, 1024], BF16, "Vm")
    for mc in range(2):
        for hf in range(2):
            ps = C.PS[hf]
            for kc in range(8):
                P.mm(ps, mT[:, kc, mc * 128:(mc + 1) * 128], Wk[:, kc, hf * 512:(hf + 1) * 512], start=(kc == 0), stop=(kc == 7))
            P.copy(Vm[:, mc, hf * 512:(hf + 1) * 512], ps, eng="act")
    import os
    CM = os.environ.get("CROSS_MODE", "all")
    if CM == "pre":
        P.pop()
        return
    load_w(P, Wq, rows(wq), 8)
    load_w(P, Wo, rows(wo), 8)
    lnp = load_ln(P, C, *lnp_aps)
    ln_pools(P, C)
    onesb = C.onesb
    blk = Pool(P, 2, [128, 8, 512], BF16, "hblk")
    qTp = Pool(P, 1, [128, 8, 512], BF16, "qT")
    pTp = Pool(P, 2, [128, 2, 512], BF16, "pT")
    rdp = Pool(P, 2, [128, 512], F32, "rden")
    oTp = Pool(P, 2, [128, 8, 512], BF16, "oT")
    pend = None

    def prefetch(b):
        ib_ = blk.next()
        P.dma(ib_, C.hT.v(C.hT.ap.rearrange("(c p) t -> p c t", p=128)[:, :, b * 512:(b + 1) * 512], hT_keys(b)))
        return ib_, [load_h(P, C, b * 4 + t4, h_src) for t4 in range(4)]

    nxt = prefetch(0)
    for b in range(8):
        ib, hts_ = nxt
        if b < 7:
            nxt = prefetch(b + 1)
        qT = qTp.next()
        for oc in range(8):
            ps = C.PS[2 + oc % 2]
            for kc in range(8):
                P.mm(ps, Wq[:, kc, oc * 128:(oc + 1) * 128], ib[:, kc, :], start=(kc == 0), stop=(kc == 7))
            P.copy(qT[:, oc, :], ps, eng="act")
        oT = oTp.next()
        if CM == "q":
            continue
        for hh in range(4):
            pT = pTp.next()
            for mc in range(2):
                ps = C.PS[2 + mc]
                for c in range(2):
                    P.mm(ps, kT[:, hh * 2 + c, mc * 128:(mc + 1) * 128], qT[:, hh * 2 + c, :], start=(c == 0), stop=(c == 1))
                P.act(pT[:, mc, :], ps, AF.Exp, scale=1.0 / 16.0)
            pd = C.PS[4]
            for mc in range(2):
                P.mm(pd, onesb, pT[:, mc, :], start=(mc == 0), stop=(mc == 1))
            rd = rdp.next()
            P.recip(rd, pd)
            for c in range(2):
                po = C.PS[5 + c]
                for mc in range(2):
                    P.mm(po, Vm[:, mc, hh * 256 + c * 128: hh * 256 + (c + 1) * 128], pT[:, mc, :], start=(mc == 0), stop=(mc == 1))
                P.tt(oT[:, hh * 2 + c, :], po, rd, ALU.mult)
        if CM == "attn":
            continue
        for t4 in range(4):
            tt = b * 4 + t4
            sub = [C.PS[0], C.PS[1]]
            for hf in range(2):
                for kc in range(8):
                    P.mm(sub[hf], oT[:, kc, t4 * 128:(t4 + 1) * 128], Wo[:, kc, hf * 512:(hf + 1) * 512], start=(kc == 0), stop=(kc == 7))
            ln_epilogue_b(P, C, pend)
            pend = ln_epilogue(P, C, lnp, tt, sub, hts_[t4])
    ln_epilogue_b(P, C, pend)
    P.pop()


WEIGHT_SPECS = [
    ("hyb_w_in", [2, 1024, 3592]), ("dn_conv_w", [2, 4, 1536]), ("hyb_w_out", [2, 1024, 1024]),
    ("s5_glu_wo", [2, 1024, 1024]), ("s5_glu_wg", [2, 1024, 1024]),
    ("ln_mix_g", [4, 1024]), ("ln_mix_b", [4, 1024]),
    ("xq_w", [4, 1024, 1024]), ("xk_w", [4, 1024, 1024]), ("xv_w", [4, 1024, 1024]), ("xo_w", [4, 1024, 1024]),
    ("ln_x_g", [4, 1024]), ("ln_x_b", [4, 1024]),
    ("ffn_wg", [4, 1024, 2816]), ("ffn_wu", [4, 1024, 2816]), ("ffn_wd", [4, 2816, 1024]),
    ("ln_ffn_g", [4, 1024]), ("ln_ffn_b", [4, 1024]),
]


def build_program(plan):
    nc = bass.Bass("TRN2", target_bir_lowering=False)
    P = Prog(nc)
    C = Ctx()
    I = {}

    def inp(name, shape, dt=F32):
        I[name] = P.dram(name, shape, dt, kind="ExternalInput")

    inp("x", [S, D])
    inp("mem", [256, D])
    inp("pos", [1, S], I32)
    for n, sh in WEIGHT_SPECS:
        inp(n, sh)
    hc = host_consts()
    for n, a in hc.items():
        inp(n, list(a.shape))
    for l in range(2):
        inp("s5col%d" % l, [128, 3, 32])
        inp("s5row%d" % l, [3, 4096])
        for nm in ("bre", "bim", "cre", "cim"):
            inp("s5%s%d" % (nm, l), [128, 4096])
        inp("s5d%d" % l, [128, 8])
        inp("dnp%d" % l, [1, 4 + 4 + 128])
        inp("dncw%d" % l, [128, 12, 4])
    C.out = P.dram("out", [S, D], F32, kind="ExternalOutput")
    C.hT = P.dram("hT", [D, S], BF16)
    C.mixT = P.dram("mixT", [D, S], BF16)
    C.aT = P.dram("aT", [FF, S], BF16)
    C.memT = P.dram("memT", [D, 256], BF16)
    C.cosT = P.dram("cosT", [128, S], F32)
    C.sinT = P.dram("sinT", [128, S], F32)
    C.vd = P.dram("vd", [S, 512], BF16)
    C.PS = [P.ps([128, 512], F32, "ps%d" % i) for i in range(7)]
    C.pst = P.ps([128, 1024], BF16, "pst")
    idf = P.sb([128, 128], F32, "idf")
    P.dma(idf, I["c_ident"].v(I["c_ident"].ap))
    C.identb = P.sb([128, 128], BF16, "identb")
    P.copy(C.identb, idf)
    C.identf = idf
    C.onesb = P.sb([128, 128], BF16, "onesb")
    P.memset(C.onesb, 1.0)

    bvec = lambda nm, l: I[nm].v(I[nm].ap[l:l + 1, :].partition_broadcast(128))
    h_src = I["x"]
    for ph in plan:
        kind = ph[0]
        if kind == "prep":
            phase_prep(P, C, I)
        elif kind == "cross":
            l = ph[1]
            phase_cross(P, C, I, l, (bvec("ln_x_g", l), bvec("ln_x_b", l)), h_src)
            h_src = C.out
        elif kind == "ffn":
            l = ph[1]
            wg, wu, wd = I["ffn_wg"], I["ffn_wu"], I["ffn_wd"]
            phase_ffn_up(P, C, lambda kc: wg.v(wg.ap[l, kc * 128:(kc + 1) * 128, :]), lambda kc: wu.v(wu.ap[l, kc * 128:(kc + 1) * 128, :]))
            phase_proj_ln(P, C, C.aT, FF, lambda kc: wd.v(wd.ap[l, kc * 128:(kc + 1) * 128, :]),
                          (bvec("ln_ffn_g", l), bvec("ln_ffn_b", l)), h_src)
            h_src = C.out
        elif kind == "s5":
            l = ph[1]
            i = l // 2
            phase_s5(P, C, I, i)
            wo, wg = I["s5_glu_wo"], I["s5_glu_wg"]
            phase_proj_ln(P, C, C.mixT, D, lambda kc: wo.v(wo.ap[i, kc * 128:(kc + 1) * 128, :]),
                          (bvec("ln_mix_g", l), bvec("ln_mix_b", l)), h_src,
                          gate_ap=lambda kc: wg.v(wg.ap[i, kc * 128:(kc + 1) * 128, :]))
            h_src = C.out
        elif kind == "hyb":
            l = ph[1]
            i = l // 2
            phase_dilated(P, C, I, i)
            phase_gdn(P, C, I, i)
            wo = I["hyb_w_out"]
            phase_proj_ln(P, C, C.mixT, D, lambda kc: wo.v(wo.ap[i, kc * 128:(kc + 1) * 128, :]),
                          (bvec("ln_mix_g", l), bvec("ln_mix_b", l)), h_src)
            h_src = C.out
    P.emit()
    return nc


def make_in_maps(inputs, cores):
    hc = host_consts()
    hl = host_layer_inputs(inputs)
    shared = {n: np.ascontiguousarray(inputs[n], np.float32) for n, _ in WEIGHT_SPECS}
    shared.update(hc)
    shared.update(hl)
    for l in range(2):
        shared["dnp%d" % l] = np.concatenate([inputs["dn_a_log"][l], inputs["dn_dt_bias"][l], inputs["dn_norm_g"][l]]).astype(np.float32).reshape(1, 136)
    maps = []
    for b in cores:
        m = dict(shared)
        m["x"] = np.ascontiguousarray(inputs["x"][b], np.float32)
        m["mem"] = np.ascontiguousarray(inputs["mem"][b], np.float32)
        m["pos"] = np.ascontiguousarray(inputs["positions"][b].reshape(1, S), np.int32)
        maps.append(m)
    return maps


FULL_PLAN = [("prep",)]
for _l in range(4):
    FULL_PLAN += [("hyb" if _l % 2 == 0 else "s5", _l), ("cross", _l), ("ffn", _l)]


def kernel(**inputs):
    inputs = {k: np.asarray(v) for k, v in inputs.items()}
    nc = build_program(FULL_PLAN)
    maps = make_in_maps(inputs, list(range(8)))
    res = run_bass_kernel_spmd(nc, maps, core_ids=list(range(8)))
    return np.stack([r["out"] for r in res.results], 0).astype(np.float32)


def b3(v, shape, axis):
    return V(v.ap.unsqueeze(axis).to_broadcast(shape), v.bufs)


def phase_s5(P, C, I, l):
    P.push()
    f32t = lambda shape, n: P.sb(shape, F32, n)
    col = f32t([128, 3, 32], "col")
    P.dma(col, I["s5col%d" % l].v(I["s5col%d" % l].ap))
    dtc = f32t([128, 32], "dtc")
    P.act(dtc, col[:, 2, :], AF.Exp)
    t0 = f32t([128, 32], "t0")
    P.tt(t0, col[:, 0, :], dtc, ALU.mult)
    rho = f32t([128, 32], "rho")
    P.act(rho, t0, AF.Exp)
    phi = f32t([128, 32], "phi")
    P.tt(phi, col[:, 1, :], dtc, ALU.mult)
    P.ts(phi, phi, 1.0 / TWO_PI, None, ALU.mult)
    iota = f32t([128, 64], "iota")
    P.dma(iota, I["c_iota"].v(I["c_iota"].ap))
    dcol = f32t([128, 8], "dcol")
    P.dma(dcol, I["s5d%d" % l].v(I["s5d%d" % l].ap))
    E = {k: f32t([128, 2048], "F" + k) for k in ("lo", "hi")}
    Bb = {k: P.sb([128, 4096], BF16, "Bb" + k) for k in ("re", "im")}
    CreB = P.sb([128, 4096], BF16, "CreB")
    CreN = P.sb([128, 4096], BF16, "CreN")
    CimN = P.sb([128, 4096], BF16, "CimN")
    P.dma(CreB, I["s5cre%d" % l].v(I["s5cre%d" % l].ap), eng="pool")
    P.dma(CimN, I["s5cim%d" % l].v(I["s5cim%d" % l].ap), eng="pool")
    P.ts(CreN, CreB, -1.0, None, ALU.mult)
    P.ts(CimN, CimN, -1.0, None, ALU.mult)
    P.push()
    scr = frac_scratch(P, [128, 2048])
    tu = f32t([128, 2048], "tu")
    tu3 = tu.re("p (a b) -> p a b", b=64)
    iota3 = b3(iota, [128, 32, 64], 1)
    P.tt(tu3, b3(phi, [128, 32, 64], 2), iota3, ALU.mult)
    frac_wrap(P, E["lo"], tu, 0.0, scr)
    p64 = f32t([128, 32], "p64")
    P.ts(p64, phi, 64.0, None, ALU.mult)
    scr2 = frac_scratch(P, [128, 32])
    f64 = f32t([128, 32], "f64")
    frac_wrap(P, f64, p64, 0.0, scr2)
    P.tt(tu3, b3(f64, [128, 32, 64], 2), iota3, ALU.mult)
    frac_wrap(P, E["hi"], tu, 0.0, scr)
    P.pop()
    P.push()
    row = I["s5row%d" % l]
    T = {k: f32t([128, 1024], "r" + k) for k in ("are", "aim", "dt", "er", "tu", "f", "sn", "cs", "abr", "abi", "nr", "den", "cr", "ci", "Bre", "Bim", "x1", "x2")}
    scr = frac_scratch(P, [128, 1024])
    for qq in range(4):
        cs_ = slice(qq * 1024, (qq + 1) * 1024)
        P.dma(T["are"], row.v(row.ap[0:1, cs_].partition_broadcast(128)))
        P.dma(T["aim"], row.v(row.ap[1:2, cs_].partition_broadcast(128)))
        P.dma(T["dt"], row.v(row.ap[2:3, cs_].partition_broadcast(128)))
        P.dma(T["Bre"], I["s5bre%d" % l].v(I["s5bre%d" % l].ap[:, cs_]))
        P.dma(T["Bim"], I["s5bim%d" % l].v(I["s5bim%d" % l].ap[:, cs_]))
        P.act(T["dt"], T["dt"], AF.Exp)
        P.tt(T["er"], T["are"], T["dt"], ALU.mult)
        P.act(T["er"], T["er"], AF.Exp)
        P.tt(T["tu"], T["aim"], T["dt"], ALU.mult)
        P.ts(T["tu"], T["tu"], 1.0 / TWO_PI, None, ALU.mult)
        frac_wrap(P, T["f"], T["tu"], 0.0, scr)
        P.act(T["sn"], T["f"], AF.Sin, scale=TWO_PI)
        frac_wrap(P, T["f"], T["tu"], 0.25, scr)
        P.act(T["cs"], T["f"], AF.Sin, scale=TWO_PI)
        P.tt(T["abr"], T["er"], T["cs"], ALU.mult)
        P.tt(T["abi"], T["er"], T["sn"], ALU.mult)
        P.ts(T["nr"], T["abr"], -1.0, None, ALU.add)
        P.tt(T["den"], T["are"], T["are"], ALU.mult)
        P.tt(T["x1"], T["aim"], T["aim"], ALU.mult)
        P.tt(T["den"], T["den"], T["x1"], ALU.add)
        P.recip(T["den"], T["den"])
        P.tt(T["x1"], T["nr"], T["are"], ALU.mult)
        P.tt(T["x2"], T["abi"], T["aim"], ALU.mult)
        P.tt(T["cr"], T["x1"], T["x2"], ALU.add)
        P.tt(T["cr"], T["cr"], T["den"], ALU.mult)
        P.tt(T["x1"], T["abi"], T["are"], ALU.mult)
        P.tt(T["x2"], T["nr"], T["aim"], ALU.mult)
        P.tt(T["ci"], T["x1"], T["x2"], ALU.subtract)
        P.tt(T["ci"], T["ci"], T["den"], ALU.mult)
        P.tt(T["x1"], T["cr"], T["Bre"], ALU.mult)
        P.tt(T["x2"], T["ci"], T["Bim"], ALU.mult)
        P.tt(Bb["re"][:, cs_], T["x1"], T["x2"], ALU.subtract)
        P.tt(T["x1"], T["cr"], T["Bim"], ALU.mult)
        P.tt(T["x2"], T["ci"], T["Bre"], ALU.mult)
        P.tt(Bb["im"][:, cs_], T["x1"], T["x2"], ALU.add)
    P.pop()
    BW = 512
    NBLK = S // BW
    HC = BW // 64
    ones = f32t([128, BW], "ones")
    P.memset(ones, 1.0)
    halfpi = f32t([128, 1], "halfpi")
    P.memset(halfpi, math.pi / 2)
    onec = f32t([128, 1], "onec")
    P.memset(onec, 1.0)
    stre = [f32t([128, 1], "stre%d" % i) for i in range(32)]
    stim = [f32t([128, 1], "stim%d" % i) for i in range(32)]
    for t_ in stre + stim:
        P.memset(t_, 0.0)
    uTp = Pool(P, 2, [128, S], BF16, "uT")
    RHO = [[f32t([128, BW], "RHO%d_%d" % (par, q)) for q in range(4)] for par in range(2)]
    mkp = lambda n, dt=F32, k=4: Pool(P, k, [128, BW], dt, n)
    p_fs, p_s1, p_sq, p_ei = mkp("fs", F32, 2), mkp("s1", F32, 2), mkp("sq"), mkp("ei")
    p_m = [mkp("m%d" % i, F32, 3) for i in range(4)]
    p_xr, p_xi = mkp("xr", F32, 3), mkp("xi", F32, 3)
    p_pp = [mkp("pp%d" % i, BF16, 3) for i in range(4)]
    hidp = Pool(P, 2, [128, BW], BF16, "hid")
    vtp = Pool(P, 2, [128, BW], F32, "vt")
    allk = tuple(("t", t) for t in range(NT))
    v3 = lambda t: t.re("p (a b) -> p a b", b=64)
    iters = [(cc, blk, q) for cc in range(8) for blk in range(NBLK) for q in range(4)]
    ctx = {}
    uTs = {}
    bub = [0]

    def prologue(n):
        cc, blk, q = iters[n]
        ctx[n] = {}
        if blk == 0 and q == 0:
            uTs[cc] = uTp.next()
            P.dma(uTs[cc], C.hT.v(C.hT.ap[cc * 128:(cc + 1) * 128, :], allk))
            for q2 in range(4):
                P.ts(RHO[cc % 2][q2], ones, rho[:, 4 * cc + q2:4 * cc + q2 + 1], None, ALU.mult)

    def sA(n):
        cc, blk, q = iters[n]
        pair = 4 * cc + q
        d = ctx[n]
        hs = slice(pair * 64 + blk * HC, pair * 64 + (blk + 1) * HC)
        ls = slice(pair * 64, pair * 64 + 64)
        d["fs"], d["s1"], d["sq"], d["ei"] = p_fs.next(), p_s1.next(), p_sq.next(), p_ei.next()
        P.tt(v3(d["fs"]), b3(E["hi"][:, hs], [128, HC, 64], 2), b3(E["lo"][:, ls], [128, HC, 64], 1), ALU.add, eng="pool")

    def sD(n):
        d = ctx[n]
        fs, s1, sq = d["fs"], d["s1"], d["sq"]
        P.act(s1, fs, AF.Sin, scale=math.pi)
        P.act(fs, fs, AF.Abs)
        P.act(fs, fs, AF.Sin, scale=-math.pi, bias=halfpi[:, 0:1])
        P.act(sq, s1, AF.Square)
        P.act(sq, sq, AF.Identity, scale=-2.0, bias=onec[:, 0:1])

    def sF(n):
        d = ctx[n]
        P.stt(d["ei"], d["s1"], 2.0, d["fs"], ALU.mult, ALU.mult)
        d["Ere"], d["Eim"] = d["sq"], d["ei"]

    def sB(n):
        cc, blk, q = iters[n]
        pair = 4 * cc + q
        d = ctx[n]
        tk = slice(blk * BW, (blk + 1) * BW)
        pc = slice(pair * 128, (pair + 1) * 128)
        d["bre"], d["bim"] = C.PS[bub[0]], C.PS[bub[0] + 1]
        bub[0] = (bub[0] + 2) % 4
        P.mm(d["bre"], Bb["re"][:, pc], uTs[cc][:, tk])
        P.mm(d["bim"], Bb["im"][:, pc], uTs[cc][:, tk])

    def sE(n):
        d = ctx[n]
        m = [p.next() for p in p_m]
        bre, bim = d["bre"], d["bim"]
        P.tt(m[0], bre, d["Ere"], ALU.mult)
        P.tt(m[1], bim, d["Eim"], ALU.mult)
        P.tt(m[2], bim, d["Ere"], ALU.mult)
        P.tt(m[3], bre, d["Eim"], ALU.mult)
        P.tt(m[0], m[0], m[1], ALU.add)
        P.tt(m[2], m[2], m[3], ALU.subtract)
        d["m"] = m

    def sC(n):
        cc, blk, q = iters[n]
        pair = 4 * cc + q
        d = ctx[n]
        m = d["m"]
        xr, xi = p_xr.next(), p_xi.next()
        P.scan(xr, RHO[cc % 2][q], m[0], stre[pair], ALU.mult, ALU.add)
        P.scan(xi, RHO[cc % 2][q], m[2], stim[pair], ALU.mult, ALU.add)
        P.copy(stre[pair], xr[:, BW - 1:BW], eng="act")
        P.copy(stim[pair], xi[:, BW - 1:BW], eng="act")
        pp = d["pp"] = [p.next() for p in p_pp]
        P.tt(pp[0], d["Ere"], xr, ALU.mult)
        P.tt(pp[1], d["Eim"], xi, ALU.mult, eng="pool")
        P.tt(pp[2], d["Ere"], xi, ALU.mult, eng="pool")
        P.tt(pp[3], d["Eim"], xr, ALU.mult, eng="pool")

    def sG(n):
        cc, blk, q = iters[n]
        pair = 4 * cc + q
        d = ctx.pop(n)
        uT = uTs[cc]
        tk = slice(blk * BW, (blk + 1) * BW)
        pc = slice(pair * 128, (pair + 1) * 128)
        yps = C.PS[4 + blk % 2]
        pp = d["pp"]
        for k, (W_, p_) in enumerate(((CreB, pp[0]), (CreN, pp[1]), (CimN, pp[2]), (CimN, pp[3]))):
            P.mm(yps, W_[:, pc], p_, start=(q == 0 and k == 0), stop=(q == 3 and k == 3))
        if q == 3:
            vt = vtp.next()
            P.stt(vt, uT[:, tk], dcol[:, cc:cc + 1], yps, ALU.mult, ALU.add)
            hid = hidp.next()
            P.act(hid, vt, AF.Gelu)
            P.dma(C.mixT.v(C.mixT.ap[cc * 128:(cc + 1) * 128, tk], hT_keys(blk)), hid)

    N_ = len(iters)
    ok = lambda k: 0 <= k < N_
    for n in range(N_ + 2):
        if ok(n):
            prologue(n)
            sA(n)
        if ok(n - 1):
            sB(n - 1)
        if ok(n - 2):
            sC(n - 2)
        if ok(n):
            sD(n)
        if ok(n - 1):
            sE(n - 1)
        if ok(n):
            sF(n)
        if ok(n - 2):
            sG(n - 2)
    P.pop()


def _zero_mix_rows(P, C, r0, r1):
    P.push()
    z = P.sb([128, 1024], BF16, "zmix")
    P.memset(z, 0.0)
    for c in range(r0 // 128, r1 // 128):
        for blk in range(4):
            P.dma(C.mixT.v(C.mixT.ap[c * 128:(c + 1) * 128, blk * 1024:(blk + 1) * 1024],
                           tuple(("t", t) for t in range(blk * 8, blk * 8 + 8))), z)
    P.pop()


def phase_dilated(P, C, I, i):
    w_in = I["hyb_w_in"]
    w3 = w_in.ap[i].rearrange("(kc p) n -> p kc n", p=128)
    hT3 = C.hT.ap.rearrange("(c p) t -> p c t", p=128)
    QOFF, KOFF, VOFF = 2056, 2568, 3080
    P.push()
    Wv = P.sb([128, 8, 512], BF16, "Wv")
    P.dma(Wv, w_in.v(w3[:, :, VOFF:VOFF + 512]), eng="pool")
    blkp = Pool(P, 2, [128, 8, 512], BF16, "hblk")
    vtp = Pool(P, 2, [128, 512], BF16, "vt")
    for b in range(8):
        ib = blkp.next()
        P.dma(ib, C.hT.v(hT3[:, :, b * 512:(b + 1) * 512], hT_keys(b)))
        for t4 in range(4):
            tt = b * 4 + t4
            ps = C.PS[t4 % 2]
            for kc in range(8):
                P.mm(ps, ib[:, kc, t4 * 128:(t4 + 1) * 128], Wv[:, kc, :], start=(kc == 0), stop=(kc == 7))
            vt = vtp.next()
            P.copy(vt, ps, eng="act")
            P.dma(C.vd.v(C.vd.ap[tt * 128:(tt + 1) * 128, :], 0), vt)
    P.pop()
    import os
    DM = os.environ.get("DIL_MODE", "all")
    if DM == "v":
        return
    P.push()
    COS = P.sb([128, S], F32, "COS")
    SIN = P.sb([128, S], F32, "SIN")
    P.dma(COS, C.cosT.v(C.cosT.ap, (0, 1, 2, 3)))
    P.dma(SIN, C.sinT.v(C.sinT.ap, (0, 1, 2, 3)))
    dmask = P.sb([128, 256], F32, "dmask")
    P.dma(dmask, I["c_dmask"].v(I["c_dmask"].ap))
    sel = P.sb([128, 64], F32, "sel")
    P.dma(sel, I["c_sel"].v(I["c_sel"].ap))
    Wt = {k: P.sb([128, 8, 128], BF16, "W" + k) for k in ("q", "qs", "k", "ks")}
    blkp = Pool(P, 2, [128, 8, 512], BF16, "hblk")
    RT = {k: P.sb([128, S], BF16, "R" + k) for k in ("q", "k")}
    CM = {(k, r): P.sb([128, S], BF16, "C%s%d" % (k, r)) for k in ("q", "k") for r in (4, 16)}
    OT = [P.sb([128, S], F32, "OT%d" % h) for h in range(2)]
    t1p = Pool(P, 2, [128, 512], F32, "t1")
    t2p = Pool(P, 2, [128, 512], F32, "t2")
    vap = Pool(P, 4, [128, 8, 128], BF16, "vaug")
    for va in vap.tiles:
        P.memset(va, 1.0)
    pep = Pool(P, 3, [128, 256], BF16, "pe")
    ptp = Pool(P, 6, [128, 256], BF16, "pt")
    dnegf = P.sb([128, 256], F32, "dnegf")
    P.dma(dnegf, I["c_dneg"].v(I["c_dneg"].ap))
    dnegb = P.sb([128, 256], BF16, "dnegb")
    P.copy(dnegb, dnegf)
    sbank = [0]
    rdp = Pool(P, 2, [64, 512], F32, "rd")
    onp = Pool(P, 2, [64, 512], BF16, "on")
    slots = [[V(C.PS[3 + 2 * h + sidx].ap[:, 0:128], C.PS[3 + 2 * h + sidx].bufs) for sidx in range(2)] for h in range(2)]
    for hp in range(4):
        for k, off in (("q", QOFF), ("k", KOFF)):
            base = off + hp * 128
            P.dma(Wt[k], w_in.v(w3[:, :, base:base + 128]), eng="pool")
            for hh in range(2):
                P.dma(Wt[k + "s"][:, :, hh * 64:hh * 64 + 32], w_in.v(w3[:, :, base + hh * 64 + 32:base + hh * 64 + 64]), eng="pool")
                P.dma(Wt[k + "s"][:, :, hh * 64 + 32:hh * 64 + 64], w_in.v(w3[:, :, base + hh * 64:base + hh * 64 + 32]), eng="pool")
        for b in range(8):
            ib = blkp.next()
            P.dma(ib, C.hT.v(hT3[:, :, b * 512:(b + 1) * 512], hT_keys(b)))
            cs_ = slice(b * 512, (b + 1) * 512)
            for k in ("q", "k"):
                for kc in range(8):
                    P.mm(C.PS[0], Wt[k][:, kc, :], ib[:, kc, :], start=(kc == 0), stop=(kc == 7))
                for kc in range(8):
                    P.mm(C.PS[1], Wt[k + "s"][:, kc, :], ib[:, kc, :], start=(kc == 0), stop=(kc == 7))
                t1 = t1p.next()
                t2 = t2p.next()
                P.tt(t1, C.PS[0], COS[:, cs_], ALU.mult)
                P.tt(t2, C.PS[1], SIN[:, cs_], ALU.mult)
                P.tt(RT[k][:, cs_], t1, t2, ALU.add, eng="pool")
        if DM == "proj":
            continue
        for k in ("q", "k"):
            for r in (4, 16):
                P.copy(CM[(k, r)].re("p (r u) -> p r u", r=r), RT[k].re("p (u r) -> p r u", r=r), eng="pool")
        if DM == "cm":
            continue
        its = []
        for bi, r in enumerate((1, 4, 16)):
            L = S // r
            nb_ = L // 128
            for rho in range(r):
                for m in range(nb_):
                    for hh in range(2):
                        its.append((bi, r, rho, m, hh, nb_, L))
        ictx = {}

        def st_S(n):
            bi, r, rho, m, hh, nb_, L = its[n]
            d = ictx[n] = {}
            Qs = RT["q"] if r == 1 else CM[("q", r)]
            Ks = RT["k"] if r == 1 else CM[("k", r)]
            if hh == 0:
                va = vap.next()
                st = (128 * m) * r + rho
                P.dma(va[:, :, 0:64], C.vd.v(C.vd.ap[st:st + 127 * r + 1:r, :].rearrange("t (h d) -> t h d", h=8), 0))
                ictx["va"] = va
            d["va"] = ictx["va"]
            nq = d["nq"] = 256 if m < nb_ - 1 else 128
            c0 = rho * L + m * 128
            hb = hh * 64
            sps = d["sps"] = C.PS[sbank[0]]
            sbank[0] = (sbank[0] + 1) % 3
            P.mm(sps[:, 0:nq], Ks[hb:hb + 64, c0:c0 + 128], Qs[hb:hb + 64, c0:c0 + nq], start=True, stop=False)
            P.mm(sps[:, 0:nq], C.identb, dnegb[:, 0:nq], start=False, stop=True)

        def st_X(n):
            d = ictx[n]
            nq = d["nq"]
            pt = d["pt"] = ptp.next()
            P.act(pt[:, 0:nq], d["sps"][:, 0:nq], AF.Exp, scale=0.125)

        def st_V(n):
            bi, r, rho, m, hh, nb_, L = its[n]
            d = ictx.pop(n)
            pt, va, nq = d["pt"], d["va"], d["nq"]
            head = hp * 2 + hh
            sl = slots[hh][m % 2]
            P.mm(sl, va[:, head, :], pt[:, 0:128], start=(m == 0), stop=True)
            dst = OT[hh].re("p (u r) -> p r u", r=r)[:, rho, m * 128:(m + 1) * 128] if r > 1 else OT[hh][:, m * 128:(m + 1) * 128]
            if bi == 0:
                P.copy(dst, sl, eng="dve")
            else:
                P.tt(dst, sl, dst, ALU.add)
            if nq == 256:
                sl2 = slots[hh][(m + 1) % 2]
                P.mm(sl2, va[:, head, :], pt[:, 128:256], start=True, stop=False)

        NI = len(its)
        for n in range(NI + 2):
            if n < NI:
                st_S(n)
            if 0 <= n - 1 < NI:
                st_X(n - 1)
            if 0 <= n - 2 < NI:
                st_V(n - 2)
        if DM in ("br0a", "br0b", "br0b1", "br0b2"):
            continue
        for hh in range(2):
            head = hp * 2 + hh
            for b in range(8):
                cs_ = slice(b * 512, (b + 1) * 512)
                pd = C.PS[hh]
                P.mm(pd[0:64, :], sel, OT[hh][:, cs_])
                rd = rdp.next()
                P.recip(rd, pd[0:64, :])
                on = onp.next()
                P.tt(on, OT[hh][0:64, cs_], rd, ALU.mult)
                r0 = 512 + head * 64
                P.dma(C.mixT.v(C.mixT.ap[r0:r0 + 64, cs_], hT_keys(b)), on)
    P.pop()


def phase_gdn(P, C, I, i):
    w_in = I["hyb_w_in"]
    w3 = w_in.ap[i].rearrange("(kc p) n -> p kc n", p=128)
    hT3 = C.hT.ap.rearrange("(c p) t -> p c t", p=128)
    dnp = I["dnp%d" % i]
    P.push()
    f32t = lambda shape, n: P.sb(shape, F32, n)
    cst = {}
    for nm in ("c_tri", "c_ones", "c_negU", "c_posL", "c_strU", "c_strL"):
        cst[nm] = f32t([128, 128], nm)
        P.dma(cst[nm], I[nm].v(I[nm].ap))
    TRI, ONESf, NEGU, POSL, STRU, STRL = (cst[n] for n in ("c_tri", "c_ones", "c_negU", "c_posL", "c_strU", "c_strL"))
    IDF = C.identf
    onecol = f32t([128, 1], "onecol")
    P.memset(onecol, 1.0)
    NG = f32t([128, 128], "NG")
    P.dma(NG, dnp.v(dnp.ap[0:1, 8:136].partition_broadcast(128)))
    ab = f32t([128, 8], "ab")
    P.dma(ab, dnp.v(dnp.ap[0:1, 0:8].partition_broadcast(128)))
    nA = f32t([128, 4], "nA")
    P.act(nA, ab[:, 0:4], AF.Exp)
    P.ts(nA, nA, -1.0, None, ALU.mult)
    cw = f32t([128, 12, 4], "cw")
    P.dma(cw, I["dncw%d" % i].v(I["dncw%d" % i].ap))
    bank = [0]

    def nb():
        bank[0] = (bank[0] + 1) % 7
        return C.PS[bank[0]]

    Wg8 = P.sb([128, 8, 8], BF16, "Wg8")
    P.dma(Wg8, w_in.v(w3[:, :, 2048:2056]), eng="pool")
    blkp = Pool(P, 2, [128, 8, 512], BF16, "hblk")
    GR = f32t([128, 32, 8], "GR")
    for b in range(8):
        ib = blkp.next()
        P.dma(ib, C.hT.v(hT3[:, :, b * 512:(b + 1) * 512], hT_keys(b)))
        for t4 in range(4):
            ps = nb()
            for kc in range(8):
                P.mm(ps[:, 0:8], ib[:, kc, t4 * 128:(t4 + 1) * 128], Wg8[:, kc, :], start=(kc == 0), stop=(kc == 7))
            P.copy(GR[:, b * 4 + t4, :], ps[:, 0:8], eng="act")
    BETA = f32t([128, 128], "BETA")
    NBETA = f32t([128, 128], "NBETA")
    Gt = f32t([128, 128], "Gt")
    v34 = lambda t: t.re("p (c h) -> p c h", h=4)
    P.copy(v34(BETA), GR[:, :, 0:4], eng="dve")
    P.act(BETA, BETA, AF.Sigmoid)
    P.ts(NBETA, BETA, -1.0, None, ALU.mult)
    P.tt(v34(Gt), GR[:, :, 4:8], b3(ab[:, 4:8], [128, 32, 4], 1), ALU.add)
    P.act(Gt, Gt, AF.Exp)
    P.act(Gt, Gt, AF.Ln, bias=onecol[:, 0:1])
    P.tt(v34(Gt), v34(Gt), b3(nA, [128, 32, 4], 1), ALU.mult)
    GC, GL, EG, EDL, EGL = (f32t([128, 128], n) for n in ("GC", "GL", "EG", "EDL", "EGL"))
    ps = nb()
    P.mm(ps[:, 0:128], TRI, Gt)
    P.copy(GC, ps[:, 0:128], eng="dve")
    ps = nb()
    P.mm(ps[:, 0:128], ONESf, Gt)
    P.copy(GL, ps[:, 0:128], eng="dve")
    P.act(EG, GC, AF.Exp)
    P.act(EGL, GL, AF.Exp)
    P.tt(EDL, GL, GC, ALU.subtract)
    P.act(EDL, EDL, AF.Exp)
    import os
    GM = os.environ.get("GDN_MODE", "all")
    if GM == "gates":
        P.pop()
        return
    Wh = P.sb([128, 8, 3, 128], BF16, "Wh")
    Wz = P.sb([128, 8, 128], BF16, "Wz")
    X = [f32t([128, S + 3], "X%d" % x) for x in range(3)]
    for x in range(3):
        P.memset(X[x][:, 0:3], 0.0)
    sq = P.sb([128, S], BF16, "sq")
    Ycv = f32t([128, S], "Ycv")
    QT = P.sb([128, S], BF16, "QT")
    KT = P.sb([128, S], BF16, "KT")
    VT = P.sb([128, S], BF16, "VT")
    KTM = P.sb([128, 32, 128], BF16, "KTM")
    VTM = P.sb([128, 32, 128], BF16, "VTM")
    rnp = Pool(P, 2, [128, 512], F32, "rn")
    Sst = f32t([128, 128], "Sst")
    Sb = P.sb([128, 128], BF16, "Sb")
    NBATCH = 4
    mk = lambda n, dt=F32, k=NBATCH: Pool(P, k, [128, 128], dt, n)
    p_gb, p_bb, p_DT, p_DL, p_tmp, p_tmp2 = mk("gb"), mk("bb"), mk("DT"), mk("DL"), mk("tmpa"), mk("tmpb")
    p_A, p_AT, p_P = mk("A", F32, 2 * NBATCH), mk("AT", F32, 2 * NBATCH), mk("P", F32, 2 * NBATCH)
    p_intra, p_Pb, p_ub, p_Keg, p_WT, p_Kdec, p_qdT, p_EGr = (mk("intra", BF16), mk("Pb", BF16), mk("ub"), mk("Keg", BF16),
                                                              mk("WT", BF16), mk("Kdec", BF16), mk("qdT", BF16), mk("EGr"))
    p_vn, p_zs, p_t, p_ao, p_aoT = mk("vn", BF16, 2), mk("zs", F32, 2), mk("tt", F32, 2), mk("ao", BF16, 2), mk("aoT", BF16, 2)
    p_st = Pool(P, 2, [128, 6], F32, "gst")
    p_mv = Pool(P, 2, [128, 8], F32, "gmv")
    hcp = Pool(P, 2, [128, 8, 128], BF16, "hchunk")
    for hd in range(4):
        for x in range(3):
            P.dma(Wh[:, :, x, :], w_in.v(w3[:, :, x * 512 + hd * 128: x * 512 + (hd + 1) * 128]), eng="pool")
        P.dma(Wz, w_in.v(w3[:, :, 1536 + hd * 128:1536 + (hd + 1) * 128]), eng="pool")
        for b in range(8):
            ib = blkp.next()
            P.dma(ib, C.hT.v(hT3[:, :, b * 512:(b + 1) * 512], hT_keys(b)))
            for x in range(3):
                ps = nb()
                for kc in range(8):
                    P.mm(ps, Wh[:, kc, x, :], ib[:, kc, :], start=(kc == 0), stop=(kc == 7))
                P.copy(X[x][:, 3 + b * 512:3 + (b + 1) * 512], ps, eng="act")
        for x in range(3):
            Y = X[x]
            ch = x * 4 + hd
            Yc = Ycv
            P.ts(Yc, Y[:, 0:S], cw[:, ch, 0:1], None, ALU.mult)
            for j in range(1, 4):
                P.stt(Yc, Y[:, j:S + j], cw[:, ch, j:j + 1], Yc, ALU.mult, ALU.add)
            P.act(Yc, Yc, AF.Silu)
            if x < 2:
                P.tt(sq, Yc, Yc, ALU.mult, eng="pool")
                dst = QT if x == 0 else KT
                for b in range(8):
                    cs_ = slice(b * 512, (b + 1) * 512)
                    ps = nb()
                    P.mm(ps, C.onesb, sq[:, cs_])
                    rn = rnp.next()
                    P.ts(rn, ps, RMS_EPS, None, ALU.add)
                    P.act(rn, rn, AF.Sqrt)
                    P.recip(rn, rn)
                    if x == 0:
                        P.stt(dst[:, cs_], Yc[:, cs_], 128 ** -0.5, rn, ALU.mult, ALU.mult)
                    else:
                        P.tt(dst[:, cs_], Yc[:, cs_], rn, ALU.mult)
            else:
                P.copy(VT, Yc, eng="act")
        for src, dstm in ((KT, KTM), (VT, VTM)):
            for g8 in range(4):
                for c8 in range(8):
                    c = g8 * 8 + c8
                    P.tr(C.pst[:, c8 * 128:(c8 + 1) * 128], src[:, c * 128:(c + 1) * 128], C.identb)
                P.copy(dstm[:, g8 * 8:(g8 + 1) * 8, :], C.pst.re("p (c t) -> p c t", c=8), eng="act")
        P.memset(Sst, 0.0)
        P.memset(Sb, 0.0)
        if GM == "stageA":
            continue
        for cb in range(32 // NBATCH):
            chunks = list(range(cb * NBATCH, (cb + 1) * NBATCH))
            st = {}
            for c in chunks:
                d = st[c] = {}
                chx = c * 4 + hd
                col = lambda T_: T_[:, chx:chx + 1]
                cs_ = slice(c * 128, (c + 1) * 128)
                gb, bb = p_gb.next(), p_bb.next()
                P.ts(gb, ONESf, col(Gt), None, ALU.mult)
                P.ts(bb, ONESf, col(BETA), None, ALU.mult)
                gcr = nb()[:, 0:128]
                P.mm(gcr, gb, TRI)
                ber = nb()[:, 0:128]
                P.mm(ber, bb, IDF)
                DT, DL, tmp, tmp2 = p_DT.next(), p_DL.next(), p_tmp.next(), p_tmp2.next()
                P.stt(tmp, gcr, col(GC), NEGU, ALU.subtract, ALU.add)
                P.act(DT, tmp, AF.Exp)
                P.stt(tmp2, gcr, col(GC), POSL, ALU.subtract, ALU.add)
                P.act(DL, tmp2, AF.Exp, scale=-1.0)
                EGr = p_EGr.next()
                P.act(EGr, gcr, AF.Exp)
                qdT = d["qdT"] = p_qdT.next()
                P.tt(qdT, QT[:, cs_], EGr, ALU.mult, eng="pool")
                gps = nb()[:, 0:128]
                P.mm(gps, KT[:, cs_], KT[:, cs_])
                qkps = nb()[:, 0:128]
                P.mm(qkps, KT[:, cs_], QT[:, cs_])
                intra = d["intra"] = p_intra.next()
                P.tt(intra, qkps, DT, ALU.mult)
                P.tt(DT, DT, STRU, ALU.mult, eng="pool")
                P.tt(DL, DL, STRL, ALU.mult, eng="pool")
                A, AT = p_A.next(), p_AT.next()
                P.stt(A, gps, col(BETA), DT, ALU.mult, ALU.mult)
                P.tt(tmp, ber, DL, ALU.mult)
                P.tt(AT, gps, tmp, ALU.mult)
                Pm = p_P.next()
                P.tt(Pm, IDF, A, ALU.subtract, eng="pool")
                d["A"], d["AT"], d["P"] = A, AT, Pm
                Keg, Kdec = d["Keg"], d["Kdec"] = p_Keg.next(), p_Kdec.next()
                P.ts(Keg, KTM[:, c, :], col(EG), None, ALU.mult)
                P.ts(Kdec, KTM[:, c, :], col(EDL), None, ALU.mult)
            if GM == "s1":
                continue
            for lev in range(6):
                for c in chunks:
                    d = st[c]
                    psa, psb = nb()[:, 0:128], nb()[:, 0:128]
                    P.mm(psa, d["AT"], d["A"])
                    P.mm(psb, d["A"], d["AT"])
                    A2, AT2 = p_A.next(), p_AT.next()
                    P.copy(A2, psa, eng="act")
                    P.copy(AT2, psb, eng="dve")
                    d["A"], d["AT"] = A2, AT2
                for c in chunks:
                    d = st[c]
                    psc = nb()[:, 0:128]
                    P.mm(psc, d["AT"], d["P"])
                    P2 = p_P.next()
                    P.tt(P2, psc, d["P"], ALU.add)
                    d["P"] = P2
            if GM.startswith("neu"):
                continue
            for c in chunks:
                d = st[c]
                d["Pb"] = p_Pb.next()
                P.copy(d["Pb"], d["P"], eng="act")
            for c in chunks:
                d = st[c]
                chx = c * 4 + hd
                col = lambda T_: T_[:, chx:chx + 1]
                psu, psw = nb()[:, 0:128], nb()[:, 0:128]
                P.mm(psu, d["Pb"], VTM[:, c, :])
                P.mm(psw, d["Keg"], d["Pb"])
                ub = d["ub"] = p_ub.next()
                P.ts(ub, psu, col(BETA), None, ALU.mult)
                WT = d["WT"] = p_WT.next()
                P.copy(WT, psw, eng="act")
            if GM == "uw":
                continue
            for c in chunks:
                d = st[c]
                chx = c * 4 + hd
                col = lambda T_: T_[:, chx:chx + 1]
                cs_ = slice(c * 128, (c + 1) * 128)
                ps1 = nb()[:, 0:128]
                P.mm(ps1, d["WT"], Sb)
                vn = p_vn.next()
                P.stt(vn, ps1, col(NBETA), d["ub"], ALU.mult, ALU.add)
                pso = nb()[:, 0:128]
                P.mm(pso, d["qdT"], Sb, start=True, stop=False)
                P.mm(pso, d["intra"], vn, start=False, stop=True)
                ps3 = nb()[:, 0:128]
                P.mm(ps3, d["Kdec"], vn)
                P.stt(Sb, Sst, col(EGL), ps3, ALU.mult, ALU.add)
                P.stt(Sst, Sst, col(EGL), ps3, ALU.mult, ALU.add)
                hc = hcp.next()
                P.dma(hc, C.hT.v(hT3[:, :, cs_], ("t", c)))
                psz = nb()[:, 0:128]
                for kc in range(8):
                    P.mm(psz, hc[:, kc, :], Wz[:, kc, :], start=(kc == 0), stop=(kc == 7))
                zs = p_zs.next()
                P.act(zs, psz, AF.Silu)
                gst, gmv = p_st.next(), p_mv.next()
                P.add("dve", lambda e, o=gst, i_=pso: e.bn_stats(o.ap, i_.ap), [pso], [gst])
                P.add("dve", lambda e, o=gmv[:, 0:2], i_=gst: e.bn_aggr(o.ap, i_.ap), [gst], [gmv])
                P.stt(gmv[:, 2:3], gmv[:, 0:1], gmv[:, 0:1], gmv[:, 1:2], ALU.mult, ALU.add)
                P.ts(gmv[:, 2:3], gmv[:, 2:3], RMS_EPS, None, ALU.add)
                P.act(gmv[:, 3:4], gmv[:, 2:3], AF.Sqrt)
                P.recip(gmv[:, 4:5], gmv[:, 3:4])
                t_ = p_t.next()
                P.stt(t_, pso, gmv[:, 4:5], NG, ALU.mult, ALU.mult)
                ao = p_ao.next()
                P.tt(ao, t_, zs, ALU.mult, eng="pool")
                P.tr(C.pst[:, 0:128], ao, C.identb)
                aoT = p_aoT.next()
                P.copy(aoT, C.pst[:, 0:128], eng="act")
                P.dma(C.mixT.v(C.mixT.ap[hd * 128:(hd + 1) * 128, cs_], ("t", c)), aoT)
    P.pop()
```

```python
import numpy as np
from contextlib import ExitStack
import concourse.bass as bass
import concourse.mybir as mybir
from concourse.bass_utils import run_bass_kernel_spmd

F32 = mybir.dt.float32
BF16 = mybir.dt.bfloat16
I32 = mybir.dt.int32
AF = mybir.ActivationFunctionType
ALU = mybir.AluOpType
AX = mybir.AxisListType

NDMA = 16
NSDMA = 8


def is_dma(t):
    return t.startswith("dma") or t.startswith("sdma")


class Buf:
    __slots__ = ("name", "w", "r")

    def __init__(self, name=""):
        self.name = name
        self.w = None
        self.r = []


class V:
    def __init__(self, ap, bufs):
        self.ap = ap
        self.bufs = bufs if isinstance(bufs, tuple) else (bufs,)

    def __getitem__(self, k):
        return V(self.ap[k], self.bufs)

    def re(self, s, **kw):
        return V(self.ap.rearrange(s, **kw), self.bufs)

    def bc(self, dt):
        return V(self.ap.bitcast(dt), self.bufs)

    def bcast(self, shape):
        return V(self.ap.to_broadcast(shape), self.bufs)


class Op:
    __slots__ = ("eng", "fn", "deps", "track", "tidx", "inc", "val")


class Prog:
    ENG = ("pe", "act", "dve", "pool", "sp")
    BLK = {"pe": "tensor", "act": "scalar", "dve": "vector", "pool": "gpsimd", "sp": "sync"}

    def __init__(self, nc):
        self.nc = nc
        self.es = ExitStack()
        self.ops = {e: [] for e in self.ENG}
        self.tracks = {e: [] for e in self.ENG}
        for i in range(NDMA):
            self.tracks["dma%d" % i] = []
        for i in range(NSDMA):
            self.tracks["sdma%d" % i] = []
        self.dma_rr = 0
        self.sdma_rr = 0
        self.seen = {e: {} for e in self.ENG}
        self.nalloc = 0
        self.out_ops = []
        self.stack = [self.es]

    def sb(self, shape, dt, name=None):
        self.nalloc += 1
        name = name or ("t%d" % self.nalloc)
        t = self.stack[-1].enter_context(self.nc.sbuf_tensor(name + "_%d" % self.nalloc, list(shape), dt))
        return V(t.ap() if hasattr(t, "ap") and callable(t.ap) else t[:], Buf(name))

    def ps(self, shape, dt=F32, name=None):
        self.nalloc += 1
        name = name or ("p%d" % self.nalloc)
        t = self.stack[-1].enter_context(self.nc.psum_tensor(name + "_%d" % self.nalloc, list(shape), dt))
        return V(t.ap() if hasattr(t, "ap") and callable(t.ap) else t[:], Buf(name))

    def dram(self, name, shape, dt, kind="Internal"):
        t = self.nc.dram_tensor(name, list(shape), dt, kind=kind)
        return DT(t.ap(), name)


    def push(self):
        self.stack.append(ExitStack())

    def pop(self):
        self.barrier()
        self.stack.pop().close()

    def barrier(self):
        last = {t: len(l) - 1 for t, l in self.tracks.items() if l}
        for eng in self.ENG:
            op = Op()
            op.eng = eng
            op.fn = None
            op.inc = False
            op.val = None
            op.track = eng
            op.tidx = len(self.tracks[eng])
            op.deps = {}
            seen = self.seen[eng]
            for t, i in last.items():
                if seen.get(t, -1) < i:
                    seen[t] = i
                    op.deps[t] = i
                    self.tracks[t][i].inc = True
            self.ops[eng].append(op)

    def add(self, eng, fn, reads, writes, dma=False):
        op = Op()
        op.eng = eng
        op.fn = fn
        op.inc = dma
        op.val = None
        if dma and eng == "pool":
            tr = "sdma%d" % self.sdma_rr
            self.sdma_rr = (self.sdma_rr + 1) % NSDMA
        elif dma:
            tr = "dma%d" % self.dma_rr
            self.dma_rr = (self.dma_rr + 1) % NDMA
        else:
            tr = eng
        op.track = tr
        op.tidx = len(self.tracks[tr])
        deps = {}

        def need(t, i):
            if deps.get(t, -1) < i:
                deps[t] = i

        for v in reads:
            for b in v.bufs:
                if b.w is not None:
                    t, i = b.w
                    if t == tr and eng == "pe":
                        continue
                    need(t, i)
        for v in writes:
            for b in v.bufs:
                if b.w is not None:
                    t, i = b.w
                    if not (t == tr and eng == "pe"):
                        need(t, i)
                for (t, i) in b.r:
                    if not (t == tr and eng == "pe"):
                        need(t, i)
        if dma and op.tidx > 0:
            need(tr, op.tidx - 1)
        seen = self.seen[eng]
        op.deps = {}
        for t, i in deps.items():
            if seen.get(t, -1) < i:
                seen[t] = i
                op.deps[t] = i
                self.tracks[t][i].inc = True
        self.tracks[tr].append(op)
        self.ops[eng].append(op)
        for v in reads:
            for b in v.bufs:
                b.r.append((tr, op.tidx))
        for v in writes:
            for b in v.bufs:
                b.w = (tr, op.tidx)
                b.r = []
        return op

    def mm(self, out, lhsT, rhs, start=True, stop=True):
        return self.add("pe", lambda e: e.matmul(out.ap, lhsT.ap, rhs.ap, start=start, stop=stop),
                        [lhsT, rhs] + ([] if start else [out]), [out])

    def tr(self, out, in_, ident):
        return self.add("pe", lambda e: e.transpose(out.ap, in_.ap, ident.ap), [in_, ident], [out])

    def act(self, out, in_, func, bias=None, scale=None, accum=None, eng="act"):
        kw = {}
        rd = [in_]
        wr = [out]
        if bias is not None:
            if isinstance(bias, V):
                kw["bias"] = bias.ap
                rd.append(bias)
            else:
                kw["bias"] = bias
        if scale is not None:
            if isinstance(scale, V):
                kw["scale"] = scale.ap
                rd.append(scale)
            else:
                kw["scale"] = scale
        if accum is not None:
            kw["accum_out"] = accum.ap
            wr.append(accum)
        return self.add("act", lambda e: e.activation(out.ap, in_.ap, func, **kw), rd, wr)

    def tt(self, out, a, b, op, eng="dve"):
        return self.add(eng, lambda e: e.tensor_tensor(out.ap, a.ap, b.ap, op), [a, b], [out])

    def ts(self, out, a, s1, s2, op0, op1=None, accum=None, eng="dve"):
        rd = [a]
        wr = [out]
        s1a = s1.ap if isinstance(s1, V) else s1
        s2a = s2.ap if isinstance(s2, V) else s2
        if isinstance(s1, V):
            rd.append(s1)
        if isinstance(s2, V):
            rd.append(s2)
        kw = {}
        if op1 is not None:
            kw["op1"] = op1
        if accum is not None:
            kw["accum_out"] = accum.ap
            wr.append(accum)
        return self.add(eng, lambda e: e.tensor_scalar(out.ap, a.ap, s1a, s2a, op0, **kw), rd, wr)

    def stt(self, out, a, s, b, op0, op1, eng="dve"):
        rd = [a, b]
        sa = s.ap if isinstance(s, V) else s
        if isinstance(s, V):
            rd.append(s)
        return self.add(eng, lambda e: e.scalar_tensor_tensor(out.ap, a.ap, sa, b.ap, op0, op1), rd, [out])

    def copy(self, out, in_, eng="dve"):
        if eng == "act":
            return self.add("act", lambda e: e.copy(out.ap, in_.ap), [in_], [out])
        return self.add(eng, lambda e: e.tensor_copy(out.ap, in_.ap), [in_], [out])

    def memset(self, out, val, eng="pool"):
        return self.add(eng, lambda e: e.memset(out.ap, val), [], [out])

    def recip(self, out, in_):
        return self.add("dve", lambda e: e.reciprocal(out.ap, in_.ap), [in_], [out])

    def scan(self, out, d0, d1, init, op0, op1):
        rd = [d0, d1]
        ia = init.ap if isinstance(init, V) else init
        if isinstance(init, V):
            rd.append(init)
        return self.add("dve", lambda e: e.tensor_tensor_scan(out.ap, d0.ap, d1.ap, ia, op0, op1), rd, [out])

    def dma(self, out, in_, eng="sp", is_out=False, **kw):
        op = self.add(eng, lambda e: e.dma_start(out=out.ap, in_=in_.ap, **kw), [in_], [out], dma=True)
        if is_out:
            self.out_ops.append(op)
        return op

    def emit(self):
        nc = self.nc
        fin = Op()
        fin.eng = "sp"
        fin.fn = None
        fin.inc = False
        fin.track = "sp"
        fin.tidx = len(self.tracks["sp"])
        fin.deps = {}
        for tname, lst in self.tracks.items():
            if is_dma(tname) and lst:
                fin.deps[tname] = len(lst) - 1
        self.ops["sp"].append(fin)
        for tname, lst in self.tracks.items():
            c = 0
            for op in lst:
                if is_dma(tname):
                    c += 16
                    op.val = c
                else:
                    if op.inc:
                        c += 1
                    op.val = c
        sems = {}
        for tname, lst in self.tracks.items():
            if lst:
                sems[tname] = self.es.enter_context(nc.semaphore("s_" + tname))
        block = self.es.enter_context(nc.Block())
        tracks = self.tracks
        for eng in self.ENG:
            ops = self.ops[eng]
            if not ops:
                continue

            def body(e, ops=ops):
                for op in ops:
                    for t, i in op.deps.items():
                        e.wait_ge(sems[t], tracks[t][i].val)
                    if op.fn is None:
                        continue
                    ins = op.fn(e)
                    if is_dma(op.track):
                        ins.then_inc(sems[op.track], 16)
                    elif op.inc:
                        ins.then_inc(sems[op.track], 1)

            getattr(block, self.BLK[eng])(body)
        self.es.close()


class DT:
    def __init__(self, ap, name):
        self.ap = ap
        self.name = name
        self.bufs = {}

    def v(self, ap, keys=(0,)):
        if not isinstance(keys, (tuple, list)):
            keys = (keys,)
        bs = []
        for k in keys:
            if k not in self.bufs:
                self.bufs[k] = Buf("%s[%s]" % (self.name, k))
            bs.append(self.bufs[k])
        return V(ap, tuple(bs))


class Pool:
    def __init__(self, P, n, shape, dt, name, psum=False):
        self.tiles = [(P.ps if psum else P.sb)(shape, dt, name="%s%d" % (name, i)) for i in range(n)]
        self.i = 0

    def next(self):
        t = self.tiles[self.i]
        self.i = (self.i + 1) % len(self.tiles)
        return t


import math

S = 4096
D = 1024
NT = 32
FF = 2816
ALPHA = 8 ** 0.25
LN_EPS = 1e-5
RMS_EPS = 1e-6
TWO_PI = 2.0 * math.pi


def host_consts():
    c = {}
    i = np.arange(128)
    c["c_ident"] = np.eye(128, dtype=np.float32)
    c["c_tri"] = (i[:, None] <= i[None, :]).astype(np.float32)
    c["c_ones"] = np.ones((128, 128), np.float32)
    c["c_negU"] = np.where(i[None, :] >= i[:, None], 0.0, -30000.0).astype(np.float32)
    c["c_posL"] = np.where(i[None, :] <= i[:, None], 0.0, 30000.0).astype(np.float32)
    c["c_strU"] = (i[None, :] > i[:, None]).astype(np.float32)
    c["c_strL"] = (i[None, :] < i[:, None]).astype(np.float32)
    dm = np.zeros((128, 256), np.float32)
    dm[:, :128] = (i[:, None] <= i[None, :])
    dm[:, 128:] = (i[:, None] >= i[None, :])
    c["c_dmask"] = dm
    c["c_dneg"] = ((dm - 1.0) * 30000.0).astype(np.float32)
    invf = (10000.0 ** (-np.arange(0, 64, 2, dtype=np.float32) / 64)).astype(np.float32)
    r = np.arange(128)
    c["c_invf"] = (invf[r % 32] / np.float32(TWO_PI)).astype(np.float32).reshape(128, 1)
    c["c_sgn"] = np.where((r % 64) < 32, -1.0, 1.0).astype(np.float32).reshape(128, 1)
    c["c_iota"] = np.broadcast_to(np.arange(64, dtype=np.float32)[None, :], (128, 64)).copy()
    sel = np.zeros((128, 64), np.float32)
    sel[64 + np.arange(64), np.arange(64)] = 1.0
    c["c_sel"] = sel
    return c


def host_layer_inputs(inp):
    o = {}
    for l in range(2):
        a_re, a_im, ldt = inp["s5_a_re"][l], inp["s5_a_im"][l], inp["s5_log_dt"][l]
        col = np.stack([a_re, a_im, np.repeat(ldt[:, None], 64, 1)], 0)
        col = col.reshape(3, 32, 2, 64).transpose(2, 3, 0, 1).reshape(128, 3, 32)
        o["s5col%d" % l] = np.ascontiguousarray(col, np.float32)
        o["s5row%d" % l] = np.ascontiguousarray(np.stack([a_re.reshape(-1), a_im.reshape(-1), np.repeat(ldt, 64)], 0), np.float32)
        for nm, src in (("re", inp["s5_b_re"][l]), ("im", inp["s5_b_im"][l])):
            bp = np.zeros((4, 2, 16, 32, 2, 64), np.float32)
            sp = src.reshape(8, 4, 2, 64, 16)
            for q in range(4):
                for gl in range(2):
                    bp[q, gl, :, q::4, gl, :] = sp[:, q, gl].transpose(2, 0, 1)
            o["s5b%s%d" % (nm, l)] = bp.reshape(128, 32 * 128)
        for nm, src in (("re", inp["s5_c_re"][l]), ("im", inp["s5_c_im"][l])):
            cp = np.zeros((2, 64, 32, 4, 2, 16), np.float32)
            sp = src.reshape(8, 4, 2, 16, 64)
            for q in range(4):
                for gl in range(2):
                    cp[gl, :, q::4, q, gl, :] = sp[:, q, gl].transpose(2, 0, 1)
            o["s5c%s%d" % (nm, l)] = cp.reshape(128, 32 * 128)
        o["s5d%d" % l] = np.ascontiguousarray(inp["s5_d"][l].reshape(8, 128).T, np.float32)
        o["dncw%d" % l] = np.ascontiguousarray(inp["dn_conv_w"][l].T.reshape(12, 128, 4).transpose(1, 0, 2), np.float32)
    return o


class Ctx:
    pass


def load_w(P, dst, src_ap, KC, eng="pool"):
    for kc in range(KC):
        P.dma(dst[:, kc, :], src_ap(kc), eng=eng)


def load_h(P, C, tt, h_src):
    ht = C.p_h.next()
    P.dma(ht, h_src.v(h_src.ap[tt * 128:(tt + 1) * 128, :], tt))
    return ht


def ln_epilogue(P, C, lnp, tt, sub_halves, ht):
    z = C.p_z.next()
    st = C.p_st.next()
    for hf in range(2):
        P.stt(z[:, hf * 512:(hf + 1) * 512], ht[:, hf * 512:(hf + 1) * 512], ALPHA, sub_halves[hf], ALU.mult, ALU.add)
        zz = z[:, hf * 512:(hf + 1) * 512]
        P.add("dve", lambda e, o=st[:, hf, :], i=zz: e.bn_stats(o.ap, i.ap), [zz], [st])
    mv = C.p_mv.next()
    P.add("dve", lambda e, o=mv[:, 0:2], i=st.re("p a b -> p (a b)"): e.bn_aggr(o.ap, i.ap), [st], [mv])
    P.ts(mv[:, 2:3], mv[:, 1:2], LN_EPS, None, ALU.add)
    P.act(mv[:, 3:4], mv[:, 2:3], AF.Sqrt)
    P.recip(mv[:, 4:5], mv[:, 3:4])
    zn = C.p_zn.next()
    P.ts(zn, z, mv[:, 0:1], mv[:, 4:5], ALU.subtract, ALU.mult)
    hn = C.p_hn.next()
    P.tt(zn, zn, lnp[0], ALU.mult, eng="pool")
    P.tt(hn, zn, lnp[1], ALU.add, eng="pool")
    P.dma(C.out.v(C.out.ap[tt * 128:(tt + 1) * 128, :], tt), hn, is_out=True)
    hb = C.p_hb.next()
    P.copy(hb, hn, eng="act")
    return (tt, hb)


def ln_epilogue_b(P, C, pend):
    if pend is None:
        return
    tt, hb = pend
    pt = C.pst
    for c in range(8):
        P.tr(pt[:, c * 128:(c + 1) * 128], hb[:, c * 128:(c + 1) * 128], C.identb)
    hts = C.p_hts.next()
    P.copy(hts, pt.re("p (c t) -> p c t", c=8), eng="act")
    P.dma(C.hT.v(C.hT.ap.rearrange("(c p) t -> p c t", p=128)[:, :, tt * 128:(tt + 1) * 128], ("t", tt)), hts)


def ln_pools(P, C):
    C.p_h = Pool(P, 8, [128, 1024], F32, "h")
    C.p_z = Pool(P, 2, [128, 1024], F32, "z")
    C.p_zn = Pool(P, 2, [128, 1024], F32, "zn")
    C.p_hn = Pool(P, 2, [128, 1024], F32, "hn")
    C.p_hb = Pool(P, 3, [128, 1024], BF16, "hb")
    C.p_hts = Pool(P, 2, [128, 8, 128], BF16, "hts")
    C.p_st = Pool(P, 2, [128, 2, 6], F32, "st")
    C.p_mv = Pool(P, 2, [128, 8], F32, "mv")


def load_ln(P, C, g_ap, b_ap):
    G = P.sb([128, 1024], F32, "lng")
    B = P.sb([128, 1024], F32, "lnb")
    P.dma(G, g_ap)
    P.dma(B, b_ap)
    return (G, B)


def hT_keys(b):
    return tuple(("t", t) for t in range(b * 4, b * 4 + 4))


def phase_prep(P, C, I):
    import os
    MODE = os.environ.get("PREP_MODE", "all")
    P.push()
    xs = Pool(P, 2, [128, 1024], F32, "xs")
    xb = Pool(P, 2, [128, 1024], BF16, "xb")
    hts = Pool(P, 2, [128, 8, 128], BF16, "hts")
    for tt in range(NT if MODE in ('all', 'A') else 0):
        x_t = xs.next()
        P.dma(x_t, I["x"].v(I["x"].ap[tt * 128:(tt + 1) * 128, :], tt))
        b_t = xb.next()
        P.copy(b_t, x_t, eng="dve")
        for c in range(8):
            P.tr(C.pst[:, c * 128:(c + 1) * 128], b_t[:, c * 128:(c + 1) * 128], C.identb)
        h_t = hts.next()
        P.copy(h_t, C.pst.re("p (c t) -> p c t", c=8), eng="act")
        P.dma(C.hT.v(C.hT.ap.rearrange("(c p) t -> p c t", p=128)[:, :, tt * 128:(tt + 1) * 128], ("t", tt)), h_t)
    for mt in range(2 if MODE in ('all', 'A2') else 0):
        x_t = xs.next()
        P.dma(x_t, I["mem"].v(I["mem"].ap[mt * 128:(mt + 1) * 128, :], mt))
        b_t = xb.next()
        P.copy(b_t, x_t, eng="dve")
        for c in range(8):
            P.tr(C.pst[:, c * 128:(c + 1) * 128], b_t[:, c * 128:(c + 1) * 128], C.identb)
        h_t = hts.next()
        P.copy(h_t, C.pst.re("p (c t) -> p c t", c=8), eng="act")
        P.dma(C.memT.v(C.memT.ap.rearrange("(c p) t -> p c t", p=128)[:, :, mt * 128:(mt + 1) * 128], 0), h_t)
    invf = P.sb([128, 1], F32, "invf")
    sgn = P.sb([128, 1], F32, "sgn")
    P.dma(invf, I["c_invf"].v(I["c_invf"].ap))
    P.dma(sgn, I["c_sgn"].v(I["c_sgn"].ap))
    W = 1024
    pi_ = P.sb([128, W], I32, "posi")
    pf = P.sb([128, W], F32, "posf")
    tu = P.sb([128, W], F32, "tu")
    f = P.sb([128, W], F32, "f")
    tb = P.sb([128, W], F32, "tb")
    scr = frac_scratch(P, [128, W])
    for cq in range(S // W if MODE in ('all', 'B') else 0):
        P.dma(pi_, I["pos"].v(I["pos"].ap[:, cq * W:(cq + 1) * W].partition_broadcast(128)))
        P.copy(pf, pi_, eng="dve")
        P.ts(tu, pf, invf[:, 0:1], None, ALU.mult)
        for which, shift, dst in (("sin", 0.0, C.sinT), ("cos", 0.25, C.cosT)):
            frac_wrap(P, f, tu, shift, scr)
            P.act(tb, f, AF.Sin, scale=TWO_PI)
            if which == "sin":
                P.ts(tb, tb, sgn[:, 0:1], None, ALU.mult)
            P.dma(dst.v(dst.ap[:, cq * W:(cq + 1) * W], cq), tb)
    P.pop()


def frac_scratch(P, shape):
    return (P.sb(shape, I32, "fw_ki"), P.sb(shape, F32, "fw_kf"), P.sb(shape, F32, "fw_m"))


def frac_wrap(P, f, tu, shift, scr):
    ki, kf, m = scr
    if shift != 0.0:
        P.ts(f, tu, shift, None, ALU.add)
        src = f
    else:
        src = tu
    P.copy(ki, src, eng="dve")
    P.copy(kf, ki, eng="dve")
    P.tt(f, src, kf, ALU.subtract)
    P.ts(m, f, 0.5, None, ALU.is_gt)
    P.tt(f, f, m, ALU.subtract)
    P.ts(m, f, -0.5, None, ALU.is_lt)
    P.tt(f, f, m, ALU.add)


def phase_proj_ln(P, C, inT, K, W_ap, lnp_aps, h_src, gate_ap=None):
    P.push()
    KC = K // 128
    Wb = P.sb([128, KC, 1024], BF16, "Wb")
    load_w(P, Wb, W_ap, KC)
    Wg = None
    if gate_ap is not None:
        Wg = P.sb([128, KC, 1024], BF16, "Wg")
        load_w(P, Wg, gate_ap, KC)
    lnp = load_ln(P, C, *lnp_aps)
    ln_pools(P, C)
    blk = Pool(P, 2, [128, KC, 512], BF16, "inblk")
    sig = Pool(P, 2, [128, 1024], F32, "sig") if Wg is not None else None
    pend = None

    def prefetch(b):
        ib_ = blk.next()
        P.dma(ib_, inT.v(inT.ap.rearrange("(c p) t -> p c t", p=128)[:, :, b * 512:(b + 1) * 512], hT_keys(b)))
        return ib_, [load_h(P, C, b * 4 + t4, h_src) for t4 in range(4)]

    nxt = prefetch(0)
    for b in range(8):
        ib, hts_ = nxt
        if b < 7:
            nxt = prefetch(b + 1)
        for t4 in range(4):
            tt = b * 4 + t4
            par = (tt % 2) * 2 if Wg is None else 0
            sub = [C.PS[par], C.PS[par + 1]]
            for hf in range(2):
                for kc in range(KC):
                    P.mm(sub[hf], ib[:, kc, t4 * 128:(t4 + 1) * 128], Wb[:, kc, hf * 512:(hf + 1) * 512], start=(kc == 0), stop=(kc == KC - 1))
            if Wg is not None:
                gp = [C.PS[2], C.PS[3]]
                sg = sig.next()
                for hf in range(2):
                    for kc in range(KC):
                        P.mm(gp[hf], ib[:, kc, t4 * 128:(t4 + 1) * 128], Wg[:, kc, hf * 512:(hf + 1) * 512], start=(kc == 0), stop=(kc == KC - 1))
                    P.act(sg[:, hf * 512:(hf + 1) * 512], gp[hf], AF.Sigmoid)
                    P.tt(sg[:, hf * 512:(hf + 1) * 512], sub[hf], sg[:, hf * 512:(hf + 1) * 512], ALU.mult)
                sub = [sg[:, 0:512], sg[:, 512:1024]]
            ln_epilogue_b(P, C, pend)
            pend = ln_epilogue(P, C, lnp, tt, sub, hts_[t4])
    ln_epilogue_b(P, C, pend)
    P.pop()


def phase_ffn_up(P, C, wg_ap, wu_ap):
    P.push()
    Wg = P.sb([128, 8, FF], BF16, "Wg")
    Wu = P.sb([128, 8, FF], BF16, "Wu")
    load_w(P, Wg, wg_ap, 8)
    load_w(P, Wu, wu_ap, 8)
    blk = Pool(P, 2, [128, 8, 512], BF16, "hblk")
    ab = Pool(P, 2, [128, 22, 512], BF16, "ablk")
    sgp = Pool(P, 2, [128, 512], F32, "sg")
    def prefetch(b):
        ib_ = blk.next()
        P.dma(ib_, C.hT.v(C.hT.ap.rearrange("(c p) t -> p c t", p=128)[:, :, b * 512:(b + 1) * 512], hT_keys(b)))
        return ib_

    nxt = prefetch(0)
    for b in range(8):
        ib = nxt
        if b < 7:
            nxt = prefetch(b + 1)
        a = ab.next()
        for j in range(22):
            pg = C.PS[(j % 2) * 2]
            pu = C.PS[(j % 2) * 2 + 1]
            for kc in range(8):
                P.mm(pg, Wg[:, kc, j * 128:(j + 1) * 128], ib[:, kc, :], start=(kc == 0), stop=(kc == 7))
            for kc in range(8):
                P.mm(pu, Wu[:, kc, j * 128:(j + 1) * 128], ib[:, kc, :], start=(kc == 0), stop=(kc == 7))
            sg = sgp.next()
            P.act(sg, pg, AF.Silu)
            P.tt(a[:, j, :], pu, sg, ALU.mult)
        P.dma(C.aT.v(C.aT.ap.rearrange("(c p) t -> p c t", p=128)[:, :, b * 512:(b + 1) * 512], hT_keys(b)), a)
    P.pop()


def phase_cross(P, C, I, layer, lnp_aps, h_src):
    P.push()
    wq, wk, wv, wo = (I[n] for n in ("xq_w", "xk_w", "xv_w", "xo_w"))
    rows = lambda w: (lambda kc: w.v(w.ap[layer, kc * 128:(kc + 1) * 128, :]))
    Wq = P.sb([128, 8, 1024], BF16, "Wq")
    Wo = P.sb([128, 8, 1024], BF16, "Wo")
    Wk = P.sb([128, 8, 1024], BF16, "Wk")
    load_w(P, Wk, rows(wk), 8)
    mT = P.sb([128, 8, 256], BF16, "mT")
    P.dma(mT, C.memT.v(C.memT.ap.rearrange("(c p) t -> p c t", p=128), 0))
    kT = P.sb([128, 8, 256], BF16, "kT")
    import os
    if os.environ.get("CROSS_MODE", "all") == "pre0":
        dbg = P.sb([128, 1024], F32, "dbg")
        P.copy(dbg, Wk[:, 3, :])
        P.dma(C.out.v(C.out.ap[0:128, :], 0), dbg, is_out=True)
        P.pop()
        return
    for oc in range(8):
        ps = C.PS[oc % 2]
        for kc in range(8):
            P.mm(ps[:, 0:256], Wk[:, kc, oc * 128:(oc + 1) * 128], mT[:, kc, :], start=(kc == 0), stop=(kc == 7))
        P.copy(kT[:, oc, :], ps[:, 0:256], eng="act")
    load_w(P, Wk, rows(wv), 8)
    Vm = P.sb([128, 2, 1024], BF16, "Vm")
    for mc in range(2):
        for hf in range(2):
            ps = C.PS[hf]
            for kc in range(8):
                P.mm(ps, mT[:, kc, mc * 128:(mc + 1) * 128], Wk[:, kc, hf * 512:(hf + 1) * 512], start=(kc == 0), stop=(kc == 7))
            P.copy(Vm[:, mc, hf * 512:(hf + 1) * 512], ps, eng="act")
    import os
    CM = os.environ.get("CROSS_MODE", "all")
    if CM == "pre":
        P.pop()
        return
    load_w(P, Wq, rows(wq), 8)
    load_w(P, Wo, rows(wo), 8)
    lnp = load_ln(P, C, *lnp_aps)
    ln_pools(P, C)
    onesb = C.onesb
    blk = Pool(P, 2, [128, 8, 512], BF16, "hblk")
    qTp = Pool(P, 1, [128, 8, 512], BF16, "qT")
    pTp = Pool(P, 2, [128, 2, 512], BF16, "pT")
    rdp = Pool(P, 2, [128, 512], F32, "rden")
    oTp = Pool(P, 2, [128, 8, 512], BF16, "oT")
    pend = None

    def prefetch(b):
        ib_ = blk.next()
        P.dma(ib_, C.hT.v(C.hT.ap.rearrange("(c p) t -> p c t", p=128)[:, :, b * 512:(b + 1) * 512], hT_keys(b)))
        return ib_, [load_h(P, C, b * 4 + t4, h_src) for t4 in range(4)]

    nxt = prefetch(0)
    for b in range(8):
        ib, hts_ = nxt
        if b < 7:
            nxt = prefetch(b + 1)
        qT = qTp.next()
        for oc in range(8):
            ps = C.PS[2 + oc % 2]
            for kc in range(8):
                P.mm(ps, Wq[:, kc, oc * 128:(oc + 1) * 128], ib[:, kc, :], start=(kc == 0), stop=(kc == 7))
            P.copy(qT[:, oc, :], ps, eng="act")
        oT = oTp.next()
        if CM == "q":
            continue
        for hh in range(4):
            pT = pTp.next()
            for mc in range(2):
                ps = C.PS[2 + mc]
                for c in range(2):
                    P.mm(ps, kT[:, hh * 2 + c, mc * 128:(mc + 1) * 128], qT[:, hh * 2 + c, :], start=(c == 0), stop=(c == 1))
                P.act(pT[:, mc, :], ps, AF.Exp, scale=1.0 / 16.0)
            pd = C.PS[4]
            for mc in range(2):
                P.mm(pd, onesb, pT[:, mc, :], start=(mc == 0), stop=(mc == 1))
            rd = rdp.next()
            P.recip(rd, pd)
            for c in range(2):
                po = C.PS[5 + c]
                for mc in range(2):
                    P.mm(po, Vm[:, mc, hh * 256 + c * 128: hh * 256 + (c + 1) * 128], pT[:, mc, :], start=(mc == 0), stop=(mc == 1))
                P.tt(oT[:, hh * 2 + c, :], po, rd, ALU.mult)
        if CM == "attn":
            continue
        for t4 in range(4):
            tt = b * 4 + t4
            sub = [C.PS[0], C.PS[1]]
            for hf in range(2):
                for kc in range(8):
                    P.mm(sub[hf], oT[:, kc, t4 * 128:(t4 + 1) * 128], Wo[:, kc, hf * 512:(hf + 1) * 512], start=(kc == 0), stop=(kc == 7))
            ln_epilogue_b(P, C, pend)
            pend = ln_epilogue(P, C, lnp, tt, sub, hts_[t4])
    ln_epilogue_b(P, C, pend)
    P.pop()


WEIGHT_SPECS = [
    ("hyb_w_in", [2, 1024, 3592]), ("dn_conv_w", [2, 4, 1536]), ("hyb_w_out", [2, 1024, 1024]),
    ("s5_glu_wo", [2, 1024, 1024]), ("s5_glu_wg", [2, 1024, 1024]),
    ("ln_mix_g", [4, 1024]), ("ln_mix_b", [4, 1024]),
    ("xq_w", [4, 1024, 1024]), ("xk_w", [4, 1024, 1024]), ("xv_w", [4, 1024, 1024]), ("xo_w", [4, 1024, 1024]),
    ("ln_x_g", [4, 1024]), ("ln_x_b", [4, 1024]),
    ("ffn_wg", [4, 1024, 2816]), ("ffn_wu", [4, 1024, 2816]), ("ffn_wd", [4, 2816, 1024]),
    ("ln_ffn_g", [4, 1024]), ("ln_ffn_b", [4, 1024]),
]


def build_program(plan):
    nc = bass.Bass("TRN2", target_bir_lowering=False)
    P = Prog(nc)
    C = Ctx()
    I = {}

    def inp(name, shape, dt=F32):
        I[name] = P.dram(name, shape, dt, kind="ExternalInput")

    inp("x", [S, D])
    inp("mem", [256, D])
    inp("pos", [1, S], I32)
    for n, sh in WEIGHT_SPECS:
        inp(n, sh)
    hc = host_consts()
    for n, a in hc.items():
        inp(n, list(a.shape))
    for l in range(2):
        inp("s5col%d" % l, [128, 3, 32])
        inp("s5row%d" % l, [3, 4096])
        for nm in ("bre", "bim", "cre", "cim"):
            inp("s5%s%d" % (nm, l), [128, 4096])
        inp("s5d%d" % l, [128, 8])
        inp("dnp%d" % l, [1, 4 + 4 + 128])
        inp("dncw%d" % l, [128, 12, 4])
    C.out = P.dram("out", [S, D], F32, kind="ExternalOutput")
    C.hT = P.dram("hT", [D, S], BF16)
    C.mixT = P.dram("mixT", [D, S], BF16)
    C.aT = P.dram("aT", [FF, S], BF16)
    C.memT = P.dram("memT", [D, 256], BF16)
    C.cosT = P.dram("cosT", [128, S], F32)
    C.sinT = P.dram("sinT", [128, S], F32)
    C.vd = P.dram("vd", [S, 512], BF16)
    C.PS = [P.ps([128, 512], F32, "ps%d" % i) for i in range(7)]
    C.pst = P.ps([128, 1024], BF16, "pst")
    idf = P.sb([128, 128], F32, "idf")
    P.dma(idf, I["c_ident"].v(I["c_ident"].ap))
    C.identb = P.sb([128, 128], BF16, "identb")
    P.copy(C.identb, idf)
    C.identf = idf
    C.onesb = P.sb([128, 128], BF16, "onesb")
    P.memset(C.onesb, 1.0)

    bvec = lambda nm, l: I[nm].v(I[nm].ap[l:l + 1, :].partition_broadcast(128))
    h_src = I["x"]
    for ph in plan:
        kind = ph[0]
        if kind == "prep":
            phase_prep(P, C, I)
        elif kind == "cross":
            l = ph[1]
            phase_cross(P, C, I, l, (bvec("ln_x_g", l), bvec("ln_x_b", l)), h_src)
            h_src = C.out
        elif kind == "ffn":
            l = ph[1]
            wg, wu, wd = I["ffn_wg"], I["ffn_wu"], I["ffn_wd"]
            phase_ffn_up(P, C, lambda kc: wg.v(wg.ap[l, kc * 128:(kc + 1) * 128, :]), lambda kc: wu.v(wu.ap[l, kc * 128:(kc + 1) * 128, :]))
            phase_proj_ln(P, C, C.aT, FF, lambda kc: wd.v(wd.ap[l, kc * 128:(kc + 1) * 128, :]),
                          (bvec("ln_ffn_g", l), bvec("ln_ffn_b", l)), h_src)
            h_src = C.out
        elif kind == "s5":
            l = ph[1]
            i = l // 2
            phase_s5(P, C, I, i)
            wo, wg = I["s5_glu_wo"], I["s5_glu_wg"]
            phase_proj_ln(P, C, C.mixT, D, lambda kc: wo.v(wo.ap[i, kc * 128:(kc + 1) * 128, :]),
                          (bvec("ln_mix_g", l), bvec("ln_mix_b", l)), h_src,
                          gate_ap=lambda kc: wg.v(wg.ap[i, kc * 128:(kc + 1) * 128, :]))
            h_src = C.out
        elif kind == "hyb":
            l = ph[1]
            i = l // 2
            phase_dilated(P, C, I, i)
            phase_gdn(P, C, I, i)
            wo = I["hyb_w_out"]
            phase_proj_ln(P, C, C.mixT, D, lambda kc: wo.v(wo.ap[i, kc * 128:(kc + 1) * 128, :]),
                          (bvec("ln_mix_g", l), bvec("ln_mix_b", l)), h_src)
            h_src = C.out
    P.emit()
    return nc


def make_in_maps(inputs, cores):
    hc = host_consts()
    hl = host_layer_inputs(inputs)
    shared = {n: np.ascontiguousarray(inputs[n], np.float32) for n, _ in WEIGHT_SPECS}
    shared.update(hc)
    shared.update(hl)
    for l in range(2):
        shared["dnp%d" % l] = np.concatenate([inputs["dn_a_log"][l], inputs["dn_dt_bias"][l], inputs["dn_norm_g"][l]]).astype(np.float32).reshape(1, 136)
    maps = []
    for b in cores:
        m = dict(shared)
        m["x"] = np.ascontiguousarray(inputs["x"][b], np.float32)
        m["mem"] = np.ascontiguousarray(inputs["mem"][b], np.float32)
        m["pos"] = np.ascontiguousarray(inputs["positions"][b].reshape(1, S), np.int32)
        maps.append(m)
    return maps


FULL_PLAN = [("prep",)]
for _l in range(4):
    FULL_PLAN += [("hyb" if _l % 2 == 0 else "s5", _l), ("cross", _l), ("ffn", _l)]


def kernel(**inputs):
    inputs = {k: np.asarray(v) for k, v in inputs.items()}
    nc = build_program(FULL_PLAN)
    maps = make_in_maps(inputs, list(range(8)))
    res = run_bass_kernel_spmd(nc, maps, core_ids=list(range(8)))
    return np.stack([r["out"] for r in res.results], 0).astype(np.float32)


def b3(v, shape, axis):
    return V(v.ap.unsqueeze(axis).to_broadcast(shape), v.bufs)


def phase_s5(P, C, I, l):
    P.push()
    f32t = lambda shape, n: P.sb(shape, F32, n)
    col = f32t([128, 3, 32], "col")
    P.dma(col, I["s5col%d" % l].v(I["s5col%d" % l].ap))
    dtc = f32t([128, 32], "dtc")
    P.act(dtc, col[:, 2, :], AF.Exp)
    t0 = f32t([128, 32], "t0")
    P.tt(t0, col[:, 0, :], dtc, ALU.mult)
    rho = f32t([128, 32], "rho")
    P.act(rho, t0, AF.Exp)
    phi = f32t([128, 32], "phi")
    P.tt(phi, col[:, 1, :], dtc, ALU.mult)
    P.ts(phi, phi, 1.0 / TWO_PI, None, ALU.mult)
    iota = f32t([128, 64], "iota")
    P.dma(iota, I["c_iota"].v(I["c_iota"].ap))
    dcol = f32t([128, 8], "dcol")
    P.dma(dcol, I["s5d%d" % l].v(I["s5d%d" % l].ap))
    E = {k: f32t([128, 2048], "F" + k) for k in ("lo", "hi")}
    Bb = {k: P.sb([128, 4096], BF16, "Bb" + k) for k in ("re", "im")}
    CreB = P.sb([128, 4096], BF16, "CreB")
    CreN = P.sb([128, 4096], BF16, "CreN")
    CimN = P.sb([128, 4096], BF16, "CimN")
    P.dma(CreB, I["s5cre%d" % l].v(I["s5cre%d" % l].ap), eng="pool")
    P.dma(CimN, I["s5cim%d" % l].v(I["s5cim%d" % l].ap), eng="pool")
    P.ts(CreN, CreB, -1.0, None, ALU.mult)
    P.ts(CimN, CimN, -1.0, None, ALU.mult)
    P.push()
    scr = frac_scratch(P, [128, 2048])
    tu = f32t([128, 2048], "tu")
    tu3 = tu.re("p (a b) -> p a b", b=64)
    iota3 = b3(iota, [128, 32, 64], 1)
    P.tt(tu3, b3(phi, [128, 32, 64], 2), iota3, ALU.mult)
    frac_wrap(P, E["lo"], tu, 0.0, scr)
    p64 = f32t([128, 32], "p64")
    P.ts(p64, phi, 64.0, None, ALU.mult)
    scr2 = frac_scratch(P, [128, 32])
    f64 = f32t([128, 32], "f64")
    frac_wrap(P, f64, p64, 0.0, scr2)
    P.tt(tu3, b3(f64, [128, 32, 64], 2), iota3, ALU.mult)
    frac_wrap(P, E["hi"], tu, 0.0, scr)
    P.pop()
    P.push()
    row = I["s5row%d" % l]
    T = {k: f32t([128, 1024], "r" + k) for k in ("are", "aim", "dt", "er", "tu", "f", "sn", "cs", "abr", "abi", "nr", "den", "cr", "ci", "Bre", "Bim", "x1", "x2")}
    scr = frac_scratch(P, [128, 1024])
    for qq in range(4):
        cs_ = slice(qq * 1024, (qq + 1) * 1024)
        P.dma(T["are"], row.v(row.ap[0:1, cs_].partition_broadcast(128)))
        P.dma(T["aim"], row.v(row.ap[1:2, cs_].partition_broadcast(128)))
        P.dma(T["dt"], row.v(row.ap[2:3, cs_].partition_broadcast(128)))
        P.dma(T["Bre"], I["s5bre%d" % l].v(I["s5bre%d" % l].ap[:, cs_]))
        P.dma(T["Bim"], I["s5bim%d" % l].v(I["s5bim%d" % l].ap[:, cs_]))
        P.act(T["dt"], T["dt"], AF.Exp)
        P.tt(T["er"], T["are"], T["dt"], ALU.mult)
        P.act(T["er"], T["er"], AF.Exp)
        P.tt(T["tu"], T["aim"], T["dt"], ALU.mult)
        P.ts(T["tu"], T["tu"], 1.0 / TWO_PI, None, ALU.mult)
        frac_wrap(P, T["f"], T["tu"], 0.0, scr)
        P.act(T["sn"], T["f"], AF.Sin, scale=TWO_PI)
        frac_wrap(P, T["f"], T["tu"], 0.25, scr)
        P.act(T["cs"], T["f"], AF.Sin, scale=TWO_PI)
        P.tt(T["abr"], T["er"], T["cs"], ALU.mult)
        P.tt(T["abi"], T["er"], T["sn"], ALU.mult)
        P.ts(T["nr"], T["abr"], -1.0, None, ALU.add)
        P.tt(T["den"], T["are"], T["are"], ALU.mult)
        P.tt(T["x1"], T["aim"], T["aim"], ALU.mult)
        P.tt(T["den"], T["den"], T["x1"], ALU.add)
        P.recip(T["den"], T["den"])
        P.tt(T["x1"], T["nr"], T["are"], ALU.mult)
        P.tt(T["x2"], T["abi"], T["aim"], ALU.mult)
        P.tt(T["cr"], T["x1"], T["x2"], ALU.add)
        P.tt(T["cr"], T["cr"], T["den"], ALU.mult)
        P.tt(T["x1"], T["abi"], T["are"], ALU.mult)
        P.tt(T["x2"], T["nr"], T["aim"], ALU.mult)
        P.tt(T["ci"], T["x1"], T["x2"], ALU.subtract)
        P.tt(T["ci"], T["ci"], T["den"], ALU.mult)
        P.tt(T["x1"], T["cr"], T["Bre"], ALU.mult)
        P.tt(T["x2"], T["ci"], T["Bim"], ALU.mult)
        P.tt(Bb["re"][:, cs_], T["x1"], T["x2"], ALU.subtract)
        P.tt(T["x1"], T["cr"], T["Bim"], ALU.mult)
        P.tt(T["x2"], T["ci"], T["Bre"], ALU.mult)
        P.tt(Bb["im"][:, cs_], T["x1"], T["x2"], ALU.add)
    P.pop()
    BW = 512
    NBLK = S // BW
    HC = BW // 64
    ones = f32t([128, BW], "ones")
    P.memset(ones, 1.0)
    halfpi = f32t([128, 1], "halfpi")
    P.memset(halfpi, math.pi / 2)
    onec = f32t([128, 1], "onec")
    P.memset(onec, 1.0)
    stre = [f32t([128, 1], "stre%d" % i) for i in range(32)]
    stim = [f32t([128, 1], "stim%d" % i) for i in range(32)]
    for t_ in stre + stim:
        P.memset(t_, 0.0)
    uTp = Pool(P, 2, [128, S], BF16, "uT")
    RHO = [[f32t([128, BW], "RHO%d_%d" % (par, q)) for q in range(4)] for par in range(2)]
    mkp = lambda n, dt=F32, k=4: Pool(P, k, [128, BW], dt, n)
    p_fs, p_s1, p_sq, p_ei = mkp("fs", F32, 2), mkp("s1", F32, 2), mkp("sq"), mkp("ei")
    p_m = [mkp("m%d" % i, F32, 3) for i in range(4)]
    p_xr, p_xi = mkp("xr", F32, 3), mkp("xi", F32, 3)
    p_pp = [mkp("pp%d" % i, BF16, 3) for i in range(4)]
    hidp = Pool(P, 2, [128, BW], BF16, "hid")
    vtp = Pool(P, 2, [128, BW], F32, "vt")
    allk = tuple(("t", t) for t in range(NT))
    v3 = lambda t: t.re("p (a b) -> p a b", b=64)
    iters = [(cc, blk, q) for cc in range(8) for blk in range(NBLK) for q in range(4)]
    ctx = {}
    uTs = {}
    bub = [0]
    btb = [0]
    IDN = f32t([128, 128], "IDN")
    P.ts(IDN, C.identf, -1.0, None, ALU.mult)

    def prologue(n):
        cc, blk, q = iters[n]
        ctx[n] = {}
        if blk == 0 and q == 0:
            uTs[cc] = uTp.next()
            P.dma(uTs[cc], C.hT.v(C.hT.ap[cc * 128:(cc + 1) * 128, :], allk))
            for q2 in range(4):
                P.ts(RHO[cc % 2][q2], ones, rho[:, 4 * cc + q2:4 * cc + q2 + 1], None, ALU.mult)

    def sA(n):
        cc, blk, q = iters[n]
        pair = 4 * cc + q
        d = ctx[n]
        hs = slice(pair * 64 + blk * HC, pair * 64 + (blk + 1) * HC)
        ls = slice(pair * 64, pair * 64 + 64)
        d["fs"], d["s1"], d["sq"], d["ei"] = p_fs.next(), p_s1.next(), p_sq.next(), p_ei.next()
        P.tt(v3(d["fs"]), b3(E["hi"][:, hs], [128, HC, 64], 2), b3(E["lo"][:, ls], [128, HC, 64], 1), ALU.add, eng="pool")

    def sD(n):
        d = ctx[n]
        fs, s1, sq = d["fs"], d["s1"], d["sq"]
        P.act(s1, fs, AF.Sin, scale=math.pi)
        P.act(fs, fs, AF.Abs)
        P.act(fs, fs, AF.Sin, scale=-math.pi, bias=halfpi[:, 0:1])
        P.act(sq, s1, AF.Square)
        P.act(sq, sq, AF.Identity, scale=-2.0, bias=onec[:, 0:1])

    def sF(n):
        d = ctx[n]
        P.stt(d["ei"], d["s1"], 2.0, d["fs"], ALU.mult, ALU.mult)
        d["Ere"], d["Eim"] = d["sq"], d["ei"]

    def sB(n):
        cc, blk, q = iters[n]
        pair = 4 * cc + q
        d = ctx[n]
        tk = slice(blk * BW, (blk + 1) * BW)
        pc = slice(pair * 128, (pair + 1) * 128)
        d["bre"], d["bim"] = C.PS[0], C.PS[1]
        P.mm(d["bre"], Bb["re"][:, pc], uTs[cc][:, tk])
        P.mm(d["bim"], Bb["im"][:, pc], uTs[cc][:, tk])

    def sE(n):
        d = ctx[n]
        m = [p.next() for p in p_m]
        bre, bim = d["bre"], d["bim"]
        P.tt(m[0], bre, d["Ere"], ALU.mult)
        P.tt(m[1], bim, d["Eim"], ALU.mult)
        P.tt(m[2], bim, d["Ere"], ALU.mult)
        P.tt(m[3], bre, d["Eim"], ALU.mult)
        btr, bti = C.PS[2 + btb[0]], C.PS[3 + btb[0]]
        btb[0] = (btb[0] + 2) % 4
        P.mm(btr, C.identf, m[0], start=True, stop=False)
        P.mm(btr, C.identf, m[1], start=False, stop=True)
        P.mm(bti, C.identf, m[2], start=True, stop=False)
        P.mm(bti, IDN, m[3], start=False, stop=True)
        d["m"] = [btr, None, bti, None]

    def sC(n):
        cc, blk, q = iters[n]
        pair = 4 * cc + q
        d = ctx[n]
        m = d["m"]
        xr, xi = p_xr.next(), p_xi.next()
        P.scan(xr, RHO[cc % 2][q], m[0], stre[pair], ALU.mult, ALU.add)
        P.scan(xi, RHO[cc % 2][q], m[2], stim[pair], ALU.mult, ALU.add)
        P.copy(stre[pair], xr[:, BW - 1:BW], eng="act")
        P.copy(stim[pair], xi[:, BW - 1:BW], eng="act")
        pp = d["pp"] = [p.next() for p in p_pp]
        P.tt(pp[0], d["Ere"], xr, ALU.mult)
        P.tt(pp[1], d["Eim"], xi, ALU.mult, eng="pool")
        P.tt(pp[2], d["Ere"], xi, ALU.mult, eng="pool")
        P.tt(pp[3], d["Eim"], xr, ALU.mult, eng="pool")

    def sG(n):
        cc, blk, q = iters[n]
        pair = 4 * cc + q
        d = ctx.pop(n)
        uT = uTs[cc]
        tk = slice(blk * BW, (blk + 1) * BW)
        pc = slice(pair * 128, (pair + 1) * 128)
        yps = C.PS[6]
        pp = d["pp"]
        for k, (W_, p_) in enumerate(((CreB, pp[0]), (CreN, pp[1]), (CimN, pp[2]), (CimN, pp[3]))):
            P.mm(yps, W_[:, pc], p_, start=(q == 0 and k == 0), stop=(q == 3 and k == 3))
        if q == 3:
            vt = vtp.next()
            P.stt(vt, uT[:, tk], dcol[:, cc:cc + 1], yps, ALU.mult, ALU.add)
            hid = hidp.next()
            P.act(hid, vt, AF.Gelu)
            P.dma(C.mixT.v(C.mixT.ap[cc * 128:(cc + 1) * 128, tk], hT_keys(blk)), hid)

    N_ = len(iters)
    ok = lambda k: 0 <= k < N_
    for n in range(N_ + 2):
        if ok(n):
            prologue(n)
            sA(n)
        if ok(n - 1):
            sB(n - 1)
        if ok(n - 2):
            sC(n - 2)
        if ok(n):
            sD(n)
        if ok(n - 1):
            sE(n - 1)
        if ok(n):
            sF(n)
        if ok(n - 2):
            sG(n - 2)
    P.pop()


def _zero_mix_rows(P, C, r0, r1):
    P.push()
    z = P.sb([128, 1024], BF16, "zmix")
    P.memset(z, 0.0)
    for c in range(r0 // 128, r1 // 128):
        for blk in range(4):
            P.dma(C.mixT.v(C.mixT.ap[c * 128:(c + 1) * 128, blk * 1024:(blk + 1) * 1024],
                           tuple(("t", t) for t in range(blk * 8, blk * 8 + 8))), z)
    P.pop()


def phase_dilated(P, C, I, i):
    w_in = I["hyb_w_in"]
    w3 = w_in.ap[i].rearrange("(kc p) n -> p kc n", p=128)
    hT3 = C.hT.ap.rearrange("(c p) t -> p c t", p=128)
    QOFF, KOFF, VOFF = 2056, 2568, 3080
    P.push()
    Wv = P.sb([128, 8, 512], BF16, "Wv")
    P.dma(Wv, w_in.v(w3[:, :, VOFF:VOFF + 512]), eng="pool")
    blkp = Pool(P, 2, [128, 8, 512], BF16, "hblk")
    vtp = Pool(P, 2, [128, 512], BF16, "vt")
    for b in range(8):
        ib = blkp.next()
        P.dma(ib, C.hT.v(hT3[:, :, b * 512:(b + 1) * 512], hT_keys(b)))
        for t4 in range(4):
            tt = b * 4 + t4
            ps = C.PS[t4 % 2]
            for kc in range(8):
                P.mm(ps, ib[:, kc, t4 * 128:(t4 + 1) * 128], Wv[:, kc, :], start=(kc == 0), stop=(kc == 7))
            vt = vtp.next()
            P.copy(vt, ps, eng="act")
            P.dma(C.vd.v(C.vd.ap[tt * 128:(tt + 1) * 128, :], 0), vt)
    P.pop()
    import os
    DM = os.environ.get("DIL_MODE", "all")
    if DM == "v":
        return
    P.push()
    COS = P.sb([128, S], F32, "COS")
    SIN = P.sb([128, S], F32, "SIN")
    P.dma(COS, C.cosT.v(C.cosT.ap, (0, 1, 2, 3)))
    P.dma(SIN, C.sinT.v(C.sinT.ap, (0, 1, 2, 3)))
    dmask = P.sb([128, 256], F32, "dmask")
    P.dma(dmask, I["c_dmask"].v(I["c_dmask"].ap))
    sel = P.sb([128, 64], F32, "sel")
    P.dma(sel, I["c_sel"].v(I["c_sel"].ap))
    Wt = {k: P.sb([128, 8, 128], BF16, "W" + k) for k in ("q", "qs", "k", "ks")}
    blkp = Pool(P, 2, [128, 8, 512], BF16, "hblk")
    RT = {k: P.sb([128, S], BF16, "R" + k) for k in ("q", "k")}
    CM = {(k, r): P.sb([128, S], BF16, "C%s%d" % (k, r)) for k in ("q", "k") for r in (4, 16)}
    OT = [P.sb([128, S], F32, "OT%d" % h) for h in range(2)]
    t1p = Pool(P, 2, [128, 512], F32, "t1")
    t2p = Pool(P, 2, [128, 512], F32, "t2")
    vap = Pool(P, 4, [128, 8, 128], BF16, "vaug")
    for va in vap.tiles:
        P.memset(va, 1.0)
    pep = Pool(P, 3, [128, 256], BF16, "pe")
    ptp = Pool(P, 6, [128, 256], BF16, "pt")
    dnegf = P.sb([128, 256], F32, "dnegf")
    P.dma(dnegf, I["c_dneg"].v(I["c_dneg"].ap))
    dnegb = P.sb([128, 256], BF16, "dnegb")
    P.copy(dnegb, dnegf)
    sbank = [0]
    rdp = Pool(P, 2, [64, 512], F32, "rd")
    onp = Pool(P, 2, [64, 512], BF16, "on")
    slots = [[V(C.PS[3 + 2 * h + sidx].ap[:, 0:128], C.PS[3 + 2 * h + sidx].bufs) for sidx in range(2)] for h in range(2)]
    for hp in range(4):
        for k, off in (("q", QOFF), ("k", KOFF)):
            base = off + hp * 128
            P.dma(Wt[k], w_in.v(w3[:, :, base:base + 128]), eng="pool")
            for hh in range(2):
                P.dma(Wt[k + "s"][:, :, hh * 64:hh * 64 + 32], w_in.v(w3[:, :, base + hh * 64 + 32:base + hh * 64 + 64]), eng="pool")
                P.dma(Wt[k + "s"][:, :, hh * 64 + 32:hh * 64 + 64], w_in.v(w3[:, :, base + hh * 64:base + hh * 64 + 32]), eng="pool")
        for b in range(8):
            ib = blkp.next()
            P.dma(ib, C.hT.v(hT3[:, :, b * 512:(b + 1) * 512], hT_keys(b)))
            cs_ = slice(b * 512, (b + 1) * 512)
            for k in ("q", "k"):
                for kc in range(8):
                    P.mm(C.PS[0], Wt[k][:, kc, :], ib[:, kc, :], start=(kc == 0), stop=(kc == 7))
                for kc in range(8):
                    P.mm(C.PS[1], Wt[k + "s"][:, kc, :], ib[:, kc, :], start=(kc == 0), stop=(kc == 7))
                t1 = t1p.next()
                t2 = t2p.next()
                P.tt(t1, C.PS[0], COS[:, cs_], ALU.mult)
                P.tt(t2, C.PS[1], SIN[:, cs_], ALU.mult)
                P.tt(RT[k][:, cs_], t1, t2, ALU.add, eng="pool")
        if DM == "proj":
            continue
        for k in ("q", "k"):
            for r in (4, 16):
                P.copy(CM[(k, r)].re("p (r u) -> p r u", r=r), RT[k].re("p (u r) -> p r u", r=r), eng="pool")
        if DM == "cm":
            continue
        its = []
        for bi, r in enumerate((1, 4, 16)):
            L = S // r
            nb_ = L // 128
            for rho in range(r):
                for m in range(nb_):
                    for hh in range(2):
                        its.append((bi, r, rho, m, hh, nb_, L))
        ictx = {}

        def st_S(n):
            bi, r, rho, m, hh, nb_, L = its[n]
            d = ictx[n] = {}
            Qs = RT["q"] if r == 1 else CM[("q", r)]
            Ks = RT["k"] if r == 1 else CM[("k", r)]
            if hh == 0:
                va = vap.next()
                st = (128 * m) * r + rho
                P.dma(va[:, :, 0:64], C.vd.v(C.vd.ap[st:st + 127 * r + 1:r, :].rearrange("t (h d) -> t h d", h=8), 0))
                ictx["va"] = va
            d["va"] = ictx["va"]
            nq = d["nq"] = 256 if m < nb_ - 1 else 128
            c0 = rho * L + m * 128
            hb = hh * 64
            sps = d["sps"] = C.PS[sbank[0]]
            sbank[0] = (sbank[0] + 1) % 3
            P.mm(sps[:, 0:nq], Ks[hb:hb + 64, c0:c0 + 128], Qs[hb:hb + 64, c0:c0 + nq], start=True, stop=False)
            P.mm(sps[:, 0:nq], C.identb, dnegb[:, 0:nq], start=False, stop=True)

        def st_X(n):
            d = ictx[n]
            nq = d["nq"]
            pt = d["pt"] = ptp.next()
            P.act(pt[:, 0:nq], d["sps"][:, 0:nq], AF.Exp, scale=0.125)

        def st_V(n):
            bi, r, rho, m, hh, nb_, L = its[n]
            d = ictx.pop(n)
            pt, va, nq = d["pt"], d["va"], d["nq"]
            head = hp * 2 + hh
            sl = slots[hh][m % 2]
            P.mm(sl, va[:, head, :], pt[:, 0:128], start=(m == 0), stop=True)
            dst = OT[hh].re("p (u r) -> p r u", r=r)[:, rho, m * 128:(m + 1) * 128] if r > 1 else OT[hh][:, m * 128:(m + 1) * 128]
            if bi == 0:
                P.copy(dst, sl, eng="dve")
            else:
                P.tt(dst, sl, dst, ALU.add)
            if nq == 256:
                sl2 = slots[hh][(m + 1) % 2]
                P.mm(sl2, va[:, head, :], pt[:, 128:256], start=True, stop=False)

        NI = len(its)
        for n in range(NI + 2):
            if n < NI:
                st_S(n)
            if 0 <= n - 1 < NI:
                st_X(n - 1)
            if 0 <= n - 2 < NI:
                st_V(n - 2)
        if DM in ("br0a", "br0b", "br0b1", "br0b2"):
            continue
        for hh in range(2):
            head = hp * 2 + hh
            for b in range(8):
                cs_ = slice(b * 512, (b + 1) * 512)
                pd = C.PS[hh]
                P.mm(pd[0:64, :], sel, OT[hh][:, cs_])
                rd = rdp.next()
                P.recip(rd, pd[0:64, :])
                on = onp.next()
                P.tt(on, OT[hh][0:64, cs_], rd, ALU.mult)
                r0 = 512 + head * 64
                P.dma(C.mixT.v(C.mixT.ap[r0:r0 + 64, cs_], hT_keys(b)), on)
    P.pop()


def phase_gdn(P, C, I, i):
    w_in = I["hyb_w_in"]
    w3 = w_in.ap[i].rearrange("(kc p) n -> p kc n", p=128)
    hT3 = C.hT.ap.rearrange("(c p) t -> p c t", p=128)
    dnp = I["dnp%d" % i]
    P.push()
    f32t = lambda shape, n: P.sb(shape, F32, n)
    cst = {}
    for nm in ("c_tri", "c_ones", "c_negU", "c_posL", "c_strU", "c_strL"):
        cst[nm] = f32t([128, 128], nm)
        P.dma(cst[nm], I[nm].v(I[nm].ap))
    TRI, ONESf, NEGU, POSL, STRU, STRL = (cst[n] for n in ("c_tri", "c_ones", "c_negU", "c_posL", "c_strU", "c_strL"))
    IDF = C.identf
    onecol = f32t([128, 1], "onecol")
    P.memset(onecol, 1.0)
    NG = f32t([128, 128], "NG")
    P.dma(NG, dnp.v(dnp.ap[0:1, 8:136].partition_broadcast(128)))
    ab = f32t([128, 8], "ab")
    P.dma(ab, dnp.v(dnp.ap[0:1, 0:8].partition_broadcast(128)))
    nA = f32t([128, 4], "nA")
    P.act(nA, ab[:, 0:4], AF.Exp)
    P.ts(nA, nA, -1.0, None, ALU.mult)
    cw = f32t([128, 12, 4], "cw")
    P.dma(cw, I["dncw%d" % i].v(I["dncw%d" % i].ap))
    bank = [0]

    def nb():
        bank[0] = (bank[0] + 1) % 7
        return C.PS[bank[0]]

    Wg8 = P.sb([128, 8, 8], BF16, "Wg8")
    P.dma(Wg8, w_in.v(w3[:, :, 2048:2056]), eng="pool")
    blkp = Pool(P, 2, [128, 8, 512], BF16, "hblk")
    GR = f32t([128, 32, 8], "GR")
    for b in range(8):
        ib = blkp.next()
        P.dma(ib, C.hT.v(hT3[:, :, b * 512:(b + 1) * 512], hT_keys(b)))
        for t4 in range(4):
            ps = nb()
            for kc in range(8):
                P.mm(ps[:, 0:8], ib[:, kc, t4 * 128:(t4 + 1) * 128], Wg8[:, kc, :], start=(kc == 0), stop=(kc == 7))
            P.copy(GR[:, b * 4 + t4, :], ps[:, 0:8], eng="act")
    BETA = f32t([128, 128], "BETA")
    NBETA = f32t([128, 128], "NBETA")
    Gt = f32t([128, 128], "Gt")
    v34 = lambda t: t.re("p (c h) -> p c h", h=4)
    P.copy(v34(BETA), GR[:, :, 0:4], eng="dve")
    P.act(BETA, BETA, AF.Sigmoid)
    P.ts(NBETA, BETA, -1.0, None, ALU.mult)
    P.tt(v34(Gt), GR[:, :, 4:8], b3(ab[:, 4:8], [128, 32, 4], 1), ALU.add)
    P.act(Gt, Gt, AF.Exp)
    P.act(Gt, Gt, AF.Ln, bias=onecol[:, 0:1])
    P.tt(v34(Gt), v34(Gt), b3(nA, [128, 32, 4], 1), ALU.mult)
    GC, GL, EG, EDL, EGL = (f32t([128, 128], n) for n in ("GC", "GL", "EG", "EDL", "EGL"))
    ps = nb()
    P.mm(ps[:, 0:128], TRI, Gt)
    P.copy(GC, ps[:, 0:128], eng="dve")
    ps = nb()
    P.mm(ps[:, 0:128], ONESf, Gt)
    P.copy(GL, ps[:, 0:128], eng="dve")
    P.act(EG, GC, AF.Exp)
    P.act(EGL, GL, AF.Exp)
    P.tt(EDL, GL, GC, ALU.subtract)
    P.act(EDL, EDL, AF.Exp)
    import os
    GM = os.environ.get("GDN_MODE", "all")
    if GM == "gates":
        P.pop()
        return
    Wh = P.sb([128, 8, 3, 128], BF16, "Wh")
    Wz = P.sb([128, 8, 128], BF16, "Wz")
    X = [f32t([128, S + 3], "X%d" % x) for x in range(3)]
    for x in range(3):
        P.memset(X[x][:, 0:3], 0.0)
    sq = P.sb([128, S], BF16, "sq")
    Ycv = f32t([128, S], "Ycv")
    QT = P.sb([128, S], BF16, "QT")
    KT = P.sb([128, S], BF16, "KT")
    VT = P.sb([128, S], BF16, "VT")
    KTM = P.sb([128, 32, 128], BF16, "KTM")
    VTM = P.sb([128, 32, 128], BF16, "VTM")
    rnp = Pool(P, 2, [128, 512], F32, "rn")
    Sst = f32t([128, 128], "Sst")
    Sb = P.sb([128, 128], BF16, "Sb")
    NBATCH = 4
    mk = lambda n, dt=F32, k=NBATCH: Pool(P, k, [128, 128], dt, n)
    p_gb, p_bb, p_DT, p_DL, p_tmp, p_tmp2 = mk("gb"), mk("bb"), mk("DT"), mk("DL"), mk("tmpa"), mk("tmpb")
    p_A, p_AT, p_P = mk("A", F32, 2 * NBATCH), mk("AT", F32, 2 * NBATCH), mk("P", F32, 2 * NBATCH)
    p_intra, p_Pb, p_ub, p_Keg, p_WT, p_Kdec, p_qdT, p_EGr = (mk("intra", BF16), mk("Pb", BF16), mk("ub"), mk("Keg", BF16),
                                                              mk("WT", BF16), mk("Kdec", BF16), mk("qdT", BF16), mk("EGr"))
    p_vn, p_zs, p_t, p_ao, p_aoT = mk("vn", BF16, 2), mk("zs", F32, 2), mk("tt", F32, 2), mk("ao", BF16, 2), mk("aoT", BF16, 2)
    p_st = Pool(P, 2, [128, 6], F32, "gst")
    p_mv = Pool(P, 2, [128, 8], F32, "gmv")
    hcp = Pool(P, 2, [128, 8, 128], BF16, "hchunk")
    for hd in range(4):
        for x in range(3):
            P.dma(Wh[:, :, x, :], w_in.v(w3[:, :, x * 512 + hd * 128: x * 512 + (hd + 1) * 128]), eng="pool")
        P.dma(Wz, w_in.v(w3[:, :, 1536 + hd * 128:1536 + (hd + 1) * 128]), eng="pool")
        for b in range(8):
            ib = blkp.next()
            P.dma(ib, C.hT.v(hT3[:, :, b * 512:(b + 1) * 512], hT_keys(b)))
            for x in range(3):
                ps = nb()
                for kc in range(8):
                    P.mm(ps, Wh[:, kc, x, :], ib[:, kc, :], start=(kc == 0), stop=(kc == 7))
                P.copy(X[x][:, 3 + b * 512:3 + (b + 1) * 512], ps, eng="act")
        for x in range(3):
            Y = X[x]
            ch = x * 4 + hd
            Yc = Ycv
            P.ts(Yc, Y[:, 0:S], cw[:, ch, 0:1], None, ALU.mult)
            for j in range(1, 4):
                P.stt(Yc, Y[:, j:S + j], cw[:, ch, j:j + 1], Yc, ALU.mult, ALU.add)
            P.act(Yc, Yc, AF.Silu)
            if x < 2:
                P.tt(sq, Yc, Yc, ALU.mult, eng="pool")
                dst = QT if x == 0 else KT
                for b in range(8):
                    cs_ = slice(b * 512, (b + 1) * 512)
                    ps = nb()
                    P.mm(ps, C.onesb, sq[:, cs_])
                    rn = rnp.next()
                    P.ts(rn, ps, RMS_EPS, None, ALU.add)
                    P.act(rn, rn, AF.Sqrt)
                    P.recip(rn, rn)
                    if x == 0:
                        P.stt(dst[:, cs_], Yc[:, cs_], 128 ** -0.5, rn, ALU.mult, ALU.mult)
                    else:
                        P.tt(dst[:, cs_], Yc[:, cs_], rn, ALU.mult)
            else:
                P.copy(VT, Yc, eng="act")
        for src, dstm in ((KT, KTM), (VT, VTM)):
            for g8 in range(4):
                for c8 in range(8):
                    c = g8 * 8 + c8
                    P.tr(C.pst[:, c8 * 128:(c8 + 1) * 128], src[:, c * 128:(c + 1) * 128], C.identb)
                P.copy(dstm[:, g8 * 8:(g8 + 1) * 8, :], C.pst.re("p (c t) -> p c t", c=8), eng="act")
        P.memset(Sst, 0.0)
        P.memset(Sb, 0.0)
        if GM == "stageA":
            continue
        for cb in range(32 // NBATCH):
            chunks = list(range(cb * NBATCH, (cb + 1) * NBATCH))
            st = {}
            for c in chunks:
                d = st[c] = {}
                chx = c * 4 + hd
                col = lambda T_: T_[:, chx:chx + 1]
                cs_ = slice(c * 128, (c + 1) * 128)
                gb, bb = p_gb.next(), p_bb.next()
                P.ts(gb, ONESf, col(Gt), None, ALU.mult)
                P.ts(bb, ONESf, col(BETA), None, ALU.mult)
                gcr = nb()[:, 0:128]
                P.mm(gcr, gb, TRI)
                ber = nb()[:, 0:128]
                P.mm(ber, bb, IDF)
                DT, DL, tmp, tmp2 = p_DT.next(), p_DL.next(), p_tmp.next(), p_tmp2.next()
                P.stt(tmp, gcr, col(GC), NEGU, ALU.subtract, ALU.add)
                P.act(DT, tmp, AF.Exp)
                P.stt(tmp2, gcr, col(GC), POSL, ALU.subtract, ALU.add)
                P.act(DL, tmp2, AF.Exp, scale=-1.0)
                EGr = p_EGr.next()
                P.act(EGr, gcr, AF.Exp)
                qdT = d["qdT"] = p_qdT.next()
                P.tt(qdT, QT[:, cs_], EGr, ALU.mult, eng="pool")
                gps = nb()[:, 0:128]
                P.mm(gps, KT[:, cs_], KT[:, cs_])
                qkps = nb()[:, 0:128]
                P.mm(qkps, KT[:, cs_], QT[:, cs_])
                intra = d["intra"] = p_intra.next()
                P.tt(intra, qkps, DT, ALU.mult)
                P.tt(DT, DT, STRU, ALU.mult, eng="pool")
                P.tt(DL, DL, STRL, ALU.mult, eng="pool")
                A, AT = p_A.next(), p_AT.next()
                P.stt(A, gps, col(BETA), DT, ALU.mult, ALU.mult)
                P.tt(tmp, ber, DL, ALU.mult)
                P.tt(AT, gps, tmp, ALU.mult)
                Pm = p_P.next()
                P.tt(Pm, IDF, A, ALU.subtract, eng="pool")
                d["A"], d["AT"], d["P"] = A, AT, Pm
                Keg, Kdec = d["Keg"], d["Kdec"] = p_Keg.next(), p_Kdec.next()
                P.ts(Keg, KTM[:, c, :], col(EG), None, ALU.mult)
                P.ts(Kdec, KTM[:, c, :], col(EDL), None, ALU.mult)
            if GM == "s1":
                continue
            for lev in range(6):
                for c in chunks:
                    d = st[c]
                    psa, psb = nb()[:, 0:128], nb()[:, 0:128]
                    P.mm(psa, d["AT"], d["A"])
                    P.mm(psb, d["A"], d["AT"])
                    A2, AT2 = p_A.next(), p_AT.next()
                    P.copy(A2, psa, eng="act")
                    P.copy(AT2, psb, eng="dve")
                    d["A"], d["AT"] = A2, AT2
                for c in chunks:
                    d = st[c]
                    psc = nb()[:, 0:128]
                    P.mm(psc, d["AT"], d["P"])
                    P2 = p_P.next()
                    P.tt(P2, psc, d["P"], ALU.add)
                    d["P"] = P2
            if GM.startswith("neu"):
                continue
            for c in chunks:
                d = st[c]
                d["Pb"] = p_Pb.next()
                P.copy(d["Pb"], d["P"], eng="act")
            for c in chunks:
                d = st[c]
                chx = c * 4 + hd
                col = lambda T_: T_[:, chx:chx + 1]
                psu, psw = nb()[:, 0:128], nb()[:, 0:128]
                P.mm(psu, d["Pb"], VTM[:, c, :])
                P.mm(psw, d["Keg"], d["Pb"])
                ub = d["ub"] = p_ub.next()
                P.ts(ub, psu, col(BETA), None, ALU.mult)
                WT = d["WT"] = p_WT.next()
                P.copy(WT, psw, eng="act")
            if GM == "uw":
                continue
            for c in chunks:
                d = st[c]
                chx = c * 4 + hd
                col = lambda T_: T_[:, chx:chx + 1]
                cs_ = slice(c * 128, (c + 1) * 128)
                ps1 = nb()[:, 0:128]
                P.mm(ps1, d["WT"], Sb)
                vn = p_vn.next()
                P.stt(vn, ps1, col(NBETA), d["ub"], ALU.mult, ALU.add)
                pso = nb()[:, 0:128]
                P.mm(pso, d["qdT"], Sb, start=True, stop=False)
                P.mm(pso, d["intra"], vn, start=False, stop=True)
                ps3 = nb()[:, 0:128]
                P.mm(ps3, d["Kdec"], vn)
                P.stt(Sb, Sst, col(EGL), ps3, ALU.mult, ALU.add)
                P.stt(Sst, Sst, col(EGL), ps3, ALU.mult, ALU.add)
                hc = hcp.next()
                P.dma(hc, C.hT.v(hT3[:, :, cs_], ("t", c)))
                psz = nb()[:, 0:128]
                for kc in range(8):
                    P.mm(psz, hc[:, kc, :], Wz[:, kc, :], start=(kc == 0), stop=(kc == 7))
                zs = p_zs.next()
                P.act(zs, psz, AF.Silu)
                gst, gmv = p_st.next(), p_mv.next()
                P.add("dve", lambda e, o=gst, i_=pso: e.bn_stats(o.ap, i_.ap), [pso], [gst])
                P.add("dve", lambda e, o=gmv[:, 0:2], i_=gst: e.bn_aggr(o.ap, i_.ap), [gst], [gmv])
                P.stt(gmv[:, 2:3], gmv[:, 0:1], gmv[:, 0:1], gmv[:, 1:2], ALU.mult, ALU.add)
                P.ts(gmv[:, 2:3], gmv[:, 2:3], RMS_EPS, None, ALU.add)
                P.act(gmv[:, 3:4], gmv[:, 2:3], AF.Sqrt)
                P.recip(gmv[:, 4:5], gmv[:, 3:4])
                t_ = p_t.next()
                P.stt(t_, pso, gmv[:, 4:5], NG, ALU.mult, ALU.mult)
                ao = p_ao.next()
                P.tt(ao, t_, zs, ALU.mult, eng="pool")
                P.tr(C.pst[:, 0:128], ao, C.identb)
                aoT = p_aoT.next()
                P.copy(aoT, C.pst[:, 0:128], eng="act")
                P.dma(C.mixT.v(C.mixT.ap[hd * 128:(hd + 1) * 128, cs_], ("t", c)), aoT)
    P.pop()
```

```python
import numpy as np
from contextlib import ExitStack
import concourse.bass as bass
import concourse.mybir as mybir
from concourse.bass_utils import run_bass_kernel_spmd

F32 = mybir.dt.float32
BF16 = mybir.dt.bfloat16
I32 = mybir.dt.int32
AF = mybir.ActivationFunctionType
ALU = mybir.AluOpType
AX = mybir.AxisListType

NDMA = 16
NSDMA = 8


def is_dma(t):
    return t.startswith("dma") or t.startswith("sdma")


class Buf:
    __slots__ = ("name", "w", "r")

    def __init__(self, name=""):
        self.name = name
        self.w = None
        self.r = []


class V:
    def __init__(self, ap, bufs):
        self.ap = ap
        self.bufs = bufs if isinstance(bufs, tuple) else (bufs,)

    def __getitem__(self, k):
        return V(self.ap[k], self.bufs)

    def re(self, s, **kw):
        return V(self.ap.rearrange(s, **kw), self.bufs)

    def bc(self, dt):
        return V(self.ap.bitcast(dt), self.bufs)

    def bcast(self, shape):
        return V(self.ap.to_broadcast(shape), self.bufs)


class Op:
    __slots__ = ("eng", "fn", "deps", "track", "tidx", "inc", "val")


class Prog:
    ENG = ("pe", "act", "dve", "pool", "sp")
    BLK = {"pe": "tensor", "act": "scalar", "dve": "vector", "pool": "gpsimd", "sp": "sync"}

    def __init__(self, nc):
        self.nc = nc
        self.es = ExitStack()
        self.ops = {e: [] for e in self.ENG}
        self.tracks = {e: [] for e in self.ENG}
        for i in range(NDMA):
            self.tracks["dma%d" % i] = []
        for i in range(NSDMA):
            self.tracks["sdma%d" % i] = []
        self.dma_rr = 0
        self.sdma_rr = 0
        self.seen = {e: {} for e in self.ENG}
        self.nalloc = 0
        self.out_ops = []
        self.stack = [self.es]

    def sb(self, shape, dt, name=None):
        self.nalloc += 1
        name = name or ("t%d" % self.nalloc)
        t = self.stack[-1].enter_context(self.nc.sbuf_tensor(name + "_%d" % self.nalloc, list(shape), dt))
        return V(t.ap() if hasattr(t, "ap") and callable(t.ap) else t[:], Buf(name))

    def ps(self, shape, dt=F32, name=None):
        self.nalloc += 1
        name = name or ("p%d" % self.nalloc)
        t = self.stack[-1].enter_context(self.nc.psum_tensor(name + "_%d" % self.nalloc, list(shape), dt))
        return V(t.ap() if hasattr(t, "ap") and callable(t.ap) else t[:], Buf(name))

    def dram(self, name, shape, dt, kind="Internal"):
        t = self.nc.dram_tensor(name, list(shape), dt, kind=kind)
        return DT(t.ap(), name)


    def push(self):
        self.stack.append(ExitStack())

    def pop(self):
        self.barrier()
        self.stack.pop().close()

    def barrier(self):
        last = {t: len(l) - 1 for t, l in self.tracks.items() if l}
        for eng in self.ENG:
            op = Op()
            op.eng = eng
            op.fn = None
            op.inc = False
            op.val = None
            op.track = eng
            op.tidx = len(self.tracks[eng])
            op.deps = {}
            seen = self.seen[eng]
            for t, i in last.items():
                if seen.get(t, -1) < i:
                    seen[t] = i
                    op.deps[t] = i
                    self.tracks[t][i].inc = True
            self.ops[eng].append(op)

    def add(self, eng, fn, reads, writes, dma=False):
        op = Op()
        op.eng = eng
        op.fn = fn
        op.inc = dma
        op.val = None
        if dma and eng == "pool":
            tr = "sdma%d" % self.sdma_rr
            self.sdma_rr = (self.sdma_rr + 1) % NSDMA
        elif dma:
            tr = "dma%d" % self.dma_rr
            self.dma_rr = (self.dma_rr + 1) % NDMA
        else:
            tr = eng
        op.track = tr
        op.tidx = len(self.tracks[tr])
        deps = {}

        def need(t, i):
            if deps.get(t, -1) < i:
                deps[t] = i

        for v in reads:
            for b in v.bufs:
                if b.w is not None:
                    t, i = b.w
                    if t == tr and eng == "pe":
                        continue
                    need(t, i)
        for v in writes:
            for b in v.bufs:
                if b.w is not None:
                    t, i = b.w
                    if not (t == tr and eng == "pe"):
                        need(t, i)
                for (t, i) in b.r:
                    if not (t == tr and eng == "pe"):
                        need(t, i)
        if dma and op.tidx > 0:
            need(tr, op.tidx - 1)
        seen = self.seen[eng]
        op.deps = {}
        for t, i in deps.items():
            if seen.get(t, -1) < i:
                seen[t] = i
                op.deps[t] = i
                self.tracks[t][i].inc = True
        self.tracks[tr].append(op)
        self.ops[eng].append(op)
        for v in reads:
            for b in v.bufs:
                b.r.append((tr, op.tidx))
        for v in writes:
            for b in v.bufs:
                b.w = (tr, op.tidx)
                b.r = []
        return op

    def mm(self, out, lhsT, rhs, start=True, stop=True):
        return self.add("pe", lambda e: e.matmul(out.ap, lhsT.ap, rhs.ap, start=start, stop=stop),
                        [lhsT, rhs] + ([] if start else [out]), [out])

    def tr(self, out, in_, ident):
        return self.add("pe", lambda e: e.transpose(out.ap, in_.ap, ident.ap), [in_, ident], [out])

    def act(self, out, in_, func, bias=None, scale=None, accum=None, eng="act"):
        kw = {}
        rd = [in_]
        wr = [out]
        if bias is not None:
            if isinstance(bias, V):
                kw["bias"] = bias.ap
                rd.append(bias)
            else:
                kw["bias"] = bias
        if scale is not None:
            if isinstance(scale, V):
                kw["scale"] = scale.ap
                rd.append(scale)
            else:
                kw["scale"] = scale
        if accum is not None:
            kw["accum_out"] = accum.ap
            wr.append(accum)
        return self.add("act", lambda e: e.activation(out.ap, in_.ap, func, **kw), rd, wr)

    def tt(self, out, a, b, op, eng="dve"):
        return self.add(eng, lambda e: e.tensor_tensor(out.ap, a.ap, b.ap, op), [a, b], [out])

    def ts(self, out, a, s1, s2, op0, op1=None, accum=None, eng="dve"):
        rd = [a]
        wr = [out]
        s1a = s1.ap if isinstance(s1, V) else s1
        s2a = s2.ap if isinstance(s2, V) else s2
        if isinstance(s1, V):
            rd.append(s1)
        if isinstance(s2, V):
            rd.append(s2)
        kw = {}
        if op1 is not None:
            kw["op1"] = op1
        if accum is not None:
            kw["accum_out"] = accum.ap
            wr.append(accum)
        return self.add(eng, lambda e: e.tensor_scalar(out.ap, a.ap, s1a, s2a, op0, **kw), rd, wr)

    def stt(self, out, a, s, b, op0, op1, eng="dve"):
        rd = [a, b]
        sa = s.ap if isinstance(s, V) else s
        if isinstance(s, V):
            rd.append(s)
        return self.add(eng, lambda e: e.scalar_tensor_tensor(out.ap, a.ap, sa, b.ap, op0, op1), rd, [out])

    def copy(self, out, in_, eng="dve"):
        if eng == "act":
            return self.add("act", lambda e: e.copy(out.ap, in_.ap), [in_], [out])
        return self.add(eng, lambda e: e.tensor_copy(out.ap, in_.ap), [in_], [out])

    def memset(self, out, val, eng="pool"):
        return self.add(eng, lambda e: e.memset(out.ap, val), [], [out])

    def recip(self, out, in_):
        return self.add("dve", lambda e: e.reciprocal(out.ap, in_.ap), [in_], [out])

    def scan(self, out, d0, d1, init, op0, op1):
        rd = [d0, d1]
        ia = init.ap if isinstance(init, V) else init
        if isinstance(init, V):
            rd.append(init)
        return self.add("dve", lambda e: e.tensor_tensor_scan(out.ap, d0.ap, d1.ap, ia, op0, op1), rd, [out])

    def dma(self, out, in_, eng="sp", is_out=False, **kw):
        op = self.add(eng, lambda e: e.dma_start(out=out.ap, in_=in_.ap, **kw), [in_], [out], dma=True)
        if is_out:
            self.out_ops.append(op)
        return op

    def emit(self):
        nc = self.nc
        fin = Op()
        fin.eng = "sp"
        fin.fn = None
        fin.inc = False
        fin.track = "sp"
        fin.tidx = len(self.tracks["sp"])
        fin.deps = {}
        for tname, lst in self.tracks.items():
            if is_dma(tname) and lst:
                fin.deps[tname] = len(lst) - 1
        self.ops["sp"].append(fin)
        for tname, lst in self.tracks.items():
            c = 0
            for op in lst:
                if is_dma(tname):
                    c += 16
                    op.val = c
                else:
                    if op.inc:
                        c += 1
                    op.val = c
        sems = {}
        for tname, lst in self.tracks.items():
            if lst:
                sems[tname] = self.es.enter_context(nc.semaphore("s_" + tname))
        block = self.es.enter_context(nc.Block())
        tracks = self.tracks
        for eng in self.ENG:
            ops = self.ops[eng]
            if not ops:
                continue

            def body(e, ops=ops):
                for op in ops:
                    for t, i in op.deps.items():
                        e.wait_ge(sems[t], tracks[t][i].val)
                    if op.fn is None:
                        continue
                    ins = op.fn(e)
                    if is_dma(op.track):
                        ins.then_inc(sems[op.track], 16)
                    elif op.inc:
                        ins.then_inc(sems[op.track], 1)

            getattr(block, self.BLK[eng])(body)
        self.es.close()


class DT:
    def __init__(self, ap, name):
        self.ap = ap
        self.name = name
        self.bufs = {}

    def v(self, ap, keys=(0,)):
        if not isinstance(keys, (tuple, list)):
            keys = (keys,)
        bs = []
        for k in keys:
            if k not in self.bufs:
                self.bufs[k] = Buf("%s[%s]" % (self.name, k))
            bs.append(self.bufs[k])
        return V(ap, tuple(bs))


class Pool:
    def __init__(self, P, n, shape, dt, name, psum=False):
        self.tiles = [(P.ps if psum else P.sb)(shape, dt, name="%s%d" % (name, i)) for i in range(n)]
        self.i = 0

    def next(self):
        t = self.tiles[self.i]
        self.i = (self.i + 1) % len(self.tiles)
        return t


import math

S = 4096
D = 1024
NT = 32
FF = 2816
ALPHA = 8 ** 0.25
LN_EPS = 1e-5
RMS_EPS = 1e-6
TWO_PI = 2.0 * math.pi


def host_consts():
    c = {}
    i = np.arange(128)
    c["c_ident"] = np.eye(128, dtype=np.float32)
    c["c_tri"] = (i[:, None] <= i[None, :]).astype(np.float32)
    c["c_ones"] = np.ones((128, 128), np.float32)
    c["c_negU"] = np.where(i[None, :] >= i[:, None], 0.0, -30000.0).astype(np.float32)
    c["c_posL"] = np.where(i[None, :] <= i[:, None], 0.0, 30000.0).astype(np.float32)
    c["c_strU"] = (i[None, :] > i[:, None]).astype(np.float32)
    c["c_strL"] = (i[None, :] < i[:, None]).astype(np.float32)
    dm = np.zeros((128, 256), np.float32)
    dm[:, :128] = (i[:, None] <= i[None, :])
    dm[:, 128:] = (i[:, None] >= i[None, :])
    c["c_dmask"] = dm
    c["c_dneg"] = ((dm - 1.0) * 30000.0).astype(np.float32)
    invf = (10000.0 ** (-np.arange(0, 64, 2, dtype=np.float32) / 64)).astype(np.float32)
    r = np.arange(128)
    c["c_invf"] = (invf[r % 32] / np.float32(TWO_PI)).astype(np.float32).reshape(128, 1)
    c["c_sgn"] = np.where((r % 64) < 32, -1.0, 1.0).astype(np.float32).reshape(128, 1)
    c["c_iota"] = np.broadcast_to(np.arange(64, dtype=np.float32)[None, :], (128, 64)).copy()
    sel = np.zeros((128, 64), np.float32)
    sel[64 + np.arange(64), np.arange(64)] = 1.0
    c["c_sel"] = sel
    return c


def host_layer_inputs(inp):
    o = {}
    for l in range(2):
        a_re, a_im, ldt = inp["s5_a_re"][l], inp["s5_a_im"][l], inp["s5_log_dt"][l]
        col = np.stack([a_re, a_im, np.repeat(ldt[:, None], 64, 1)], 0)
        col = col.reshape(3, 32, 2, 64).transpose(2, 3, 0, 1).reshape(128, 3, 32)
        o["s5col%d" % l] = np.ascontiguousarray(col, np.float32)
        o["s5row%d" % l] = np.ascontiguousarray(np.stack([a_re.reshape(-1), a_im.reshape(-1), np.repeat(ldt, 64)], 0), np.float32)
        for nm, src in (("re", inp["s5_b_re"][l]), ("im", inp["s5_b_im"][l])):
            bp = np.zeros((4, 2, 16, 32, 2, 64), np.float32)
            sp = src.reshape(8, 4, 2, 64, 16)
            for q in range(4):
                for gl in range(2):
                    bp[q, gl, :, q::4, gl, :] = sp[:, q, gl].transpose(2, 0, 1)
            o["s5b%s%d" % (nm, l)] = bp.reshape(128, 32 * 128)
        for nm, src in (("re", inp["s5_c_re"][l]), ("im", inp["s5_c_im"][l])):
            cp = np.zeros((2, 64, 32, 4, 2, 16), np.float32)
            sp = src.reshape(8, 4, 2, 16, 64)
            for q in range(4):
                for gl in range(2):
                    cp[gl, :, q::4, q, gl, :] = sp[:, q, gl].transpose(2, 0, 1)
            o["s5c%s%d" % (nm, l)] = cp.reshape(128, 32 * 128)
        o["s5d%d" % l] = np.ascontiguousarray(inp["s5_d"][l].reshape(8, 128).T, np.float32)
        o["dncw%d" % l] = np.ascontiguousarray(inp["dn_conv_w"][l].T.reshape(12, 128, 4).transpose(1, 0, 2), np.float32)
    return o


class Ctx:
    pass


def load_w(P, dst, src_ap, KC, eng="pool"):
    for kc in range(KC):
        P.dma(dst[:, kc, :], src_ap(kc), eng=eng)


def load_h(P, C, tt, h_src):
    ht = C.p_h.next()
    P.dma(ht, h_src.v(h_src.ap[tt * 128:(tt + 1) * 128, :], tt))
    return ht


def ln_epilogue(P, C, lnp, tt, sub_halves, ht):
    z = C.p_z.next()
    st = C.p_st.next()
    for hf in range(2):
        P.stt(z[:, hf * 512:(hf + 1) * 512], ht[:, hf * 512:(hf + 1) * 512], ALPHA, sub_halves[hf], ALU.mult, ALU.add)
        zz = z[:, hf * 512:(hf + 1) * 512]
        P.add("dve", lambda e, o=st[:, hf, :], i=zz: e.bn_stats(o.ap, i.ap), [zz], [st])
    mv = C.p_mv.next()
    P.add("dve", lambda e, o=mv[:, 0:2], i=st.re("p a b -> p (a b)"): e.bn_aggr(o.ap, i.ap), [st], [mv])
    P.ts(mv[:, 2:3], mv[:, 1:2], LN_EPS, None, ALU.add)
    P.act(mv[:, 3:4], mv[:, 2:3], AF.Sqrt)
    P.recip(mv[:, 4:5], mv[:, 3:4])
    zn = C.p_zn.next()
    P.ts(zn, z, mv[:, 0:1], mv[:, 4:5], ALU.subtract, ALU.mult)
    hn = C.p_hn.next()
    P.tt(zn, zn, lnp[0], ALU.mult, eng="pool")
    P.tt(hn, zn, lnp[1], ALU.add, eng="pool")
    P.dma(C.out.v(C.out.ap[tt * 128:(tt + 1) * 128, :], tt), hn, is_out=True)
    hb = C.p_hb.next()
    P.copy(hb, hn, eng="act")
    return (tt, hb)


def ln_epilogue_b(P, C, pend):
    if pend is None:
        return
    tt, hb = pend
    pt = C.pst
    for c in range(8):
        P.tr(pt[:, c * 128:(c + 1) * 128], hb[:, c * 128:(c + 1) * 128], C.identb)
    hts = C.p_hts.next()
    P.copy(hts, pt.re("p (c t) -> p c t", c=8), eng="act")
    P.dma(C.hT.v(C.hT.ap.rearrange("(c p) t -> p c t", p=128)[:, :, tt * 128:(tt + 1) * 128], ("t", tt)), hts)


def ln_pools(P, C):
    C.p_h = Pool(P, 8, [128, 1024], F32, "h")
    C.p_z = Pool(P, 2, [128, 1024], F32, "z")
    C.p_zn = Pool(P, 2, [128, 1024], F32, "zn")
    C.p_hn = Pool(P, 2, [128, 1024], F32, "hn")
    C.p_hb = Pool(P, 3, [128, 1024], BF16, "hb")
    C.p_hts = Pool(P, 2, [128, 8, 128], BF16, "hts")
    C.p_st = Pool(P, 2, [128, 2, 6], F32, "st")
    C.p_mv = Pool(P, 2, [128, 8], F32, "mv")


def load_ln(P, C, g_ap, b_ap):
    G = P.sb([128, 1024], F32, "lng")
    B = P.sb([128, 1024], F32, "lnb")
    P.dma(G, g_ap)
    P.dma(B, b_ap)
    return (G, B)


def hT_keys(b):
    return tuple(("t", t) for t in range(b * 4, b * 4 + 4))


def phase_prep(P, C, I):
    import os
    MODE = os.environ.get("PREP_MODE", "all")
    P.push()
    xs = Pool(P, 2, [128, 1024], F32, "xs")
    xb = Pool(P, 2, [128, 1024], BF16, "xb")
    hts = Pool(P, 2, [128, 8, 128], BF16, "hts")
    for tt in range(NT if MODE in ('all', 'A') else 0):
        x_t = xs.next()
        P.dma(x_t, I["x"].v(I["x"].ap[tt * 128:(tt + 1) * 128, :], tt))
        b_t = xb.next()
        P.copy(b_t, x_t, eng="dve")
        for c in range(8):
            P.tr(C.pst[:, c * 128:(c + 1) * 128], b_t[:, c * 128:(c + 1) * 128], C.identb)
        h_t = hts.next()
        P.copy(h_t, C.pst.re("p (c t) -> p c t", c=8), eng="act")
        P.dma(C.hT.v(C.hT.ap.rearrange("(c p) t -> p c t", p=128)[:, :, tt * 128:(tt + 1) * 128], ("t", tt)), h_t)
    for mt in range(2 if MODE in ('all', 'A2') else 0):
        x_t = xs.next()
        P.dma(x_t, I["mem"].v(I["mem"].ap[mt * 128:(mt + 1) * 128, :], mt))
        b_t = xb.next()
        P.copy(b_t, x_t, eng="dve")
        for c in range(8):
            P.tr(C.pst[:, c * 128:(c + 1) * 128], b_t[:, c * 128:(c + 1) * 128], C.identb)
        h_t = hts.next()
        P.copy(h_t, C.pst.re("p (c t) -> p c t", c=8), eng="act")
        P.dma(C.memT.v(C.memT.ap.rearrange("(c p) t -> p c t", p=128)[:, :, mt * 128:(mt + 1) * 128], 0), h_t)
    invf = P.sb([128, 1], F32, "invf")
    sgn = P.sb([128, 1], F32, "sgn")
    P.dma(invf, I["c_invf"].v(I["c_invf"].ap))
    P.dma(sgn, I["c_sgn"].v(I["c_sgn"].ap))
    W = 1024
    pi_ = P.sb([128, W], I32, "posi")
    pf = P.sb([128, W], F32, "posf")
    tu = P.sb([128, W], F32, "tu")
    f = P.sb([128, W], F32, "f")
    tb = P.sb([128, W], F32, "tb")
    scr = frac_scratch(P, [128, W])
    for cq in range(S // W if MODE in ('all', 'B') else 0):
        P.dma(pi_, I["pos"].v(I["pos"].ap[:, cq * W:(cq + 1) * W].partition_broadcast(128)))
        P.copy(pf, pi_, eng="dve")
        P.ts(tu, pf, invf[:, 0:1], None, ALU.mult)
        for which, shift, dst in (("sin", 0.0, C.sinT), ("cos", 0.25, C.cosT)):
            frac_wrap(P, f, tu, shift, scr)
            P.act(tb, f, AF.Sin, scale=TWO_PI)
            if which == "sin":
                P.ts(tb, tb, sgn[:, 0:1], None, ALU.mult)
            P.dma(dst.v(dst.ap[:, cq * W:(cq + 1) * W], cq), tb)
    P.pop()


def frac_scratch(P, shape):
    return (P.sb(shape, I32, "fw_ki"), P.sb(shape, F32, "fw_kf"), P.sb(shape, F32, "fw_m"))


def frac_wrap(P, f, tu, shift, scr):
    ki, kf, m = scr
    if shift != 0.0:
        P.ts(f, tu, shift, None, ALU.add)
        src = f
    else:
        src = tu
    P.copy(ki, src, eng="dve")
    P.copy(kf, ki, eng="dve")
    P.tt(f, src, kf, ALU.subtract)
    P.ts(m, f, 0.5, None, ALU.is_gt)
    P.tt(f, f, m, ALU.subtract)
    P.ts(m, f, -0.5, None, ALU.is_lt)
    P.tt(f, f, m, ALU.add)


def phase_proj_ln(P, C, inT, K, W_ap, lnp_aps, h_src, gate_ap=None, Wpre=None):
    P.push()
    KC = K // 128
    if Wpre is not None:
        Wb = Wpre
    else:
        Wb = P.sb([128, KC, 1024], BF16, "Wb")
        load_w(P, Wb, W_ap, KC)
    Wg = None
    if gate_ap is not None:
        Wg = P.sb([128, KC, 1024], BF16, "Wg")
        load_w(P, Wg, gate_ap, KC)
    lnp = load_ln(P, C, *lnp_aps)
    ln_pools(P, C)
    blk = Pool(P, 2, [128, KC, 512], BF16, "inblk")
    sig = Pool(P, 2, [128, 1024], F32, "sig") if Wg is not None else None
    pend = None

    def prefetch(b):
        ib_ = blk.next()
        P.dma(ib_, inT.v(inT.ap.rearrange("(c p) t -> p c t", p=128)[:, :, b * 512:(b + 1) * 512], hT_keys(b)))
        return ib_, [load_h(P, C, b * 4 + t4, h_src) for t4 in range(4)]

    nxt = prefetch(0)
    for b in range(8):
        ib, hts_ = nxt
        if b < 7:
            nxt = prefetch(b + 1)
        for t4 in range(4):
            tt = b * 4 + t4
            par = (tt % 2) * 2 if Wg is None else 0
            sub = [C.PS[par], C.PS[par + 1]]
            for hf in range(2):
                for kc in range(KC):
                    P.mm(sub[hf], ib[:, kc, t4 * 128:(t4 + 1) * 128], Wb[:, kc, hf * 512:(hf + 1) * 512], start=(kc == 0), stop=(kc == KC - 1))
            if Wg is not None:
                gp = [C.PS[2], C.PS[3]]
                sg = sig.next()
                for hf in range(2):
                    for kc in range(KC):
                        P.mm(gp[hf], ib[:, kc, t4 * 128:(t4 + 1) * 128], Wg[:, kc, hf * 512:(hf + 1) * 512], start=(kc == 0), stop=(kc == KC - 1))
                    P.act(sg[:, hf * 512:(hf + 1) * 512], gp[hf], AF.Sigmoid)
                    P.tt(sg[:, hf * 512:(hf + 1) * 512], sub[hf], sg[:, hf * 512:(hf + 1) * 512], ALU.mult)
                sub = [sg[:, 0:512], sg[:, 512:1024]]
            ln_epilogue_b(P, C, pend)
            pend = ln_epilogue(P, C, lnp, tt, sub, hts_[t4])
    ln_epilogue_b(P, C, pend)
    P.pop()


def phase_ffn_up(P, C, wg_ap, wu_ap, Wgpre=None, after_loads=None):
    P.push()
    if Wgpre is not None:
        Wg = Wgpre
    else:
        Wg = P.sb([128, 8, FF], BF16, "Wg")
        load_w(P, Wg, wg_ap, 8)
    Wu = P.sb([128, 8, FF], BF16, "Wu")
    load_w(P, Wu, wu_ap, 8)
    if after_loads is not None:
        after_loads()
    blk = Pool(P, 2, [128, 8, 512], BF16, "hblk")
    ab = Pool(P, 2, [128, 22, 512], BF16, "ablk")
    sgp = Pool(P, 2, [128, 512], F32, "sg")
    def prefetch(b):
        ib_ = blk.next()
        P.dma(ib_, C.hT.v(C.hT.ap.rearrange("(c p) t -> p c t", p=128)[:, :, b * 512:(b + 1) * 512], hT_keys(b)))
        return ib_

    nxt = prefetch(0)
    for b in range(8):
        ib = nxt
        if b < 7:
            nxt = prefetch(b + 1)
        a = ab.next()
        for j in range(22):
            pg = C.PS[(j % 2) * 2]
            pu = C.PS[(j % 2) * 2 + 1]
            for kc in range(8):
                P.mm(pg, Wg[:, kc, j * 128:(j + 1) * 128], ib[:, kc, :], start=(kc == 0), stop=(kc == 7))
            for kc in range(8):
                P.mm(pu, Wu[:, kc, j * 128:(j + 1) * 128], ib[:, kc, :], start=(kc == 0), stop=(kc == 7))
            sg = sgp.next()
            P.act(sg, pg, AF.Silu)
            P.tt(a[:, j, :], pu, sg, ALU.mult)
        P.dma(C.aT.v(C.aT.ap.rearrange("(c p) t -> p c t", p=128)[:, :, b * 512:(b + 1) * 512], hT_keys(b)), a)
    P.pop()


def phase_cross(P, C, I, layer, lnp_aps, h_src, after_loads=None):
    P.push()
    wq, wk, wv, wo = (I[n] for n in ("xq_w", "xk_w", "xv_w", "xo_w"))
    rows = lambda w: (lambda kc: w.v(w.ap[layer, kc * 128:(kc + 1) * 128, :]))
    Wq = P.sb([128, 8, 1024], BF16, "Wq")
    Wo = P.sb([128, 8, 1024], BF16, "Wo")
    Wk = P.sb([128, 8, 1024], BF16, "Wk")
    load_w(P, Wk, rows(wk), 8)
    mT = P.sb([128, 8, 256], BF16, "mT")
    P.dma(mT, C.memT.v(C.memT.ap.rearrange("(c p) t -> p c t", p=128), 0))
    kT = P.sb([128, 8, 256], BF16, "kT")
    import os
    if os.environ.get("CROSS_MODE", "all") == "pre0":
        dbg = P.sb([128, 1024], F32, "dbg")
        P.copy(dbg, Wk[:, 3, :])
        P.dma(C.out.v(C.out.ap[0:128, :], 0), dbg, is_out=True)
        P.pop()
        return
    for oc in range(8):
        ps = C.PS[oc % 2]
        for kc in range(8):
            P.mm(ps[:, 0:256], Wk[:, kc, oc * 128:(oc + 1) * 128], mT[:, kc, :], start=(kc == 0), stop=(kc == 7))
        P.copy(kT[:, oc, :], ps[:, 0:256], eng="act")
    load_w(P, Wk, rows(wv), 8)
    Vm = P.sb([128, 2, 1024], BF16, "Vm")
    for mc in range(2):
        for hf in range(2):
            ps = C.PS[hf]
            for kc in range(8):
                P.mm(ps, mT[:, kc, mc * 128:(mc + 1) * 128], Wk[:, kc, hf * 512:(hf + 1) * 512], start=(kc == 0), stop=(kc == 7))
            P.copy(Vm[:, mc, hf * 512:(hf + 1) * 512], ps, eng="act")
    import os
    CM = os.environ.get("CROSS_MODE", "all")
    if CM == "pre":
        P.pop()
        return
    load_w(P, Wq, rows(wq), 8)
    load_w(P, Wo, rows(wo), 8)
    if after_loads is not None:
        after_loads()
    lnp = load_ln(P, C, *lnp_aps)
    ln_pools(P, C)
    onesb = C.onesb
    blk = Pool(P, 2, [128, 8, 512], BF16, "hblk")
    qTp = Pool(P, 1, [128, 8, 512], BF16, "qT")
    pTp = Pool(P, 2, [128, 2, 512], BF16, "pT")
    rdp = Pool(P, 2, [128, 512], F32, "rden")
    oTp = Pool(P, 2, [128, 8, 512], BF16, "oT")
    pend = None

    def prefetch(b):
        ib_ = blk.next()
        P.dma(ib_, C.hT.v(C.hT.ap.rearrange("(c p) t -> p c t", p=128)[:, :, b * 512:(b + 1) * 512], hT_keys(b)))
        return ib_, [load_h(P, C, b * 4 + t4, h_src) for t4 in range(4)]

    nxt = prefetch(0)
    for b in range(8):
        ib, hts_ = nxt
        if b < 7:
            nxt = prefetch(b + 1)
        qT = qTp.next()
        for oc in range(8):
            ps = C.PS[2 + oc % 2]
            for kc in range(8):
                P.mm(ps, Wq[:, kc, oc * 128:(oc + 1) * 128], ib[:, kc, :], start=(kc == 0), stop=(kc == 7))
            P.copy(qT[:, oc, :], ps, eng="act")
        oT = oTp.next()
        if CM == "q":
            continue
        for hh in range(4):
            pT = pTp.next()
            for mc in range(2):
                ps = C.PS[2 + mc]
                for c in range(2):
                    P.mm(ps, kT[:, hh * 2 + c, mc * 128:(mc + 1) * 128], qT[:, hh * 2 + c, :], start=(c == 0), stop=(c == 1))
                P.act(pT[:, mc, :], ps, AF.Exp, scale=1.0 / 16.0)
            pd = C.PS[4]
            for mc in range(2):
                P.mm(pd, onesb, pT[:, mc, :], start=(mc == 0), stop=(mc == 1))
            rd = rdp.next()
            P.recip(rd, pd)
            for c in range(2):
                po = C.PS[5 + c]
                for mc in range(2):
                    P.mm(po, Vm[:, mc, hh * 256 + c * 128: hh * 256 + (c + 1) * 128], pT[:, mc, :], start=(mc == 0), stop=(mc == 1))
                P.tt(oT[:, hh * 2 + c, :], po, rd, ALU.mult)
        if CM == "attn":
            continue
        for t4 in range(4):
            tt = b * 4 + t4
            sub = [C.PS[0], C.PS[1]]
            for hf in range(2):
                for kc in range(8):
                    P.mm(sub[hf], oT[:, kc, t4 * 128:(t4 + 1) * 128], Wo[:, kc, hf * 512:(hf + 1) * 512], start=(kc == 0), stop=(kc == 7))
            ln_epilogue_b(P, C, pend)
            pend = ln_epilogue(P, C, lnp, tt, sub, hts_[t4])
    ln_epilogue_b(P, C, pend)
    P.pop()


WEIGHT_SPECS = [
    ("hyb_w_in", [2, 1024, 3592]), ("dn_conv_w", [2, 4, 1536]), ("hyb_w_out", [2, 1024, 1024]),
    ("s5_glu_wo", [2, 1024, 1024]), ("s5_glu_wg", [2, 1024, 1024]),
    ("ln_mix_g", [4, 1024]), ("ln_mix_b", [4, 1024]),
    ("xq_w", [4, 1024, 1024]), ("xk_w", [4, 1024, 1024]), ("xv_w", [4, 1024, 1024]), ("xo_w", [4, 1024, 1024]),
    ("ln_x_g", [4, 1024]), ("ln_x_b", [4, 1024]),
    ("ffn_wg", [4, 1024, 2816]), ("ffn_wu", [4, 1024, 2816]), ("ffn_wd", [4, 2816, 1024]),
    ("ln_ffn_g", [4, 1024]), ("ln_ffn_b", [4, 1024]),
]


def build_program(plan):
    nc = bass.Bass("TRN2", target_bir_lowering=False)
    P = Prog(nc)
    C = Ctx()
    I = {}

    def inp(name, shape, dt=F32):
        I[name] = P.dram(name, shape, dt, kind="ExternalInput")

    inp("x", [S, D])
    inp("mem", [256, D])
    inp("pos", [1, S], I32)
    for n, sh in WEIGHT_SPECS:
        inp(n, sh)
    hc = host_consts()
    for n, a in hc.items():
        inp(n, list(a.shape))
    for l in range(2):
        inp("s5col%d" % l, [128, 3, 32])
        inp("s5row%d" % l, [3, 4096])
        for nm in ("bre", "bim", "cre", "cim"):
            inp("s5%s%d" % (nm, l), [128, 4096])
        inp("s5d%d" % l, [128, 8])
        inp("dnp%d" % l, [1, 4 + 4 + 128])
        inp("dncw%d" % l, [128, 12, 4])
    C.out = P.dram("out", [S, D], F32, kind="ExternalOutput")
    C.hT = P.dram("hT", [D, S], BF16)
    C.mixT = P.dram("mixT", [D, S], BF16)
    C.aT = P.dram("aT", [FF, S], BF16)
    C.memT = P.dram("memT", [D, 256], BF16)
    C.cosT = P.dram("cosT", [128, S], F32)
    C.sinT = P.dram("sinT", [128, S], F32)
    C.vd = P.dram("vd", [S, 512], BF16)
    C.PS = [P.ps([128, 512], F32, "ps%d" % i) for i in range(7)]
    C.pst = P.ps([128, 1024], BF16, "pst")
    idf = P.sb([128, 128], F32, "idf")
    P.dma(idf, I["c_ident"].v(I["c_ident"].ap))
    C.identb = P.sb([128, 128], BF16, "identb")
    P.copy(C.identb, idf)
    C.identf = idf
    C.onesb = P.sb([128, 128], BF16, "onesb")
    P.memset(C.onesb, 1.0)

    bvec = lambda nm, l: I[nm].v(I[nm].ap[l:l + 1, :].partition_broadcast(128))
    h_src = I["x"]
    for ph in plan:
        kind = ph[0]
        if kind == "prep":
            phase_prep(P, C, I)
        elif kind == "cross":
            l = ph[1]
            phase_cross(P, C, I, l, (bvec("ln_x_g", l), bvec("ln_x_b", l)), h_src)
            h_src = C.out
        elif kind == "ffn":
            l = ph[1]
            wg, wu, wd = I["ffn_wg"], I["ffn_wu"], I["ffn_wd"]
            P.push()
            WdPre = P.sb([128, 22, 1024], BF16, "WdPre")
            pre = lambda: load_w(P, WdPre, lambda kc: wd.v(wd.ap[l, kc * 128:(kc + 1) * 128, :]), 22)
            phase_ffn_up(P, C, lambda kc: wg.v(wg.ap[l, kc * 128:(kc + 1) * 128, :]), lambda kc: wu.v(wu.ap[l, kc * 128:(kc + 1) * 128, :]), after_loads=pre)
            phase_proj_ln(P, C, C.aT, FF, None, (bvec("ln_ffn_g", l), bvec("ln_ffn_b", l)), h_src, Wpre=WdPre)
            P.pop()
            h_src = C.out
        elif kind == "s5":
            l = ph[1]
            i = l // 2
            phase_s5(P, C, I, i)
            wo, wg = I["s5_glu_wo"], I["s5_glu_wg"]
            phase_proj_ln(P, C, C.mixT, D, lambda kc: wo.v(wo.ap[i, kc * 128:(kc + 1) * 128, :]),
                          (bvec("ln_mix_g", l), bvec("ln_mix_b", l)), h_src,
                          gate_ap=lambda kc: wg.v(wg.ap[i, kc * 128:(kc + 1) * 128, :]))
            h_src = C.out
        elif kind == "hyb":
            l = ph[1]
            i = l // 2
            phase_dilated(P, C, I, i)
            phase_gdn(P, C, I, i)
            wo = I["hyb_w_out"]
            phase_proj_ln(P, C, C.mixT, D, lambda kc: wo.v(wo.ap[i, kc * 128:(kc + 1) * 128, :]),
                          (bvec("ln_mix_g", l), bvec("ln_mix_b", l)), h_src)
            h_src = C.out
    P.emit()
    return nc


def make_in_maps(inputs, cores):
    hc = host_consts()
    hl = host_layer_inputs(inputs)
    shared = {n: np.ascontiguousarray(inputs[n], np.float32) for n, _ in WEIGHT_SPECS}
    shared.update(hc)
    shared.update(hl)
    for l in range(2):
        shared["dnp%d" % l] = np.concatenate([inputs["dn_a_log"][l], inputs["dn_dt_bias"][l], inputs["dn_norm_g"][l]]).astype(np.float32).reshape(1, 136)
    maps = []
    for b in cores:
        m = dict(shared)
        m["x"] = np.ascontiguousarray(inputs["x"][b], np.float32)
        m["mem"] = np.ascontiguousarray(inputs["mem"][b], np.float32)
        m["pos"] = np.ascontiguousarray(inputs["positions"][b].reshape(1, S), np.int32)
        maps.append(m)
    return maps


FULL_PLAN = [("prep",)]
for _l in range(4):
    FULL_PLAN += [("hyb" if _l % 2 == 0 else "s5", _l), ("cross", _l), ("ffn", _l)]


def kernel(**inputs):
    inputs = {k: np.asarray(v) for k, v in inputs.items()}
    nc = build_program(FULL_PLAN)
    maps = make_in_maps(inputs, list(range(8)))
    res = run_bass_kernel_spmd(nc, maps, core_ids=list(range(8)))
    return np.stack([r["out"] for r in res.results], 0).astype(np.float32)


def b3(v, shape, axis):
    return V(v.ap.unsqueeze(axis).to_broadcast(shape), v.bufs)


def phase_s5(P, C, I, l):
    P.push()
    f32t = lambda shape, n: P.sb(shape, F32, n)
    col = f32t([128, 3, 32], "col")
    P.dma(col, I["s5col%d" % l].v(I["s5col%d" % l].ap))
    dtc = f32t([128, 32], "dtc")
    P.act(dtc, col[:, 2, :], AF.Exp)
    t0 = f32t([128, 32], "t0")
    P.tt(t0, col[:, 0, :], dtc, ALU.mult)
    rho = f32t([128, 32], "rho")
    P.act(rho, t0, AF.Exp)
    phi = f32t([128, 32], "phi")
    P.tt(phi, col[:, 1, :], dtc, ALU.mult)
    P.ts(phi, phi, 1.0 / TWO_PI, None, ALU.mult)
    iota = f32t([128, 64], "iota")
    P.dma(iota, I["c_iota"].v(I["c_iota"].ap))
    dcol = f32t([128, 8], "dcol")
    P.dma(dcol, I["s5d%d" % l].v(I["s5d%d" % l].ap))
    E = {k: f32t([128, 2048], "F" + k) for k in ("lo", "hi")}
    Bb = {k: P.sb([128, 4096], BF16, "Bb" + k) for k in ("re", "im")}
    CreB = P.sb([128, 4096], BF16, "CreB")
    CreN = P.sb([128, 4096], BF16, "CreN")
    CimN = P.sb([128, 4096], BF16, "CimN")
    P.dma(CreB, I["s5cre%d" % l].v(I["s5cre%d" % l].ap), eng="pool")
    P.dma(CimN, I["s5cim%d" % l].v(I["s5cim%d" % l].ap), eng="pool")
    P.ts(CreN, CreB, -1.0, None, ALU.mult)
    P.ts(CimN, CimN, -1.0, None, ALU.mult)
    P.push()
    scr = frac_scratch(P, [128, 2048])
    tu = f32t([128, 2048], "tu")
    tu3 = tu.re("p (a b) -> p a b", b=64)
    iota3 = b3(iota, [128, 32, 64], 1)
    P.tt(tu3, b3(phi, [128, 32, 64], 2), iota3, ALU.mult)
    frac_wrap(P, E["lo"], tu, 0.0, scr)
    p64 = f32t([128, 32], "p64")
    P.ts(p64, phi, 64.0, None, ALU.mult)
    scr2 = frac_scratch(P, [128, 32])
    f64 = f32t([128, 32], "f64")
    frac_wrap(P, f64, p64, 0.0, scr2)
    P.tt(tu3, b3(f64, [128, 32, 64], 2), iota3, ALU.mult)
    frac_wrap(P, E["hi"], tu, 0.0, scr)
    P.pop()
    P.push()
    row = I["s5row%d" % l]
    T = {k: f32t([128, 1024], "r" + k) for k in ("are", "aim", "dt", "er", "tu", "f", "sn", "cs", "abr", "abi", "nr", "den", "cr", "ci", "Bre", "Bim", "x1", "x2")}
    scr = frac_scratch(P, [128, 1024])
    for qq in range(4):
        cs_ = slice(qq * 1024, (qq + 1) * 1024)
        P.dma(T["are"], row.v(row.ap[0:1, cs_].partition_broadcast(128)))
        P.dma(T["aim"], row.v(row.ap[1:2, cs_].partition_broadcast(128)))
        P.dma(T["dt"], row.v(row.ap[2:3, cs_].partition_broadcast(128)))
        P.dma(T["Bre"], I["s5bre%d" % l].v(I["s5bre%d" % l].ap[:, cs_]))
        P.dma(T["Bim"], I["s5bim%d" % l].v(I["s5bim%d" % l].ap[:, cs_]))
        P.act(T["dt"], T["dt"], AF.Exp)
        P.tt(T["er"], T["are"], T["dt"], ALU.mult)
        P.act(T["er"], T["er"], AF.Exp)
        P.tt(T["tu"], T["aim"], T["dt"], ALU.mult)
        P.ts(T["tu"], T["tu"], 1.0 / TWO_PI, None, ALU.mult)
        frac_wrap(P, T["f"], T["tu"], 0.0, scr)
        P.act(T["sn"], T["f"], AF.Sin, scale=TWO_PI)
        frac_wrap(P, T["f"], T["tu"], 0.25, scr)
        P.act(T["cs"], T["f"], AF.Sin, scale=TWO_PI)
        P.tt(T["abr"], T["er"], T["cs"], ALU.mult)
        P.tt(T["abi"], T["er"], T["sn"], ALU.mult)
        P.ts(T["nr"], T["abr"], -1.0, None, ALU.add)
        P.tt(T["den"], T["are"], T["are"], ALU.mult)
        P.tt(T["x1"], T["aim"], T["aim"], ALU.mult)
        P.tt(T["den"], T["den"], T["x1"], ALU.add)
        P.recip(T["den"], T["den"])
        P.tt(T["x1"], T["nr"], T["are"], ALU.mult)
        P.tt(T["x2"], T["abi"], T["aim"], ALU.mult)
        P.tt(T["cr"], T["x1"], T["x2"], ALU.add)
        P.tt(T["cr"], T["cr"], T["den"], ALU.mult)
        P.tt(T["x1"], T["abi"], T["are"], ALU.mult)
        P.tt(T["x2"], T["nr"], T["aim"], ALU.mult)
        P.tt(T["ci"], T["x1"], T["x2"], ALU.subtract)
        P.tt(T["ci"], T["ci"], T["den"], ALU.mult)
        P.tt(T["x1"], T["cr"], T["Bre"], ALU.mult)
        P.tt(T["x2"], T["ci"], T["Bim"], ALU.mult)
        P.tt(Bb["re"][:, cs_], T["x1"], T["x2"], ALU.subtract)
        P.tt(T["x1"], T["cr"], T["Bim"], ALU.mult)
        P.tt(T["x2"], T["ci"], T["Bre"], ALU.mult)
        P.tt(Bb["im"][:, cs_], T["x1"], T["x2"], ALU.add)
    P.pop()
    BW = 512
    NBLK = S // BW
    HC = BW // 64
    ones = f32t([128, BW], "ones")
    P.memset(ones, 1.0)
    halfpi = f32t([128, 1], "halfpi")
    P.memset(halfpi, math.pi / 2)
    onec = f32t([128, 1], "onec")
    P.memset(onec, 1.0)
    stre = [f32t([128, 1], "stre%d" % i) for i in range(32)]
    stim = [f32t([128, 1], "stim%d" % i) for i in range(32)]
    for t_ in stre + stim:
        P.memset(t_, 0.0)
    uTp = Pool(P, 2, [128, S], BF16, "uT")
    RHO = [[f32t([128, BW], "RHO%d_%d" % (par, q)) for q in range(4)] for par in range(2)]
    mkp = lambda n, dt=F32, k=4: Pool(P, k, [128, BW], dt, n)
    p_fs, p_s1, p_sq, p_ei = mkp("fs", F32, 2), mkp("s1", F32, 2), mkp("sq"), mkp("ei")
    p_m = [mkp("m%d" % i, F32, 3) for i in range(4)]
    p_xr, p_xi = mkp("xr", F32, 3), mkp("xi", F32, 3)
    p_pp = [mkp("pp%d" % i, BF16, 3) for i in range(4)]
    hidp = Pool(P, 2, [128, BW], BF16, "hid")
    vtp = Pool(P, 2, [128, BW], F32, "vt")
    allk = tuple(("t", t) for t in range(NT))
    v3 = lambda t: t.re("p (a b) -> p a b", b=64)
    iters = [(cc, blk, q) for cc in range(8) for blk in range(NBLK) for q in range(4)]
    ctx = {}
    uTs = {}
    bub = [0]
    btb = [0]
    IDN = f32t([128, 128], "IDN")
    P.ts(IDN, C.identf, -1.0, None, ALU.mult)

    def prologue(n):
        cc, blk, q = iters[n]
        ctx[n] = {}
        if blk == 0 and q == 0:
            uTs[cc] = uTp.next()
            P.dma(uTs[cc], C.hT.v(C.hT.ap[cc * 128:(cc + 1) * 128, :], allk))
            for q2 in range(4):
                P.ts(RHO[cc % 2][q2], ones, rho[:, 4 * cc + q2:4 * cc + q2 + 1], None, ALU.mult)

    def sA(n):
        cc, blk, q = iters[n]
        pair = 4 * cc + q
        d = ctx[n]
        hs = slice(pair * 64 + blk * HC, pair * 64 + (blk + 1) * HC)
        ls = slice(pair * 64, pair * 64 + 64)
        d["fs"], d["s1"], d["sq"], d["ei"] = p_fs.next(), p_s1.next(), p_sq.next(), p_ei.next()
        P.tt(v3(d["fs"]), b3(E["hi"][:, hs], [128, HC, 64], 2), b3(E["lo"][:, ls], [128, HC, 64], 1), ALU.add, eng="pool")

    def sD(n):
        d = ctx[n]
        fs, s1, sq = d["fs"], d["s1"], d["sq"]
        P.act(s1, fs, AF.Sin, scale=math.pi)
        P.act(fs, fs, AF.Abs)
        P.act(fs, fs, AF.Sin, scale=-math.pi, bias=halfpi[:, 0:1])
        P.act(sq, s1, AF.Square)
        P.act(sq, sq, AF.Identity, scale=-2.0, bias=onec[:, 0:1])

    def sF(n):
        d = ctx[n]
        P.stt(d["ei"], d["s1"], 2.0, d["fs"], ALU.mult, ALU.mult)
        d["Ere"], d["Eim"] = d["sq"], d["ei"]

    def sB(n):
        cc, blk, q = iters[n]
        pair = 4 * cc + q
        d = ctx[n]
        tk = slice(blk * BW, (blk + 1) * BW)
        pc = slice(pair * 128, (pair + 1) * 128)
        d["bre"], d["bim"] = C.PS[0], C.PS[1]
        P.mm(d["bre"], Bb["re"][:, pc], uTs[cc][:, tk])
        P.mm(d["bim"], Bb["im"][:, pc], uTs[cc][:, tk])

    def sE(n):
        d = ctx[n]
        m = [p.next() for p in p_m]
        bre, bim = d["bre"], d["bim"]
        P.tt(m[0], bre, d["Ere"], ALU.mult)
        P.tt(m[1], bim, d["Eim"], ALU.mult)
        P.tt(m[2], bim, d["Ere"], ALU.mult)
        P.tt(m[3], bre, d["Eim"], ALU.mult)
        btr, bti = C.PS[2 + btb[0]], C.PS[3 + btb[0]]
        btb[0] = (btb[0] + 2) % 4
        P.mm(btr, C.identf, m[0], start=True, stop=False)
        P.mm(btr, C.identf, m[1], start=False, stop=True)
        P.mm(bti, C.identf, m[2], start=True, stop=False)
        P.mm(bti, IDN, m[3], start=False, stop=True)
        d["m"] = [btr, None, bti, None]

    def sC(n):
        cc, blk, q = iters[n]
        pair = 4 * cc + q
        d = ctx[n]
        m = d["m"]
        xr, xi = p_xr.next(), p_xi.next()
        P.scan(xr, RHO[cc % 2][q], m[0], stre[pair], ALU.mult, ALU.add)
        P.scan(xi, RHO[cc % 2][q], m[2], stim[pair], ALU.mult, ALU.add)
        P.copy(stre[pair], xr[:, BW - 1:BW], eng="act")
        P.copy(stim[pair], xi[:, BW - 1:BW], eng="act")
        pp = d["pp"] = [p.next() for p in p_pp]
        P.tt(pp[0], d["Ere"], xr, ALU.mult)
        P.tt(pp[1], d["Eim"], xi, ALU.mult, eng="pool")
        P.tt(pp[2], d["Ere"], xi, ALU.mult, eng="pool")
        P.tt(pp[3], d["Eim"], xr, ALU.mult, eng="pool")

    def sG(n):
        cc, blk, q = iters[n]
        pair = 4 * cc + q
        d = ctx.pop(n)
        uT = uTs[cc]
        tk = slice(blk * BW, (blk + 1) * BW)
        pc = slice(pair * 128, (pair + 1) * 128)
        yps = C.PS[6]
        pp = d["pp"]
        for k, (W_, p_) in enumerate(((CreB, pp[0]), (CreN, pp[1]), (CimN, pp[2]), (CimN, pp[3]))):
            P.mm(yps, W_[:, pc], p_, start=(q == 0 and k == 0), stop=(q == 3 and k == 3))
        if q == 3:
            vt = vtp.next()
            P.stt(vt, uT[:, tk], dcol[:, cc:cc + 1], yps, ALU.mult, ALU.add)
            hid = hidp.next()
            P.act(hid, vt, AF.Gelu)
            P.dma(C.mixT.v(C.mixT.ap[cc * 128:(cc + 1) * 128, tk], hT_keys(blk)), hid)

    N_ = len(iters)
    ok = lambda k: 0 <= k < N_
    for n in range(N_ + 2):
        if ok(n):
            prologue(n)
            sA(n)
        if ok(n - 1):
            sB(n - 1)
        if ok(n - 2):
            sC(n - 2)
        if ok(n):
            sD(n)
        if ok(n - 1):
            sE(n - 1)
        if ok(n):
            sF(n)
        if ok(n - 2):
            sG(n - 2)
    P.pop()


def _zero_mix_rows(P, C, r0, r1):
    P.push()
    z = P.sb([128, 1024], BF16, "zmix")
    P.memset(z, 0.0)
    for c in range(r0 // 128, r1 // 128):
        for blk in range(4):
            P.dma(C.mixT.v(C.mixT.ap[c * 128:(c + 1) * 128, blk * 1024:(blk + 1) * 1024],
                           tuple(("t", t) for t in range(blk * 8, blk * 8 + 8))), z)
    P.pop()


def phase_dilated(P, C, I, i):
    w_in = I["hyb_w_in"]
    w3 = w_in.ap[i].rearrange("(kc p) n -> p kc n", p=128)
    hT3 = C.hT.ap.rearrange("(c p) t -> p c t", p=128)
    QOFF, KOFF, VOFF = 2056, 2568, 3080
    P.push()
    Wv = P.sb([128, 8, 512], BF16, "Wv")
    P.dma(Wv, w_in.v(w3[:, :, VOFF:VOFF + 512]), eng="pool")
    blkp = Pool(P, 2, [128, 8, 512], BF16, "hblk")
    vtp = Pool(P, 2, [128, 512], BF16, "vt")
    for b in range(8):
        ib = blkp.next()
        P.dma(ib, C.hT.v(hT3[:, :, b * 512:(b + 1) * 512], hT_keys(b)))
        for t4 in range(4):
            tt = b * 4 + t4
            ps = C.PS[t4 % 2]
            for kc in range(8):
                P.mm(ps, ib[:, kc, t4 * 128:(t4 + 1) * 128], Wv[:, kc, :], start=(kc == 0), stop=(kc == 7))
            vt = vtp.next()
            P.copy(vt, ps, eng="act")
            P.dma(C.vd.v(C.vd.ap[tt * 128:(tt + 1) * 128, :], 0), vt)
    P.pop()
    import os
    DM = os.environ.get("DIL_MODE", "all")
    if DM == "v":
        return
    P.push()
    COS = P.sb([128, S], F32, "COS")
    SIN = P.sb([128, S], F32, "SIN")
    P.dma(COS, C.cosT.v(C.cosT.ap, (0, 1, 2, 3)))
    P.dma(SIN, C.sinT.v(C.sinT.ap, (0, 1, 2, 3)))
    dmask = P.sb([128, 256], F32, "dmask")
    P.dma(dmask, I["c_dmask"].v(I["c_dmask"].ap))
    sel = P.sb([128, 64], F32, "sel")
    P.dma(sel, I["c_sel"].v(I["c_sel"].ap))
    Wt = {k: P.sb([128, 8, 128], BF16, "W" + k) for k in ("q", "qs", "k", "ks")}
    blkp = Pool(P, 2, [128, 8, 512], BF16, "hblk")
    RT = {k: P.sb([128, S], BF16, "R" + k) for k in ("q", "k")}
    CM = {(k, r): P.sb([128, S], BF16, "C%s%d" % (k, r)) for k in ("q", "k") for r in (4, 16)}
    OT = [P.sb([128, S], F32, "OT%d" % h) for h in range(2)]
    t1p = Pool(P, 2, [128, 512], F32, "t1")
    t2p = Pool(P, 2, [128, 512], F32, "t2")
    vap = Pool(P, 4, [128, 8, 128], BF16, "vaug")
    for va in vap.tiles:
        P.memset(va, 1.0)
    pep = Pool(P, 3, [128, 256], BF16, "pe")
    ptp = Pool(P, 6, [128, 256], BF16, "pt")
    dnegf = P.sb([128, 256], F32, "dnegf")
    P.dma(dnegf, I["c_dneg"].v(I["c_dneg"].ap))
    dnegb = P.sb([128, 256], BF16, "dnegb")
    P.copy(dnegb, dnegf)
    sbank = [0]
    rdp = Pool(P, 2, [64, 512], F32, "rd")
    onp = Pool(P, 2, [64, 512], BF16, "on")
    slots = [[V(C.PS[3 + 2 * h + sidx].ap[:, 0:128], C.PS[3 + 2 * h + sidx].bufs) for sidx in range(2)] for h in range(2)]
    for hp in range(4):
        for k, off in (("q", QOFF), ("k", KOFF)):
            base = off + hp * 128
            P.dma(Wt[k], w_in.v(w3[:, :, base:base + 128]), eng="pool")
            for hh in range(2):
                P.dma(Wt[k + "s"][:, :, hh * 64:hh * 64 + 32], w_in.v(w3[:, :, base + hh * 64 + 32:base + hh * 64 + 64]), eng="pool")
                P.dma(Wt[k + "s"][:, :, hh * 64 + 32:hh * 64 + 64], w_in.v(w3[:, :, base + hh * 64:base + hh * 64 + 32]), eng="pool")
        for b in range(8):
            ib = blkp.next()
            P.dma(ib, C.hT.v(hT3[:, :, b * 512:(b + 1) * 512], hT_keys(b)))
            cs_ = slice(b * 512, (b + 1) * 512)
            for k in ("q", "k"):
                for kc in range(8):
                    P.mm(C.PS[0], Wt[k][:, kc, :], ib[:, kc, :], start=(kc == 0), stop=(kc == 7))
                for kc in range(8):
                    P.mm(C.PS[1], Wt[k + "s"][:, kc, :], ib[:, kc, :], start=(kc == 0), stop=(kc == 7))
                t1 = t1p.next()
                t2 = t2p.next()
                P.tt(t1, C.PS[0], COS[:, cs_], ALU.mult)
                P.tt(t2, C.PS[1], SIN[:, cs_], ALU.mult)
                P.tt(RT[k][:, cs_], t1, t2, ALU.add, eng="pool")
        if DM == "proj":
            continue
        for k in ("q", "k"):
            for r in (4, 16):
                P.copy(CM[(k, r)].re("p (r u) -> p r u", r=r), RT[k].re("p (u r) -> p r u", r=r), eng="pool")
        if DM == "cm":
            continue
        its = []
        for bi, r in enumerate((1, 4, 16)):
            L = S // r
            nb_ = L // 128
            for rho in range(r):
                for m in range(nb_):
                    for hh in range(2):
                        its.append((bi, r, rho, m, hh, nb_, L))
        ictx = {}

        def st_S(n):
            bi, r, rho, m, hh, nb_, L = its[n]
            d = ictx[n] = {}
            Qs = RT["q"] if r == 1 else CM[("q", r)]
            Ks = RT["k"] if r == 1 else CM[("k", r)]
            if hh == 0:
                va = vap.next()
                st = (128 * m) * r + rho
                P.dma(va[:, :, 0:64], C.vd.v(C.vd.ap[st:st + 127 * r + 1:r, :].rearrange("t (h d) -> t h d", h=8), 0))
                ictx["va"] = va
            d["va"] = ictx["va"]
            nq = d["nq"] = 256 if m < nb_ - 1 else 128
            c0 = rho * L + m * 128
            hb = hh * 64
            sps = d["sps"] = C.PS[sbank[0]]
            sbank[0] = (sbank[0] + 1) % 3
            P.mm(sps[:, 0:nq], Ks[hb:hb + 64, c0:c0 + 128], Qs[hb:hb + 64, c0:c0 + nq], start=True, stop=False)
            P.mm(sps[:, 0:nq], C.identb, dnegb[:, 0:nq], start=False, stop=True)

        def st_X(n):
            d = ictx[n]
            nq = d["nq"]
            pt = d["pt"] = ptp.next()
            P.act(pt[:, 0:nq], d["sps"][:, 0:nq], AF.Exp, scale=0.125)

        def st_V(n):
            bi, r, rho, m, hh, nb_, L = its[n]
            d = ictx.pop(n)
            pt, va, nq = d["pt"], d["va"], d["nq"]
            head = hp * 2 + hh
            sl = slots[hh][m % 2]
            P.mm(sl, va[:, head, :], pt[:, 0:128], start=(m == 0), stop=True)
            dst = OT[hh].re("p (u r) -> p r u", r=r)[:, rho, m * 128:(m + 1) * 128] if r > 1 else OT[hh][:, m * 128:(m + 1) * 128]
            if bi == 0:
                P.copy(dst, sl, eng="dve")
            else:
                P.tt(dst, sl, dst, ALU.add)
            if nq == 256:
                sl2 = slots[hh][(m + 1) % 2]
                P.mm(sl2, va[:, head, :], pt[:, 128:256], start=True, stop=False)

        NI = len(its)
        for n in range(NI + 2):
            if n < NI:
                st_S(n)
            if 0 <= n - 1 < NI:
                st_X(n - 1)
            if 0 <= n - 2 < NI:
                st_V(n - 2)
        if DM in ("br0a", "br0b", "br0b1", "br0b2"):
            continue
        for hh in range(2):
            head = hp * 2 + hh
            for b in range(8):
                cs_ = slice(b * 512, (b + 1) * 512)
                pd = C.PS[hh]
                P.mm(pd[0:64, :], sel, OT[hh][:, cs_])
                rd = rdp.next()
                P.recip(rd, pd[0:64, :])
                on = onp.next()
                P.tt(on, OT[hh][0:64, cs_], rd, ALU.mult)
                r0 = 512 + head * 64
                P.dma(C.mixT.v(C.mixT.ap[r0:r0 + 64, cs_], hT_keys(b)), on)
    P.pop()


def phase_gdn(P, C, I, i):
    w_in = I["hyb_w_in"]
    w3 = w_in.ap[i].rearrange("(kc p) n -> p kc n", p=128)
    hT3 = C.hT.ap.rearrange("(c p) t -> p c t", p=128)
    dnp = I["dnp%d" % i]
    P.push()
    f32t = lambda shape, n: P.sb(shape, F32, n)
    cst = {}
    for nm in ("c_tri", "c_ones", "c_negU", "c_posL", "c_strU", "c_strL"):
        cst[nm] = f32t([128, 128], nm)
        P.dma(cst[nm], I[nm].v(I[nm].ap))
    TRI, ONESf, NEGU, POSL, STRU, STRL = (cst[n] for n in ("c_tri", "c_ones", "c_negU", "c_posL", "c_strU", "c_strL"))
    IDF = C.identf
    onecol = f32t([128, 1], "onecol")
    P.memset(onecol, 1.0)
    NG = f32t([128, 128], "NG")
    P.dma(NG, dnp.v(dnp.ap[0:1, 8:136].partition_broadcast(128)))
    ab = f32t([128, 8], "ab")
    P.dma(ab, dnp.v(dnp.ap[0:1, 0:8].partition_broadcast(128)))
    nA = f32t([128, 4], "nA")
    P.act(nA, ab[:, 0:4], AF.Exp)
    P.ts(nA, nA, -1.0, None, ALU.mult)
    cw = f32t([128, 12, 4], "cw")
    P.dma(cw, I["dncw%d" % i].v(I["dncw%d" % i].ap))
    bank = [0]

    def nb():
        bank[0] = (bank[0] + 1) % 7
        return C.PS[bank[0]]

    Wg8 = P.sb([128, 8, 8], BF16, "Wg8")
    P.dma(Wg8, w_in.v(w3[:, :, 2048:2056]), eng="pool")
    blkp = Pool(P, 2, [128, 8, 512], BF16, "hblk")
    GR = f32t([128, 32, 8], "GR")
    for b in range(8):
        ib = blkp.next()
        P.dma(ib, C.hT.v(hT3[:, :, b * 512:(b + 1) * 512], hT_keys(b)))
        for t4 in range(4):
            ps = nb()
            for kc in range(8):
                P.mm(ps[:, 0:8], ib[:, kc, t4 * 128:(t4 + 1) * 128], Wg8[:, kc, :], start=(kc == 0), stop=(kc == 7))
            P.copy(GR[:, b * 4 + t4, :], ps[:, 0:8], eng="act")
    BETA = f32t([128, 128], "BETA")
    NBETA = f32t([128, 128], "NBETA")
    Gt = f32t([128, 128], "Gt")
    v34 = lambda t: t.re("p (c h) -> p c h", h=4)
    P.copy(v34(BETA), GR[:, :, 0:4], eng="dve")
    P.act(BETA, BETA, AF.Sigmoid)
    P.ts(NBETA, BETA, -1.0, None, ALU.mult)
    P.tt(v34(Gt), GR[:, :, 4:8], b3(ab[:, 4:8], [128, 32, 4], 1), ALU.add)
    P.act(Gt, Gt, AF.Exp)
    P.act(Gt, Gt, AF.Ln, bias=onecol[:, 0:1])
    P.tt(v34(Gt), v34(Gt), b3(nA, [128, 32, 4], 1), ALU.mult)
    GC, GL, EG, EDL, EGL = (f32t([128, 128], n) for n in ("GC", "GL", "EG", "EDL", "EGL"))
    ps = nb()
    P.mm(ps[:, 0:128], TRI, Gt)
    P.copy(GC, ps[:, 0:128], eng="dve")
    ps = nb()
    P.mm(ps[:, 0:128], ONESf, Gt)
    P.copy(GL, ps[:, 0:128], eng="dve")
    P.act(EG, GC, AF.Exp)
    P.act(EGL, GL, AF.Exp)
    P.tt(EDL, GL, GC, ALU.subtract)
    P.act(EDL, EDL, AF.Exp)
    import os
    GM = os.environ.get("GDN_MODE", "all")
    if GM == "gates":
        P.pop()
        return
    Wh = P.sb([128, 8, 3, 128], BF16, "Wh")
    Wz = P.sb([128, 8, 128], BF16, "Wz")
    X = [f32t([128, S + 3], "X%d" % x) for x in range(3)]
    for x in range(3):
        P.memset(X[x][:, 0:3], 0.0)
    sq = P.sb([128, S], BF16, "sq")
    Ycv = f32t([128, S], "Ycv")
    QT = P.sb([128, S], BF16, "QT")
    KT = P.sb([128, S], BF16, "KT")
    VT = P.sb([128, S], BF16, "VT")
    KTM = P.sb([128, 32, 128], BF16, "KTM")
    VTM = P.sb([128, 32, 128], BF16, "VTM")
    rnp = Pool(P, 2, [128, 512], F32, "rn")
    Sst = f32t([128, 128], "Sst")
    Sb = P.sb([128, 128], BF16, "Sb")
    NBATCH = 4
    mk = lambda n, dt=F32, k=NBATCH: Pool(P, k, [128, 128], dt, n)
    p_gb, p_bb, p_DT, p_DL, p_tmp, p_tmp2 = mk("gb"), mk("bb"), mk("DT"), mk("DL"), mk("tmpa"), mk("tmpb")
    p_A, p_AT, p_P = mk("A", F32, 2 * NBATCH), mk("AT", F32, 2 * NBATCH), mk("P", F32, 2 * NBATCH)
    p_intra, p_Pb, p_ub, p_Keg, p_WT, p_Kdec, p_qdT, p_EGr = (mk("intra", BF16), mk("Pb", BF16), mk("ub"), mk("Keg", BF16),
                                                              mk("WT", BF16), mk("Kdec", BF16), mk("qdT", BF16), mk("EGr"))
    p_vn, p_zs, p_t, p_ao, p_aoT = mk("vn", BF16, 2), mk("zs", F32, 2), mk("tt", F32, 2), mk("ao", BF16, 2), mk("aoT", BF16, 2)
    p_st = Pool(P, 2, [128, 6], F32, "gst")
    p_mv = Pool(P, 2, [128, 8], F32, "gmv")
    hcp = Pool(P, 2, [128, 8, 128], BF16, "hchunk")
    for hd in range(4):
        for x in range(3):
            P.dma(Wh[:, :, x, :], w_in.v(w3[:, :, x * 512 + hd * 128: x * 512 + (hd + 1) * 128]), eng="pool")
        P.dma(Wz, w_in.v(w3[:, :, 1536 + hd * 128:1536 + (hd + 1) * 128]), eng="pool")
        for b in range(8):
            ib = blkp.next()
            P.dma(ib, C.hT.v(hT3[:, :, b * 512:(b + 1) * 512], hT_keys(b)))
            for x in range(3):
                ps = nb()
                for kc in range(8):
                    P.mm(ps, Wh[:, kc, x, :], ib[:, kc, :], start=(kc == 0), stop=(kc == 7))
                P.copy(X[x][:, 3 + b * 512:3 + (b + 1) * 512], ps, eng="act")
        for x in range(3):
            Y = X[x]
            ch = x * 4 + hd
            Yc = Ycv
            P.ts(Yc, Y[:, 0:S], cw[:, ch, 0:1], None, ALU.mult)
            for j in range(1, 4):
                P.stt(Yc, Y[:, j:S + j], cw[:, ch, j:j + 1], Yc, ALU.mult, ALU.add)
            P.act(Yc, Yc, AF.Silu)
            if x < 2:
                P.tt(sq, Yc, Yc, ALU.mult, eng="pool")
                dst = QT if x == 0 else KT
                for b in range(8):
                    cs_ = slice(b * 512, (b + 1) * 512)
                    ps = nb()
                    P.mm(ps, C.onesb, sq[:, cs_])
                    rn = rnp.next()
                    P.ts(rn, ps, RMS_EPS, None, ALU.add)
                    P.act(rn, rn, AF.Sqrt)
                    P.recip(rn, rn)
                    if x == 0:
                        P.stt(dst[:, cs_], Yc[:, cs_], 128 ** -0.5, rn, ALU.mult, ALU.mult)
                    else:
                        P.tt(dst[:, cs_], Yc[:, cs_], rn, ALU.mult)
            else:
                P.copy(VT, Yc, eng="act")
        for src, dstm in ((KT, KTM), (VT, VTM)):
            for g8 in range(4):
                for c8 in range(8):
                    c = g8 * 8 + c8
                    P.tr(C.pst[:, c8 * 128:(c8 + 1) * 128], src[:, c * 128:(c + 1) * 128], C.identb)
                P.copy(dstm[:, g8 * 8:(g8 + 1) * 8, :], C.pst.re("p (c t) -> p c t", c=8), eng="act")
        P.memset(Sst, 0.0)
        P.memset(Sb, 0.0)
        if GM == "stageA":
            continue
        for cb in range(32 // NBATCH):
            chunks = list(range(cb * NBATCH, (cb + 1) * NBATCH))
            st = {}
            for c in chunks:
                d = st[c] = {}
                chx = c * 4 + hd
                col = lambda T_: T_[:, chx:chx + 1]
                cs_ = slice(c * 128, (c + 1) * 128)
                gb, bb = p_gb.next(), p_bb.next()
                P.ts(gb, ONESf, col(Gt), None, ALU.mult)
                P.ts(bb, ONESf, col(BETA), None, ALU.mult)
                gcr = nb()[:, 0:128]
                P.mm(gcr, gb, TRI)
                ber = nb()[:, 0:128]
                P.mm(ber, bb, IDF)
                DT, DL, tmp, tmp2 = p_DT.next(), p_DL.next(), p_tmp.next(), p_tmp2.next()
                P.stt(tmp, gcr, col(GC), NEGU, ALU.subtract, ALU.add)
                P.act(DT, tmp, AF.Exp)
                P.stt(tmp2, gcr, col(GC), POSL, ALU.subtract, ALU.add)
                P.act(DL, tmp2, AF.Exp, scale=-1.0)
                EGr = p_EGr.next()
                P.act(EGr, gcr, AF.Exp)
                qdT = d["qdT"] = p_qdT.next()
                P.tt(qdT, QT[:, cs_], EGr, ALU.mult, eng="pool")
                gps = nb()[:, 0:128]
                P.mm(gps, KT[:, cs_], KT[:, cs_])
                qkps = nb()[:, 0:128]
                P.mm(qkps, KT[:, cs_], QT[:, cs_])
                intra = d["intra"] = p_intra.next()
                P.tt(intra, qkps, DT, ALU.mult)
                P.tt(DT, DT, STRU, ALU.mult, eng="pool")
                P.tt(DL, DL, STRL, ALU.mult, eng="pool")
                A, AT = p_A.next(), p_AT.next()
                P.stt(A, gps, col(BETA), DT, ALU.mult, ALU.mult)
                P.tt(tmp, ber, DL, ALU.mult)
                P.tt(AT, gps, tmp, ALU.mult)
                Pm = p_P.next()
                P.tt(Pm, IDF, A, ALU.subtract, eng="pool")
                d["A"], d["AT"], d["P"] = A, AT, Pm
                Keg, Kdec = d["Keg"], d["Kdec"] = p_Keg.next(), p_Kdec.next()
                P.ts(Keg, KTM[:, c, :], col(EG), None, ALU.mult)
                P.ts(Kdec, KTM[:, c, :], col(EDL), None, ALU.mult)
            if GM == "s1":
                continue
            for lev in range(6):
                for c in chunks:
                    d = st[c]
                    psa, psb = nb()[:, 0:128], nb()[:, 0:128]
                    P.mm(psa, d["AT"], d["A"])
                    P.mm(psb, d["A"], d["AT"])
                    A2, AT2 = p_A.next(), p_AT.next()
                    P.copy(A2, psa, eng="act")
                    P.copy(AT2, psb, eng="dve")
                    d["A"], d["AT"] = A2, AT2
                for c in chunks:
                    d = st[c]
                    psc = nb()[:, 0:128]
                    P.mm(psc, d["AT"], d["P"])
                    P2 = p_P.next()
                    P.tt(P2, psc, d["P"], ALU.add)
                    d["P"] = P2
            if GM.startswith("neu"):
                continue
            for c in chunks:
                d = st[c]
                d["Pb"] = p_Pb.next()
                P.copy(d["Pb"], d["P"], eng="act")
            for c in chunks:
                d = st[c]
                chx = c * 4 + hd
                col = lambda T_: T_[:, chx:chx + 1]
                psu, psw = nb()[:, 0:128], nb()[:, 0:128]
                P.mm(psu, d["Pb"], VTM[:, c, :])
                P.mm(psw, d["Keg"], d["Pb"])
                ub = d["ub"] = p_ub.next()
                P.ts(ub, psu, col(BETA), None, ALU.mult)
                WT = d["WT"] = p_WT.next()
                P.copy(WT, psw, eng="act")
            if GM == "uw":
                continue
            for c in chunks:
                d = st[c]
                chx = c * 4 + hd
                col = lambda T_: T_[:, chx:chx + 1]
                cs_ = slice(c * 128, (c + 1) * 128)
                ps1 = nb()[:, 0:128]
                P.mm(ps1, d["WT"], Sb)
                vn = p_vn.next()
                P.stt(vn, ps1, col(NBETA), d["ub"], ALU.mult, ALU.add)
                pso = nb()[:, 0:128]
                P.mm(pso, d["qdT"], Sb, start=True, stop=False)
                P.mm(pso, d["intra"], vn, start=False, stop=True)
                ps3 = nb()[:, 0:128]
                P.mm(ps3, d["Kdec"], vn)
                P.stt(Sb, Sst, col(EGL), ps3, ALU.mult, ALU.add)
                P.stt(Sst, Sst, col(EGL), ps3, ALU.mult, ALU.add)
                hc = hcp.next()
                P.dma(hc, C.hT.v(hT3[:, :, cs_], ("t", c)))
                psz = nb()[:, 0:128]
                for kc in range(8):
                    P.mm(psz, hc[:, kc, :], Wz[:, kc, :], start=(kc == 0), stop=(kc == 7))
                zs = p_zs.next()
                P.act(zs, psz, AF.Silu)
                gst, gmv = p_st.next(), p_mv.next()
                P.add("dve", lambda e, o=gst, i_=pso: e.bn_stats(o.ap, i_.ap), [pso], [gst])
                P.add("dve", lambda e, o=gmv[:, 0:2], i_=gst: e.bn_aggr(o.ap, i_.ap), [gst], [gmv])
                P.stt(gmv[:, 2:3], gmv[:, 0:1], gmv[:, 0:1], gmv[:, 1:2], ALU.mult, ALU.add)
                P.ts(gmv[:, 2:3], gmv[:, 2:3], RMS_EPS, None, ALU.add)
                P.act(gmv[:, 3:4], gmv[:, 2:3], AF.Sqrt)
                P.recip(gmv[:, 4:5], gmv[:, 3:4])
                t_ = p_t.next()
                P.stt(t_, pso, gmv[:, 4:5], NG, ALU.mult, ALU.mult)
                ao = p_ao.next()
                P.tt(ao, t_, zs, ALU.mult, eng="pool")
                P.tr(C.pst[:, 0:128], ao, C.identb)
                aoT = p_aoT.next()
                P.copy(aoT, C.pst[:, 0:128], eng="act")
                P.dma(C.mixT.v(C.mixT.ap[hd * 128:(hd + 1) * 128, cs_], ("t", c)), aoT)
    P.pop()
```

```python
import numpy as np
from contextlib import ExitStack
import concourse.bass as bass
import concourse.mybir as mybir
from concourse.bass_utils import run_bass_kernel_spmd

F32 = mybir.dt.float32
BF16 = mybir.dt.bfloat16
I32 = mybir.dt.int32
AF = mybir.ActivationFunctionType
ALU = mybir.AluOpType
AX = mybir.AxisListType

NDMA = 16
NSDMA = 8


def is_dma(t):
    return t.startswith("dma") or t.startswith("sdma")


class Buf:
    __slots__ = ("name", "w", "r")

    def __init__(self, name=""):
        self.name = name
        self.w = None
        self.r = []


class V:
    def __init__(self, ap, bufs):
        self.ap = ap
        self.bufs = bufs if isinstance(bufs, tuple) else (bufs,)

    def __getitem__(self, k):
        return V(self.ap[k], self.bufs)

    def re(self, s, **kw):
        return V(self.ap.rearrange(s, **kw), self.bufs)

    def bc(self, dt):
        return V(self.ap.bitcast(dt), self.bufs)

    def bcast(self, shape):
        return V(self.ap.to_broadcast(shape), self.bufs)


class Op:
    __slots__ = ("eng", "fn", "deps", "track", "tidx", "inc", "val")


class Prog:
    ENG = ("pe", "act", "dve", "pool", "sp")
    BLK = {"pe": "tensor", "act": "scalar", "dve": "vector", "pool": "gpsimd", "sp": "sync"}

    def __init__(self, nc):
        self.nc = nc
        self.es = ExitStack()
        self.ops = {e: [] for e in self.ENG}
        self.tracks = {e: [] for e in self.ENG}
        for i in range(NDMA):
            self.tracks["dma%d" % i] = []
        for i in range(NSDMA):
            self.tracks["sdma%d" % i] = []
        self.dma_rr = 0
        self.sdma_rr = 0
        self.seen = {e: {} for e in self.ENG}
        self.nalloc = 0
        self.out_ops = []
        self.stack = [self.es]

    def sb(self, shape, dt, name=None):
        self.nalloc += 1
        name = name or ("t%d" % self.nalloc)
        t = self.stack[-1].enter_context(self.nc.sbuf_tensor(name + "_%d" % self.nalloc, list(shape), dt))
        return V(t.ap() if hasattr(t, "ap") and callable(t.ap) else t[:], Buf(name))

    def ps(self, shape, dt=F32, name=None):
        self.nalloc += 1
        name = name or ("p%d" % self.nalloc)
        t = self.stack[-1].enter_context(self.nc.psum_tensor(name + "_%d" % self.nalloc, list(shape), dt))
        return V(t.ap() if hasattr(t, "ap") and callable(t.ap) else t[:], Buf(name))

    def dram(self, name, shape, dt, kind="Internal"):
        t = self.nc.dram_tensor(name, list(shape), dt, kind=kind)
        return DT(t.ap(), name)


    def push(self):
        self.stack.append(ExitStack())

    def pop(self):
        self.barrier()
        self.stack.pop().close()

    def barrier(self):
        last = {t: len(l) - 1 for t, l in self.tracks.items() if l}
        for eng in self.ENG:
            op = Op()
            op.eng = eng
            op.fn = None
            op.inc = False
            op.val = None
            op.track = eng
            op.tidx = len(self.tracks[eng])
            op.deps = {}
            seen = self.seen[eng]
            for t, i in last.items():
                if seen.get(t, -1) < i:
                    seen[t] = i
                    op.deps[t] = i
                    self.tracks[t][i].inc = True
            self.ops[eng].append(op)

    def add(self, eng, fn, reads, writes, dma=False):
        op = Op()
        op.eng = eng
        op.fn = fn
        op.inc = dma
        op.val = None
        if dma and eng == "pool":
            tr = "sdma%d" % self.sdma_rr
            self.sdma_rr = (self.sdma_rr + 1) % NSDMA
        elif dma:
            tr = "dma%d" % self.dma_rr
            self.dma_rr = (self.dma_rr + 1) % NDMA
        else:
            tr = eng
        op.track = tr
        op.tidx = len(self.tracks[tr])
        deps = {}

        def need(t, i):
            if deps.get(t, -1) < i:
                deps[t] = i

        for v in reads:
            for b in v.bufs:
                if b.w is not None:
                    t, i = b.w
                    if t == tr and eng == "pe":
                        continue
                    need(t, i)
        for v in writes:
            for b in v.bufs:
                if b.w is not None:
                    t, i = b.w
                    if not (t == tr and eng == "pe"):
                        need(t, i)
                for (t, i) in b.r:
                    if not (t == tr and eng == "pe"):
                        need(t, i)
        if dma and op.tidx > 0:
            need(tr, op.tidx - 1)
        seen = self.seen[eng]
        op.deps = {}
        for t, i in deps.items():
            if seen.get(t, -1) < i:
                seen[t] = i
                op.deps[t] = i
                self.tracks[t][i].inc = True
        self.tracks[tr].append(op)
        self.ops[eng].append(op)
        for v in reads:
            for b in v.bufs:
                b.r.append((tr, op.tidx))
        for v in writes:
            for b in v.bufs:
                b.w = (tr, op.tidx)
                b.r = []
        return op

    def mm(self, out, lhsT, rhs, start=True, stop=True):
        return self.add("pe", lambda e: e.matmul(out.ap, lhsT.ap, rhs.ap, start=start, stop=stop),
                        [lhsT, rhs] + ([] if start else [out]), [out])

    def tr(self, out, in_, ident):
        return self.add("pe", lambda e: e.transpose(out.ap, in_.ap, ident.ap), [in_, ident], [out])

    def act(self, out, in_, func, bias=None, scale=None, accum=None, eng="act"):
        kw = {}
        rd = [in_]
        wr = [out]
        if bias is not None:
            if isinstance(bias, V):
                kw["bias"] = bias.ap
                rd.append(bias)
            else:
                kw["bias"] = bias
        if scale is not None:
            if isinstance(scale, V):
                kw["scale"] = scale.ap
                rd.append(scale)
            else:
                kw["scale"] = scale
        if accum is not None:
            kw["accum_out"] = accum.ap
            wr.append(accum)
        return self.add("act", lambda e: e.activation(out.ap, in_.ap, func, **kw), rd, wr)

    def tt(self, out, a, b, op, eng="dve"):
        return self.add(eng, lambda e: e.tensor_tensor(out.ap, a.ap, b.ap, op), [a, b], [out])

    def ts(self, out, a, s1, s2, op0, op1=None, accum=None, eng="dve"):
        rd = [a]
        wr = [out]
        s1a = s1.ap if isinstance(s1, V) else s1
        s2a = s2.ap if isinstance(s2, V) else s2
        if isinstance(s1, V):
            rd.append(s1)
        if isinstance(s2, V):
            rd.append(s2)
        kw = {}
        if op1 is not None:
            kw["op1"] = op1
        if accum is not None:
            kw["accum_out"] = accum.ap
            wr.append(accum)
        return self.add(eng, lambda e: e.tensor_scalar(out.ap, a.ap, s1a, s2a, op0, **kw), rd, wr)

    def stt(self, out, a, s, b, op0, op1, eng="dve"):
        rd = [a, b]
        sa = s.ap if isinstance(s, V) else s
        if isinstance(s, V):
            rd.append(s)
        return self.add(eng, lambda e: e.scalar_tensor_tensor(out.ap, a.ap, sa, b.ap, op0, op1), rd, [out])

    def copy(self, out, in_, eng="dve"):
        if eng == "act":
            return self.add("act", lambda e: e.copy(out.ap, in_.ap), [in_], [out])
        return self.add(eng, lambda e: e.tensor_copy(out.ap, in_.ap), [in_], [out])

    def memset(self, out, val, eng="pool"):
        return self.add(eng, lambda e: e.memset(out.ap, val), [], [out])

    def recip(self, out, in_):
        return self.add("dve", lambda e: e.reciprocal(out.ap, in_.ap), [in_], [out])

    def scan(self, out, d0, d1, init, op0, op1):
        rd = [d0, d1]
        ia = init.ap if isinstance(init, V) else init
        if isinstance(init, V):
            rd.append(init)
        return self.add("dve", lambda e: e.tensor_tensor_scan(out.ap, d0.ap, d1.ap, ia, op0, op1), rd, [out])

    def dma(self, out, in_, eng="sp", is_out=False, **kw):
        op = self.add(eng, lambda e: e.dma_start(out=out.ap, in_=in_.ap, **kw), [in_], [out], dma=True)
        if is_out:
            self.out_ops.append(op)
        return op

    def emit(self):
        nc = self.nc
        fin = Op()
        fin.eng = "sp"
        fin.fn = None
        fin.inc = False
        fin.track = "sp"
        fin.tidx = len(self.tracks["sp"])
        fin.deps = {}
        for tname, lst in self.tracks.items():
            if is_dma(tname) and lst:
                fin.deps[tname] = len(lst) - 1
        self.ops["sp"].append(fin)
        for tname, lst in self.tracks.items():
            c = 0
            for op in lst:
                if is_dma(tname):
                    c += 16
                    op.val = c
                else:
                    if op.inc:
                        c += 1
                    op.val = c
        sems = {}
        for tname, lst in self.tracks.items():
            if lst:
                sems[tname] = self.es.enter_context(nc.semaphore("s_" + tname))
        block = self.es.enter_context(nc.Block())
        tracks = self.tracks
        for eng in self.ENG:
            ops = self.ops[eng]
            if not ops:
                continue

            def body(e, ops=ops):
                for op in ops:
                    for t, i in op.deps.items():
                        e.wait_ge(sems[t], tracks[t][i].val)
                    if op.fn is None:
                        continue
                    ins = op.fn(e)
                    if is_dma(op.track):
                        ins.then_inc(sems[op.track], 16)
                    elif op.inc:
                        ins.then_inc(sems[op.track], 1)

            getattr(block, self.BLK[eng])(body)
        self.es.close()


class DT:
    def __init__(self, ap, name):
        self.ap = ap
        self.name = name
        self.bufs = {}

    def v(self, ap, keys=(0,)):
        if not isinstance(keys, (tuple, list)):
            keys = (keys,)
        bs = []
        for k in keys:
            if k not in self.bufs:
                self.bufs[k] = Buf("%s[%s]" % (self.name, k))
            bs.append(self.bufs[k])
        return V(ap, tuple(bs))


class Pool:
    def __init__(self, P, n, shape, dt, name, psum=False):
        self.tiles = [(P.ps if psum else P.sb)(shape, dt, name="%s%d" % (name, i)) for i in range(n)]
        self.i = 0

    def next(self):
        t = self.tiles[self.i]
        self.i = (self.i + 1) % len(self.tiles)
        return t


import math

S = 4096
D = 1024
NT = 32
FF = 2816
ALPHA = 8 ** 0.25
LN_EPS = 1e-5
RMS_EPS = 1e-6
TWO_PI = 2.0 * math.pi


def host_consts():
    c = {}
    i = np.arange(128)
    c["c_ident"] = np.eye(128, dtype=np.float32)
    c["c_tri"] = (i[:, None] <= i[None, :]).astype(np.float32)
    c["c_ones"] = np.ones((128, 128), np.float32)
    c["c_negU"] = np.where(i[None, :] >= i[:, None], 0.0, -30000.0).astype(np.float32)
    c["c_posL"] = np.where(i[None, :] <= i[:, None], 0.0, 30000.0).astype(np.float32)
    c["c_strU"] = (i[None, :] > i[:, None]).astype(np.float32)
    c["c_strL"] = (i[None, :] < i[:, None]).astype(np.float32)
    dm = np.zeros((128, 256), np.float32)
    dm[:, :128] = (i[:, None] <= i[None, :])
    dm[:, 128:] = (i[:, None] >= i[None, :])
    c["c_dmask"] = dm
    c["c_dneg"] = ((dm - 1.0) * 30000.0).astype(np.float32)
    invf = (10000.0 ** (-np.arange(0, 64, 2, dtype=np.float32) / 64)).astype(np.float32)
    r = np.arange(128)
    c["c_invf"] = (invf[r % 32] / np.float32(TWO_PI)).astype(np.float32).reshape(128, 1)
    c["c_sgn"] = np.where((r % 64) < 32, -1.0, 1.0).astype(np.float32).reshape(128, 1)
    c["c_iota"] = np.broadcast_to(np.arange(64, dtype=np.float32)[None, :], (128, 64)).copy()
    sel = np.zeros((128, 64), np.float32)
    sel[64 + np.arange(64), np.arange(64)] = 1.0
    c["c_sel"] = sel
    return c


def host_layer_inputs(inp):
    o = {}
    for l in range(2):
        a_re, a_im, ldt = inp["s5_a_re"][l], inp["s5_a_im"][l], inp["s5_log_dt"][l]
        col = np.stack([a_re, a_im, np.repeat(ldt[:, None], 64, 1)], 0)
        col = col.reshape(3, 32, 2, 64).transpose(2, 3, 0, 1).reshape(128, 3, 32)
        o["s5col%d" % l] = np.ascontiguousarray(col, np.float32)
        o["s5row%d" % l] = np.ascontiguousarray(np.stack([a_re.reshape(-1), a_im.reshape(-1), np.repeat(ldt, 64)], 0), np.float32)
        for nm, src in (("re", inp["s5_b_re"][l]), ("im", inp["s5_b_im"][l])):
            bp = np.zeros((4, 2, 16, 32, 2, 64), np.float32)
            sp = src.reshape(8, 4, 2, 64, 16)
            for q in range(4):
                for gl in range(2):
                    bp[q, gl, :, q::4, gl, :] = sp[:, q, gl].transpose(2, 0, 1)
            o["s5b%s%d" % (nm, l)] = bp.reshape(128, 32 * 128)
        for nm, src in (("re", inp["s5_c_re"][l]), ("im", inp["s5_c_im"][l])):
            cp = np.zeros((2, 64, 32, 4, 2, 16), np.float32)
            sp = src.reshape(8, 4, 2, 16, 64)
            for q in range(4):
                for gl in range(2):
                    cp[gl, :, q::4, q, gl, :] = sp[:, q, gl].transpose(2, 0, 1)
            o["s5c%s%d" % (nm, l)] = cp.reshape(128, 32 * 128)
        o["s5d%d" % l] = np.ascontiguousarray(inp["s5_d"][l].reshape(8, 128).T, np.float32)
        o["dncw%d" % l] = np.ascontiguousarray(inp["dn_conv_w"][l].T.reshape(12, 128, 4).transpose(1, 0, 2), np.float32)
    return o


class Ctx:
    pass


def load_w(P, dst, src_ap, KC, eng="pool"):
    for kc in range(KC):
        P.dma(dst[:, kc, :], src_ap(kc), eng=eng)


def load_h(P, C, tt, h_src):
    ht = C.p_h.next()
    P.dma(ht, h_src.v(h_src.ap[tt * 128:(tt + 1) * 128, :], tt))
    return ht


def ln_epilogue(P, C, lnp, tt, sub_halves, ht):
    z = C.p_z.next()
    st = C.p_st.next()
    for hf in range(2):
        P.stt(z[:, hf * 512:(hf + 1) * 512], ht[:, hf * 512:(hf + 1) * 512], ALPHA, sub_halves[hf], ALU.mult, ALU.add)
        zz = z[:, hf * 512:(hf + 1) * 512]
        P.add("dve", lambda e, o=st[:, hf, :], i=zz: e.bn_stats(o.ap, i.ap), [zz], [st])
    mv = C.p_mv.next()
    P.add("dve", lambda e, o=mv[:, 0:2], i=st.re("p a b -> p (a b)"): e.bn_aggr(o.ap, i.ap), [st], [mv])
    P.ts(mv[:, 2:3], mv[:, 1:2], LN_EPS, None, ALU.add)
    P.act(mv[:, 3:4], mv[:, 2:3], AF.Sqrt)
    P.recip(mv[:, 4:5], mv[:, 3:4])
    zn = C.p_zn.next()
    P.ts(zn, z, mv[:, 0:1], mv[:, 4:5], ALU.subtract, ALU.mult)
    hn = C.p_hn.next()
    P.tt(zn, zn, lnp[0], ALU.mult, eng="pool")
    P.tt(hn, zn, lnp[1], ALU.add, eng="dve")
    P.dma(C.out.v(C.out.ap[tt * 128:(tt + 1) * 128, :], tt), hn, is_out=True)
    hb = C.p_hb.next()
    P.copy(hb, hn, eng="act")
    return (tt, hb)


def ln_epilogue_b(P, C, pend):
    if pend is None:
        return
    tt, hb = pend
    pt = C.pst
    for c in range(8):
        P.tr(pt[:, c * 128:(c + 1) * 128], hb[:, c * 128:(c + 1) * 128], C.identb)
    hts = C.p_hts.next()
    P.copy(hts, pt.re("p (c t) -> p c t", c=8), eng="act")
    P.dma(C.hT.v(C.hT.ap.rearrange("(c p) t -> p c t", p=128)[:, :, tt * 128:(tt + 1) * 128], ("t", tt)), hts)


def ln_pools(P, C):
    C.p_h = Pool(P, 8, [128, 1024], F32, "h")
    C.p_z = Pool(P, 2, [128, 1024], F32, "z")
    C.p_zn = Pool(P, 2, [128, 1024], F32, "zn")
    C.p_hn = Pool(P, 3, [128, 1024], F32, "hn")
    C.p_hb = Pool(P, 5, [128, 1024], BF16, "hb")
    C.p_hts = Pool(P, 2, [128, 8, 128], BF16, "hts")
    C.p_st = Pool(P, 2, [128, 2, 6], F32, "st")
    C.p_mv = Pool(P, 2, [128, 8], F32, "mv")


def load_ln(P, C, g_ap, b_ap):
    G = P.sb([128, 1024], F32, "lng")
    B = P.sb([128, 1024], F32, "lnb")
    P.dma(G, g_ap)
    P.dma(B, b_ap)
    return (G, B)


def hT_keys(b):
    return tuple(("t", t) for t in range(b * 4, b * 4 + 4))


def phase_prep(P, C, I):
    import os
    MODE = os.environ.get("PREP_MODE", "all")
    P.push()
    xs = Pool(P, 2, [128, 1024], F32, "xs")
    xb = Pool(P, 2, [128, 1024], BF16, "xb")
    hts = Pool(P, 2, [128, 8, 128], BF16, "hts")
    for tt in range(NT if MODE in ('all', 'A') else 0):
        x_t = xs.next()
        P.dma(x_t, I["x"].v(I["x"].ap[tt * 128:(tt + 1) * 128, :], tt))
        b_t = xb.next()
        P.copy(b_t, x_t, eng="dve")
        for c in range(8):
            P.tr(C.pst[:, c * 128:(c + 1) * 128], b_t[:, c * 128:(c + 1) * 128], C.identb)
        h_t = hts.next()
        P.copy(h_t, C.pst.re("p (c t) -> p c t", c=8), eng="act")
        P.dma(C.hT.v(C.hT.ap.rearrange("(c p) t -> p c t", p=128)[:, :, tt * 128:(tt + 1) * 128], ("t", tt)), h_t)
    for mt in range(2 if MODE in ('all', 'A2') else 0):
        x_t = xs.next()
        P.dma(x_t, I["mem"].v(I["mem"].ap[mt * 128:(mt + 1) * 128, :], mt))
        b_t = xb.next()
        P.copy(b_t, x_t, eng="dve")
        for c in range(8):
            P.tr(C.pst[:, c * 128:(c + 1) * 128], b_t[:, c * 128:(c + 1) * 128], C.identb)
        h_t = hts.next()
        P.copy(h_t, C.pst.re("p (c t) -> p c t", c=8), eng="act")
        P.dma(C.memT.v(C.memT.ap.rearrange("(c p) t -> p c t", p=128)[:, :, mt * 128:(mt + 1) * 128], 0), h_t)
    invf = P.sb([128, 1], F32, "invf")
    sgn = P.sb([128, 1], F32, "sgn")
    P.dma(invf, I["c_invf"].v(I["c_invf"].ap))
    P.dma(sgn, I["c_sgn"].v(I["c_sgn"].ap))
    W = 1024
    pi_ = P.sb([128, W], I32, "posi")
    pf = P.sb([128, W], F32, "posf")
    tu = P.sb([128, W], F32, "tu")
    f = P.sb([128, W], F32, "f")
    tb = P.sb([128, W], F32, "tb")
    scr = frac_scratch(P, [128, W])
    for cq in range(S // W if MODE in ('all', 'B') else 0):
        P.dma(pi_, I["pos"].v(I["pos"].ap[:, cq * W:(cq + 1) * W].partition_broadcast(128)))
        P.copy(pf, pi_, eng="dve")
        P.ts(tu, pf, invf[:, 0:1], None, ALU.mult)
        for which, shift, dst in (("sin", 0.0, C.sinT), ("cos", 0.25, C.cosT)):
            frac_wrap(P, f, tu, shift, scr)
            P.act(tb, f, AF.Sin, scale=TWO_PI)
            if which == "sin":
                P.ts(tb, tb, sgn[:, 0:1], None, ALU.mult)
            P.dma(dst.v(dst.ap[:, cq * W:(cq + 1) * W], cq), tb)
    P.pop()


def frac_scratch(P, shape):
    return (P.sb(shape, I32, "fw_ki"), P.sb(shape, F32, "fw_kf"), P.sb(shape, F32, "fw_m"))


def frac_wrap(P, f, tu, shift, scr):
    ki, kf, m = scr
    if shift != 0.0:
        P.ts(f, tu, shift, None, ALU.add)
        src = f
    else:
        src = tu
    P.copy(ki, src, eng="dve")
    P.copy(kf, ki, eng="dve")
    P.tt(f, src, kf, ALU.subtract)
    P.ts(m, f, 0.5, None, ALU.is_gt)
    P.tt(f, f, m, ALU.subtract)
    P.ts(m, f, -0.5, None, ALU.is_lt)
    P.tt(f, f, m, ALU.add)


def phase_proj_ln(P, C, inT, K, W_ap, lnp_aps, h_src, gate_ap=None, Wpre=None):
    P.push()
    KC = K // 128
    if Wpre is not None:
        Wb = Wpre
    else:
        Wb = P.sb([128, KC, 1024], BF16, "Wb")
        load_w(P, Wb, W_ap, KC)
    Wg = None
    if gate_ap is not None:
        Wg = P.sb([128, KC, 1024], BF16, "Wg")
        load_w(P, Wg, gate_ap, KC)
    lnp = load_ln(P, C, *lnp_aps)
    ln_pools(P, C)
    blk = Pool(P, 2, [128, KC, 512], BF16, "inblk")
    sig = Pool(P, 2, [128, 1024], F32, "sig") if Wg is not None else None
    pendq = []
    DEPTH = 3 if KC <= 8 else 1

    def prefetch(b):
        ib_ = blk.next()
        P.dma(ib_, inT.v(inT.ap.rearrange("(c p) t -> p c t", p=128)[:, :, b * 512:(b + 1) * 512], hT_keys(b)))
        return ib_, [load_h(P, C, b * 4 + t4, h_src) for t4 in range(4)]

    nxt = prefetch(0)
    for b in range(8):
        ib, hts_ = nxt
        if b < 7:
            nxt = prefetch(b + 1)
        for t4 in range(4):
            tt = b * 4 + t4
            par = (tt % 2) * 2 if Wg is None else 0
            sub = [C.PS[par], C.PS[par + 1]]
            for hf in range(2):
                for kc in range(KC):
                    P.mm(sub[hf], ib[:, kc, t4 * 128:(t4 + 1) * 128], Wb[:, kc, hf * 512:(hf + 1) * 512], start=(kc == 0), stop=(kc == KC - 1))
            if Wg is not None:
                gp = [C.PS[2], C.PS[3]]
                sg = sig.next()
                for hf in range(2):
                    for kc in range(KC):
                        P.mm(gp[hf], ib[:, kc, t4 * 128:(t4 + 1) * 128], Wg[:, kc, hf * 512:(hf + 1) * 512], start=(kc == 0), stop=(kc == KC - 1))
                    P.act(sg[:, hf * 512:(hf + 1) * 512], gp[hf], AF.Sigmoid)
                    P.tt(sg[:, hf * 512:(hf + 1) * 512], sub[hf], sg[:, hf * 512:(hf + 1) * 512], ALU.mult)
                sub = [sg[:, 0:512], sg[:, 512:1024]]
            if len(pendq) >= DEPTH:
                ln_epilogue_b(P, C, pendq.pop(0))
            pendq.append(ln_epilogue(P, C, lnp, tt, sub, hts_[t4]))
    for pe_ in pendq:
        ln_epilogue_b(P, C, pe_)
    P.pop()


def phase_ffn_up(P, C, wg_ap, wu_ap, Wgpre=None, after_loads=None):
    P.push()
    if Wgpre is not None:
        Wg = Wgpre
    else:
        Wg = P.sb([128, 8, FF], BF16, "Wg")
        load_w(P, Wg, wg_ap, 8)
    Wu = P.sb([128, 8, FF], BF16, "Wu")
    load_w(P, Wu, wu_ap, 8)
    if after_loads is not None:
        after_loads()
    blk = Pool(P, 2, [128, 8, 512], BF16, "hblk")
    ab = Pool(P, 2, [128, 22, 512], BF16, "ablk")
    sgp = Pool(P, 2, [128, 512], F32, "sg")
    def prefetch(b):
        ib_ = blk.next()
        P.dma(ib_, C.hT.v(C.hT.ap.rearrange("(c p) t -> p c t", p=128)[:, :, b * 512:(b + 1) * 512], hT_keys(b)))
        return ib_

    nxt = prefetch(0)
    for b in range(8):
        ib = nxt
        if b < 7:
            nxt = prefetch(b + 1)
        a = ab.next()
        for j in range(22):
            pg = C.PS[(j % 2) * 2]
            pu = C.PS[(j % 2) * 2 + 1]
            for kc in range(8):
                P.mm(pg, Wg[:, kc, j * 128:(j + 1) * 128], ib[:, kc, :], start=(kc == 0), stop=(kc == 7))
            for kc in range(8):
                P.mm(pu, Wu[:, kc, j * 128:(j + 1) * 128], ib[:, kc, :], start=(kc == 0), stop=(kc == 7))
            sg = sgp.next()
            P.act(sg, pg, AF.Silu)
            P.tt(a[:, j, :], pu, sg, ALU.mult)
        P.dma(C.aT.v(C.aT.ap.rearrange("(c p) t -> p c t", p=128)[:, :, b * 512:(b + 1) * 512], hT_keys(b)), a)
    P.pop()


def phase_cross(P, C, I, layer, lnp_aps, h_src, after_loads=None):
    P.push()
    wq, wk, wv, wo = (I[n] for n in ("xq_w", "xk_w", "xv_w", "xo_w"))
    rows = lambda w: (lambda kc: w.v(w.ap[layer, kc * 128:(kc + 1) * 128, :]))
    Wq = P.sb([128, 8, 1024], BF16, "Wq")
    Wo = P.sb([128, 8, 1024], BF16, "Wo")
    Wk = P.sb([128, 8, 1024], BF16, "Wk")
    load_w(P, Wk, rows(wk), 8)
    mT = P.sb([128, 8, 256], BF16, "mT")
    P.dma(mT, C.memT.v(C.memT.ap.rearrange("(c p) t -> p c t", p=128), 0))
    kT = P.sb([128, 8, 256], BF16, "kT")
    import os
    if os.environ.get("CROSS_MODE", "all") == "pre0":
        dbg = P.sb([128, 1024], F32, "dbg")
        P.copy(dbg, Wk[:, 3, :])
        P.dma(C.out.v(C.out.ap[0:128, :], 0), dbg, is_out=True)
        P.pop()
        return
    for oc in range(8):
        ps = C.PS[oc % 2]
        for kc in range(8):
            P.mm(ps[:, 0:256], Wk[:, kc, oc * 128:(oc + 1) * 128], mT[:, kc, :], start=(kc == 0), stop=(kc == 7))
        P.copy(kT[:, oc, :], ps[:, 0:256], eng="act")
    load_w(P, Wk, rows(wv), 8)
    Vm = P.sb([128, 2, 1024], BF16, "Vm")
    for mc in range(2):
        for hf in range(2):
            ps = C.PS[hf]
            for kc in range(8):
                P.mm(ps, mT[:, kc, mc * 128:(mc + 1) * 128], Wk[:, kc, hf * 512:(hf + 1) * 512], start=(kc == 0), stop=(kc == 7))
            P.copy(Vm[:, mc, hf * 512:(hf + 1) * 512], ps, eng="act")
    import os
    CM = os.environ.get("CROSS_MODE", "all")
    if CM == "pre":
        P.pop()
        return
    load_w(P, Wq, rows(wq), 8)
    load_w(P, Wo, rows(wo), 8)
    if after_loads is not None:
        after_loads()
    lnp = load_ln(P, C, *lnp_aps)
    ln_pools(P, C)
    onesb = C.onesb
    blk = Pool(P, 2, [128, 8, 512], BF16, "hblk")
    qTp = Pool(P, 1, [128, 8, 512], BF16, "qT")
    pTp = Pool(P, 2, [128, 2, 512], BF16, "pT")
    rdp = Pool(P, 2, [128, 512], F32, "rden")
    oTp = Pool(P, 2, [128, 8, 512], BF16, "oT")
    pendq = []

    def prefetch(b):
        ib_ = blk.next()
        P.dma(ib_, C.hT.v(C.hT.ap.rearrange("(c p) t -> p c t", p=128)[:, :, b * 512:(b + 1) * 512], hT_keys(b)))
        return ib_, [load_h(P, C, b * 4 + t4, h_src) for t4 in range(4)]

    nxt = prefetch(0)
    for b in range(8):
        ib, hts_ = nxt
        if b < 7:
            nxt = prefetch(b + 1)
        qT = qTp.next()
        for oc in range(8):
            ps = C.PS[2 + oc % 2]
            for kc in range(8):
                P.mm(ps, Wq[:, kc, oc * 128:(oc + 1) * 128], ib[:, kc, :], start=(kc == 0), stop=(kc == 7))
            P.copy(qT[:, oc, :], ps, eng="act")
        oT = oTp.next()
        if CM == "q":
            continue
        for hh in range(4):
            pT = pTp.next()
            for mc in range(2):
                ps = C.PS[2 + mc]
                for c in range(2):
                    P.mm(ps, kT[:, hh * 2 + c, mc * 128:(mc + 1) * 128], qT[:, hh * 2 + c, :], start=(c == 0), stop=(c == 1))
                P.act(pT[:, mc, :], ps, AF.Exp, scale=1.0 / 16.0)
            pd = C.PS[4]
            for mc in range(2):
                P.mm(pd, onesb, pT[:, mc, :], start=(mc == 0), stop=(mc == 1))
            rd = rdp.next()
            P.recip(rd, pd)
            for c in range(2):
                po = C.PS[5 + c]
                for mc in range(2):
                    P.mm(po, Vm[:, mc, hh * 256 + c * 128: hh * 256 + (c + 1) * 128], pT[:, mc, :], start=(mc == 0), stop=(mc == 1))
                P.tt(oT[:, hh * 2 + c, :], po, rd, ALU.mult)
        if CM == "attn":
            continue
        for t4 in range(4):
            tt = b * 4 + t4
            sub = [C.PS[0], C.PS[1]]
            for hf in range(2):
                for kc in range(8):
                    P.mm(sub[hf], oT[:, kc, t4 * 128:(t4 + 1) * 128], Wo[:, kc, hf * 512:(hf + 1) * 512], start=(kc == 0), stop=(kc == 7))
            if len(pendq) >= 3:
                ln_epilogue_b(P, C, pendq.pop(0))
            pendq.append(ln_epilogue(P, C, lnp, tt, sub, hts_[t4]))
    for pe_ in pendq:
        ln_epilogue_b(P, C, pe_)
    P.pop()


WEIGHT_SPECS = [
    ("hyb_w_in", [2, 1024, 3592]), ("dn_conv_w", [2, 4, 1536]), ("hyb_w_out", [2, 1024, 1024]),
    ("s5_glu_wo", [2, 1024, 1024]), ("s5_glu_wg", [2, 1024, 1024]),
    ("ln_mix_g", [4, 1024]), ("ln_mix_b", [4, 1024]),
    ("xq_w", [4, 1024, 1024]), ("xk_w", [4, 1024, 1024]), ("xv_w", [4, 1024, 1024]), ("xo_w", [4, 1024, 1024]),
    ("ln_x_g", [4, 1024]), ("ln_x_b", [4, 1024]),
    ("ffn_wg", [4, 1024, 2816]), ("ffn_wu", [4, 1024, 2816]), ("ffn_wd", [4, 2816, 1024]),
    ("ln_ffn_g", [4, 1024]), ("ln_ffn_b", [4, 1024]),
]


def build_program(plan):
    nc = bass.Bass("TRN2", target_bir_lowering=False)
    P = Prog(nc)
    C = Ctx()
    I = {}

    def inp(name, shape, dt=F32):
        I[name] = P.dram(name, shape, dt, kind="ExternalInput")

    inp("x", [S, D])
    inp("mem", [256, D])
    inp("pos", [1, S], I32)
    for n, sh in WEIGHT_SPECS:
        inp(n, sh)
    hc = host_consts()
    for n, a in hc.items():
        inp(n, list(a.shape))
    for l in range(2):
        inp("s5col%d" % l, [128, 3, 32])
        inp("s5row%d" % l, [3, 4096])
        for nm in ("bre", "bim", "cre", "cim"):
            inp("s5%s%d" % (nm, l), [128, 4096])
        inp("s5d%d" % l, [128, 8])
        inp("dnp%d" % l, [1, 4 + 4 + 128])
        inp("dncw%d" % l, [128, 12, 4])
    C.out = P.dram("out", [S, D], F32, kind="ExternalOutput")
    C.hT = P.dram("hT", [D, S], BF16)
    C.mixT = P.dram("mixT", [D, S], BF16)
    C.aT = P.dram("aT", [FF, S], BF16)
    C.memT = P.dram("memT", [D, 256], BF16)
    C.cosT = P.dram("cosT", [128, S], F32)
    C.sinT = P.dram("sinT", [128, S], F32)
    C.vd = P.dram("vd", [S, 512], BF16)
    C.PS = [P.ps([128, 512], F32, "ps%d" % i) for i in range(7)]
    C.pst = P.ps([128, 1024], BF16, "pst")
    idf = P.sb([128, 128], F32, "idf")
    P.dma(idf, I["c_ident"].v(I["c_ident"].ap))
    C.identb = P.sb([128, 128], BF16, "identb")
    P.copy(C.identb, idf)
    C.identf = idf
    C.onesb = P.sb([128, 128], BF16, "onesb")
    P.memset(C.onesb, 1.0)

    bvec = lambda nm, l: I[nm].v(I[nm].ap[l:l + 1, :].partition_broadcast(128))
    h_src = I["x"]
    for ph in plan:
        kind = ph[0]
        if kind == "prep":
            phase_prep(P, C, I)
        elif kind == "cross":
            l = ph[1]
            phase_cross(P, C, I, l, (bvec("ln_x_g", l), bvec("ln_x_b", l)), h_src)
            h_src = C.out
        elif kind == "ffn":
            l = ph[1]
            wg, wu, wd = I["ffn_wg"], I["ffn_wu"], I["ffn_wd"]
            P.push()
            WdPre = P.sb([128, 22, 1024], BF16, "WdPre")
            pre = lambda: load_w(P, WdPre, lambda kc: wd.v(wd.ap[l, kc * 128:(kc + 1) * 128, :]), 22)
            phase_ffn_up(P, C, lambda kc: wg.v(wg.ap[l, kc * 128:(kc + 1) * 128, :]), lambda kc: wu.v(wu.ap[l, kc * 128:(kc + 1) * 128, :]), after_loads=pre)
            phase_proj_ln(P, C, C.aT, FF, None, (bvec("ln_ffn_g", l), bvec("ln_ffn_b", l)), h_src, Wpre=WdPre)
            P.pop()
            h_src = C.out
        elif kind == "s5":
            l = ph[1]
            i = l // 2
            phase_s5(P, C, I, i)
            wo, wg = I["s5_glu_wo"], I["s5_glu_wg"]
            phase_proj_ln(P, C, C.mixT, D, lambda kc: wo.v(wo.ap[i, kc * 128:(kc + 1) * 128, :]),
                          (bvec("ln_mix_g", l), bvec("ln_mix_b", l)), h_src,
                          gate_ap=lambda kc: wg.v(wg.ap[i, kc * 128:(kc + 1) * 128, :]))
            h_src = C.out
        elif kind == "hyb":
            l = ph[1]
            i = l // 2
            phase_dilated(P, C, I, i)
            phase_gdn(P, C, I, i)
            wo = I["hyb_w_out"]
            phase_proj_ln(P, C, C.mixT, D, lambda kc: wo.v(wo.ap[i, kc * 128:(kc + 1) * 128, :]),
                          (bvec("ln_mix_g", l), bvec("ln_mix_b", l)), h_src)
            h_src = C.out
    P.emit()
    return nc


def make_in_maps(inputs, cores):
    hc = host_consts()
    hl = host_layer_inputs(inputs)
    shared = {n: np.ascontiguousarray(inputs[n], np.float32) for n, _ in WEIGHT_SPECS}
    shared.update(hc)
    shared.update(hl)
    for l in range(2):
        shared["dnp%d" % l] = np.concatenate([inputs["dn_a_log"][l], inputs["dn_dt_bias"][l], inputs["dn_norm_g"][l]]).astype(np.float32).reshape(1, 136)
    maps = []
    for b in cores:
        m = dict(shared)
        m["x"] = np.ascontiguousarray(inputs["x"][b], np.float32)
        m["mem"] = np.ascontiguousarray(inputs["mem"][b], np.float32)
        m["pos"] = np.ascontiguousarray(inputs["positions"][b].reshape(1, S), np.int32)
        maps.append(m)
    return maps


FULL_PLAN = [("prep",)]
for _l in range(4):
    FULL_PLAN += [("hyb" if _l % 2 == 0 else "s5", _l), ("cross", _l), ("ffn", _l)]


def kernel(**inputs):
    inputs = {k: np.asarray(v) for k, v in inputs.items()}
    nc = build_program(FULL_PLAN)
    maps = make_in_maps(inputs, list(range(8)))
    res = run_bass_kernel_spmd(nc, maps, core_ids=list(range(8)))
    return np.stack([r["out"] for r in res.results], 0).astype(np.float32)


def b3(v, shape, axis):
    return V(v.ap.unsqueeze(axis).to_broadcast(shape), v.bufs)


def phase_s5(P, C, I, l):
    P.push()
    f32t = lambda shape, n: P.sb(shape, F32, n)
    col = f32t([128, 3, 32], "col")
    P.dma(col, I["s5col%d" % l].v(I["s5col%d" % l].ap))
    dtc = f32t([128, 32], "dtc")
    P.act(dtc, col[:, 2, :], AF.Exp)
    t0 = f32t([128, 32], "t0")
    P.tt(t0, col[:, 0, :], dtc, ALU.mult)
    rho = f32t([128, 32], "rho")
    P.act(rho, t0, AF.Exp)
    phi = f32t([128, 32], "phi")
    P.tt(phi, col[:, 1, :], dtc, ALU.mult)
    P.ts(phi, phi, 1.0 / TWO_PI, None, ALU.mult)
    iota = f32t([128, 64], "iota")
    P.dma(iota, I["c_iota"].v(I["c_iota"].ap))
    dcol = f32t([128, 8], "dcol")
    P.dma(dcol, I["s5d%d" % l].v(I["s5d%d" % l].ap))
    E = {k: f32t([128, 2048], "F" + k) for k in ("lo", "hi")}
    Bb = {k: P.sb([128, 4096], BF16, "Bb" + k) for k in ("re", "im")}
    CreB = P.sb([128, 4096], BF16, "CreB")
    CreN = P.sb([128, 4096], BF16, "CreN")
    CimN = P.sb([128, 4096], BF16, "CimN")
    P.dma(CreB, I["s5cre%d" % l].v(I["s5cre%d" % l].ap), eng="pool")
    P.dma(CimN, I["s5cim%d" % l].v(I["s5cim%d" % l].ap), eng="pool")
    P.ts(CreN, CreB, -1.0, None, ALU.mult)
    P.ts(CimN, CimN, -1.0, None, ALU.mult)
    P.push()
    scr = frac_scratch(P, [128, 2048])
    tu = f32t([128, 2048], "tu")
    tu3 = tu.re("p (a b) -> p a b", b=64)
    iota3 = b3(iota, [128, 32, 64], 1)
    P.tt(tu3, b3(phi, [128, 32, 64], 2), iota3, ALU.mult)
    frac_wrap(P, E["lo"], tu, 0.0, scr)
    p64 = f32t([128, 32], "p64")
    P.ts(p64, phi, 64.0, None, ALU.mult)
    scr2 = frac_scratch(P, [128, 32])
    f64 = f32t([128, 32], "f64")
    frac_wrap(P, f64, p64, 0.0, scr2)
    P.tt(tu3, b3(f64, [128, 32, 64], 2), iota3, ALU.mult)
    frac_wrap(P, E["hi"], tu, 0.0, scr)
    P.pop()
    P.push()
    row = I["s5row%d" % l]
    T = {k: f32t([128, 1024], "r" + k) for k in ("are", "aim", "dt", "er", "tu", "f", "sn", "cs", "abr", "abi", "nr", "den", "cr", "ci", "Bre", "Bim", "x1", "x2")}
    scr = frac_scratch(P, [128, 1024])
    for qq in range(4):
        cs_ = slice(qq * 1024, (qq + 1) * 1024)
        P.dma(T["are"], row.v(row.ap[0:1, cs_].partition_broadcast(128)))
        P.dma(T["aim"], row.v(row.ap[1:2, cs_].partition_broadcast(128)))
        P.dma(T["dt"], row.v(row.ap[2:3, cs_].partition_broadcast(128)))
        P.dma(T["Bre"], I["s5bre%d" % l].v(I["s5bre%d" % l].ap[:, cs_]))
        P.dma(T["Bim"], I["s5bim%d" % l].v(I["s5bim%d" % l].ap[:, cs_]))
        P.act(T["dt"], T["dt"], AF.Exp)
        P.tt(T["er"], T["are"], T["dt"], ALU.mult)
        P.act(T["er"], T["er"], AF.Exp)
        P.tt(T["tu"], T["aim"], T["dt"], ALU.mult)
        P.ts(T["tu"], T["tu"], 1.0 / TWO_PI, None, ALU.mult)
        frac_wrap(P, T["f"], T["tu"], 0.0, scr)
        P.act(T["sn"], T["f"], AF.Sin, scale=TWO_PI)
        frac_wrap(P, T["f"], T["tu"], 0.25, scr)
        P.act(T["cs"], T["f"], AF.Sin, scale=TWO_PI)
        P.tt(T["abr"], T["er"], T["cs"], ALU.mult)
        P.tt(T["abi"], T["er"], T["sn"], ALU.mult)
        P.ts(T["nr"], T["abr"], -1.0, None, ALU.add)
        P.tt(T["den"], T["are"], T["are"], ALU.mult)
        P.tt(T["x1"], T["aim"], T["aim"], ALU.mult)
        P.tt(T["den"], T["den"], T["x1"], ALU.add)
        P.recip(T["den"], T["den"])
        P.tt(T["x1"], T["nr"], T["are"], ALU.mult)
        P.tt(T["x2"], T["abi"], T["aim"], ALU.mult)
        P.tt(T["cr"], T["x1"], T["x2"], ALU.add)
        P.tt(T["cr"], T["cr"], T["den"], ALU.mult)
        P.tt(T["x1"], T["abi"], T["are"], ALU.mult)
        P.tt(T["x2"], T["nr"], T["aim"], ALU.mult)
        P.tt(T["ci"], T["x1"], T["x2"], ALU.subtract)
        P.tt(T["ci"], T["ci"], T["den"], ALU.mult)
        P.tt(T["x1"], T["cr"], T["Bre"], ALU.mult)
        P.tt(T["x2"], T["ci"], T["Bim"], ALU.mult)
        P.tt(Bb["re"][:, cs_], T["x1"], T["x2"], ALU.subtract)
        P.tt(T["x1"], T["cr"], T["Bim"], ALU.mult)
        P.tt(T["x2"], T["ci"], T["Bre"], ALU.mult)
        P.tt(Bb["im"][:, cs_], T["x1"], T["x2"], ALU.add)
    P.pop()
    BW = 512
    NBLK = S // BW
    HC = BW // 64
    ones = f32t([128, BW], "ones")
    P.memset(ones, 1.0)
    halfpi = f32t([128, 1], "halfpi")
    P.memset(halfpi, math.pi / 2)
    onec = f32t([128, 1], "onec")
    P.memset(onec, 1.0)
    stre = [f32t([128, 1], "stre%d" % i) for i in range(32)]
    stim = [f32t([128, 1], "stim%d" % i) for i in range(32)]
    for t_ in stre + stim:
        P.memset(t_, 0.0)
    uTp = Pool(P, 2, [128, S], BF16, "uT")
    RHO = [[f32t([128, BW], "RHO%d_%d" % (par, q)) for q in range(4)] for par in range(2)]
    mkp = lambda n, dt=F32, k=4: Pool(P, k, [128, BW], dt, n)
    p_fs, p_s1, p_sq, p_ei = mkp("fs", F32, 2), mkp("s1", F32, 2), mkp("sq"), mkp("ei")
    p_m = [mkp("m%d" % i, F32, 3) for i in range(4)]
    p_xr, p_xi = mkp("xr", F32, 3), mkp("xi", F32, 3)
    p_pp = [mkp("pp%d" % i, BF16, 3) for i in range(4)]
    hidp = Pool(P, 2, [128, BW], BF16, "hid")
    vtp = Pool(P, 2, [128, BW], F32, "vt")
    allk = tuple(("t", t) for t in range(NT))
    v3 = lambda t: t.re("p (a b) -> p a b", b=64)
    iters = [(cc, blk, q) for cc in range(8) for blk in range(NBLK) for q in range(4)]
    ctx = {}
    uTs = {}
    bub = [0]
    btb = [0]
    IDN = f32t([128, 128], "IDN")
    P.ts(IDN, C.identf, -1.0, None, ALU.mult)

    def prologue(n):
        cc, blk, q = iters[n]
        ctx[n] = {}
        if blk == 0 and q == 0:
            uTs[cc] = uTp.next()
            P.dma(uTs[cc], C.hT.v(C.hT.ap[cc * 128:(cc + 1) * 128, :], allk))
            for q2 in range(4):
                P.ts(RHO[cc % 2][q2], ones, rho[:, 4 * cc + q2:4 * cc + q2 + 1], None, ALU.mult)

    def sA(n):
        cc, blk, q = iters[n]
        pair = 4 * cc + q
        d = ctx[n]
        hs = slice(pair * 64 + blk * HC, pair * 64 + (blk + 1) * HC)
        ls = slice(pair * 64, pair * 64 + 64)
        d["fs"], d["s1"], d["sq"], d["ei"] = p_fs.next(), p_s1.next(), p_sq.next(), p_ei.next()
        P.tt(v3(d["fs"]), b3(E["hi"][:, hs], [128, HC, 64], 2), b3(E["lo"][:, ls], [128, HC, 64], 1), ALU.add, eng="pool")

    def sD(n):
        d = ctx[n]
        fs, s1, sq = d["fs"], d["s1"], d["sq"]
        P.act(s1, fs, AF.Sin, scale=math.pi)
        P.act(fs, fs, AF.Abs)
        P.act(fs, fs, AF.Sin, scale=-math.pi, bias=halfpi[:, 0:1])
        P.act(sq, s1, AF.Square)
        P.act(sq, sq, AF.Identity, scale=-2.0, bias=onec[:, 0:1])

    def sF(n):
        d = ctx[n]
        P.stt(d["ei"], d["s1"], 2.0, d["fs"], ALU.mult, ALU.mult)
        d["Ere"], d["Eim"] = d["sq"], d["ei"]

    def sB(n):
        cc, blk, q = iters[n]
        pair = 4 * cc + q
        d = ctx[n]
        tk = slice(blk * BW, (blk + 1) * BW)
        pc = slice(pair * 128, (pair + 1) * 128)
        d["bre"], d["bim"] = C.PS[0], C.PS[1]
        P.mm(d["bre"], Bb["re"][:, pc], uTs[cc][:, tk])
        P.mm(d["bim"], Bb["im"][:, pc], uTs[cc][:, tk])

    def sE(n):
        d = ctx[n]
        m = [p.next() for p in p_m]
        bre, bim = d["bre"], d["bim"]
        P.tt(m[0], bre, d["Ere"], ALU.mult)
        P.tt(m[1], bim, d["Eim"], ALU.mult)
        P.tt(m[2], bim, d["Ere"], ALU.mult)
        P.tt(m[3], bre, d["Eim"], ALU.mult)
        btr, bti = C.PS[2 + btb[0]], C.PS[3 + btb[0]]
        btb[0] = (btb[0] + 2) % 4
        P.mm(btr, C.identf, m[0], start=True, stop=False)
        P.mm(btr, C.identf, m[1], start=False, stop=True)
        P.mm(bti, C.identf, m[2], start=True, stop=False)
        P.mm(bti, IDN, m[3], start=False, stop=True)
        d["m"] = [btr, None, bti, None]

    def sC(n):
        cc, blk, q = iters[n]
        pair = 4 * cc + q
        d = ctx[n]
        m = d["m"]
        xr, xi = p_xr.next(), p_xi.next()
        P.scan(xr, RHO[cc % 2][q], m[0], stre[pair], ALU.mult, ALU.add)
        P.scan(xi, RHO[cc % 2][q], m[2], stim[pair], ALU.mult, ALU.add)
        P.copy(stre[pair], xr[:, BW - 1:BW], eng="act")
        P.copy(stim[pair], xi[:, BW - 1:BW], eng="act")
        pp = d["pp"] = [p.next() for p in p_pp]
        P.tt(pp[0], d["Ere"], xr, ALU.mult)
        P.tt(pp[1], d["Eim"], xi, ALU.mult, eng="pool")
        P.tt(pp[2], d["Ere"], xi, ALU.mult, eng="pool")
        P.tt(pp[3], d["Eim"], xr, ALU.mult, eng="pool")

    def sG(n):
        cc, blk, q = iters[n]
        pair = 4 * cc + q
        d = ctx.pop(n)
        uT = uTs[cc]
        tk = slice(blk * BW, (blk + 1) * BW)
        pc = slice(pair * 128, (pair + 1) * 128)
        yps = C.PS[6]
        pp = d["pp"]
        for k, (W_, p_) in enumerate(((CreB, pp[0]), (CreN, pp[1]), (CimN, pp[2]), (CimN, pp[3]))):
            P.mm(yps, W_[:, pc], p_, start=(q == 0 and k == 0), stop=(q == 3 and k == 3))
        if q == 3:
            vt = vtp.next()
            P.stt(vt, uT[:, tk], dcol[:, cc:cc + 1], yps, ALU.mult, ALU.add)
            hid = hidp.next()
            P.act(hid, vt, AF.Gelu)
            P.dma(C.mixT.v(C.mixT.ap[cc * 128:(cc + 1) * 128, tk], hT_keys(blk)), hid)

    N_ = len(iters)
    ok = lambda k: 0 <= k < N_
    for n in range(N_ + 2):
        if ok(n):
            prologue(n)
            sA(n)
        if ok(n - 1):
            sB(n - 1)
        if ok(n - 2):
            sC(n - 2)
        if ok(n):
            sD(n)
        if ok(n - 1):
            sE(n - 1)
        if ok(n):
            sF(n)
        if ok(n - 2):
            sG(n - 2)
    P.pop()


def _zero_mix_rows(P, C, r0, r1):
    P.push()
    z = P.sb([128, 1024], BF16, "zmix")
    P.memset(z, 0.0)
    for c in range(r0 // 128, r1 // 128):
        for blk in range(4):
            P.dma(C.mixT.v(C.mixT.ap[c * 128:(c + 1) * 128, blk * 1024:(blk + 1) * 1024],
                           tuple(("t", t) for t in range(blk * 8, blk * 8 + 8))), z)
    P.pop()


def phase_dilated(P, C, I, i):
    w_in = I["hyb_w_in"]
    w3 = w_in.ap[i].rearrange("(kc p) n -> p kc n", p=128)
    hT3 = C.hT.ap.rearrange("(c p) t -> p c t", p=128)
    QOFF, KOFF, VOFF = 2056, 2568, 3080
    P.push()
    Wv = P.sb([128, 8, 512], BF16, "Wv")
    P.dma(Wv, w_in.v(w3[:, :, VOFF:VOFF + 512]), eng="pool")
    blkp = Pool(P, 2, [128, 8, 512], BF16, "hblk")
    vtp = Pool(P, 2, [128, 512], BF16, "vt")
    for b in range(8):
        ib = blkp.next()
        P.dma(ib, C.hT.v(hT3[:, :, b * 512:(b + 1) * 512], hT_keys(b)))
        for t4 in range(4):
            tt = b * 4 + t4
            ps = C.PS[t4 % 2]
            for kc in range(8):
                P.mm(ps, ib[:, kc, t4 * 128:(t4 + 1) * 128], Wv[:, kc, :], start=(kc == 0), stop=(kc == 7))
            vt = vtp.next()
            P.copy(vt, ps, eng="act")
            P.dma(C.vd.v(C.vd.ap[tt * 128:(tt + 1) * 128, :], 0), vt)
    P.pop()
    import os
    DM = os.environ.get("DIL_MODE", "all")
    if DM == "v":
        return
    P.push()
    COS = P.sb([128, S], F32, "COS")
    SIN = P.sb([128, S], F32, "SIN")
    P.dma(COS, C.cosT.v(C.cosT.ap, (0, 1, 2, 3)))
    P.dma(SIN, C.sinT.v(C.sinT.ap, (0, 1, 2, 3)))
    dmask = P.sb([128, 256], F32, "dmask")
    P.dma(dmask, I["c_dmask"].v(I["c_dmask"].ap))
    sel = P.sb([128, 64], F32, "sel")
    P.dma(sel, I["c_sel"].v(I["c_sel"].ap))
    Wt = {k: P.sb([128, 8, 128], BF16, "W" + k) for k in ("q", "qs", "k", "ks")}
    blkp = Pool(P, 2, [128, 8, 512], BF16, "hblk")
    RT = {k: P.sb([128, S], BF16, "R" + k) for k in ("q", "k")}
    CM = {(k, r): P.sb([128, S], BF16, "C%s%d" % (k, r)) for k in ("q", "k") for r in (4, 16)}
    OT = [P.sb([128, S], F32, "OT%d" % h) for h in range(2)]
    t1p = Pool(P, 2, [128, 512], F32, "t1")
    t2p = Pool(P, 2, [128, 512], F32, "t2")
    vap = Pool(P, 4, [128, 8, 128], BF16, "vaug")
    for va in vap.tiles:
        P.memset(va, 1.0)
    pep = Pool(P, 3, [128, 256], BF16, "pe")
    ptp = Pool(P, 6, [128, 256], BF16, "pt")
    dnegf = P.sb([128, 256], F32, "dnegf")
    P.dma(dnegf, I["c_dneg"].v(I["c_dneg"].ap))
    dnegb = P.sb([128, 256], BF16, "dnegb")
    P.copy(dnegb, dnegf)
    sbank = [0]
    rdp = Pool(P, 2, [64, 512], F32, "rd")
    onp = Pool(P, 2, [64, 512], BF16, "on")
    slots = [[V(C.PS[3 + 2 * h + sidx].ap[:, 0:128], C.PS[3 + 2 * h + sidx].bufs) for sidx in range(2)] for h in range(2)]
    for hp in range(4):
        for k, off in (("q", QOFF), ("k", KOFF)):
            base = off + hp * 128
            P.dma(Wt[k], w_in.v(w3[:, :, base:base + 128]), eng="pool")
            for hh in range(2):
                P.dma(Wt[k + "s"][:, :, hh * 64:hh * 64 + 32], w_in.v(w3[:, :, base + hh * 64 + 32:base + hh * 64 + 64]), eng="pool")
                P.dma(Wt[k + "s"][:, :, hh * 64 + 32:hh * 64 + 64], w_in.v(w3[:, :, base + hh * 64:base + hh * 64 + 32]), eng="pool")
        for b in range(8):
            ib = blkp.next()
            P.dma(ib, C.hT.v(hT3[:, :, b * 512:(b + 1) * 512], hT_keys(b)))
            cs_ = slice(b * 512, (b + 1) * 512)
            for k in ("q", "k"):
                for kc in range(8):
                    P.mm(C.PS[0], Wt[k][:, kc, :], ib[:, kc, :], start=(kc == 0), stop=(kc == 7))
                for kc in range(8):
                    P.mm(C.PS[1], Wt[k + "s"][:, kc, :], ib[:, kc, :], start=(kc == 0), stop=(kc == 7))
                t1 = t1p.next()
                t2 = t2p.next()
                P.tt(t1, C.PS[0], COS[:, cs_], ALU.mult)
                P.tt(t2, C.PS[1], SIN[:, cs_], ALU.mult)
                P.tt(RT[k][:, cs_], t1, t2, ALU.add, eng="pool")
        if DM == "proj":
            continue
        for k in ("q", "k"):
            for r in (4, 16):
                P.copy(CM[(k, r)].re("p (r u) -> p r u", r=r), RT[k].re("p (u r) -> p r u", r=r), eng="pool")
        if DM == "cm":
            continue
        its = []
        for bi, r in enumerate((1, 4, 16)):
            L = S // r
            nb_ = L // 128
            for rho in range(r):
                for m in range(nb_):
                    for hh in range(2):
                        its.append((bi, r, rho, m, hh, nb_, L))
        ictx = {}

        def st_S(n):
            bi, r, rho, m, hh, nb_, L = its[n]
            d = ictx[n] = {}
            Qs = RT["q"] if r == 1 else CM[("q", r)]
            Ks = RT["k"] if r == 1 else CM[("k", r)]
            if hh == 0:
                va = vap.next()
                st = (128 * m) * r + rho
                P.dma(va[:, :, 0:64], C.vd.v(C.vd.ap[st:st + 127 * r + 1:r, :].rearrange("t (h d) -> t h d", h=8), 0))
                ictx["va"] = va
            d["va"] = ictx["va"]
            nq = d["nq"] = 256 if m < nb_ - 1 else 128
            c0 = rho * L + m * 128
            hb = hh * 64
            sps = d["sps"] = C.PS[sbank[0]]
            sbank[0] = (sbank[0] + 1) % 3
            P.mm(sps[:, 0:nq], Ks[hb:hb + 64, c0:c0 + 128], Qs[hb:hb + 64, c0:c0 + nq], start=True, stop=False)
            P.mm(sps[:, 0:nq], C.identb, dnegb[:, 0:nq], start=False, stop=True)

        def st_X(n):
            d = ictx[n]
            nq = d["nq"]
            pt = d["pt"] = ptp.next()
            P.act(pt[:, 0:nq], d["sps"][:, 0:nq], AF.Exp, scale=0.125)

        def st_V(n):
            bi, r, rho, m, hh, nb_, L = its[n]
            d = ictx.pop(n)
            pt, va, nq = d["pt"], d["va"], d["nq"]
            head = hp * 2 + hh
            sl = slots[hh][m % 2]
            P.mm(sl, va[:, head, :], pt[:, 0:128], start=(m == 0), stop=True)
            dst = OT[hh].re("p (u r) -> p r u", r=r)[:, rho, m * 128:(m + 1) * 128] if r > 1 else OT[hh][:, m * 128:(m + 1) * 128]
            if bi == 0:
                P.copy(dst, sl, eng="dve")
            else:
                P.tt(dst, sl, dst, ALU.add)
            if nq == 256:
                sl2 = slots[hh][(m + 1) % 2]
                P.mm(sl2, va[:, head, :], pt[:, 128:256], start=True, stop=False)

        NI = len(its)
        for n in range(NI + 2):
            if n < NI:
                st_S(n)
            if 0 <= n - 1 < NI:
                st_X(n - 1)
            if 0 <= n - 2 < NI:
                st_V(n - 2)
        if DM in ("br0a", "br0b", "br0b1", "br0b2"):
            continue
        for hh in range(2):
            head = hp * 2 + hh
            for b in range(8):
                cs_ = slice(b * 512, (b + 1) * 512)
                pd = C.PS[hh]
                P.mm(pd[0:64, :], sel, OT[hh][:, cs_])
                rd = rdp.next()
                P.recip(rd, pd[0:64, :])
                on = onp.next()
                P.tt(on, OT[hh][0:64, cs_], rd, ALU.mult)
                r0 = 512 + head * 64
                P.dma(C.mixT.v(C.mixT.ap[r0:r0 + 64, cs_], hT_keys(b)), on)
    P.pop()


def phase_gdn(P, C, I, i):
    w_in = I["hyb_w_in"]
    w3 = w_in.ap[i].rearrange("(kc p) n -> p kc n", p=128)
    hT3 = C.hT.ap.rearrange("(c p) t -> p c t", p=128)
    dnp = I["dnp%d" % i]
    P.push()
    f32t = lambda shape, n: P.sb(shape, F32, n)
    cst = {}
    for nm in ("c_tri", "c_ones", "c_negU", "c_posL", "c_strU", "c_strL"):
        cst[nm] = f32t([128, 128], nm)
        P.dma(cst[nm], I[nm].v(I[nm].ap))
    TRI, ONESf, NEGU, POSL, STRU, STRL = (cst[n] for n in ("c_tri", "c_ones", "c_negU", "c_posL", "c_strU", "c_strL"))
    IDF = C.identf
    onecol = f32t([128, 1], "onecol")
    P.memset(onecol, 1.0)
    NG = f32t([128, 128], "NG")
    P.dma(NG, dnp.v(dnp.ap[0:1, 8:136].partition_broadcast(128)))
    ab = f32t([128, 8], "ab")
    P.dma(ab, dnp.v(dnp.ap[0:1, 0:8].partition_broadcast(128)))
    nA = f32t([128, 4], "nA")
    P.act(nA, ab[:, 0:4], AF.Exp)
    P.ts(nA, nA, -1.0, None, ALU.mult)
    cw = f32t([128, 12, 4], "cw")
    P.dma(cw, I["dncw%d" % i].v(I["dncw%d" % i].ap))
    bank = [0]

    def nb():
        bank[0] = (bank[0] + 1) % 7
        return C.PS[bank[0]]

    Wg8 = P.sb([128, 8, 8], BF16, "Wg8")
    P.dma(Wg8, w_in.v(w3[:, :, 2048:2056]), eng="pool")
    blkp = Pool(P, 2, [128, 8, 512], BF16, "hblk")
    GR = f32t([128, 32, 8], "GR")
    for b in range(8):
        ib = blkp.next()
        P.dma(ib, C.hT.v(hT3[:, :, b * 512:(b + 1) * 512], hT_keys(b)))
        for t4 in range(4):
            ps = nb()
            for kc in range(8):
                P.mm(ps[:, 0:8], ib[:, kc, t4 * 128:(t4 + 1) * 128], Wg8[:, kc, :], start=(kc == 0), stop=(kc == 7))
            P.copy(GR[:, b * 4 + t4, :], ps[:, 0:8], eng="act")
    BETA = f32t([128, 128], "BETA")
    NBETA = f32t([128, 128], "NBETA")
    Gt = f32t([128, 128], "Gt")
    v34 = lambda t: t.re("p (c h) -> p c h", h=4)
    P.copy(v34(BETA), GR[:, :, 0:4], eng="dve")
    P.act(BETA, BETA, AF.Sigmoid)
    P.ts(NBETA, BETA, -1.0, None, ALU.mult)
    P.tt(v34(Gt), GR[:, :, 4:8], b3(ab[:, 4:8], [128, 32, 4], 1), ALU.add)
    P.act(Gt, Gt, AF.Exp)
    P.act(Gt, Gt, AF.Ln, bias=onecol[:, 0:1])
    P.tt(v34(Gt), v34(Gt), b3(nA, [128, 32, 4], 1), ALU.mult)
    GC, GL, EG, EDL, EGL = (f32t([128, 128], n) for n in ("GC", "GL", "EG", "EDL", "EGL"))
    ps = nb()
    P.mm(ps[:, 0:128], TRI, Gt)
    P.copy(GC, ps[:, 0:128], eng="dve")
    ps = nb()
    P.mm(ps[:, 0:128], ONESf, Gt)
    P.copy(GL, ps[:, 0:128], eng="dve")
    P.act(EG, GC, AF.Exp)
    P.act(EGL, GL, AF.Exp)
    P.tt(EDL, GL, GC, ALU.subtract)
    P.act(EDL, EDL, AF.Exp)
    import os
    GM = os.environ.get("GDN_MODE", "all")
    if GM == "gates":
        P.pop()
        return
    Wh = P.sb([128, 8, 3, 128], BF16, "Wh")
    Wz = P.sb([128, 8, 128], BF16, "Wz")
    X = [f32t([128, S + 3], "X%d" % x) for x in range(3)]
    for x in range(3):
        P.memset(X[x][:, 0:3], 0.0)
    sq = P.sb([128, S], BF16, "sq")
    Ycv = f32t([128, S], "Ycv")
    QT = P.sb([128, S], BF16, "QT")
    KT = P.sb([128, S], BF16, "KT")
    VT = P.sb([128, S], BF16, "VT")
    KTM = P.sb([128, 32, 128], BF16, "KTM")
    VTM = P.sb([128, 32, 128], BF16, "VTM")
    rnp = Pool(P, 2, [128, 512], F32, "rn")
    Sst = f32t([128, 128], "Sst")
    Sb = P.sb([128, 128], BF16, "Sb")
    NBATCH = 4
    mk = lambda n, dt=F32, k=NBATCH: Pool(P, k, [128, 128], dt, n)
    p_gb, p_bb, p_DT, p_DL, p_tmp, p_tmp2 = mk("gb"), mk("bb"), mk("DT"), mk("DL"), mk("tmpa"), mk("tmpb")
    p_A, p_AT, p_P = mk("A", F32, 2 * NBATCH), mk("AT", F32, 2 * NBATCH), mk("P", F32, 2 * NBATCH)
    p_intra, p_Pb, p_ub, p_Keg, p_WT, p_Kdec, p_qdT, p_EGr = (mk("intra", BF16), mk("Pb", BF16), mk("ub"), mk("Keg", BF16),
                                                              mk("WT", BF16), mk("Kdec", BF16), mk("qdT", BF16), mk("EGr"))
    p_vn, p_zs, p_t, p_ao, p_aoT = mk("vn", BF16, 2), mk("zs", F32, 2), mk("tt", F32, 2), mk("ao", BF16, 2), mk("aoT", BF16, 2)
    p_st = Pool(P, 2, [128, 6], F32, "gst")
    p_mv = Pool(P, 2, [128, 8], F32, "gmv")
    hcp = Pool(P, 2, [128, 8, 128], BF16, "hchunk")
    for hd in range(4):
        for x in range(3):
            P.dma(Wh[:, :, x, :], w_in.v(w3[:, :, x * 512 + hd * 128: x * 512 + (hd + 1) * 128]), eng="pool")
        P.dma(Wz, w_in.v(w3[:, :, 1536 + hd * 128:1536 + (hd + 1) * 128]), eng="pool")
        for b in range(8):
            ib = blkp.next()
            P.dma(ib, C.hT.v(hT3[:, :, b * 512:(b + 1) * 512], hT_keys(b)))
            for x in range(3):
                ps = nb()
                for kc in range(8):
                    P.mm(ps, Wh[:, kc, x, :], ib[:, kc, :], start=(kc == 0), stop=(kc == 7))
                P.copy(X[x][:, 3 + b * 512:3 + (b + 1) * 512], ps, eng="act")
        for x in range(3):
            Y = X[x]
            ch = x * 4 + hd
            Yc = Ycv
            P.ts(Yc, Y[:, 0:S], cw[:, ch, 0:1], None, ALU.mult)
            for j in range(1, 4):
                P.stt(Yc, Y[:, j:S + j], cw[:, ch, j:j + 1], Yc, ALU.mult, ALU.add)
            P.act(Yc, Yc, AF.Silu)
            if x < 2:
                P.tt(sq, Yc, Yc, ALU.mult, eng="pool")
                dst = QT if x == 0 else KT
                for b in range(8):
                    cs_ = slice(b * 512, (b + 1) * 512)
                    ps = nb()
                    P.mm(ps, C.onesb, sq[:, cs_])
                    rn = rnp.next()
                    P.ts(rn, ps, RMS_EPS, None, ALU.add)
                    P.act(rn, rn, AF.Sqrt)
                    P.recip(rn, rn)
                    if x == 0:
                        P.stt(dst[:, cs_], Yc[:, cs_], 128 ** -0.5, rn, ALU.mult, ALU.mult)
                    else:
                        P.tt(dst[:, cs_], Yc[:, cs_], rn, ALU.mult)
            else:
                P.copy(VT, Yc, eng="act")
        for src, dstm in ((KT, KTM), (VT, VTM)):
            for g8 in range(4):
                for c8 in range(8):
                    c = g8 * 8 + c8
                    P.tr(C.pst[:, c8 * 128:(c8 + 1) * 128], src[:, c * 128:(c + 1) * 128], C.identb)
                P.copy(dstm[:, g8 * 8:(g8 + 1) * 8, :], C.pst.re("p (c t) -> p c t", c=8), eng="act")
        P.memset(Sst, 0.0)
        P.memset(Sb, 0.0)
        if GM == "stageA":
            continue
        for cb in range(32 // NBATCH):
            chunks = list(range(cb * NBATCH, (cb + 1) * NBATCH))
            st = {}
            for c in chunks:
                d = st[c] = {}
                chx = c * 4 + hd
                col = lambda T_: T_[:, chx:chx + 1]
                cs_ = slice(c * 128, (c + 1) * 128)
                gb, bb = p_gb.next(), p_bb.next()
                P.ts(gb, ONESf, col(Gt), None, ALU.mult)
                P.ts(bb, ONESf, col(BETA), None, ALU.mult)
                gcr = nb()[:, 0:128]
                P.mm(gcr, gb, TRI)
                ber = nb()[:, 0:128]
                P.mm(ber, bb, IDF)
                DT, DL, tmp, tmp2 = p_DT.next(), p_DL.next(), p_tmp.next(), p_tmp2.next()
                P.stt(tmp, gcr, col(GC), NEGU, ALU.subtract, ALU.add)
                P.act(DT, tmp, AF.Exp)
                P.stt(tmp2, gcr, col(GC), POSL, ALU.subtract, ALU.add)
                P.act(DL, tmp2, AF.Exp, scale=-1.0)
                EGr = p_EGr.next()
                P.act(EGr, gcr, AF.Exp)
                qdT = d["qdT"] = p_qdT.next()
                P.tt(qdT, QT[:, cs_], EGr, ALU.mult, eng="pool")
                gps = nb()[:, 0:128]
                P.mm(gps, KT[:, cs_], KT[:, cs_])
                qkps = nb()[:, 0:128]
                P.mm(qkps, KT[:, cs_], QT[:, cs_])
                intra = d["intra"] = p_intra.next()
                P.tt(intra, qkps, DT, ALU.mult)
                P.tt(DT, DT, STRU, ALU.mult, eng="pool")
                P.tt(DL, DL, STRL, ALU.mult, eng="pool")
                A, AT = p_A.next(), p_AT.next()
                P.stt(A, gps, col(BETA), DT, ALU.mult, ALU.mult)
                P.tt(tmp, ber, DL, ALU.mult)
                P.tt(AT, gps, tmp, ALU.mult)
                Pm = p_P.next()
                P.tt(Pm, IDF, A, ALU.subtract, eng="pool")
                d["A"], d["AT"], d["P"] = A, AT, Pm
                Keg, Kdec = d["Keg"], d["Kdec"] = p_Keg.next(), p_Kdec.next()
                P.ts(Keg, KTM[:, c, :], col(EG), None, ALU.mult)
                P.ts(Kdec, KTM[:, c, :], col(EDL), None, ALU.mult)
            if GM == "s1":
                continue
            for lev in range(6):
                for c in chunks:
                    d = st[c]
                    psa, psb = nb()[:, 0:128], nb()[:, 0:128]
                    P.mm(psa, d["AT"], d["A"])
                    P.mm(psb, d["A"], d["AT"])
                    A2, AT2 = p_A.next(), p_AT.next()
                    P.copy(A2, psa, eng="act")
                    P.copy(AT2, psb, eng="dve")
                    d["A"], d["AT"] = A2, AT2
                for c in chunks:
                    d = st[c]
                    psc = nb()[:, 0:128]
                    P.mm(psc, d["AT"], d["P"])
                    P2 = p_P.next()
                    P.tt(P2, psc, d["P"], ALU.add)
                    d["P"] = P2
            if GM.startswith("neu"):
                continue
            for c in chunks:
                d = st[c]
                d["Pb"] = p_Pb.next()
                P.copy(d["Pb"], d["P"], eng="act")
            for c in chunks:
                d = st[c]
                chx = c * 4 + hd
                col = lambda T_: T_[:, chx:chx + 1]
                psu, psw = nb()[:, 0:128], nb()[:, 0:128]
                P.mm(psu, d["Pb"], VTM[:, c, :])
                P.mm(psw, d["Keg"], d["Pb"])
                ub = d["ub"] = p_ub.next()
                P.ts(ub, psu, col(BETA), None, ALU.mult)
                WT = d["WT"] = p_WT.next()
                P.copy(WT, psw, eng="act")
            if GM == "uw":
                continue
            for c in chunks:
                d = st[c]
                chx = c * 4 + hd
                col = lambda T_: T_[:, chx:chx + 1]
                cs_ = slice(c * 128, (c + 1) * 128)
                ps1 = nb()[:, 0:128]
                P.mm(ps1, d["WT"], Sb)
                vn = p_vn.next()
                P.stt(vn, ps1, col(NBETA), d["ub"], ALU.mult, ALU.add)
                pso = nb()[:, 0:128]
                P.mm(pso, d["qdT"], Sb, start=True, stop=False)
                P.mm(pso, d["intra"], vn, start=False, stop=True)
                ps3 = nb()[:, 0:128]
                P.mm(ps3, d["Kdec"], vn)
                P.stt(Sb, Sst, col(EGL), ps3, ALU.mult, ALU.add)
                P.stt(Sst, Sst, col(EGL), ps3, ALU.mult, ALU.add)
                hc = hcp.next()
                P.dma(hc, C.hT.v(hT3[:, :, cs_], ("t", c)))
                psz = nb()[:, 0:128]
                for kc in range(8):
                    P.mm(psz, hc[:, kc, :], Wz[:, kc, :], start=(kc == 0), stop=(kc == 7))
                zs = p_zs.next()
                P.act(zs, psz, AF.Silu)
                gst, gmv = p_st.next(), p_mv.next()
                P.add("dve", lambda e, o=gst, i_=pso: e.bn_stats(o.ap, i_.ap), [pso], [gst])
                P.add("dve", lambda e, o=gmv[:, 0:2], i_=gst: e.bn_aggr(o.ap, i_.ap), [gst], [gmv])
                P.stt(gmv[:, 2:3], gmv[:, 0:1], gmv[:, 0:1], gmv[:, 1:2], ALU.mult, ALU.add)
                P.ts(gmv[:, 2:3], gmv[:, 2:3], RMS_EPS, None, ALU.add)
                P.act(gmv[:, 3:4], gmv[:, 2:3], AF.Sqrt)
                P.recip(gmv[:, 4:5], gmv[:, 3:4])
                t_ = p_t.next()
                P.stt(t_, pso, gmv[:, 4:5], NG, ALU.mult, ALU.mult)
                ao = p_ao.next()
                P.tt(ao, t_, zs, ALU.mult, eng="pool")
                P.tr(C.pst[:, 0:128], ao, C.identb)
                aoT = p_aoT.next()
                P.copy(aoT, C.pst[:, 0:128], eng="act")
                P.dma(C.mixT.v(C.mixT.ap[hd * 128:(hd + 1) * 128, cs_], ("t", c)), aoT)
    P.pop()
```

```python
import numpy as np
from contextlib import ExitStack
import concourse.bass as bass
import concourse.mybir as mybir
from concourse.bass_utils import run_bass_kernel_spmd

F32 = mybir.dt.float32
BF16 = mybir.dt.bfloat16
I32 = mybir.dt.int32
AF = mybir.ActivationFunctionType
ALU = mybir.AluOpType
AX = mybir.AxisListType

NDMA = 16
NSDMA = 8


def is_dma(t):
    return t.startswith("dma") or t.startswith("sdma")


class Buf:
    __slots__ = ("name", "w", "r")

    def __init__(self, name=""):
        self.name = name
        self.w = None
        self.r = []


class V:
    def __init__(self, ap, bufs):
        self.ap = ap
        self.bufs = bufs if isinstance(bufs, tuple) else (bufs,)

    def __getitem__(self, k):
        return V(self.ap[k], self.bufs)

    def re(self, s, **kw):
        return V(self.ap.rearrange(s, **kw), self.bufs)

    def bc(self, dt):
        return V(self.ap.bitcast(dt), self.bufs)

    def bcast(self, shape):
        return V(self.ap.to_broadcast(shape), self.bufs)


class Op:
    __slots__ = ("eng", "fn", "deps", "track", "tidx", "inc", "val")


class Prog:
    ENG = ("pe", "act", "dve", "pool", "sp")
    BLK = {"pe": "tensor", "act": "scalar", "dve": "vector", "pool": "gpsimd", "sp": "sync"}

    def __init__(self, nc):
        self.nc = nc
        self.es = ExitStack()
        self.ops = {e: [] for e in self.ENG}
        self.tracks = {e: [] for e in self.ENG}
        for i in range(NDMA):
            self.tracks["dma%d" % i] = []
        for i in range(NSDMA):
            self.tracks["sdma%d" % i] = []
        self.dma_rr = 0
        self.sdma_rr = 0
        self.seen = {e: {} for e in self.ENG}
        self.nalloc = 0
        self.out_ops = []
        self.stack = [self.es]

    def sb(self, shape, dt, name=None):
        self.nalloc += 1
        name = name or ("t%d" % self.nalloc)
        t = self.stack[-1].enter_context(self.nc.sbuf_tensor(name + "_%d" % self.nalloc, list(shape), dt))
        return V(t.ap() if hasattr(t, "ap") and callable(t.ap) else t[:], Buf(name))

    def ps(self, shape, dt=F32, name=None):
        self.nalloc += 1
        name = name or ("p%d" % self.nalloc)
        t = self.stack[-1].enter_context(self.nc.psum_tensor(name + "_%d" % self.nalloc, list(shape), dt))
        return V(t.ap() if hasattr(t, "ap") and callable(t.ap) else t[:], Buf(name))

    def dram(self, name, shape, dt, kind="Internal"):
        t = self.nc.dram_tensor(name, list(shape), dt, kind=kind)
        return DT(t.ap(), name)


    def push(self):
        self.stack.append(ExitStack())

    def pop(self):
        self.barrier()
        self.stack.pop().close()

    def barrier(self):
        last = {t: len(l) - 1 for t, l in self.tracks.items() if l}
        for eng in self.ENG:
            op = Op()
            op.eng = eng
            op.fn = None
            op.inc = False
            op.val = None
            op.track = eng
            op.tidx = len(self.tracks[eng])
            op.deps = {}
            seen = self.seen[eng]
            for t, i in last.items():
                if seen.get(t, -1) < i:
                    seen[t] = i
                    op.deps[t] = i
                    self.tracks[t][i].inc = True
            self.ops[eng].append(op)

    def add(self, eng, fn, reads, writes, dma=False):
        op = Op()
        op.eng = eng
        op.fn = fn
        op.inc = dma
        op.val = None
        if dma and eng == "pool":
            tr = "sdma%d" % self.sdma_rr
            self.sdma_rr = (self.sdma_rr + 1) % NSDMA
        elif dma:
            tr = "dma%d" % self.dma_rr
            self.dma_rr = (self.dma_rr + 1) % NDMA
        else:
            tr = eng
        op.track = tr
        op.tidx = len(self.tracks[tr])
        deps = {}

        def need(t, i):
            if deps.get(t, -1) < i:
                deps[t] = i

        for v in reads:
            for b in v.bufs:
                if b.w is not None:
                    t, i = b.w
                    if t == tr and eng == "pe":
                        continue
                    need(t, i)
        for v in writes:
            for b in v.bufs:
                if b.w is not None:
                    t, i = b.w
                    if not (t == tr and eng == "pe"):
                        need(t, i)
                for (t, i) in b.r:
                    if not (t == tr and eng == "pe"):
                        need(t, i)
        if dma and op.tidx > 0:
            need(tr, op.tidx - 1)
        seen = self.seen[eng]
        op.deps = {}
        for t, i in deps.items():
            if seen.get(t, -1) < i:
                seen[t] = i
                op.deps[t] = i
                self.tracks[t][i].inc = True
        self.tracks[tr].append(op)
        self.ops[eng].append(op)
        for v in reads:
            for b in v.bufs:
                b.r.append((tr, op.tidx))
        for v in writes:
            for b in v.bufs:
                b.w = (tr, op.tidx)
                b.r = []
        return op

    def mm(self, out, lhsT, rhs, start=True, stop=True):
        return self.add("pe", lambda e: e.matmul(out.ap, lhsT.ap, rhs.ap, start=start, stop=stop),
                        [lhsT, rhs] + ([] if start else [out]), [out])

    def tr(self, out, in_, ident):
        return self.add("pe", lambda e: e.transpose(out.ap, in_.ap, ident.ap), [in_, ident], [out])

    def act(self, out, in_, func, bias=None, scale=None, accum=None, eng="act"):
        kw = {}
        rd = [in_]
        wr = [out]
        if bias is not None:
            if isinstance(bias, V):
                kw["bias"] = bias.ap
                rd.append(bias)
            else:
                kw["bias"] = bias
        if scale is not None:
            if isinstance(scale, V):
                kw["scale"] = scale.ap
                rd.append(scale)
            else:
                kw["scale"] = scale
        if accum is not None:
            kw["accum_out"] = accum.ap
            wr.append(accum)
        return self.add("act", lambda e: e.activation(out.ap, in_.ap, func, **kw), rd, wr)

    def tt(self, out, a, b, op, eng="dve"):
        return self.add(eng, lambda e: e.tensor_tensor(out.ap, a.ap, b.ap, op), [a, b], [out])

    def ts(self, out, a, s1, s2, op0, op1=None, accum=None, eng="dve"):
        rd = [a]
        wr = [out]
        s1a = s1.ap if isinstance(s1, V) else s1
        s2a = s2.ap if isinstance(s2, V) else s2
        if isinstance(s1, V):
            rd.append(s1)
        if isinstance(s2, V):
            rd.append(s2)
        kw = {}
        if op1 is not None:
            kw["op1"] = op1
        if accum is not None:
            kw["accum_out"] = accum.ap
            wr.append(accum)
        return self.add(eng, lambda e: e.tensor_scalar(out.ap, a.ap, s1a, s2a, op0, **kw), rd, wr)

    def stt(self, out, a, s, b, op0, op1, eng="dve"):
        rd = [a, b]
        sa = s.ap if isinstance(s, V) else s
        if isinstance(s, V):
            rd.append(s)
        return self.add(eng, lambda e: e.scalar_tensor_tensor(out.ap, a.ap, sa, b.ap, op0, op1), rd, [out])

    def copy(self, out, in_, eng="dve"):
        if eng == "act":
            return self.add("act", lambda e: e.copy(out.ap, in_.ap), [in_], [out])
        return self.add(eng, lambda e: e.tensor_copy(out.ap, in_.ap), [in_], [out])

    def memset(self, out, val, eng="pool"):
        return self.add(eng, lambda e: e.memset(out.ap, val), [], [out])

    def recip(self, out, in_):
        return self.add("dve", lambda e: e.reciprocal(out.ap, in_.ap), [in_], [out])

    def scan(self, out, d0, d1, init, op0, op1):
        rd = [d0, d1]
        ia = init.ap if isinstance(init, V) else init
        if isinstance(init, V):
            rd.append(init)
        return self.add("dve", lambda e: e.tensor_tensor_scan(out.ap, d0.ap, d1.ap, ia, op0, op1), rd, [out])

    def dma(self, out, in_, eng="sp", is_out=False, **kw):
        op = self.add(eng, lambda e: e.dma_start(out=out.ap, in_=in_.ap, **kw), [in_], [out], dma=True)
        if is_out:
            self.out_ops.append(op)
        return op

    def emit(self):
        nc = self.nc
        fin = Op()
        fin.eng = "sp"
        fin.fn = None
        fin.inc = False
        fin.track = "sp"
        fin.tidx = len(self.tracks["sp"])
        fin.deps = {}
        for tname, lst in self.tracks.items():
            if is_dma(tname) and lst:
                fin.deps[tname] = len(lst) - 1
        self.ops["sp"].append(fin)
        for tname, lst in self.tracks.items():
            c = 0
            for op in lst:
                if is_dma(tname):
                    c += 16
                    op.val = c
                else:
                    if op.inc:
                        c += 1
                    op.val = c
        sems = {}
        for tname, lst in self.tracks.items():
            if lst:
                sems[tname] = self.es.enter_context(nc.semaphore("s_" + tname))
        block = self.es.enter_context(nc.Block())
        tracks = self.tracks
        for eng in self.ENG:
            ops = self.ops[eng]
            if not ops:
                continue

            def body(e, ops=ops):
                for op in ops:
                    for t, i in op.deps.items():
                        e.wait_ge(sems[t], tracks[t][i].val)
                    if op.fn is None:
                        continue
                    ins = op.fn(e)
                    if is_dma(op.track):
                        ins.then_inc(sems[op.track], 16)
                    elif op.inc:
                        ins.then_inc(sems[op.track], 1)

            getattr(block, self.BLK[eng])(body)
        self.es.close()


class DT:
    def __init__(self, ap, name):
        self.ap = ap
        self.name = name
        self.bufs = {}

    def v(self, ap, keys=(0,)):
        if not isinstance(keys, (tuple, list)):
            keys = (keys,)
        bs = []
        for k in keys:
            if k not in self.bufs:
                self.bufs[k] = Buf("%s[%s]" % (self.name, k))
            bs.append(self.bufs[k])
        return V(ap, tuple(bs))


class Pool:
    def __init__(self, P, n, shape, dt, name, psum=False):
        self.tiles = [(P.ps if psum else P.sb)(shape, dt, name="%s%d" % (name, i)) for i in range(n)]
        self.i = 0

    def next(self):
        t = self.tiles[self.i]
        self.i = (self.i + 1) % len(self.tiles)
        return t


import math

S = 4096
D = 1024
NT = 32
FF = 2816
ALPHA = 8 ** 0.25
LN_EPS = 1e-5
RMS_EPS = 1e-6
TWO_PI = 2.0 * math.pi


def host_consts():
    c = {}
    i = np.arange(128)
    c["c_ident"] = np.eye(128, dtype=np.float32)
    c["c_tri"] = (i[:, None] <= i[None, :]).astype(np.float32)
    c["c_ones"] = np.ones((128, 128), np.float32)
    c["c_negU"] = np.where(i[None, :] >= i[:, None], 0.0, -30000.0).astype(np.float32)
    c["c_posL"] = np.where(i[None, :] <= i[:, None], 0.0, 30000.0).astype(np.float32)
    c["c_strU"] = (i[None, :] > i[:, None]).astype(np.float32)
    c["c_strL"] = (i[None, :] < i[:, None]).astype(np.float32)
    dm = np.zeros((128, 256), np.float32)
    dm[:, :128] = (i[:, None] <= i[None, :])
    dm[:, 128:] = (i[:, None] >= i[None, :])
    c["c_dmask"] = dm
    c["c_dneg"] = ((dm - 1.0) * 30000.0).astype(np.float32)
    invf = (10000.0 ** (-np.arange(0, 64, 2, dtype=np.float32) / 64)).astype(np.float32)
    r = np.arange(128)
    c["c_invf"] = (invf[r % 32] / np.float32(TWO_PI)).astype(np.float32).reshape(128, 1)
    c["c_sgn"] = np.where((r % 64) < 32, -1.0, 1.0).astype(np.float32).reshape(128, 1)
    c["c_iota"] = np.broadcast_to(np.arange(64, dtype=np.float32)[None, :], (128, 64)).copy()
    sel = np.zeros((128, 64), np.float32)
    sel[64 + np.arange(64), np.arange(64)] = 1.0
    c["c_sel"] = sel
    return c


def host_layer_inputs(inp):
    o = {}
    for l in range(2):
        a_re, a_im, ldt = inp["s5_a_re"][l], inp["s5_a_im"][l], inp["s5_log_dt"][l]
        col = np.stack([a_re, a_im, np.repeat(ldt[:, None], 64, 1)], 0)
        col = col.reshape(3, 32, 2, 64).transpose(2, 3, 0, 1).reshape(128, 3, 32)
        o["s5col%d" % l] = np.ascontiguousarray(col, np.float32)
        o["s5row%d" % l] = np.ascontiguousarray(np.stack([a_re.reshape(-1), a_im.reshape(-1), np.repeat(ldt, 64)], 0), np.float32)
        for nm, src in (("re", inp["s5_b_re"][l]), ("im", inp["s5_b_im"][l])):
            bp = np.zeros((4, 2, 16, 32, 2, 64), np.float32)
            sp = src.reshape(8, 4, 2, 64, 16)
            for q in range(4):
                for gl in range(2):
                    bp[q, gl, :, q::4, gl, :] = sp[:, q, gl].transpose(2, 0, 1)
            o["s5b%s%d" % (nm, l)] = bp.reshape(128, 32 * 128)
        for nm, src in (("re", inp["s5_c_re"][l]), ("im", inp["s5_c_im"][l])):
            cp = np.zeros((2, 64, 32, 4, 2, 16), np.float32)
            sp = src.reshape(8, 4, 2, 16, 64)
            for q in range(4):
                for gl in range(2):
                    cp[gl, :, q::4, q, gl, :] = sp[:, q, gl].transpose(2, 0, 1)
            o["s5c%s%d" % (nm, l)] = cp.reshape(128, 32 * 128)
        o["s5d%d" % l] = np.ascontiguousarray(inp["s5_d"][l].reshape(8, 128).T, np.float32)
        o["dncw%d" % l] = np.ascontiguousarray(inp["dn_conv_w"][l].T.reshape(12, 128, 4).transpose(1, 0, 2), np.float32)
    return o


class Ctx:
    pass


def load_w(P, dst, src_ap, KC, eng="pool"):
    for kc in range(KC):
        P.dma(dst[:, kc, :], src_ap(kc), eng=eng)


def load_h(P, C, tt, h_src):
    ht = C.p_h.next()
    P.dma(ht, h_src.v(h_src.ap[tt * 128:(tt + 1) * 128, :], tt))
    return ht


def ln_epilogue(P, C, lnp, tt, sub_halves, ht):
    z = C.p_z.next()
    st = C.p_st.next()
    for hf in range(2):
        P.stt(z[:, hf * 512:(hf + 1) * 512], ht[:, hf * 512:(hf + 1) * 512], ALPHA, sub_halves[hf], ALU.mult, ALU.add)
        zz = z[:, hf * 512:(hf + 1) * 512]
        P.add("dve", lambda e, o=st[:, hf, :], i=zz: e.bn_stats(o.ap, i.ap), [zz], [st])
    mv = C.p_mv.next()
    P.add("dve", lambda e, o=mv[:, 0:2], i=st.re("p a b -> p (a b)"): e.bn_aggr(o.ap, i.ap), [st], [mv])
    P.ts(mv[:, 2:3], mv[:, 1:2], LN_EPS, None, ALU.add)
    P.act(mv[:, 3:4], mv[:, 2:3], AF.Sqrt)
    P.recip(mv[:, 4:5], mv[:, 3:4])
    zn = C.p_zn.next()
    P.ts(zn, z, mv[:, 0:1], mv[:, 4:5], ALU.subtract, ALU.mult)
    hn = C.p_hn.next()
    P.tt(zn, zn, lnp[0], ALU.mult, eng="pool")
    P.tt(hn, zn, lnp[1], ALU.add, eng="dve")
    P.dma(C.out.v(C.out.ap[tt * 128:(tt + 1) * 128, :], tt), hn, is_out=True)
    hb = C.p_hb.next()
    P.copy(hb, hn, eng="act")
    return (tt, hb)


def ln_epilogue_b(P, C, pend):
    if pend is None:
        return
    tt, hb = pend
    pt = C.pst
    for c in range(8):
        P.tr(pt[:, c * 128:(c + 1) * 128], hb[:, c * 128:(c + 1) * 128], C.identb)
    hts = C.p_hts.next()
    P.copy(hts, pt.re("p (c t) -> p c t", c=8), eng="act")
    P.dma(C.hT.v(C.hT.ap.rearrange("(c p) t -> p c t", p=128)[:, :, tt * 128:(tt + 1) * 128], ("t", tt)), hts)


def ln_pools(P, C):
    C.p_h = Pool(P, 8, [128, 1024], F32, "h")
    C.p_z = Pool(P, 2, [128, 1024], F32, "z")
    C.p_zn = Pool(P, 2, [128, 1024], F32, "zn")
    C.p_hn = Pool(P, 3, [128, 1024], F32, "hn")
    C.p_hb = Pool(P, 5, [128, 1024], BF16, "hb")
    C.p_hts = Pool(P, 2, [128, 8, 128], BF16, "hts")
    C.p_st = Pool(P, 2, [128, 2, 6], F32, "st")
    C.p_mv = Pool(P, 2, [128, 8], F32, "mv")


def load_ln(P, C, g_ap, b_ap):
    G = P.sb([128, 1024], F32, "lng")
    B = P.sb([128, 1024], F32, "lnb")
    P.dma(G, g_ap)
    P.dma(B, b_ap)
    return (G, B)


def hT_keys(b):
    return tuple(("t", t) for t in range(b * 4, b * 4 + 4))


def phase_prep(P, C, I):
    import os
    MODE = os.environ.get("PREP_MODE", "all")
    P.push()
    xs = Pool(P, 2, [128, 1024], F32, "xs")
    xb = Pool(P, 2, [128, 1024], BF16, "xb")
    hts = Pool(P, 2, [128, 8, 128], BF16, "hts")
    for tt in range(NT if MODE in ('all', 'A') else 0):
        x_t = xs.next()
        P.dma(x_t, I["x"].v(I["x"].ap[tt * 128:(tt + 1) * 128, :], tt))
        b_t = xb.next()
        P.copy(b_t, x_t, eng="dve")
        for c in range(8):
            P.tr(C.pst[:, c * 128:(c + 1) * 128], b_t[:, c * 128:(c + 1) * 128], C.identb)
        h_t = hts.next()
        P.copy(h_t, C.pst.re("p (c t) -> p c t", c=8), eng="act")
        P.dma(C.hT.v(C.hT.ap.rearrange("(c p) t -> p c t", p=128)[:, :, tt * 128:(tt + 1) * 128], ("t", tt)), h_t)
    for mt in range(2 if MODE in ('all', 'A2') else 0):
        x_t = xs.next()
        P.dma(x_t, I["mem"].v(I["mem"].ap[mt * 128:(mt + 1) * 128, :], mt))
        b_t = xb.next()
        P.copy(b_t, x_t, eng="dve")
        for c in range(8):
            P.tr(C.pst[:, c * 128:(c + 1) * 128], b_t[:, c * 128:(c + 1) * 128], C.identb)
        h_t = hts.next()
        P.copy(h_t, C.pst.re("p (c t) -> p c t", c=8), eng="act")
        P.dma(C.memT.v(C.memT.ap.rearrange("(c p) t -> p c t", p=128)[:, :, mt * 128:(mt + 1) * 128], 0), h_t)
    invf = P.sb([128, 1], F32, "invf")
    sgn = P.sb([128, 1], F32, "sgn")
    P.dma(invf, I["c_invf"].v(I["c_invf"].ap))
    P.dma(sgn, I["c_sgn"].v(I["c_sgn"].ap))
    W = 1024
    pi_ = P.sb([128, W], I32, "posi")
    pf = P.sb([128, W], F32, "posf")
    tu = P.sb([128, W], F32, "tu")
    f = P.sb([128, W], F32, "f")
    tb = P.sb([128, W], F32, "tb")
    scr = frac_scratch(P, [128, W])
    for cq in range(S // W if MODE in ('all', 'B') else 0):
        P.dma(pi_, I["pos"].v(I["pos"].ap[:, cq * W:(cq + 1) * W].partition_broadcast(128)))
        P.copy(pf, pi_, eng="dve")
        P.ts(tu, pf, invf[:, 0:1], None, ALU.mult)
        for which, shift, dst in (("sin", 0.0, C.sinT), ("cos", 0.25, C.cosT)):
            frac_wrap(P, f, tu, shift, scr)
            P.act(tb, f, AF.Sin, scale=TWO_PI)
            if which == "sin":
                P.ts(tb, tb, sgn[:, 0:1], None, ALU.mult)
            P.dma(dst.v(dst.ap[:, cq * W:(cq + 1) * W], cq), tb)
    P.pop()


def frac_scratch(P, shape):
    return (P.sb(shape, I32, "fw_ki"), P.sb(shape, F32, "fw_kf"), P.sb(shape, F32, "fw_m"))


def frac_wrap(P, f, tu, shift, scr):
    ki, kf, m = scr
    if shift != 0.0:
        P.ts(f, tu, shift, None, ALU.add)
        src = f
    else:
        src = tu
    P.copy(ki, src, eng="dve")
    P.copy(kf, ki, eng="dve")
    P.tt(f, src, kf, ALU.subtract)
    P.ts(m, f, 0.5, None, ALU.is_gt)
    P.tt(f, f, m, ALU.subtract)
    P.ts(m, f, -0.5, None, ALU.is_lt)
    P.tt(f, f, m, ALU.add)


def phase_proj_ln(P, C, inT, K, W_ap, lnp_aps, h_src, gate_ap=None, Wpre=None):
    P.push()
    KC = K // 128
    if Wpre is not None:
        Wb = Wpre
    else:
        Wb = P.sb([128, KC, 1024], BF16, "Wb")
        load_w(P, Wb, W_ap, KC)
    Wg = None
    if gate_ap is not None:
        Wg = P.sb([128, KC, 1024], BF16, "Wg")
        load_w(P, Wg, gate_ap, KC)
    lnp = load_ln(P, C, *lnp_aps)
    ln_pools(P, C)
    blk = Pool(P, 2, [128, KC, 512], BF16, "inblk")
    sig = Pool(P, 2, [128, 1024], F32, "sig") if Wg is not None else None
    pendq = []
    DEPTH = 3 if KC <= 8 else 1

    def prefetch(b):
        ib_ = blk.next()
        P.dma(ib_, inT.v(inT.ap.rearrange("(c p) t -> p c t", p=128)[:, :, b * 512:(b + 1) * 512], hT_keys(b)))
        return ib_, [load_h(P, C, b * 4 + t4, h_src) for t4 in range(4)]

    nxt = prefetch(0)
    for b in range(8):
        ib, hts_ = nxt
        if b < 7:
            nxt = prefetch(b + 1)
        for t4 in range(4):
            tt = b * 4 + t4
            par = (tt % 2) * 2 if Wg is None else 0
            sub = [C.PS[par], C.PS[par + 1]]
            for hf in range(2):
                for kc in range(KC):
                    P.mm(sub[hf], ib[:, kc, t4 * 128:(t4 + 1) * 128], Wb[:, kc, hf * 512:(hf + 1) * 512], start=(kc == 0), stop=(kc == KC - 1))
            if Wg is not None:
                gp = [C.PS[2], C.PS[3]]
                sg = sig.next()
                for hf in range(2):
                    for kc in range(KC):
                        P.mm(gp[hf], ib[:, kc, t4 * 128:(t4 + 1) * 128], Wg[:, kc, hf * 512:(hf + 1) * 512], start=(kc == 0), stop=(kc == KC - 1))
                    P.act(sg[:, hf * 512:(hf + 1) * 512], gp[hf], AF.Sigmoid)
                    P.tt(sg[:, hf * 512:(hf + 1) * 512], sub[hf], sg[:, hf * 512:(hf + 1) * 512], ALU.mult)
                sub = [sg[:, 0:512], sg[:, 512:1024]]
            if len(pendq) >= DEPTH:
                ln_epilogue_b(P, C, pendq.pop(0))
            pendq.append(ln_epilogue(P, C, lnp, tt, sub, hts_[t4]))
    for pe_ in pendq:
        ln_epilogue_b(P, C, pe_)
    P.pop()


def phase_ffn_up(P, C, wg_ap, wu_ap, Wgpre=None, after_loads=None):
    P.push()
    if Wgpre is not None:
        Wg = Wgpre
    else:
        Wg = P.sb([128, 8, FF], BF16, "Wg")
        load_w(P, Wg, wg_ap, 8)
    Wu = P.sb([128, 8, FF], BF16, "Wu")
    load_w(P, Wu, wu_ap, 8)
    if after_loads is not None:
        after_loads()
    blk = Pool(P, 2, [128, 8, 512], BF16, "hblk")
    ab = Pool(P, 2, [128, 22, 512], BF16, "ablk")
    sgp = Pool(P, 2, [128, 512], F32, "sg")
    def prefetch(b):
        ib_ = blk.next()
        P.dma(ib_, C.hT.v(C.hT.ap.rearrange("(c p) t -> p c t", p=128)[:, :, b * 512:(b + 1) * 512], hT_keys(b)))
        return ib_

    nxt = prefetch(0)
    for b in range(8):
        ib = nxt
        if b < 7:
            nxt = prefetch(b + 1)
        a = ab.next()
        for j in range(22):
            pg = C.PS[(j % 2) * 2]
            pu = C.PS[(j % 2) * 2 + 1]
            for kc in range(8):
                P.mm(pg, Wg[:, kc, j * 128:(j + 1) * 128], ib[:, kc, :], start=(kc == 0), stop=(kc == 7))
            for kc in range(8):
                P.mm(pu, Wu[:, kc, j * 128:(j + 1) * 128], ib[:, kc, :], start=(kc == 0), stop=(kc == 7))
            sg = sgp.next()
            P.act(sg, pg, AF.Silu)
            P.tt(a[:, j, :], pu, sg, ALU.mult)
        P.dma(C.aT.v(C.aT.ap.rearrange("(c p) t -> p c t", p=128)[:, :, b * 512:(b + 1) * 512], hT_keys(b)), a)
    P.pop()


def phase_cross(P, C, I, layer, lnp_aps, h_src, after_loads=None):
    P.push()
    wq, wk, wv, wo = (I[n] for n in ("xq_w", "xk_w", "xv_w", "xo_w"))
    rows = lambda w: (lambda kc: w.v(w.ap[layer, kc * 128:(kc + 1) * 128, :]))
    Wq = P.sb([128, 8, 1024], BF16, "Wq")
    Wo = P.sb([128, 8, 1024], BF16, "Wo")
    Wk = P.sb([128, 8, 1024], BF16, "Wk")
    load_w(P, Wk, rows(wk), 8)
    mT = P.sb([128, 8, 256], BF16, "mT")
    P.dma(mT, C.memT.v(C.memT.ap.rearrange("(c p) t -> p c t", p=128), 0))
    kT = P.sb([128, 8, 256], BF16, "kT")
    import os
    if os.environ.get("CROSS_MODE", "all") == "pre0":
        dbg = P.sb([128, 1024], F32, "dbg")
        P.copy(dbg, Wk[:, 3, :])
        P.dma(C.out.v(C.out.ap[0:128, :], 0), dbg, is_out=True)
        P.pop()
        return
    for oc in range(8):
        ps = C.PS[oc % 2]
        for kc in range(8):
            P.mm(ps[:, 0:256], Wk[:, kc, oc * 128:(oc + 1) * 128], mT[:, kc, :], start=(kc == 0), stop=(kc == 7))
        P.copy(kT[:, oc, :], ps[:, 0:256], eng="act")
    load_w(P, Wk, rows(wv), 8)
    Vm = P.sb([128, 2, 1024], BF16, "Vm")
    for mc in range(2):
        for hf in range(2):
            ps = C.PS[hf]
            for kc in range(8):
                P.mm(ps, mT[:, kc, mc * 128:(mc + 1) * 128], Wk[:, kc, hf * 512:(hf + 1) * 512], start=(kc == 0), stop=(kc == 7))
            P.copy(Vm[:, mc, hf * 512:(hf + 1) * 512], ps, eng="act")
    import os
    CM = os.environ.get("CROSS_MODE", "all")
    if CM == "pre":
        P.pop()
        return
    load_w(P, Wq, rows(wq), 8)
    load_w(P, Wo, rows(wo), 8)
    if after_loads is not None:
        after_loads()
    lnp = load_ln(P, C, *lnp_aps)
    ln_pools(P, C)
    onesb = C.onesb
    blk = Pool(P, 2, [128, 8, 512], BF16, "hblk")
    qTp = Pool(P, 1, [128, 8, 512], BF16, "qT")
    pTp = Pool(P, 2, [128, 2, 512], BF16, "pT")
    rdp = Pool(P, 2, [128, 512], F32, "rden")
    oTp = Pool(P, 2, [128, 8, 512], BF16, "oT")
    pendq = []

    def prefetch(b):
        ib_ = blk.next()
        P.dma(ib_, C.hT.v(C.hT.ap.rearrange("(c p) t -> p c t", p=128)[:, :, b * 512:(b + 1) * 512], hT_keys(b)))
        return ib_, [load_h(P, C, b * 4 + t4, h_src) for t4 in range(4)]

    nxt = prefetch(0)
    for b in range(8):
        ib, hts_ = nxt
        if b < 7:
            nxt = prefetch(b + 1)
        qT = qTp.next()
        for oc in range(8):
            ps = C.PS[2 + oc % 2]
            for kc in range(8):
                P.mm(ps, Wq[:, kc, oc * 128:(oc + 1) * 128], ib[:, kc, :], start=(kc == 0), stop=(kc == 7))
            P.copy(qT[:, oc, :], ps, eng="act")
        oT = oTp.next()
        if CM == "q":
            continue
        for hh in range(4):
            pT = pTp.next()
            for mc in range(2):
                ps = C.PS[2 + mc]
                for c in range(2):
                    P.mm(ps, kT[:, hh * 2 + c, mc * 128:(mc + 1) * 128], qT[:, hh * 2 + c, :], start=(c == 0), stop=(c == 1))
                P.act(pT[:, mc, :], ps, AF.Exp, scale=1.0 / 16.0)
            pd = C.PS[4]
            for mc in range(2):
                P.mm(pd, onesb, pT[:, mc, :], start=(mc == 0), stop=(mc == 1))
            rd = rdp.next()
            P.recip(rd, pd)
            for c in range(2):
                po = C.PS[5 + c]
                for mc in range(2):
                    P.mm(po, Vm[:, mc, hh * 256 + c * 128: hh * 256 + (c + 1) * 128], pT[:, mc, :], start=(mc == 0), stop=(mc == 1))
                P.tt(oT[:, hh * 2 + c, :], po, rd, ALU.mult)
        if CM == "attn":
            continue
        for t4 in range(4):
            tt = b * 4 + t4
            sub = [C.PS[0], C.PS[1]]
            for hf in range(2):
                for kc in range(8):
                    P.mm(sub[hf], oT[:, kc, t4 * 128:(t4 + 1) * 128], Wo[:, kc, hf * 512:(hf + 1) * 512], start=(kc == 0), stop=(kc == 7))
            if len(pendq) >= 3:
                ln_epilogue_b(P, C, pendq.pop(0))
            pendq.append(ln_epilogue(P, C, lnp, tt, sub, hts_[t4]))
    for pe_ in pendq:
        ln_epilogue_b(P, C, pe_)
    P.pop()


WEIGHT_SPECS = [
    ("hyb_w_in", [2, 1024, 3592]), ("dn_conv_w", [2, 4, 1536]), ("hyb_w_out", [2, 1024, 1024]),
    ("s5_glu_wo", [2, 1024, 1024]), ("s5_glu_wg", [2, 1024, 1024]),
    ("ln_mix_g", [4, 1024]), ("ln_mix_b", [4, 1024]),
    ("xq_w", [4, 1024, 1024]), ("xk_w", [4, 1024, 1024]), ("xv_w", [4, 1024, 1024]), ("xo_w", [4, 1024, 1024]),
    ("ln_x_g", [4, 1024]), ("ln_x_b", [4, 1024]),
    ("ffn_wg", [4, 1024, 2816]), ("ffn_wu", [4, 1024, 2816]), ("ffn_wd", [4, 2816, 1024]),
    ("ln_ffn_g", [4, 1024]), ("ln_ffn_b", [4, 1024]),
]


def build_program(plan):
    nc = bass.Bass("TRN2", target_bir_lowering=False)
    P = Prog(nc)
    C = Ctx()
    I = {}

    def inp(name, shape, dt=F32):
        I[name] = P.dram(name, shape, dt, kind="ExternalInput")

    inp("x", [S, D])
    inp("mem", [256, D])
    inp("pos", [1, S], I32)
    for n, sh in WEIGHT_SPECS:
        inp(n, sh)
    hc = host_consts()
    for n, a in hc.items():
        inp(n, list(a.shape))
    for l in range(2):
        inp("s5col%d" % l, [128, 3, 32])
        inp("s5row%d" % l, [3, 4096])
        for nm in ("bre", "bim", "cre", "cim"):
            inp("s5%s%d" % (nm, l), [128, 4096])
        inp("s5d%d" % l, [128, 8])
        inp("dnp%d" % l, [1, 4 + 4 + 128])
        inp("dncw%d" % l, [128, 12, 4])
    C.out = P.dram("out", [S, D], F32, kind="ExternalOutput")
    C.hT = P.dram("hT", [D, S], BF16)
    C.mixT = P.dram("mixT", [D, S], BF16)
    C.aT = P.dram("aT", [FF, S], BF16)
    C.memT = P.dram("memT", [D, 256], BF16)
    C.cosT = P.dram("cosT", [128, S], F32)
    C.sinT = P.dram("sinT", [128, S], F32)
    C.vd = P.dram("vd", [S, 512], BF16)
    C.PS = [P.ps([128, 512], F32, "ps%d" % i) for i in range(7)]
    C.pst = P.ps([128, 1024], BF16, "pst")
    idf = P.sb([128, 128], F32, "idf")
    P.dma(idf, I["c_ident"].v(I["c_ident"].ap))
    C.identb = P.sb([128, 128], BF16, "identb")
    P.copy(C.identb, idf)
    C.identf = idf
    C.onesb = P.sb([128, 128], BF16, "onesb")
    P.memset(C.onesb, 1.0)

    bvec = lambda nm, l: I[nm].v(I[nm].ap[l:l + 1, :].partition_broadcast(128))
    h_src = I["x"]
    for ph in plan:
        kind = ph[0]
        if kind == "prep":
            phase_prep(P, C, I)
        elif kind == "cross":
            l = ph[1]
            phase_cross(P, C, I, l, (bvec("ln_x_g", l), bvec("ln_x_b", l)), h_src)
            h_src = C.out
        elif kind == "ffn":
            l = ph[1]
            wg, wu, wd = I["ffn_wg"], I["ffn_wu"], I["ffn_wd"]
            P.push()
            WdPre = P.sb([128, 22, 1024], BF16, "WdPre")
            pre = lambda: load_w(P, WdPre, lambda kc: wd.v(wd.ap[l, kc * 128:(kc + 1) * 128, :]), 22)
            phase_ffn_up(P, C, lambda kc: wg.v(wg.ap[l, kc * 128:(kc + 1) * 128, :]), lambda kc: wu.v(wu.ap[l, kc * 128:(kc + 1) * 128, :]), after_loads=pre)
            phase_proj_ln(P, C, C.aT, FF, None, (bvec("ln_ffn_g", l), bvec("ln_ffn_b", l)), h_src, Wpre=WdPre)
            P.pop()
            h_src = C.out
        elif kind == "s5":
            l = ph[1]
            i = l // 2
            phase_s5(P, C, I, i)
            wo, wg = I["s5_glu_wo"], I["s5_glu_wg"]
            phase_proj_ln(P, C, C.mixT, D, lambda kc: wo.v(wo.ap[i, kc * 128:(kc + 1) * 128, :]),
                          (bvec("ln_mix_g", l), bvec("ln_mix_b", l)), h_src,
                          gate_ap=lambda kc: wg.v(wg.ap[i, kc * 128:(kc + 1) * 128, :]))
            h_src = C.out
        elif kind == "hyb":
            l = ph[1]
            i = l // 2
            phase_dilated(P, C, I, i)
            phase_gdn(P, C, I, i)
            wo = I["hyb_w_out"]
            phase_proj_ln(P, C, C.mixT, D, lambda kc: wo.v(wo.ap[i, kc * 128:(kc + 1) * 128, :]),
                          (bvec("ln_mix_g", l), bvec("ln_mix_b", l)), h_src)
            h_src = C.out
    P.emit()
    return nc


def make_in_maps(inputs, cores):
    hc = host_consts()
    hl = host_layer_inputs(inputs)
    shared = {n: np.ascontiguousarray(inputs[n], np.float32) for n, _ in WEIGHT_SPECS}
    shared.update(hc)
    shared.update(hl)
    for l in range(2):
        shared["dnp%d" % l] = np.concatenate([inputs["dn_a_log"][l], inputs["dn_dt_bias"][l], inputs["dn_norm_g"][l]]).astype(np.float32).reshape(1, 136)
    maps = []
    for b in cores:
        m = dict(shared)
        m["x"] = np.ascontiguousarray(inputs["x"][b], np.float32)
        m["mem"] = np.ascontiguousarray(inputs["mem"][b], np.float32)
        m["pos"] = np.ascontiguousarray(inputs["positions"][b].reshape(1, S), np.int32)
        maps.append(m)
    return maps


FULL_PLAN = [("prep",)]
for _l in range(4):
    FULL_PLAN += [("hyb" if _l % 2 == 0 else "s5", _l), ("cross", _l), ("ffn", _l)]


def kernel(**inputs):
    inputs = {k: np.asarray(v) for k, v in inputs.items()}
    nc = build_program(FULL_PLAN)
    maps = make_in_maps(inputs, list(range(8)))
    res = run_bass_kernel_spmd(nc, maps, core_ids=list(range(8)))
    return np.stack([r["out"] for r in res.results], 0).astype(np.float32)


def b3(v, shape, axis):
    return V(v.ap.unsqueeze(axis).to_broadcast(shape), v.bufs)


def phase_s5(P, C, I, l):
    P.push()
    f32t = lambda shape, n: P.sb(shape, F32, n)
    col = f32t([128, 3, 32], "col")
    P.dma(col, I["s5col%d" % l].v(I["s5col%d" % l].ap))
    dtc = f32t([128, 32], "dtc")
    P.act(dtc, col[:, 2, :], AF.Exp)
    t0 = f32t([128, 32], "t0")
    P.tt(t0, col[:, 0, :], dtc, ALU.mult)
    rho = f32t([128, 32], "rho")
    P.act(rho, t0, AF.Exp)
    phi = f32t([128, 32], "phi")
    P.tt(phi, col[:, 1, :], dtc, ALU.mult)
    P.ts(phi, phi, 1.0 / TWO_PI, None, ALU.mult)
    iota = f32t([128, 64], "iota")
    P.dma(iota, I["c_iota"].v(I["c_iota"].ap))
    dcol = f32t([128, 8], "dcol")
    P.dma(dcol, I["s5d%d" % l].v(I["s5d%d" % l].ap))
    E = {k: f32t([128, 2048], "F" + k) for k in ("lo", "hi")}
    Bb = {k: P.sb([128, 4096], BF16, "Bb" + k) for k in ("re", "im")}
    CreB = P.sb([128, 4096], BF16, "CreB")
    CreN = P.sb([128, 4096], BF16, "CreN")
    CimN = P.sb([128, 4096], BF16, "CimN")
    P.dma(CreB, I["s5cre%d" % l].v(I["s5cre%d" % l].ap), eng="pool")
    P.dma(CimN, I["s5cim%d" % l].v(I["s5cim%d" % l].ap), eng="pool")
    P.ts(CreN, CreB, -1.0, None, ALU.mult)
    P.ts(CimN, CimN, -1.0, None, ALU.mult)
    P.push()
    scr = frac_scratch(P, [128, 2048])
    tu = f32t([128, 2048], "tu")
    tu3 = tu.re("p (a b) -> p a b", b=64)
    iota3 = b3(iota, [128, 32, 64], 1)
    P.tt(tu3, b3(phi, [128, 32, 64], 2), iota3, ALU.mult)
    frac_wrap(P, E["lo"], tu, 0.0, scr)
    p64 = f32t([128, 32], "p64")
    P.ts(p64, phi, 64.0, None, ALU.mult)
    scr2 = frac_scratch(P, [128, 32])
    f64 = f32t([128, 32], "f64")
    frac_wrap(P, f64, p64, 0.0, scr2)
    P.tt(tu3, b3(f64, [128, 32, 64], 2), iota3, ALU.mult)
    frac_wrap(P, E["hi"], tu, 0.0, scr)
    P.pop()
    P.push()
    row = I["s5row%d" % l]
    T = {k: f32t([128, 1024], "r" + k) for k in ("are", "aim", "dt", "er", "tu", "f", "sn", "cs", "abr", "abi", "nr", "den", "cr", "ci", "Bre", "Bim", "x1", "x2")}
    scr = frac_scratch(P, [128, 1024])
    for qq in range(4):
        cs_ = slice(qq * 1024, (qq + 1) * 1024)
        P.dma(T["are"], row.v(row.ap[0:1, cs_].partition_broadcast(128)))
        P.dma(T["aim"], row.v(row.ap[1:2, cs_].partition_broadcast(128)))
        P.dma(T["dt"], row.v(row.ap[2:3, cs_].partition_broadcast(128)))
        P.dma(T["Bre"], I["s5bre%d" % l].v(I["s5bre%d" % l].ap[:, cs_]))
        P.dma(T["Bim"], I["s5bim%d" % l].v(I["s5bim%d" % l].ap[:, cs_]))
        P.act(T["dt"], T["dt"], AF.Exp)
        P.tt(T["er"], T["are"], T["dt"], ALU.mult)
        P.act(T["er"], T["er"], AF.Exp)
        P.tt(T["tu"], T["aim"], T["dt"], ALU.mult)
        P.ts(T["tu"], T["tu"], 1.0 / TWO_PI, None, ALU.mult)
        frac_wrap(P, T["f"], T["tu"], 0.0, scr)
        P.act(T["sn"], T["f"], AF.Sin, scale=TWO_PI)
        frac_wrap(P, T["f"], T["tu"], 0.25, scr)
        P.act(T["cs"], T["f"], AF.Sin, scale=TWO_PI)
        P.tt(T["abr"], T["er"], T["cs"], ALU.mult)
        P.tt(T["abi"], T["er"], T["sn"], ALU.mult)
        P.ts(T["nr"], T["abr"], -1.0, None, ALU.add)
        P.tt(T["den"], T["are"], T["are"], ALU.mult)
        P.tt(T["x1"], T["aim"], T["aim"], ALU.mult)
        P.tt(T["den"], T["den"], T["x1"], ALU.add)
        P.recip(T["den"], T["den"])
        P.tt(T["x1"], T["nr"], T["are"], ALU.mult)
        P.tt(T["x2"], T["abi"], T["aim"], ALU.mult)
        P.tt(T["cr"], T["x1"], T["x2"], ALU.add)
        P.tt(T["cr"], T["cr"], T["den"], ALU.mult)
        P.tt(T["x1"], T["abi"], T["are"], ALU.mult)
        P.tt(T["x2"], T["nr"], T["aim"], ALU.mult)
        P.tt(T["ci"], T["x1"], T["x2"], ALU.subtract)
        P.tt(T["ci"], T["ci"], T["den"], ALU.mult)
        P.tt(T["x1"], T["cr"], T["Bre"], ALU.mult)
        P.tt(T["x2"], T["ci"], T["Bim"], ALU.mult)
        P.tt(Bb["re"][:, cs_], T["x1"], T["x2"], ALU.subtract)
        P.tt(T["x1"], T["cr"], T["Bim"], ALU.mult)
        P.tt(T["x2"], T["ci"], T["Bre"], ALU.mult)
        P.tt(Bb["im"][:, cs_], T["x1"], T["x2"], ALU.add)
    P.pop()
    BW = 512
    NBLK = S // BW
    HC = BW // 64
    ones = f32t([128, BW], "ones")
    P.memset(ones, 1.0)
    halfpi = f32t([128, 1], "halfpi")
    P.memset(halfpi, math.pi / 2)
    onec = f32t([128, 1], "onec")
    P.memset(onec, 1.0)
    stre = [f32t([128, 1], "stre%d" % i) for i in range(32)]
    stim = [f32t([128, 1], "stim%d" % i) for i in range(32)]
    for t_ in stre + stim:
        P.memset(t_, 0.0)
    uTp = Pool(P, 2, [128, S], BF16, "uT")
    RHO = [[f32t([128, BW], "RHO%d_%d" % (par, q)) for q in range(4)] for par in range(2)]
    mkp = lambda n, dt=F32, k=4: Pool(P, k, [128, BW], dt, n)
    p_fs, p_s1, p_sq, p_ei = mkp("fs", F32, 2), mkp("s1", F32, 2), mkp("sq"), mkp("ei")
    p_m = [mkp("m%d" % i, F32, 3) for i in range(4)]
    p_xr, p_xi = mkp("xr", F32, 3), mkp("xi", F32, 3)
    p_pp = [mkp("pp%d" % i, BF16, 3) for i in range(4)]
    hidp = Pool(P, 2, [128, BW], BF16, "hid")
    vtp = Pool(P, 2, [128, BW], F32, "vt")
    allk = tuple(("t", t) for t in range(NT))
    v3 = lambda t: t.re("p (a b) -> p a b", b=64)
    iters = [(cc, blk, q) for cc in range(8) for blk in range(NBLK) for q in range(4)]
    ctx = {}
    uTs = {}
    bub = [0]
    btb = [0]
    IDN = f32t([128, 128], "IDN")
    P.ts(IDN, C.identf, -1.0, None, ALU.mult)

    def prologue(n):
        cc, blk, q = iters[n]
        ctx[n] = {}
        if blk == 0 and q == 0:
            uTs[cc] = uTp.next()
            P.dma(uTs[cc], C.hT.v(C.hT.ap[cc * 128:(cc + 1) * 128, :], allk))
            for q2 in range(4):
                P.ts(RHO[cc % 2][q2], ones, rho[:, 4 * cc + q2:4 * cc + q2 + 1], None, ALU.mult)

    def sA(n):
        cc, blk, q = iters[n]
        pair = 4 * cc + q
        d = ctx[n]
        hs = slice(pair * 64 + blk * HC, pair * 64 + (blk + 1) * HC)
        ls = slice(pair * 64, pair * 64 + 64)
        d["fs"], d["s1"], d["sq"], d["ei"] = p_fs.next(), p_s1.next(), p_sq.next(), p_ei.next()
        P.tt(v3(d["fs"]), b3(E["hi"][:, hs], [128, HC, 64], 2), b3(E["lo"][:, ls], [128, HC, 64], 1), ALU.add, eng="pool")

    def sD(n):
        d = ctx[n]
        fs, s1, sq = d["fs"], d["s1"], d["sq"]
        P.act(s1, fs, AF.Sin, scale=math.pi)
        P.act(fs, fs, AF.Abs)
        P.act(fs, fs, AF.Sin, scale=-math.pi, bias=halfpi[:, 0:1])
        P.act(sq, s1, AF.Square)
        P.act(sq, sq, AF.Identity, scale=-2.0, bias=onec[:, 0:1])

    def sF(n):
        d = ctx[n]
        P.stt(d["ei"], d["s1"], 2.0, d["fs"], ALU.mult, ALU.mult)
        d["Ere"], d["Eim"] = d["sq"], d["ei"]

    def sB(n):
        cc, blk, q = iters[n]
        pair = 4 * cc + q
        d = ctx[n]
        tk = slice(blk * BW, (blk + 1) * BW)
        pc = slice(pair * 128, (pair + 1) * 128)
        d["bre"], d["bim"] = C.PS[0], C.PS[1]
        P.mm(d["bre"], Bb["re"][:, pc], uTs[cc][:, tk])
        P.mm(d["bim"], Bb["im"][:, pc], uTs[cc][:, tk])

    def sE(n):
        d = ctx[n]
        m = [p.next() for p in p_m]
        bre, bim = d["bre"], d["bim"]
        P.tt(m[0], bre, d["Ere"], ALU.mult)
        P.tt(m[1], bim, d["Eim"], ALU.mult)
        P.tt(m[2], bim, d["Ere"], ALU.mult)
        P.tt(m[3], bre, d["Eim"], ALU.mult)
        btr, bti = C.PS[2 + btb[0]], C.PS[3 + btb[0]]
        btb[0] = (btb[0] + 2) % 4
        P.mm(btr, C.identf, m[0], start=True, stop=False)
        P.mm(btr, C.identf, m[1], start=False, stop=True)
        P.mm(bti, C.identf, m[2], start=True, stop=False)
        P.mm(bti, IDN, m[3], start=False, stop=True)
        d["m"] = [btr, None, bti, None]

    def sC(n):
        cc, blk, q = iters[n]
        pair = 4 * cc + q
        d = ctx[n]
        m = d["m"]
        xr, xi = p_xr.next(), p_xi.next()
        P.scan(xr, RHO[cc % 2][q], m[0], stre[pair], ALU.mult, ALU.add)
        P.scan(xi, RHO[cc % 2][q], m[2], stim[pair], ALU.mult, ALU.add)
        P.copy(stre[pair], xr[:, BW - 1:BW], eng="act")
        P.copy(stim[pair], xi[:, BW - 1:BW], eng="act")
        pp = d["pp"] = [p.next() for p in p_pp]
        P.tt(pp[0], d["Ere"], xr, ALU.mult)
        P.tt(pp[1], d["Eim"], xi, ALU.mult, eng="pool")
        P.tt(pp[2], d["Ere"], xi, ALU.mult, eng="pool")
        P.tt(pp[3], d["Eim"], xr, ALU.mult, eng="pool")

    def sG(n):
        cc, blk, q = iters[n]
        pair = 4 * cc + q
        d = ctx.pop(n)
        uT = uTs[cc]
        tk = slice(blk * BW, (blk + 1) * BW)
        pc = slice(pair * 128, (pair + 1) * 128)
        yps = C.PS[6]
        pp = d["pp"]
        for k, (W_, p_) in enumerate(((CreB, pp[0]), (CreN, pp[1]), (CimN, pp[2]), (CimN, pp[3]))):
            P.mm(yps, W_[:, pc], p_, start=(q == 0 and k == 0), stop=(q == 3 and k == 3))
        if q == 3:
            vt = vtp.next()
            P.stt(vt, uT[:, tk], dcol[:, cc:cc + 1], yps, ALU.mult, ALU.add)
            hid = hidp.next()
            P.act(hid, vt, AF.Gelu)
            P.dma(C.mixT.v(C.mixT.ap[cc * 128:(cc + 1) * 128, tk], hT_keys(blk)), hid)

    N_ = len(iters)
    ok = lambda k: 0 <= k < N_
    for n in range(N_ + 2):
        if ok(n):
            prologue(n)
            sA(n)
        if ok(n - 1):
            sB(n - 1)
        if ok(n - 2):
            sC(n - 2)
        if ok(n):
            sD(n)
        if ok(n - 1):
            sE(n - 1)
        if ok(n):
            sF(n)
        if ok(n - 2):
            sG(n - 2)
    P.pop()


def _zero_mix_rows(P, C, r0, r1):
    P.push()
    z = P.sb([128, 1024], BF16, "zmix")
    P.memset(z, 0.0)
    for c in range(r0 // 128, r1 // 128):
        for blk in range(4):
            P.dma(C.mixT.v(C.mixT.ap[c * 128:(c + 1) * 128, blk * 1024:(blk + 1) * 1024],
                           tuple(("t", t) for t in range(blk * 8, blk * 8 + 8))), z)
    P.pop()


def phase_dilated(P, C, I, i):
    w_in = I["hyb_w_in"]
    w3 = w_in.ap[i].rearrange("(kc p) n -> p kc n", p=128)
    hT3 = C.hT.ap.rearrange("(c p) t -> p c t", p=128)
    QOFF, KOFF, VOFF = 2056, 2568, 3080
    P.push()
    Wv = P.sb([128, 8, 512], BF16, "Wv")
    P.dma(Wv, w_in.v(w3[:, :, VOFF:VOFF + 512]), eng="pool")
    blkp = Pool(P, 2, [128, 8, 512], BF16, "hblk")
    vtp = Pool(P, 2, [128, 512], BF16, "vt")
    for b in range(8):
        ib = blkp.next()
        P.dma(ib, C.hT.v(hT3[:, :, b * 512:(b + 1) * 512], hT_keys(b)))
        for t4 in range(4):
            tt = b * 4 + t4
            ps = C.PS[t4 % 2]
            for kc in range(8):
                P.mm(ps, ib[:, kc, t4 * 128:(t4 + 1) * 128], Wv[:, kc, :], start=(kc == 0), stop=(kc == 7))
            vt = vtp.next()
            P.copy(vt, ps, eng="act")
            P.dma(C.vd.v(C.vd.ap[tt * 128:(tt + 1) * 128, :], 0), vt)
    P.pop()
    import os
    DM = os.environ.get("DIL_MODE", "all")
    if DM == "v":
        return
    P.push()
    COS = P.sb([128, S], F32, "COS")
    SIN = P.sb([128, S], F32, "SIN")
    P.dma(COS, C.cosT.v(C.cosT.ap, (0, 1, 2, 3)))
    P.dma(SIN, C.sinT.v(C.sinT.ap, (0, 1, 2, 3)))
    dmask = P.sb([128, 256], F32, "dmask")
    P.dma(dmask, I["c_dmask"].v(I["c_dmask"].ap))
    sel = P.sb([128, 64], F32, "sel")
    P.dma(sel, I["c_sel"].v(I["c_sel"].ap))
    Wt = {k: P.sb([128, 8, 128], BF16, "W" + k) for k in ("q", "qs", "k", "ks")}
    blkp = Pool(P, 2, [128, 8, 512], BF16, "hblk")
    RT = {k: P.sb([128, S], BF16, "R" + k) for k in ("q", "k")}
    CM = {(k, r): P.sb([128, S], BF16, "C%s%d" % (k, r)) for k in ("q", "k") for r in (4, 16)}
    OT = [P.sb([128, S], F32, "OT%d" % h) for h in range(2)]
    t1p = Pool(P, 2, [128, 512], F32, "t1")
    t2p = Pool(P, 2, [128, 512], F32, "t2")
    vap = Pool(P, 4, [128, 8, 128], BF16, "vaug")
    for va in vap.tiles:
        P.memset(va, 1.0)
    pep = Pool(P, 3, [128, 256], BF16, "pe")
    ptp = Pool(P, 6, [128, 256], BF16, "pt")
    dnegf = P.sb([128, 256], F32, "dnegf")
    P.dma(dnegf, I["c_dneg"].v(I["c_dneg"].ap))
    dnegb = P.sb([128, 256], BF16, "dnegb")
    P.copy(dnegb, dnegf)
    sbank = [0]
    rdp = Pool(P, 2, [64, 512], F32, "rd")
    onp = Pool(P, 2, [64, 512], BF16, "on")
    slots = [[V(C.PS[3 + 2 * h + sidx].ap[:, 0:128], C.PS[3 + 2 * h + sidx].bufs) for sidx in range(2)] for h in range(2)]
    for hp in range(4):
        for k, off in (("q", QOFF), ("k", KOFF)):
            base = off + hp * 128
            P.dma(Wt[k], w_in.v(w3[:, :, base:base + 128]), eng="pool")
            for hh in range(2):
                P.dma(Wt[k + "s"][:, :, hh * 64:hh * 64 + 32], w_in.v(w3[:, :, base + hh * 64 + 32:base + hh * 64 + 64]), eng="pool")
                P.dma(Wt[k + "s"][:, :, hh * 64 + 32:hh * 64 + 64], w_in.v(w3[:, :, base + hh * 64:base + hh * 64 + 32]), eng="pool")
        for b in range(8):
            ib = blkp.next()
            P.dma(ib, C.hT.v(hT3[:, :, b * 512:(b + 1) * 512], hT_keys(b)))
            cs_ = slice(b * 512, (b + 1) * 512)
            for k in ("q", "k"):
                for kc in range(8):
                    P.mm(C.PS[0], Wt[k][:, kc, :], ib[:, kc, :], start=(kc == 0), stop=(kc == 7))
                for kc in range(8):
                    P.mm(C.PS[1], Wt[k + "s"][:, kc, :], ib[:, kc, :], start=(kc == 0), stop=(kc == 7))
                t1 = t1p.next()
                t2 = t2p.next()
                P.tt(t1, C.PS[0], COS[:, cs_], ALU.mult)
                P.tt(t2, C.PS[1], SIN[:, cs_], ALU.mult)
                P.tt(RT[k][:, cs_], t1, t2, ALU.add, eng="pool")
        if DM == "proj":
            continue
        for k in ("q", "k"):
            for r in (4, 16):
                P.copy(CM[(k, r)].re("p (r u) -> p r u", r=r), RT[k].re("p (u r) -> p r u", r=r), eng="pool")
        if DM == "cm":
            continue
        its = []
        for bi, r in enumerate((1, 4, 16)):
            L = S // r
            nb_ = L // 128
            for rho in range(r):
                for m in range(nb_):
                    for hh in range(2):
                        its.append((bi, r, rho, m, hh, nb_, L))
        ictx = {}

        def st_S(n):
            bi, r, rho, m, hh, nb_, L = its[n]
            d = ictx[n] = {}
            Qs = RT["q"] if r == 1 else CM[("q", r)]
            Ks = RT["k"] if r == 1 else CM[("k", r)]
            if hh == 0:
                va = vap.next()
                st = (128 * m) * r + rho
                P.dma(va[:, :, 0:64], C.vd.v(C.vd.ap[st:st + 127 * r + 1:r, :].rearrange("t (h d) -> t h d", h=8), 0))
                ictx["va"] = va
            d["va"] = ictx["va"]
            nq = d["nq"] = 256 if m < nb_ - 1 else 128
            c0 = rho * L + m * 128
            hb = hh * 64
            sps = d["sps"] = C.PS[sbank[0]]
            sbank[0] = (sbank[0] + 1) % 3
            P.mm(sps[:, 0:nq], Ks[hb:hb + 64, c0:c0 + 128], Qs[hb:hb + 64, c0:c0 + nq], start=True, stop=False)
            P.mm(sps[:, 0:nq], C.identb, dnegb[:, 0:nq], start=False, stop=True)

        def st_X(n):
            d = ictx[n]
            nq = d["nq"]
            pt = d["pt"] = ptp.next()
            P.act(pt[:, 0:nq], d["sps"][:, 0:nq], AF.Exp, scale=0.125)

        def st_V(n):
            bi, r, rho, m, hh, nb_, L = its[n]
            d = ictx.pop(n)
            pt, va, nq = d["pt"], d["va"], d["nq"]
            head = hp * 2 + hh
            sl = slots[hh][m % 2]
            P.mm(sl, va[:, head, :], pt[:, 0:128], start=(m == 0), stop=True)
            dst = OT[hh].re("p (u r) -> p r u", r=r)[:, rho, m * 128:(m + 1) * 128] if r > 1 else OT[hh][:, m * 128:(m + 1) * 128]
            if bi == 0:
                P.copy(dst, sl, eng="dve")
            else:
                P.tt(dst, sl, dst, ALU.add)
            if nq == 256:
                sl2 = slots[hh][(m + 1) % 2]
                P.mm(sl2, va[:, head, :], pt[:, 128:256], start=True, stop=False)

        NI = len(its)
        for n in range(NI + 2):
            if n < NI:
                st_S(n)
            if 0 <= n - 1 < NI:
                st_X(n - 1)
            if 0 <= n - 2 < NI:
                st_V(n - 2)
        if DM in ("br0a", "br0b", "br0b1", "br0b2"):
            continue
        for hh in range(2):
            head = hp * 2 + hh
            for b in range(8):
                cs_ = slice(b * 512, (b + 1) * 512)
                pd = C.PS[hh]
                P.mm(pd[0:64, :], sel, OT[hh][:, cs_])
                rd = rdp.next()
                P.recip(rd, pd[0:64, :])
                on = onp.next()
                P.tt(on, OT[hh][0:64, cs_], rd, ALU.mult)
                r0 = 512 + head * 64
                P.dma(C.mixT.v(C.mixT.ap[r0:r0 + 64, cs_], hT_keys(b)), on)
    P.pop()


def phase_gdn(P, C, I, i):
    w_in = I["hyb_w_in"]
    w3 = w_in.ap[i].rearrange("(kc p) n -> p kc n", p=128)
    hT3 = C.hT.ap.rearrange("(c p) t -> p c t", p=128)
    dnp = I["dnp%d" % i]
    P.push()
    f32t = lambda shape, n: P.sb(shape, F32, n)
    cst = {}
    for nm in ("c_tri", "c_ones", "c_negU", "c_posL", "c_strU", "c_strL"):
        cst[nm] = f32t([128, 128], nm)
        P.dma(cst[nm], I[nm].v(I[nm].ap))
    TRI, ONESf, NEGU, POSL, STRU, STRL = (cst[n] for n in ("c_tri", "c_ones", "c_negU", "c_posL", "c_strU", "c_strL"))
    IDF = C.identf
    onecol = f32t([128, 1], "onecol")
    P.memset(onecol, 1.0)
    NG = f32t([128, 128], "NG")
    P.dma(NG, dnp.v(dnp.ap[0:1, 8:136].partition_broadcast(128)))
    ab = f32t([128, 8], "ab")
    P.dma(ab, dnp.v(dnp.ap[0:1, 0:8].partition_broadcast(128)))
    nA = f32t([128, 4], "nA")
    P.act(nA, ab[:, 0:4], AF.Exp)
    P.ts(nA, nA, -1.0, None, ALU.mult)
    cw = f32t([128, 12, 4], "cw")
    P.dma(cw, I["dncw%d" % i].v(I["dncw%d" % i].ap))
    bank = [0]

    def nb():
        bank[0] = (bank[0] + 1) % 7
        return C.PS[bank[0]]

    Wg8 = P.sb([128, 8, 8], BF16, "Wg8")
    P.dma(Wg8, w_in.v(w3[:, :, 2048:2056]), eng="pool")
    blkp = Pool(P, 2, [128, 8, 512], BF16, "hblk")
    GR = f32t([128, 32, 8], "GR")
    for b in range(8):
        ib = blkp.next()
        P.dma(ib, C.hT.v(hT3[:, :, b * 512:(b + 1) * 512], hT_keys(b)))
        for t4 in range(4):
            ps = nb()
            for kc in range(8):
                P.mm(ps[:, 0:8], ib[:, kc, t4 * 128:(t4 + 1) * 128], Wg8[:, kc, :], start=(kc == 0), stop=(kc == 7))
            P.copy(GR[:, b * 4 + t4, :], ps[:, 0:8], eng="act")
    BETA = f32t([128, 128], "BETA")
    NBETA = f32t([128, 128], "NBETA")
    Gt = f32t([128, 128], "Gt")
    v34 = lambda t: t.re("p (c h) -> p c h", h=4)
    P.copy(v34(BETA), GR[:, :, 0:4], eng="dve")
    P.act(BETA, BETA, AF.Sigmoid)
    P.ts(NBETA, BETA, -1.0, None, ALU.mult)
    P.tt(v34(Gt), GR[:, :, 4:8], b3(ab[:, 4:8], [128, 32, 4], 1), ALU.add)
    P.act(Gt, Gt, AF.Exp)
    P.act(Gt, Gt, AF.Ln, bias=onecol[:, 0:1])
    P.tt(v34(Gt), v34(Gt), b3(nA, [128, 32, 4], 1), ALU.mult)
    GC, GL, EG, EDL, EGL = (f32t([128, 128], n) for n in ("GC", "GL", "EG", "EDL", "EGL"))
    ps = nb()
    P.mm(ps[:, 0:128], TRI, Gt)
    P.copy(GC, ps[:, 0:128], eng="dve")
    ps = nb()
    P.mm(ps[:, 0:128], ONESf, Gt)
    P.copy(GL, ps[:, 0:128], eng="dve")
    P.act(EG, GC, AF.Exp)
    P.act(EGL, GL, AF.Exp)
    P.tt(EDL, GL, GC, ALU.subtract)
    P.act(EDL, EDL, AF.Exp)
    import os
    GM = os.environ.get("GDN_MODE", "all")
    if GM == "gates":
        P.pop()
        return
    Wh = P.sb([128, 8, 3, 128], BF16, "Wh")
    Wz = P.sb([128, 8, 128], BF16, "Wz")
    X = [f32t([128, S + 3], "X%d" % x) for x in range(3)]
    for x in range(3):
        P.memset(X[x][:, 0:3], 0.0)
    sq = P.sb([128, S], BF16, "sq")
    Ycs = [f32t([128, S], "Ycv%d" % i_) for i_ in range(2)]
    QT = P.sb([128, S], BF16, "QT")
    KT = P.sb([128, S], BF16, "KT")
    VT = sq
    KTM = P.sb([128, 32, 128], BF16, "KTM")
    VTM = P.sb([128, 32, 128], BF16, "VTM")
    rnp = Pool(P, 2, [128, 512], F32, "rn")
    Sst = f32t([128, 128], "Sst")
    Sb = P.sb([128, 128], BF16, "Sb")
    NBATCH = 4
    mk = lambda n, dt=F32, k=NBATCH: Pool(P, k, [128, 128], dt, n)
    p_gb, p_bb, p_DT, p_DL, p_tmp, p_tmp2 = mk("gb"), mk("bb"), mk("DT"), mk("DL"), mk("tmpa"), mk("tmpb")
    p_A, p_AT, p_P = mk("A", F32, 2 * NBATCH), mk("AT", F32, 2 * NBATCH), mk("P", F32, 2 * NBATCH)
    p_intra, p_Pb, p_ub, p_Keg, p_WT, p_Kdec, p_qdT, p_EGr = (mk("intra", BF16), mk("Pb", BF16), mk("ub"), mk("Keg", BF16),
                                                              mk("WT", BF16), mk("Kdec", BF16), mk("qdT", BF16), mk("EGr"))
    p_vn, p_zs, p_t, p_ao, p_aoT = mk("vn", BF16, 2), mk("zs", F32, 2), mk("tt", F32, 2), mk("ao", BF16, 2), mk("aoT", BF16, 2)
    p_st = Pool(P, 2, [128, 6], F32, "gst")
    p_mv = Pool(P, 2, [128, 8], F32, "gmv")
    hcp = Pool(P, 2, [128, 8, 128], BF16, "hchunk")
    for hd in range(4):
        for x in range(3):
            P.dma(Wh[:, :, x, :], w_in.v(w3[:, :, x * 512 + hd * 128: x * 512 + (hd + 1) * 128]), eng="pool")
        P.dma(Wz, w_in.v(w3[:, :, 1536 + hd * 128:1536 + (hd + 1) * 128]), eng="pool")
        for b in range(8):
            ib = blkp.next()
            P.dma(ib, C.hT.v(hT3[:, :, b * 512:(b + 1) * 512], hT_keys(b)))
            for x in range(3):
                ps = nb()
                for kc in range(8):
                    P.mm(ps, Wh[:, kc, x, :], ib[:, kc, :], start=(kc == 0), stop=(kc == 7))
                P.copy(X[x][:, 3 + b * 512:3 + (b + 1) * 512], ps, eng="act")
        def convx(x):
            Y = X[x]
            ch = x * 4 + hd
            Yc = Ycs[x % 2]
            P.ts(Yc, Y[:, 0:S], cw[:, ch, 0:1], None, ALU.mult)
            for j in range(1, 4):
                P.stt(Yc, Y[:, j:S + j], cw[:, ch, j:j + 1], Yc, ALU.mult, ALU.add)
            P.act(Yc, Yc, AF.Silu)
            return Yc

        def postx(x, Yc):
            if x < 2:
                P.tt(sq, Yc, Yc, ALU.mult, eng="pool")
                dst = QT if x == 0 else KT
                for b in range(8):
                    cs_ = slice(b * 512, (b + 1) * 512)
                    ps = nb()
                    P.mm(ps, C.onesb, sq[:, cs_])
                    rn = rnp.next()
                    P.ts(rn, ps, RMS_EPS, None, ALU.add)
                    P.act(rn, rn, AF.Sqrt)
                    P.recip(rn, rn)
                    if x == 0:
                        P.stt(dst[:, cs_], Yc[:, cs_], 128 ** -0.5, rn, ALU.mult, ALU.mult)
                    else:
                        P.tt(dst[:, cs_], Yc[:, cs_], rn, ALU.mult)
            else:
                P.copy(VT, Yc, eng="act")

        y0 = convx(0)
        y1 = convx(1)
        postx(0, y0)
        y2 = convx(2)
        postx(1, y1)
        postx(2, y2)
        for src, dstm in ((KT, KTM), (VT, VTM)):
            for g8 in range(4):
                for c8 in range(8):
                    c = g8 * 8 + c8
                    P.tr(C.pst[:, c8 * 128:(c8 + 1) * 128], src[:, c * 128:(c + 1) * 128], C.identb)
                P.copy(dstm[:, g8 * 8:(g8 + 1) * 8, :], C.pst.re("p (c t) -> p c t", c=8), eng="act")
        P.memset(Sst, 0.0)
        P.memset(Sb, 0.0)
        if GM == "stageA":
            continue
        for cb in range(32 // NBATCH):
            chunks = list(range(cb * NBATCH, (cb + 1) * NBATCH))
            st = {}
            for c in chunks:
                d = st[c] = {}
                chx = c * 4 + hd
                col = lambda T_: T_[:, chx:chx + 1]
                cs_ = slice(c * 128, (c + 1) * 128)
                gb, bb = p_gb.next(), p_bb.next()
                P.ts(gb, ONESf, col(Gt), None, ALU.mult)
                P.ts(bb, ONESf, col(BETA), None, ALU.mult)
                gcr = nb()[:, 0:128]
                P.mm(gcr, gb, TRI)
                ber = nb()[:, 0:128]
                P.mm(ber, bb, IDF)
                DT, DL, tmp, tmp2 = p_DT.next(), p_DL.next(), p_tmp.next(), p_tmp2.next()
                P.stt(tmp, gcr, col(GC), NEGU, ALU.subtract, ALU.add)
                P.act(DT, tmp, AF.Exp)
                P.stt(tmp2, gcr, col(GC), POSL, ALU.subtract, ALU.add)
                P.act(DL, tmp2, AF.Exp, scale=-1.0)
                EGr = p_EGr.next()
                P.act(EGr, gcr, AF.Exp)
                qdT = d["qdT"] = p_qdT.next()
                P.tt(qdT, QT[:, cs_], EGr, ALU.mult, eng="pool")
                gps = nb()[:, 0:128]
                P.mm(gps, KT[:, cs_], KT[:, cs_])
                qkps = nb()[:, 0:128]
                P.mm(qkps, KT[:, cs_], QT[:, cs_])
                intra = d["intra"] = p_intra.next()
                P.tt(intra, qkps, DT, ALU.mult)
                P.tt(DT, DT, STRU, ALU.mult, eng="pool")
                P.tt(DL, DL, STRL, ALU.mult, eng="pool")
                A, AT = p_A.next(), p_AT.next()
                P.stt(A, gps, col(BETA), DT, ALU.mult, ALU.mult)
                P.tt(tmp, ber, DL, ALU.mult)
                P.tt(AT, gps, tmp, ALU.mult)
                Pm = p_P.next()
                P.tt(Pm, IDF, A, ALU.subtract, eng="pool")
                d["A"], d["AT"], d["P"] = A, AT, Pm
                Keg, Kdec = d["Keg"], d["Kdec"] = p_Keg.next(), p_Kdec.next()
                P.ts(Keg, KTM[:, c, :], col(EG), None, ALU.mult)
                P.ts(Kdec, KTM[:, c, :], col(EDL), None, ALU.mult)
            if GM == "s1":
                continue
            for lev in range(6):
                for c in chunks:
                    d = st[c]
                    psa, psb = nb()[:, 0:128], nb()[:, 0:128]
                    P.mm(psa, d["AT"], d["A"])
                    P.mm(psb, d["A"], d["AT"])
                    A2, AT2 = p_A.next(), p_AT.next()
                    P.copy(A2, psa, eng="act")
                    P.copy(AT2, psb, eng="dve")
                    d["A"], d["AT"] = A2, AT2
                for c in chunks:
                    d = st[c]
                    psc = nb()[:, 0:128]
                    P.mm(psc, d["AT"], d["P"])
                    P2 = p_P.next()
                    P.tt(P2, psc, d["P"], ALU.add)
                    d["P"] = P2
            if GM.startswith("neu"):
                continue
            for c in chunks:
                d = st[c]
                d["Pb"] = p_Pb.next()
                P.copy(d["Pb"], d["P"], eng="act")
            for c in chunks:
                d = st[c]
                chx = c * 4 + hd
                col = lambda T_: T_[:, chx:chx + 1]
                psu, psw = nb()[:, 0:128], nb()[:, 0:128]
                P.mm(psu, d["Pb"], VTM[:, c, :])
                P.mm(psw, d["Keg"], d["Pb"])
                ub = d["ub"] = p_ub.next()
                P.ts(ub, psu, col(BETA), None, ALU.mult)
                WT = d["WT"] = p_WT.next()
                P.copy(WT, psw, eng="act")
            if GM == "uw":
                continue
            for c in chunks:
                d = st[c]
                chx = c * 4 + hd
                col = lambda T_: T_[:, chx:chx + 1]
                cs_ = slice(c * 128, (c + 1) * 128)
                ps1 = nb()[:, 0:128]
                P.mm(ps1, d["WT"], Sb)
                vn = p_vn.next()
                P.stt(vn, ps1, col(NBETA), d["ub"], ALU.mult, ALU.add)
                pso = nb()[:, 0:128]
                P.mm(pso, d["qdT"], Sb, start=True, stop=False)
                P.mm(pso, d["intra"], vn, start=False, stop=True)
                ps3 = nb()[:, 0:128]
                P.mm(ps3, d["Kdec"], vn)
                P.stt(Sb, Sst, col(EGL), ps3, ALU.mult, ALU.add)
                P.stt(Sst, Sst, col(EGL), ps3, ALU.mult, ALU.add)
                hc = hcp.next()
                P.dma(hc, C.hT.v(hT3[:, :, cs_], ("t", c)))
                psz = nb()[:, 0:128]
                for kc in range(8):
                    P.mm(psz, hc[:, kc, :], Wz[:, kc, :], start=(kc == 0), stop=(kc == 7))
                zs = p_zs.next()
                P.act(zs, psz, AF.Silu)
                gst, gmv = p_st.next(), p_mv.next()
                P.add("dve", lambda e, o=gst, i_=pso: e.bn_stats(o.ap, i_.ap), [pso], [gst])
                P.add("dve", lambda e, o=gmv[:, 0:2], i_=gst: e.bn_aggr(o.ap, i_.ap), [gst], [gmv])
                P.stt(gmv[:, 2:3], gmv[:, 0:1], gmv[:, 0:1], gmv[:, 1:2], ALU.mult, ALU.add)
                P.ts(gmv[:, 2:3], gmv[:, 2:3], RMS_EPS, None, ALU.add)
                P.act(gmv[:, 3:4], gmv[:, 2:3], AF.Sqrt)
                P.recip(gmv[:, 4:5], gmv[:, 3:4])
                t_ = p_t.next()
                P.stt(t_, pso, gmv[:, 4:5], NG, ALU.mult, ALU.mult)
                ao = p_ao.next()
                P.tt(ao, t_, zs, ALU.mult, eng="pool")
                P.tr(C.pst[:, 0:128], ao, C.identb)
                aoT = p_aoT.next()
                P.copy(aoT, C.pst[:, 0:128], eng="act")
                P.dma(C.mixT.v(C.mixT.ap[hd * 128:(hd + 1) * 128, cs_], ("t", c)), aoT)
    P.pop()
```

```python
import numpy as np
from contextlib import ExitStack
import concourse.bass as bass
import concourse.mybir as mybir
from concourse.bass_utils import run_bass_kernel_spmd

F32 = mybir.dt.float32
BF16 = mybir.dt.bfloat16
I32 = mybir.dt.int32
AF = mybir.ActivationFunctionType
ALU = mybir.AluOpType
AX = mybir.AxisListType

NDMA = 16
NSDMA = 8


def is_dma(t):
    return t.startswith("dma") or t.startswith("sdma")


class Buf:
    __slots__ = ("name", "w", "r")

    def __init__(self, name=""):
        self.name = name
        self.w = None
        self.r = []


class V:
    def __init__(self, ap, bufs):
        self.ap = ap
        self.bufs = bufs if isinstance(bufs, tuple) else (bufs,)

    def __getitem__(self, k):
        return V(self.ap[k], self.bufs)

    def re(self, s, **kw):
        return V(self.ap.rearrange(s, **kw), self.bufs)

    def bc(self, dt):
        return V(self.ap.bitcast(dt), self.bufs)

    def bcast(self, shape):
        return V(self.ap.to_broadcast(shape), self.bufs)


class Op:
    __slots__ = ("eng", "fn", "deps", "track", "tidx", "inc", "val")


class Prog:
    ENG = ("pe", "act", "dve", "pool", "sp")
    BLK = {"pe": "tensor", "act": "scalar", "dve": "vector", "pool": "gpsimd", "sp": "sync"}

    def __init__(self, nc):
        self.nc = nc
        self.es = ExitStack()
        self.ops = {e: [] for e in self.ENG}
        self.tracks = {e: [] for e in self.ENG}
        for i in range(NDMA):
            self.tracks["dma%d" % i] = []
        for i in range(NSDMA):
            self.tracks["sdma%d" % i] = []
        self.dma_rr = 0
        self.sdma_rr = 0
        self.seen = {e: {} for e in self.ENG}
        self.nalloc = 0
        self.out_ops = []
        self.stack = [self.es]

    def sb(self, shape, dt, name=None):
        self.nalloc += 1
        name = name or ("t%d" % self.nalloc)
        t = self.stack[-1].enter_context(self.nc.sbuf_tensor(name + "_%d" % self.nalloc, list(shape), dt))
        return V(t.ap() if hasattr(t, "ap") and callable(t.ap) else t[:], Buf(name))

    def ps(self, shape, dt=F32, name=None):
        self.nalloc += 1
        name = name or ("p%d" % self.nalloc)
        t = self.stack[-1].enter_context(self.nc.psum_tensor(name + "_%d" % self.nalloc, list(shape), dt))
        return V(t.ap() if hasattr(t, "ap") and callable(t.ap) else t[:], Buf(name))

    def dram(self, name, shape, dt, kind="Internal"):
        t = self.nc.dram_tensor(name, list(shape), dt, kind=kind)
        return DT(t.ap(), name)


    def push(self):
        self.stack.append(ExitStack())

    def pop(self):
        self.barrier()
        self.stack.pop().close()

    def barrier(self):
        last = {t: len(l) - 1 for t, l in self.tracks.items() if l}
        for eng in self.ENG:
            op = Op()
            op.eng = eng
            op.fn = None
            op.inc = False
            op.val = None
            op.track = eng
            op.tidx = len(self.tracks[eng])
            op.deps = {}
            seen = self.seen[eng]
            for t, i in last.items():
                if seen.get(t, -1) < i:
                    seen[t] = i
                    op.deps[t] = i
                    self.tracks[t][i].inc = True
            self.ops[eng].append(op)

    def add(self, eng, fn, reads, writes, dma=False):
        op = Op()
        op.eng = eng
        op.fn = fn
        op.inc = dma
        op.val = None
        if dma and eng == "pool":
            tr = "sdma%d" % self.sdma_rr
            self.sdma_rr = (self.sdma_rr + 1) % NSDMA
        elif dma:
            tr = "dma%d" % self.dma_rr
            self.dma_rr = (self.dma_rr + 1) % NDMA
        else:
            tr = eng
        op.track = tr
        op.tidx = len(self.tracks[tr])
        deps = {}

        def need(t, i):
            if deps.get(t, -1) < i:
                deps[t] = i

        for v in reads:
            for b in v.bufs:
                if b.w is not None:
                    t, i = b.w
                    if t == tr and eng == "pe":
                        continue
                    need(t, i)
        for v in writes:
            for b in v.bufs:
                if b.w is not None:
                    t, i = b.w
                    if not (t == tr and eng == "pe"):
                        need(t, i)
                for (t, i) in b.r:
                    if not (t == tr and eng == "pe"):
                        need(t, i)
        if dma and op.tidx > 0:
            need(tr, op.tidx - 1)
        seen = self.seen[eng]
        op.deps = {}
        for t, i in deps.items():
            if seen.get(t, -1) < i:
                seen[t] = i
                op.deps[t] = i
                self.tracks[t][i].inc = True
        self.tracks[tr].append(op)
        self.ops[eng].append(op)
        for v in reads:
            for b in v.bufs:
                b.r.append((tr, op.tidx))
        for v in writes:
            for b in v.bufs:
                b.w = (tr, op.tidx)
                b.r = []
        return op

    def mm(self, out, lhsT, rhs, start=True, stop=True):
        return self.add("pe", lambda e: e.matmul(out.ap, lhsT.ap, rhs.ap, start=start, stop=stop),
                        [lhsT, rhs] + ([] if start else [out]), [out])

    def tr(self, out, in_, ident):
        return self.add("pe", lambda e: e.transpose(out.ap, in_.ap, ident.ap), [in_, ident], [out])

    def act(self, out, in_, func, bias=None, scale=None, accum=None, eng="act"):
        kw = {}
        rd = [in_]
        wr = [out]
        if bias is not None:
            if isinstance(bias, V):
                kw["bias"] = bias.ap
                rd.append(bias)
            else:
                kw["bias"] = bias
        if scale is not None:
            if isinstance(scale, V):
                kw["scale"] = scale.ap
                rd.append(scale)
            else:
                kw["scale"] = scale
        if accum is not None:
            kw["accum_out"] = accum.ap
            wr.append(accum)
        return self.add("act", lambda e: e.activation(out.ap, in_.ap, func, **kw), rd, wr)

    def tt(self, out, a, b, op, eng="dve"):
        return self.add(eng, lambda e: e.tensor_tensor(out.ap, a.ap, b.ap, op), [a, b], [out])

    def ts(self, out, a, s1, s2, op0, op1=None, accum=None, eng="dve"):
        rd = [a]
        wr = [out]
        s1a = s1.ap if isinstance(s1, V) else s1
        s2a = s2.ap if isinstance(s2, V) else s2
        if isinstance(s1, V):
            rd.append(s1)
        if isinstance(s2, V):
            rd.append(s2)
        kw = {}
        if op1 is not None:
            kw["op1"] = op1
        if accum is not None:
            kw["accum_out"] = accum.ap
            wr.append(accum)
        return self.add(eng, lambda e: e.tensor_scalar(out.ap, a.ap, s1a, s2a, op0, **kw), rd, wr)

    def stt(self, out, a, s, b, op0, op1, eng="dve"):
        rd = [a, b]
        sa = s.ap if isinstance(s, V) else s
        if isinstance(s, V):
            rd.append(s)
        return self.add(eng, lambda e: e.scalar_tensor_tensor(out.ap, a.ap, sa, b.ap, op0, op1), rd, [out])

    def copy(self, out, in_, eng="dve"):
        if eng == "act":
            return self.add("act", lambda e: e.copy(out.ap, in_.ap), [in_], [out])
        return self.add(eng, lambda e: e.tensor_copy(out.ap, in_.ap), [in_], [out])

    def memset(self, out, val, eng="pool"):
        return self.add(eng, lambda e: e.memset(out.ap, val), [], [out])

    def recip(self, out, in_):
        return self.add("dve", lambda e: e.reciprocal(out.ap, in_.ap), [in_], [out])

    def scan(self, out, d0, d1, init, op0, op1):
        rd = [d0, d1]
        ia = init.ap if isinstance(init, V) else init
        if isinstance(init, V):
            rd.append(init)
        return self.add("dve", lambda e: e.tensor_tensor_scan(out.ap, d0.ap, d1.ap, ia, op0, op1), rd, [out])

    def dma(self, out, in_, eng="sp", is_out=False, **kw):
        op = self.add(eng, lambda e: e.dma_start(out=out.ap, in_=in_.ap, **kw), [in_], [out], dma=True)
        if is_out:
            self.out_ops.append(op)
        return op

    def emit(self):
        nc = self.nc
        fin = Op()
        fin.eng = "sp"
        fin.fn = None
        fin.inc = False
        fin.track = "sp"
        fin.tidx = len(self.tracks["sp"])
        fin.deps = {}
        for tname, lst in self.tracks.items():
            if is_dma(tname) and lst:
                fin.deps[tname] = len(lst) - 1
        self.ops["sp"].append(fin)
        for tname, lst in self.tracks.items():
            c = 0
            for op in lst:
                if is_dma(tname):
                    c += 16
                    op.val = c
                else:
                    if op.inc:
                        c += 1
                    op.val = c
        sems = {}
        for tname, lst in self.tracks.items():
            if lst:
                sems[tname] = self.es.enter_context(nc.semaphore("s_" + tname))
        block = self.es.enter_context(nc.Block())
        tracks = self.tracks
        for eng in self.ENG:
            ops = self.ops[eng]
            if not ops:
                continue

            def body(e, ops=ops):
                for op in ops:
                    for t, i in op.deps.items():
                        e.wait_ge(sems[t], tracks[t][i].val)
                    if op.fn is None:
                        continue
                    ins = op.fn(e)
                    if is_dma(op.track):
                        ins.then_inc(sems[op.track], 16)
                    elif op.inc:
                        ins.then_inc(sems[op.track], 1)

            getattr(block, self.BLK[eng])(body)
        self.es.close()


class DT:
    def __init__(self, ap, name):
        self.ap = ap
        self.name = name
        self.bufs = {}

    def v(self, ap, keys=(0,)):
        if not isinstance(keys, (tuple, list)):
            keys = (keys,)
        bs = []
        for k in keys:
            if k not in self.bufs:
                self.bufs[k] = Buf("%s[%s]" % (self.name, k))
            bs.append(self.bufs[k])
        return V(ap, tuple(bs))


class Pool:
    def __init__(self, P, n, shape, dt, name, psum=False):
        self.tiles = [(P.ps if psum else P.sb)(shape, dt, name="%s%d" % (name, i)) for i in range(n)]
        self.i = 0

    def next(self):
        t = self.tiles[self.i]
        self.i = (self.i + 1) % len(self.tiles)
        return t


import math

S = 4096
D = 1024
NT = 32
FF = 2816
ALPHA = 8 ** 0.25
LN_EPS = 1e-5
RMS_EPS = 1e-6
TWO_PI = 2.0 * math.pi


def host_consts():
    c = {}
    i = np.arange(128)
    c["c_ident"] = np.eye(128, dtype=np.float32)
    c["c_tri"] = (i[:, None] <= i[None, :]).astype(np.float32)
    c["c_ones"] = np.ones((128, 128), np.float32)
    c["c_negU"] = np.where(i[None, :] >= i[:, None], 0.0, -30000.0).astype(np.float32)
    c["c_posL"] = np.where(i[None, :] <= i[:, None], 0.0, 30000.0).astype(np.float32)
    c["c_strU"] = (i[None, :] > i[:, None]).astype(np.float32)
    c["c_strL"] = (i[None, :] < i[:, None]).astype(np.float32)
    dm = np.zeros((128, 256), np.float32)
    dm[:, :128] = (i[:, None] <= i[None, :])
    dm[:, 128:] = (i[:, None] >= i[None, :])
    c["c_dmask"] = dm
    c["c_dneg"] = ((dm - 1.0) * 30000.0).astype(np.float32)
    invf = (10000.0 ** (-np.arange(0, 64, 2, dtype=np.float32) / 64)).astype(np.float32)
    r = np.arange(128)
    c["c_invf"] = (invf[r % 32] / np.float32(TWO_PI)).astype(np.float32).reshape(128, 1)
    c["c_sgn"] = np.where((r % 64) < 32, -1.0, 1.0).astype(np.float32).reshape(128, 1)
    c["c_iota"] = np.broadcast_to(np.arange(64, dtype=np.float32)[None, :], (128, 64)).copy()
    sel = np.zeros((128, 64), np.float32)
    sel[64 + np.arange(64), np.arange(64)] = 1.0
    c["c_sel"] = sel
    return c


def host_layer_inputs(inp):
    o = {}
    for l in range(2):
        a_re, a_im, ldt = inp["s5_a_re"][l], inp["s5_a_im"][l], inp["s5_log_dt"][l]
        col = np.stack([a_re, a_im, np.repeat(ldt[:, None], 64, 1)], 0)
        col = col.reshape(3, 32, 2, 64).transpose(2, 3, 0, 1).reshape(128, 3, 32)
        o["s5col%d" % l] = np.ascontiguousarray(col, np.float32)
        o["s5row%d" % l] = np.ascontiguousarray(np.stack([a_re.reshape(-1), a_im.reshape(-1), np.repeat(ldt, 64)], 0), np.float32)
        for nm, src in (("re", inp["s5_b_re"][l]), ("im", inp["s5_b_im"][l])):
            bp = np.zeros((4, 2, 16, 32, 2, 64), np.float32)
            sp = src.reshape(8, 4, 2, 64, 16)
            for q in range(4):
                for gl in range(2):
                    bp[q, gl, :, q::4, gl, :] = sp[:, q, gl].transpose(2, 0, 1)
            o["s5b%s%d" % (nm, l)] = bp.reshape(128, 32 * 128)
        for nm, src in (("re", inp["s5_c_re"][l]), ("im", inp["s5_c_im"][l])):
            cp = np.zeros((2, 64, 32, 4, 2, 16), np.float32)
            sp = src.reshape(8, 4, 2, 16, 64)
            for q in range(4):
                for gl in range(2):
                    cp[gl, :, q::4, q, gl, :] = sp[:, q, gl].transpose(2, 0, 1)
            o["s5c%s%d" % (nm, l)] = cp.reshape(128, 32 * 128)
        o["s5d%d" % l] = np.ascontiguousarray(inp["s5_d"][l].reshape(8, 128).T, np.float32)
        o["dncw%d" % l] = np.ascontiguousarray(inp["dn_conv_w"][l].T.reshape(12, 128, 4).transpose(1, 0, 2), np.float32)
    return o


class Ctx:
    pass


def load_w(P, dst, src_ap, KC, eng="pool"):
    for kc in range(KC):
        P.dma(dst[:, kc, :], src_ap(kc), eng=eng)


def load_h(P, C, tt, h_src):
    ht = C.p_h.next()
    P.dma(ht, h_src.v(h_src.ap[tt * 128:(tt + 1) * 128, :], tt))
    return ht


def ln_epilogue(P, C, lnp, tt, sub_halves, ht):
    z = C.p_z.next()
    st = C.p_st.next()
    for hf in range(2):
        P.stt(z[:, hf * 512:(hf + 1) * 512], ht[:, hf * 512:(hf + 1) * 512], ALPHA, sub_halves[hf], ALU.mult, ALU.add)
        zz = z[:, hf * 512:(hf + 1) * 512]
        P.add("dve", lambda e, o=st[:, hf, :], i=zz: e.bn_stats(o.ap, i.ap), [zz], [st])
    mv = C.p_mv.next()
    P.add("dve", lambda e, o=mv[:, 0:2], i=st.re("p a b -> p (a b)"): e.bn_aggr(o.ap, i.ap), [st], [mv])
    P.ts(mv[:, 2:3], mv[:, 1:2], LN_EPS, None, ALU.add)
    P.act(mv[:, 3:4], mv[:, 2:3], AF.Sqrt)
    P.recip(mv[:, 4:5], mv[:, 3:4])
    zn = C.p_zn.next()
    P.ts(zn, z, mv[:, 0:1], mv[:, 4:5], ALU.subtract, ALU.mult)
    hn = C.p_hn.next()
    P.tt(zn, zn, lnp[0], ALU.mult, eng="pool")
    P.tt(hn, zn, lnp[1], ALU.add, eng="dve")
    P.dma(C.out.v(C.out.ap[tt * 128:(tt + 1) * 128, :], tt), hn, is_out=True, eng="act")
    hb = C.p_hb.next()
    P.copy(hb, hn, eng="act")
    return (tt, hb)


def ln_epilogue_b(P, C, pend):
    if pend is None:
        return
    tt, hb = pend
    pt = C.pst
    for c in range(8):
        P.tr(pt[:, c * 128:(c + 1) * 128], hb[:, c * 128:(c + 1) * 128], C.identb)
    hts = C.p_hts.next()
    P.copy(hts, pt.re("p (c t) -> p c t", c=8), eng="act")
    P.dma(C.hT.v(C.hT.ap.rearrange("(c p) t -> p c t", p=128)[:, :, tt * 128:(tt + 1) * 128], ("t", tt)), hts, eng="act")


def ln_pools(P, C):
    C.p_h = Pool(P, 8, [128, 1024], F32, "h")
    C.p_z = Pool(P, 2, [128, 1024], F32, "z")
    C.p_zn = Pool(P, 2, [128, 1024], F32, "zn")
    C.p_hn = Pool(P, 3, [128, 1024], F32, "hn")
    C.p_hb = Pool(P, 5, [128, 1024], BF16, "hb")
    C.p_hts = Pool(P, 2, [128, 8, 128], BF16, "hts")
    C.p_st = Pool(P, 2, [128, 2, 6], F32, "st")
    C.p_mv = Pool(P, 2, [128, 8], F32, "mv")


def load_ln(P, C, g_ap, b_ap):
    G = P.sb([128, 1024], F32, "lng")
    B = P.sb([128, 1024], F32, "lnb")
    P.dma(G, g_ap)
    P.dma(B, b_ap)
    return (G, B)


def hT_keys(b):
    return tuple(("t", t) for t in range(b * 4, b * 4 + 4))


def phase_prep(P, C, I):
    import os
    MODE = os.environ.get("PREP_MODE", "all")
    P.push()
    xs = Pool(P, 2, [128, 1024], F32, "xs")
    xb = Pool(P, 2, [128, 1024], BF16, "xb")
    hts = Pool(P, 2, [128, 8, 128], BF16, "hts")
    for tt in range(NT if MODE in ('all', 'A') else 0):
        x_t = xs.next()
        P.dma(x_t, I["x"].v(I["x"].ap[tt * 128:(tt + 1) * 128, :], tt))
        b_t = xb.next()
        P.copy(b_t, x_t, eng="dve")
        for c in range(8):
            P.tr(C.pst[:, c * 128:(c + 1) * 128], b_t[:, c * 128:(c + 1) * 128], C.identb)
        h_t = hts.next()
        P.copy(h_t, C.pst.re("p (c t) -> p c t", c=8), eng="act")
        P.dma(C.hT.v(C.hT.ap.rearrange("(c p) t -> p c t", p=128)[:, :, tt * 128:(tt + 1) * 128], ("t", tt)), h_t)
    for mt in range(2 if MODE in ('all', 'A2') else 0):
        x_t = xs.next()
        P.dma(x_t, I["mem"].v(I["mem"].ap[mt * 128:(mt + 1) * 128, :], mt))
        b_t = xb.next()
        P.copy(b_t, x_t, eng="dve")
        for c in range(8):
            P.tr(C.pst[:, c * 128:(c + 1) * 128], b_t[:, c * 128:(c + 1) * 128], C.identb)
        h_t = hts.next()
        P.copy(h_t, C.pst.re("p (c t) -> p c t", c=8), eng="act")
        P.dma(C.memT.v(C.memT.ap.rearrange("(c p) t -> p c t", p=128)[:, :, mt * 128:(mt + 1) * 128], 0), h_t)
    invf = P.sb([128, 1], F32, "invf")
    sgn = P.sb([128, 1], F32, "sgn")
    P.dma(invf, I["c_invf"].v(I["c_invf"].ap))
    P.dma(sgn, I["c_sgn"].v(I["c_sgn"].ap))
    W = 1024
    pi_ = P.sb([128, W], I32, "posi")
    pf = P.sb([128, W], F32, "posf")
    tu = P.sb([128, W], F32, "tu")
    f = P.sb([128, W], F32, "f")
    tb = P.sb([128, W], F32, "tb")
    scr = frac_scratch(P, [128, W])
    for cq in range(S // W if MODE in ('all', 'B') else 0):
        P.dma(pi_, I["pos"].v(I["pos"].ap[:, cq * W:(cq + 1) * W].partition_broadcast(128)))
        P.copy(pf, pi_, eng="dve")
        P.ts(tu, pf, invf[:, 0:1], None, ALU.mult)
        for which, shift, dst in (("sin", 0.0, C.sinT), ("cos", 0.25, C.cosT)):
            frac_wrap(P, f, tu, shift, scr)
            P.act(tb, f, AF.Sin, scale=TWO_PI)
            if which == "sin":
                P.ts(tb, tb, sgn[:, 0:1], None, ALU.mult)
            P.dma(dst.v(dst.ap[:, cq * W:(cq + 1) * W], cq), tb)
    P.pop()


def frac_scratch(P, shape):
    return (P.sb(shape, I32, "fw_ki"), P.sb(shape, F32, "fw_kf"), P.sb(shape, F32, "fw_m"))


def frac_wrap(P, f, tu, shift, scr):
    ki, kf, m = scr
    if shift != 0.0:
        P.ts(f, tu, shift, None, ALU.add)
        src = f
    else:
        src = tu
    P.copy(ki, src, eng="dve")
    P.copy(kf, ki, eng="dve")
    P.tt(f, src, kf, ALU.subtract)
    P.ts(m, f, 0.5, None, ALU.is_gt)
    P.tt(f, f, m, ALU.subtract)
    P.ts(m, f, -0.5, None, ALU.is_lt)
    P.tt(f, f, m, ALU.add)


def phase_proj_ln(P, C, inT, K, W_ap, lnp_aps, h_src, gate_ap=None, Wpre=None):
    P.push()
    KC = K // 128
    if Wpre is not None:
        Wb = Wpre
    else:
        Wb = P.sb([128, KC, 1024], BF16, "Wb")
        load_w(P, Wb, W_ap, KC)
    Wg = None
    if gate_ap is not None:
        Wg = P.sb([128, KC, 1024], BF16, "Wg")
        load_w(P, Wg, gate_ap, KC)
    lnp = load_ln(P, C, *lnp_aps)
    ln_pools(P, C)
    blk = Pool(P, 2, [128, KC, 512], BF16, "inblk")
    sig = Pool(P, 2, [128, 1024], F32, "sig") if Wg is not None else None
    pendq = []
    DEPTH = 3 if KC <= 8 else 1

    def prefetch(b):
        ib_ = blk.next()
        P.dma(ib_, inT.v(inT.ap.rearrange("(c p) t -> p c t", p=128)[:, :, b * 512:(b + 1) * 512], hT_keys(b)))
        return ib_, [load_h(P, C, b * 4 + t4, h_src) for t4 in range(4)]

    nxt = prefetch(0)
    for b in range(8):
        ib, hts_ = nxt
        if b < 7:
            nxt = prefetch(b + 1)
        for t4 in range(4):
            tt = b * 4 + t4
            par = (tt % 2) * 2 if Wg is None else 0
            sub = [C.PS[par], C.PS[par + 1]]
            for hf in range(2):
                for kc in range(KC):
                    P.mm(sub[hf], ib[:, kc, t4 * 128:(t4 + 1) * 128], Wb[:, kc, hf * 512:(hf + 1) * 512], start=(kc == 0), stop=(kc == KC - 1))
            if Wg is not None:
                gp = [C.PS[2], C.PS[3]]
                sg = sig.next()
                for hf in range(2):
                    for kc in range(KC):
                        P.mm(gp[hf], ib[:, kc, t4 * 128:(t4 + 1) * 128], Wg[:, kc, hf * 512:(hf + 1) * 512], start=(kc == 0), stop=(kc == KC - 1))
                    P.act(sg[:, hf * 512:(hf + 1) * 512], gp[hf], AF.Sigmoid)
                    P.tt(sg[:, hf * 512:(hf + 1) * 512], sub[hf], sg[:, hf * 512:(hf + 1) * 512], ALU.mult)
                sub = [sg[:, 0:512], sg[:, 512:1024]]
            if len(pendq) >= DEPTH:
                ln_epilogue_b(P, C, pendq.pop(0))
            pendq.append(ln_epilogue(P, C, lnp, tt, sub, hts_[t4]))
    for pe_ in pendq:
        ln_epilogue_b(P, C, pe_)
    P.pop()


def phase_ffn_up(P, C, wg_ap, wu_ap, Wgpre=None, after_loads=None):
    P.push()
    if Wgpre is not None:
        Wg = Wgpre
    else:
        Wg = P.sb([128, 8, FF], BF16, "Wg")
        load_w(P, Wg, wg_ap, 8)
    Wu = P.sb([128, 8, FF], BF16, "Wu")
    load_w(P, Wu, wu_ap, 8)
    if after_loads is not None:
        after_loads()
    blk = Pool(P, 2, [128, 8, 512], BF16, "hblk")
    ab = Pool(P, 2, [128, 22, 512], BF16, "ablk")
    sgp = Pool(P, 2, [128, 512], F32, "sg")
    def prefetch(b):
        ib_ = blk.next()
        P.dma(ib_, C.hT.v(C.hT.ap.rearrange("(c p) t -> p c t", p=128)[:, :, b * 512:(b + 1) * 512], hT_keys(b)))
        return ib_

    nxt = prefetch(0)
    for b in range(8):
        ib = nxt
        if b < 7:
            nxt = prefetch(b + 1)
        a = ab.next()
        for j in range(22):
            pg = C.PS[(j % 2) * 2]
            pu = C.PS[(j % 2) * 2 + 1]
            for kc in range(8):
                P.mm(pg, Wg[:, kc, j * 128:(j + 1) * 128], ib[:, kc, :], start=(kc == 0), stop=(kc == 7))
            for kc in range(8):
                P.mm(pu, Wu[:, kc, j * 128:(j + 1) * 128], ib[:, kc, :], start=(kc == 0), stop=(kc == 7))
            sg = sgp.next()
            P.act(sg, pg, AF.Silu)
            P.tt(a[:, j, :], pu, sg, ALU.mult)
        P.dma(C.aT.v(C.aT.ap.rearrange("(c p) t -> p c t", p=128)[:, :, b * 512:(b + 1) * 512], hT_keys(b)), a)
    P.pop()


def phase_cross(P, C, I, layer, lnp_aps, h_src, after_loads=None):
    P.push()
    wq, wk, wv, wo = (I[n] for n in ("xq_w", "xk_w", "xv_w", "xo_w"))
    rows = lambda w: (lambda kc: w.v(w.ap[layer, kc * 128:(kc + 1) * 128, :]))
    Wq = P.sb([128, 8, 1024], BF16, "Wq")
    Wo = P.sb([128, 8, 1024], BF16, "Wo")
    Wk = P.sb([128, 8, 1024], BF16, "Wk")
    load_w(P, Wk, rows(wk), 8)
    mT = P.sb([128, 8, 256], BF16, "mT")
    P.dma(mT, C.memT.v(C.memT.ap.rearrange("(c p) t -> p c t", p=128), 0))
    kT = P.sb([128, 8, 256], BF16, "kT")
    import os
    if os.environ.get("CROSS_MODE", "all") == "pre0":
        dbg = P.sb([128, 1024], F32, "dbg")
        P.copy(dbg, Wk[:, 3, :])
        P.dma(C.out.v(C.out.ap[0:128, :], 0), dbg, is_out=True)
        P.pop()
        return
    for oc in range(8):
        ps = C.PS[oc % 2]
        for kc in range(8):
            P.mm(ps[:, 0:256], Wk[:, kc, oc * 128:(oc + 1) * 128], mT[:, kc, :], start=(kc == 0), stop=(kc == 7))
        P.copy(kT[:, oc, :], ps[:, 0:256], eng="act")
    load_w(P, Wk, rows(wv), 8)
    Vm = P.sb([128, 2, 1024], BF16, "Vm")
    for mc in range(2):
        for hf in range(2):
            ps = C.PS[hf]
            for kc in range(8):
                P.mm(ps, mT[:, kc, mc * 128:(mc + 1) * 128], Wk[:, kc, hf * 512:(hf + 1) * 512], start=(kc == 0), stop=(kc == 7))
            P.copy(Vm[:, mc, hf * 512:(hf + 1) * 512], ps, eng="act")
    import os
    CM = os.environ.get("CROSS_MODE", "all")
    if CM == "pre":
        P.pop()
        return
    load_w(P, Wq, rows(wq), 8)
    load_w(P, Wo, rows(wo), 8)
    if after_loads is not None:
        after_loads()
    lnp = load_ln(P, C, *lnp_aps)
    ln_pools(P, C)
    onesb = C.onesb
    blk = Pool(P, 2, [128, 8, 512], BF16, "hblk")
    qTp = Pool(P, 1, [128, 8, 512], BF16, "qT")
    pTp = Pool(P, 2, [128, 2, 512], BF16, "pT")
    rdp = Pool(P, 2, [128, 512], F32, "rden")
    oTp = Pool(P, 2, [128, 8, 512], BF16, "oT")
    pendq = []

    def prefetch(b):
        ib_ = blk.next()
        P.dma(ib_, C.hT.v(C.hT.ap.rearrange("(c p) t -> p c t", p=128)[:, :, b * 512:(b + 1) * 512], hT_keys(b)))
        return ib_, [load_h(P, C, b * 4 + t4, h_src) for t4 in range(4)]

    nxt = prefetch(0)
    for b in range(8):
        ib, hts_ = nxt
        if b < 7:
            nxt = prefetch(b + 1)
        qT = qTp.next()
        for oc in range(8):
            ps = C.PS[2 + oc % 2]
            for kc in range(8):
                P.mm(ps, Wq[:, kc, oc * 128:(oc + 1) * 128], ib[:, kc, :], start=(kc == 0), stop=(kc == 7))
            P.copy(qT[:, oc, :], ps, eng="act")
        oT = oTp.next()
        if CM == "q":
            continue
        for hh in range(4):
            pT = pTp.next()
            for mc in range(2):
                ps = C.PS[2 + mc]
                for c in range(2):
                    P.mm(ps, kT[:, hh * 2 + c, mc * 128:(mc + 1) * 128], qT[:, hh * 2 + c, :], start=(c == 0), stop=(c == 1))
                P.act(pT[:, mc, :], ps, AF.Exp, scale=1.0 / 16.0)
            pd = C.PS[4]
            for mc in range(2):
                P.mm(pd, onesb, pT[:, mc, :], start=(mc == 0), stop=(mc == 1))
            rd = rdp.next()
            P.recip(rd, pd)
            for c in range(2):
                po = C.PS[5 + c]
                for mc in range(2):
                    P.mm(po, Vm[:, mc, hh * 256 + c * 128: hh * 256 + (c + 1) * 128], pT[:, mc, :], start=(mc == 0), stop=(mc == 1))
                P.tt(oT[:, hh * 2 + c, :], po, rd, ALU.mult)
        if CM == "attn":
            continue
        for t4 in range(4):
            tt = b * 4 + t4
            sub = [C.PS[0], C.PS[1]]
            for hf in range(2):
                for kc in range(8):
                    P.mm(sub[hf], oT[:, kc, t4 * 128:(t4 + 1) * 128], Wo[:, kc, hf * 512:(hf + 1) * 512], start=(kc == 0), stop=(kc == 7))
            if len(pendq) >= 3:
                ln_epilogue_b(P, C, pendq.pop(0))
            pendq.append(ln_epilogue(P, C, lnp, tt, sub, hts_[t4]))
    for pe_ in pendq:
        ln_epilogue_b(P, C, pe_)
    P.pop()


WEIGHT_SPECS = [
    ("hyb_w_in", [2, 1024, 3592]), ("dn_conv_w", [2, 4, 1536]), ("hyb_w_out", [2, 1024, 1024]),
    ("s5_glu_wo", [2, 1024, 1024]), ("s5_glu_wg", [2, 1024, 1024]),
    ("ln_mix_g", [4, 1024]), ("ln_mix_b", [4, 1024]),
    ("xq_w", [4, 1024, 1024]), ("xk_w", [4, 1024, 1024]), ("xv_w", [4, 1024, 1024]), ("xo_w", [4, 1024, 1024]),
    ("ln_x_g", [4, 1024]), ("ln_x_b", [4, 1024]),
    ("ffn_wg", [4, 1024, 2816]), ("ffn_wu", [4, 1024, 2816]), ("ffn_wd", [4, 2816, 1024]),
    ("ln_ffn_g", [4, 1024]), ("ln_ffn_b", [4, 1024]),
]


def build_program(plan):
    nc = bass.Bass("TRN2", target_bir_lowering=False)
    P = Prog(nc)
    C = Ctx()
    I = {}

    def inp(name, shape, dt=F32):
        I[name] = P.dram(name, shape, dt, kind="ExternalInput")

    inp("x", [S, D])
    inp("mem", [256, D])
    inp("pos", [1, S], I32)
    for n, sh in WEIGHT_SPECS:
        inp(n, sh)
    hc = host_consts()
    for n, a in hc.items():
        inp(n, list(a.shape))
    for l in range(2):
        inp("s5col%d" % l, [128, 3, 32])
        inp("s5row%d" % l, [3, 4096])
        for nm in ("bre", "bim", "cre", "cim"):
            inp("s5%s%d" % (nm, l), [128, 4096])
        inp("s5d%d" % l, [128, 8])
        inp("dnp%d" % l, [1, 4 + 4 + 128])
        inp("dncw%d" % l, [128, 12, 4])
    C.out = P.dram("out", [S, D], F32, kind="ExternalOutput")
    C.hT = P.dram("hT", [D, S], BF16)
    C.mixT = P.dram("mixT", [D, S], BF16)
    C.aT = P.dram("aT", [FF, S], BF16)
    C.memT = P.dram("memT", [D, 256], BF16)
    C.cosT = P.dram("cosT", [128, S], F32)
    C.sinT = P.dram("sinT", [128, S], F32)
    C.vd = P.dram("vd", [S, 512], BF16)
    C.PS = [P.ps([128, 512], F32, "ps%d" % i) for i in range(7)]
    C.pst = P.ps([128, 1024], BF16, "pst")
    idf = P.sb([128, 128], F32, "idf")
    P.dma(idf, I["c_ident"].v(I["c_ident"].ap))
    C.identb = P.sb([128, 128], BF16, "identb")
    P.copy(C.identb, idf)
    C.identf = idf
    C.onesb = P.sb([128, 128], BF16, "onesb")
    P.memset(C.onesb, 1.0)

    bvec = lambda nm, l: I[nm].v(I[nm].ap[l:l + 1, :].partition_broadcast(128))
    h_src = I["x"]
    for ph in plan:
        kind = ph[0]
        if kind == "prep":
            phase_prep(P, C, I)
        elif kind == "cross":
            l = ph[1]
            phase_cross(P, C, I, l, (bvec("ln_x_g", l), bvec("ln_x_b", l)), h_src)
            h_src = C.out
        elif kind == "ffn":
            l = ph[1]
            wg, wu, wd = I["ffn_wg"], I["ffn_wu"], I["ffn_wd"]
            P.push()
            WdPre = P.sb([128, 22, 1024], BF16, "WdPre")
            pre = lambda: load_w(P, WdPre, lambda kc: wd.v(wd.ap[l, kc * 128:(kc + 1) * 128, :]), 22)
            phase_ffn_up(P, C, lambda kc: wg.v(wg.ap[l, kc * 128:(kc + 1) * 128, :]), lambda kc: wu.v(wu.ap[l, kc * 128:(kc + 1) * 128, :]), after_loads=pre)
            phase_proj_ln(P, C, C.aT, FF, None, (bvec("ln_ffn_g", l), bvec("ln_ffn_b", l)), h_src, Wpre=WdPre)
            P.pop()
            h_src = C.out
        elif kind == "s5":
            l = ph[1]
            i = l // 2
            phase_s5(P, C, I, i)
            wo, wg = I["s5_glu_wo"], I["s5_glu_wg"]
            phase_proj_ln(P, C, C.mixT, D, lambda kc: wo.v(wo.ap[i, kc * 128:(kc + 1) * 128, :]),
                          (bvec("ln_mix_g", l), bvec("ln_mix_b", l)), h_src,
                          gate_ap=lambda kc: wg.v(wg.ap[i, kc * 128:(kc + 1) * 128, :]))
            h_src = C.out
        elif kind == "hyb":
            l = ph[1]
            i = l // 2
            phase_dilated(P, C, I, i)
            phase_gdn(P, C, I, i)
            wo = I["hyb_w_out"]
            phase_proj_ln(P, C, C.mixT, D, lambda kc: wo.v(wo.ap[i, kc * 128:(kc + 1) * 128, :]),
                          (bvec("ln_mix_g", l), bvec("ln_mix_b", l)), h_src)
            h_src = C.out
    P.emit()
    return nc


def make_in_maps(inputs, cores):
    hc = host_consts()
    hl = host_layer_inputs(inputs)
    shared = {n: np.ascontiguousarray(inputs[n], np.float32) for n, _ in WEIGHT_SPECS}
    shared.update(hc)
    shared.update(hl)
    for l in range(2):
        shared["dnp%d" % l] = np.concatenate([inputs["dn_a_log"][l], inputs["dn_dt_bias"][l], inputs["dn_norm_g"][l]]).astype(np.float32).reshape(1, 136)
    maps = []
    for b in cores:
        m = dict(shared)
        m["x"] = np.ascontiguousarray(inputs["x"][b], np.float32)
        m["mem"] = np.ascontiguousarray(inputs["mem"][b], np.float32)
        m["pos"] = np.ascontiguousarray(inputs["positions"][b].reshape(1, S), np.int32)
        maps.append(m)
    return maps


FULL_PLAN = [("prep",)]
for _l in range(4):
    FULL_PLAN += [("hyb" if _l % 2 == 0 else "s5", _l), ("cross", _l), ("ffn", _l)]


def kernel(**inputs):
    inputs = {k: np.asarray(v) for k, v in inputs.items()}
    nc = build_program(FULL_PLAN)
    maps = make_in_maps(inputs, list(range(8)))
    res = run_bass_kernel_spmd(nc, maps, core_ids=list(range(8)))
    return np.stack([r["out"] for r in res.results], 0).astype(np.float32)


def b3(v, shape, axis):
    return V(v.ap.unsqueeze(axis).to_broadcast(shape), v.bufs)


def phase_s5(P, C, I, l):
    P.push()
    f32t = lambda shape, n: P.sb(shape, F32, n)
    col = f32t([128, 3, 32], "col")
    P.dma(col, I["s5col%d" % l].v(I["s5col%d" % l].ap))
    dtc = f32t([128, 32], "dtc")
    P.act(dtc, col[:, 2, :], AF.Exp)
    t0 = f32t([128, 32], "t0")
    P.tt(t0, col[:, 0, :], dtc, ALU.mult)
    rho = f32t([128, 32], "rho")
    P.act(rho, t0, AF.Exp)
    phi = f32t([128, 32], "phi")
    P.tt(phi, col[:, 1, :], dtc, ALU.mult)
    P.ts(phi, phi, 1.0 / TWO_PI, None, ALU.mult)
    iota = f32t([128, 64], "iota")
    P.dma(iota, I["c_iota"].v(I["c_iota"].ap))
    dcol = f32t([128, 8], "dcol")
    P.dma(dcol, I["s5d%d" % l].v(I["s5d%d" % l].ap))
    E = {k: f32t([128, 2048], "F" + k) for k in ("lo", "hi")}
    Bb = {k: P.sb([128, 4096], BF16, "Bb" + k) for k in ("re", "im")}
    CreB = P.sb([128, 4096], BF16, "CreB")
    CreN = P.sb([128, 4096], BF16, "CreN")
    CimN = P.sb([128, 4096], BF16, "CimN")
    P.dma(CreB, I["s5cre%d" % l].v(I["s5cre%d" % l].ap), eng="pool")
    P.dma(CimN, I["s5cim%d" % l].v(I["s5cim%d" % l].ap), eng="pool")
    P.ts(CreN, CreB, -1.0, None, ALU.mult)
    P.ts(CimN, CimN, -1.0, None, ALU.mult)
    P.push()
    scr = frac_scratch(P, [128, 2048])
    tu = f32t([128, 2048], "tu")
    tu3 = tu.re("p (a b) -> p a b", b=64)
    iota3 = b3(iota, [128, 32, 64], 1)
    P.tt(tu3, b3(phi, [128, 32, 64], 2), iota3, ALU.mult)
    frac_wrap(P, E["lo"], tu, 0.0, scr)
    p64 = f32t([128, 32], "p64")
    P.ts(p64, phi, 64.0, None, ALU.mult)
    scr2 = frac_scratch(P, [128, 32])
    f64 = f32t([128, 32], "f64")
    frac_wrap(P, f64, p64, 0.0, scr2)
    P.tt(tu3, b3(f64, [128, 32, 64], 2), iota3, ALU.mult)
    frac_wrap(P, E["hi"], tu, 0.0, scr)
    P.pop()
    P.push()
    row = I["s5row%d" % l]
    T = {k: f32t([128, 1024], "r" + k) for k in ("are", "aim", "dt", "er", "tu", "f", "sn", "cs", "abr", "abi", "nr", "den", "cr", "ci", "Bre", "Bim", "x1", "x2")}
    scr = frac_scratch(P, [128, 1024])
    for qq in range(4):
        cs_ = slice(qq * 1024, (qq + 1) * 1024)
        P.dma(T["are"], row.v(row.ap[0:1, cs_].partition_broadcast(128)))
        P.dma(T["aim"], row.v(row.ap[1:2, cs_].partition_broadcast(128)))
        P.dma(T["dt"], row.v(row.ap[2:3, cs_].partition_broadcast(128)))
        P.dma(T["Bre"], I["s5bre%d" % l].v(I["s5bre%d" % l].ap[:, cs_]))
        P.dma(T["Bim"], I["s5bim%d" % l].v(I["s5bim%d" % l].ap[:, cs_]))
        P.act(T["dt"], T["dt"], AF.Exp)
        P.tt(T["er"], T["are"], T["dt"], ALU.mult)
        P.act(T["er"], T["er"], AF.Exp)
        P.tt(T["tu"], T["aim"], T["dt"], ALU.mult)
        P.ts(T["tu"], T["tu"], 1.0 / TWO_PI, None, ALU.mult)
        frac_wrap(P, T["f"], T["tu"], 0.0, scr)
        P.act(T["sn"], T["f"], AF.Sin, scale=TWO_PI)
        frac_wrap(P, T["f"], T["tu"], 0.25, scr)
        P.act(T["cs"], T["f"], AF.Sin, scale=TWO_PI)
        P.tt(T["abr"], T["er"], T["cs"], ALU.mult)
        P.tt(T["abi"], T["er"], T["sn"], ALU.mult)
        P.ts(T["nr"], T["abr"], -1.0, None, ALU.add)
        P.tt(T["den"], T["are"], T["are"], ALU.mult)
        P.tt(T["x1"], T["aim"], T["aim"], ALU.mult)
        P.tt(T["den"], T["den"], T["x1"], ALU.add)
        P.recip(T["den"], T["den"])
        P.tt(T["x1"], T["nr"], T["are"], ALU.mult)
        P.tt(T["x2"], T["abi"], T["aim"], ALU.mult)
        P.tt(T["cr"], T["x1"], T["x2"], ALU.add)
        P.tt(T["cr"], T["cr"], T["den"], ALU.mult)
        P.tt(T["x1"], T["abi"], T["are"], ALU.mult)
        P.tt(T["x2"], T["nr"], T["aim"], ALU.mult)
        P.tt(T["ci"], T["x1"], T["x2"], ALU.subtract)
        P.tt(T["ci"], T["ci"], T["den"], ALU.mult)
        P.tt(T["x1"], T["cr"], T["Bre"], ALU.mult)
        P.tt(T["x2"], T["ci"], T["Bim"], ALU.mult)
        P.tt(Bb["re"][:, cs_], T["x1"], T["x2"], ALU.subtract)
        P.tt(T["x1"], T["cr"], T["Bim"], ALU.mult)
        P.tt(T["x2"], T["ci"], T["Bre"], ALU.mult)
        P.tt(Bb["im"][:, cs_], T["x1"], T["x2"], ALU.add)
    P.pop()
    BW = 512
    NBLK = S // BW
    HC = BW // 64
    ones = f32t([128, BW], "ones")
    P.memset(ones, 1.0)
    halfpi = f32t([128, 1], "halfpi")
    P.memset(halfpi, math.pi / 2)
    onec = f32t([128, 1], "onec")
    P.memset(onec, 1.0)
    stre = [f32t([128, 1], "stre%d" % i) for i in range(32)]
    stim = [f32t([128, 1], "stim%d" % i) for i in range(32)]
    for t_ in stre + stim:
        P.memset(t_, 0.0)
    uTp = Pool(P, 2, [128, S], BF16, "uT")
    RHO = [[f32t([128, BW], "RHO%d_%d" % (par, q)) for q in range(4)] for par in range(2)]
    mkp = lambda n, dt=F32, k=4: Pool(P, k, [128, BW], dt, n)
    p_fs, p_s1, p_sq, p_ei = mkp("fs", F32, 2), mkp("s1", F32, 2), mkp("sq"), mkp("ei")
    p_m = [mkp("m%d" % i, F32, 3) for i in range(4)]
    p_xr, p_xi = mkp("xr", F32, 3), mkp("xi", F32, 3)
    p_pp = [mkp("pp%d" % i, BF16, 3) for i in range(4)]
    hidp = Pool(P, 2, [128, BW], BF16, "hid")
    vtp = Pool(P, 2, [128, BW], F32, "vt")
    allk = tuple(("t", t) for t in range(NT))
    v3 = lambda t: t.re("p (a b) -> p a b", b=64)
    iters = [(cc, blk, q) for cc in range(8) for blk in range(NBLK) for q in range(4)]
    ctx = {}
    uTs = {}
    bub = [0]
    btb = [0]
    IDN = f32t([128, 128], "IDN")
    P.ts(IDN, C.identf, -1.0, None, ALU.mult)

    def prologue(n):
        cc, blk, q = iters[n]
        ctx[n] = {}
        if blk == 0 and q == 0:
            uTs[cc] = uTp.next()
            P.dma(uTs[cc], C.hT.v(C.hT.ap[cc * 128:(cc + 1) * 128, :], allk))
            for q2 in range(4):
                P.ts(RHO[cc % 2][q2], ones, rho[:, 4 * cc + q2:4 * cc + q2 + 1], None, ALU.mult)

    def sA(n):
        cc, blk, q = iters[n]
        pair = 4 * cc + q
        d = ctx[n]
        hs = slice(pair * 64 + blk * HC, pair * 64 + (blk + 1) * HC)
        ls = slice(pair * 64, pair * 64 + 64)
        d["fs"], d["s1"], d["sq"], d["ei"] = p_fs.next(), p_s1.next(), p_sq.next(), p_ei.next()
        P.tt(v3(d["fs"]), b3(E["hi"][:, hs], [128, HC, 64], 2), b3(E["lo"][:, ls], [128, HC, 64], 1), ALU.add, eng="pool")

    def sD(n):
        d = ctx[n]
        fs, s1, sq = d["fs"], d["s1"], d["sq"]
        P.act(s1, fs, AF.Sin, scale=math.pi)
        P.act(fs, fs, AF.Abs)
        P.act(fs, fs, AF.Sin, scale=-math.pi, bias=halfpi[:, 0:1])
        P.act(sq, s1, AF.Square)
        P.act(sq, sq, AF.Identity, scale=-2.0, bias=onec[:, 0:1])

    def sF(n):
        d = ctx[n]
        P.stt(d["ei"], d["s1"], 2.0, d["fs"], ALU.mult, ALU.mult)
        d["Ere"], d["Eim"] = d["sq"], d["ei"]

    def sB(n):
        cc, blk, q = iters[n]
        pair = 4 * cc + q
        d = ctx[n]
        tk = slice(blk * BW, (blk + 1) * BW)
        pc = slice(pair * 128, (pair + 1) * 128)
        d["bre"], d["bim"] = C.PS[0], C.PS[1]
        P.mm(d["bre"], Bb["re"][:, pc], uTs[cc][:, tk])
        P.mm(d["bim"], Bb["im"][:, pc], uTs[cc][:, tk])

    def sE(n):
        d = ctx[n]
        m = [p.next() for p in p_m]
        bre, bim = d["bre"], d["bim"]
        P.tt(m[0], bre, d["Ere"], ALU.mult)
        P.tt(m[1], bim, d["Eim"], ALU.mult)
        P.tt(m[2], bim, d["Ere"], ALU.mult)
        P.tt(m[3], bre, d["Eim"], ALU.mult)
        btr, bti = C.PS[2 + btb[0]], C.PS[3 + btb[0]]
        btb[0] = (btb[0] + 2) % 4
        P.mm(btr, C.identf, m[0], start=True, stop=False)
        P.mm(btr, C.identf, m[1], start=False, stop=True)
        P.mm(bti, C.identf, m[2], start=True, stop=False)
        P.mm(bti, IDN, m[3], start=False, stop=True)
        d["m"] = [btr, None, bti, None]

    def sC(n):
        cc, blk, q = iters[n]
        pair = 4 * cc + q
        d = ctx[n]
        m = d["m"]
        xr, xi = p_xr.next(), p_xi.next()
        P.scan(xr, RHO[cc % 2][q], m[0], stre[pair], ALU.mult, ALU.add)
        P.scan(xi, RHO[cc % 2][q], m[2], stim[pair], ALU.mult, ALU.add)
        P.copy(stre[pair], xr[:, BW - 1:BW], eng="act")
        P.copy(stim[pair], xi[:, BW - 1:BW], eng="act")
        pp = d["pp"] = [p.next() for p in p_pp]
        P.tt(pp[0], d["Ere"], xr, ALU.mult)
        P.tt(pp[1], d["Eim"], xi, ALU.mult, eng="pool")
        P.tt(pp[2], d["Ere"], xi, ALU.mult, eng="pool")
        P.tt(pp[3], d["Eim"], xr, ALU.mult, eng="pool")

    def sG(n):
        cc, blk, q = iters[n]
        pair = 4 * cc + q
        d = ctx.pop(n)
        uT = uTs[cc]
        tk = slice(blk * BW, (blk + 1) * BW)
        pc = slice(pair * 128, (pair + 1) * 128)
        yps = C.PS[6]
        pp = d["pp"]
        for k, (W_, p_) in enumerate(((CreB, pp[0]), (CreN, pp[1]), (CimN, pp[2]), (CimN, pp[3]))):
            P.mm(yps, W_[:, pc], p_, start=(q == 0 and k == 0), stop=(q == 3 and k == 3))
        if q == 3:
            vt = vtp.next()
            P.stt(vt, uT[:, tk], dcol[:, cc:cc + 1], yps, ALU.mult, ALU.add)
            hid = hidp.next()
            P.act(hid, vt, AF.Gelu)
            P.dma(C.mixT.v(C.mixT.ap[cc * 128:(cc + 1) * 128, tk], hT_keys(blk)), hid)

    N_ = len(iters)
    ok = lambda k: 0 <= k < N_
    for n in range(N_ + 2):
        if ok(n):
            prologue(n)
            sA(n)
        if ok(n - 1):
            sB(n - 1)
        if ok(n - 2):
            sC(n - 2)
        if ok(n):
            sD(n)
        if ok(n - 1):
            sE(n - 1)
        if ok(n):
            sF(n)
        if ok(n - 2):
            sG(n - 2)
    P.pop()


def _zero_mix_rows(P, C, r0, r1):
    P.push()
    z = P.sb([128, 1024], BF16, "zmix")
    P.memset(z, 0.0)
    for c in range(r0 // 128, r1 // 128):
        for blk in range(4):
            P.dma(C.mixT.v(C.mixT.ap[c * 128:(c + 1) * 128, blk * 1024:(blk + 1) * 1024],
                           tuple(("t", t) for t in range(blk * 8, blk * 8 + 8))), z)
    P.pop()


def phase_dilated(P, C, I, i):
    w_in = I["hyb_w_in"]
    w3 = w_in.ap[i].rearrange("(kc p) n -> p kc n", p=128)
    hT3 = C.hT.ap.rearrange("(c p) t -> p c t", p=128)
    QOFF, KOFF, VOFF = 2056, 2568, 3080
    P.push()
    Wv = P.sb([128, 8, 512], BF16, "Wv")
    P.dma(Wv, w_in.v(w3[:, :, VOFF:VOFF + 512]), eng="pool")
    blkp = Pool(P, 2, [128, 8, 512], BF16, "hblk")
    vtp = Pool(P, 2, [128, 512], BF16, "vt")
    for b in range(8):
        ib = blkp.next()
        P.dma(ib, C.hT.v(hT3[:, :, b * 512:(b + 1) * 512], hT_keys(b)))
        for t4 in range(4):
            tt = b * 4 + t4
            ps = C.PS[t4 % 2]
            for kc in range(8):
                P.mm(ps, ib[:, kc, t4 * 128:(t4 + 1) * 128], Wv[:, kc, :], start=(kc == 0), stop=(kc == 7))
            vt = vtp.next()
            P.copy(vt, ps, eng="act")
            P.dma(C.vd.v(C.vd.ap[tt * 128:(tt + 1) * 128, :], 0), vt)
    P.pop()
    import os
    DM = os.environ.get("DIL_MODE", "all")
    if DM == "v":
        return
    P.push()
    COS = P.sb([128, S], F32, "COS")
    SIN = P.sb([128, S], F32, "SIN")
    P.dma(COS, C.cosT.v(C.cosT.ap, (0, 1, 2, 3)))
    P.dma(SIN, C.sinT.v(C.sinT.ap, (0, 1, 2, 3)))
    dmask = P.sb([128, 256], F32, "dmask")
    P.dma(dmask, I["c_dmask"].v(I["c_dmask"].ap))
    sel = P.sb([128, 64], F32, "sel")
    P.dma(sel, I["c_sel"].v(I["c_sel"].ap))
    Wt = {k: P.sb([128, 8, 128], BF16, "W" + k) for k in ("q", "qs", "k", "ks")}
    blkp = Pool(P, 2, [128, 8, 512], BF16, "hblk")
    RT = {k: P.sb([128, S], BF16, "R" + k) for k in ("q", "k")}
    CM = {(k, r): P.sb([128, S], BF16, "C%s%d" % (k, r)) for k in ("q", "k") for r in (4, 16)}
    OT = [P.sb([128, S], F32, "OT%d" % h) for h in range(2)]
    t1p = Pool(P, 2, [128, 512], F32, "t1")
    t2p = Pool(P, 2, [128, 512], F32, "t2")
    vap = Pool(P, 4, [128, 8, 128], BF16, "vaug")
    for va in vap.tiles:
        P.memset(va, 1.0)
    pep = Pool(P, 3, [128, 256], BF16, "pe")
    ptp = Pool(P, 6, [128, 256], BF16, "pt")
    dnegf = P.sb([128, 256], F32, "dnegf")
    P.dma(dnegf, I["c_dneg"].v(I["c_dneg"].ap))
    dnegb = P.sb([128, 256], BF16, "dnegb")
    P.copy(dnegb, dnegf)
    sbank = [0]
    rdp = Pool(P, 2, [64, 512], F32, "rd")
    onp = Pool(P, 2, [64, 512], BF16, "on")
    slots = [[V(C.PS[3 + 2 * h + sidx].ap[:, 0:128], C.PS[3 + 2 * h + sidx].bufs) for sidx in range(2)] for h in range(2)]
    for hp in range(4):
        for k, off in (("q", QOFF), ("k", KOFF)):
            base = off + hp * 128
            P.dma(Wt[k], w_in.v(w3[:, :, base:base + 128]), eng="pool")
            for hh in range(2):
                P.dma(Wt[k + "s"][:, :, hh * 64:hh * 64 + 32], w_in.v(w3[:, :, base + hh * 64 + 32:base + hh * 64 + 64]), eng="pool")
                P.dma(Wt[k + "s"][:, :, hh * 64 + 32:hh * 64 + 64], w_in.v(w3[:, :, base + hh * 64:base + hh * 64 + 32]), eng="pool")
        for b in range(8):
            ib = blkp.next()
            P.dma(ib, C.hT.v(hT3[:, :, b * 512:(b + 1) * 512], hT_keys(b)))
            cs_ = slice(b * 512, (b + 1) * 512)
            for k in ("q", "k"):
                for kc in range(8):
                    P.mm(C.PS[0], Wt[k][:, kc, :], ib[:, kc, :], start=(kc == 0), stop=(kc == 7))
                for kc in range(8):
                    P.mm(C.PS[1], Wt[k + "s"][:, kc, :], ib[:, kc, :], start=(kc == 0), stop=(kc == 7))
                t1 = t1p.next()
                t2 = t2p.next()
                P.tt(t1, C.PS[0], COS[:, cs_], ALU.mult)
                P.tt(t2, C.PS[1], SIN[:, cs_], ALU.mult)
                P.tt(RT[k][:, cs_], t1, t2, ALU.add, eng="pool")
        if DM == "proj":
            continue
        for k in ("q", "k"):
            for r in (4, 16):
                P.copy(CM[(k, r)].re("p (r u) -> p r u", r=r), RT[k].re("p (u r) -> p r u", r=r), eng="pool")
        if DM == "cm":
            continue
        its = []
        for bi, r in enumerate((1, 4, 16)):
            L = S // r
            nb_ = L // 128
            for rho in range(r):
                for m in range(nb_):
                    for hh in range(2):
                        its.append((bi, r, rho, m, hh, nb_, L))
        ictx = {}

        def st_S(n):
            bi, r, rho, m, hh, nb_, L = its[n]
            d = ictx[n] = {}
            Qs = RT["q"] if r == 1 else CM[("q", r)]
            Ks = RT["k"] if r == 1 else CM[("k", r)]
            if hh == 0:
                va = vap.next()
                st = (128 * m) * r + rho
                P.dma(va[:, :, 0:64], C.vd.v(C.vd.ap[st:st + 127 * r + 1:r, :].rearrange("t (h d) -> t h d", h=8), 0))
                ictx["va"] = va
            d["va"] = ictx["va"]
            nq = d["nq"] = 256 if m < nb_ - 1 else 128
            c0 = rho * L + m * 128
            hb = hh * 64
            sps = d["sps"] = C.PS[sbank[0]]
            sbank[0] = (sbank[0] + 1) % 3
            P.mm(sps[:, 0:nq], Ks[hb:hb + 64, c0:c0 + 128], Qs[hb:hb + 64, c0:c0 + nq], start=True, stop=False)
            P.mm(sps[:, 0:nq], C.identb, dnegb[:, 0:nq], start=False, stop=True)

        def st_X(n):
            d = ictx[n]
            nq = d["nq"]
            pt = d["pt"] = ptp.next()
            P.act(pt[:, 0:nq], d["sps"][:, 0:nq], AF.Exp, scale=0.125)

        def st_V(n):
            bi, r, rho, m, hh, nb_, L = its[n]
            d = ictx.pop(n)
            pt, va, nq = d["pt"], d["va"], d["nq"]
            head = hp * 2 + hh
            sl = slots[hh][m % 2]
            P.mm(sl, va[:, head, :], pt[:, 0:128], start=(m == 0), stop=True)
            dst = OT[hh].re("p (u r) -> p r u", r=r)[:, rho, m * 128:(m + 1) * 128] if r > 1 else OT[hh][:, m * 128:(m + 1) * 128]
            if bi == 0:
                P.copy(dst, sl, eng="dve")
            else:
                P.tt(dst, sl, dst, ALU.add)
            if nq == 256:
                sl2 = slots[hh][(m + 1) % 2]
                P.mm(sl2, va[:, head, :], pt[:, 128:256], start=True, stop=False)

        NI = len(its)
        for n in range(NI + 2):
            if n < NI:
                st_S(n)
            if 0 <= n - 1 < NI:
                st_X(n - 1)
            if 0 <= n - 2 < NI:
                st_V(n - 2)
        if DM in ("br0a", "br0b", "br0b1", "br0b2"):
            continue
        for hh in range(2):
            head = hp * 2 + hh
            for b in range(8):
                cs_ = slice(b * 512, (b + 1) * 512)
                pd = C.PS[hh]
                P.mm(pd[0:64, :], sel, OT[hh][:, cs_])
                rd = rdp.next()
                P.recip(rd, pd[0:64, :])
                on = onp.next()
                P.tt(on, OT[hh][0:64, cs_], rd, ALU.mult)
                r0 = 512 + head * 64
                P.dma(C.mixT.v(C.mixT.ap[r0:r0 + 64, cs_], hT_keys(b)), on)
    P.pop()


def phase_gdn(P, C, I, i):
    w_in = I["hyb_w_in"]
    w3 = w_in.ap[i].rearrange("(kc p) n -> p kc n", p=128)
    hT3 = C.hT.ap.rearrange("(c p) t -> p c t", p=128)
    dnp = I["dnp%d" % i]
    P.push()
    f32t = lambda shape, n: P.sb(shape, F32, n)
    cst = {}
    for nm in ("c_tri", "c_ones", "c_negU", "c_posL", "c_strU", "c_strL"):
        cst[nm] = f32t([128, 128], nm)
        P.dma(cst[nm], I[nm].v(I[nm].ap))
    TRI, ONESf, NEGU, POSL, STRU, STRL = (cst[n] for n in ("c_tri", "c_ones", "c_negU", "c_posL", "c_strU", "c_strL"))
    IDF = C.identf
    onecol = f32t([128, 1], "onecol")
    P.memset(onecol, 1.0)
    NG = f32t([128, 128], "NG")
    P.dma(NG, dnp.v(dnp.ap[0:1, 8:136].partition_broadcast(128)))
    ab = f32t([128, 8], "ab")
    P.dma(ab, dnp.v(dnp.ap[0:1, 0:8].partition_broadcast(128)))
    nA = f32t([128, 4], "nA")
    P.act(nA, ab[:, 0:4], AF.Exp)
    P.ts(nA, nA, -1.0, None, ALU.mult)
    cw = f32t([128, 12, 4], "cw")
    P.dma(cw, I["dncw%d" % i].v(I["dncw%d" % i].ap))
    bank = [0]

    def nb():
        bank[0] = (bank[0] + 1) % 7
        return C.PS[bank[0]]

    Wg8 = P.sb([128, 8, 8], BF16, "Wg8")
    P.dma(Wg8, w_in.v(w3[:, :, 2048:2056]), eng="pool")
    blkp = Pool(P, 2, [128, 8, 512], BF16, "hblk")
    GR = f32t([128, 32, 8], "GR")
    for b in range(8):
        ib = blkp.next()
        P.dma(ib, C.hT.v(hT3[:, :, b * 512:(b + 1) * 512], hT_keys(b)))
        for t4 in range(4):
            ps = nb()
            for kc in range(8):
                P.mm(ps[:, 0:8], ib[:, kc, t4 * 128:(t4 + 1) * 128], Wg8[:, kc, :], start=(kc == 0), stop=(kc == 7))
            P.copy(GR[:, b * 4 + t4, :], ps[:, 0:8], eng="act")
    BETA = f32t([128, 128], "BETA")
    NBETA = f32t([128, 128], "NBETA")
    Gt = f32t([128, 128], "Gt")
    v34 = lambda t: t.re("p (c h) -> p c h", h=4)
    P.copy(v34(BETA), GR[:, :, 0:4], eng="dve")
    P.act(BETA, BETA, AF.Sigmoid)
    P.ts(NBETA, BETA, -1.0, None, ALU.mult)
    P.tt(v34(Gt), GR[:, :, 4:8], b3(ab[:, 4:8], [128, 32, 4], 1), ALU.add)
    P.act(Gt, Gt, AF.Exp)
    P.act(Gt, Gt, AF.Ln, bias=onecol[:, 0:1])
    P.tt(v34(Gt), v34(Gt), b3(nA, [128, 32, 4], 1), ALU.mult)
    GC, GL, EG, EDL, EGL = (f32t([128, 128], n) for n in ("GC", "GL", "EG", "EDL", "EGL"))
    ps = nb()
    P.mm(ps[:, 0:128], TRI, Gt)
    P.copy(GC, ps[:, 0:128], eng="dve")
    ps = nb()
    P.mm(ps[:, 0:128], ONESf, Gt)
    P.copy(GL, ps[:, 0:128], eng="dve")
    P.act(EG, GC, AF.Exp)
    P.act(EGL, GL, AF.Exp)
    P.tt(EDL, GL, GC, ALU.subtract)
    P.act(EDL, EDL, AF.Exp)
    import os
    GM = os.environ.get("GDN_MODE", "all")
    if GM == "gates":
        P.pop()
        return
    Wh = P.sb([128, 8, 3, 128], BF16, "Wh")
    Wz = P.sb([128, 8, 128], BF16, "Wz")
    X = [f32t([128, S + 3], "X%d" % x) for x in range(3)]
    for x in range(3):
        P.memset(X[x][:, 0:3], 0.0)
    sq = P.sb([128, S], BF16, "sq")
    Ycs = [f32t([128, S], "Ycv%d" % i_) for i_ in range(2)]
    QT = P.sb([128, S], BF16, "QT")
    KT = P.sb([128, S], BF16, "KT")
    VT = sq
    KTM = P.sb([128, 32, 128], BF16, "KTM")
    VTM = P.sb([128, 32, 128], BF16, "VTM")
    rnp = Pool(P, 2, [128, 512], F32, "rn")
    Sst = f32t([128, 128], "Sst")
    Sb = P.sb([128, 128], BF16, "Sb")
    NBATCH = 4
    mk = lambda n, dt=F32, k=NBATCH: Pool(P, k, [128, 128], dt, n)
    p_gb, p_bb, p_DT, p_DL, p_tmp, p_tmp2 = mk("gb"), mk("bb"), mk("DT"), mk("DL"), mk("tmpa"), mk("tmpb")
    p_A, p_AT, p_P = mk("A", F32, 2 * NBATCH), mk("AT", F32, 2 * NBATCH), mk("P", F32, 2 * NBATCH)
    p_intra, p_Pb, p_ub, p_Keg, p_WT, p_Kdec, p_qdT, p_EGr = (mk("intra", BF16), mk("Pb", BF16), mk("ub"), mk("Keg", BF16),
                                                              mk("WT", BF16), mk("Kdec", BF16), mk("qdT", BF16), mk("EGr"))
    p_vn, p_zs, p_t, p_ao, p_aoT = mk("vn", BF16, 2), mk("zs", F32, 2), mk("tt", F32, 2), mk("ao", BF16, 2), mk("aoT", BF16, 2)
    p_st = Pool(P, 2, [128, 6], F32, "gst")
    p_mv = Pool(P, 2, [128, 8], F32, "gmv")
    hcp = Pool(P, 2, [128, 8, 128], BF16, "hchunk")
    for hd in range(4):
        for x in range(3):
            P.dma(Wh[:, :, x, :], w_in.v(w3[:, :, x * 512 + hd * 128: x * 512 + (hd + 1) * 128]), eng="pool")
        P.dma(Wz, w_in.v(w3[:, :, 1536 + hd * 128:1536 + (hd + 1) * 128]), eng="pool")
        for b in range(8):
            ib = blkp.next()
            P.dma(ib, C.hT.v(hT3[:, :, b * 512:(b + 1) * 512], hT_keys(b)))
            for x in range(3):
                ps = nb()
                for kc in range(8):
                    P.mm(ps, Wh[:, kc, x, :], ib[:, kc, :], start=(kc == 0), stop=(kc == 7))
                P.copy(X[x][:, 3 + b * 512:3 + (b + 1) * 512], ps, eng="act")
        def convx(x):
            Y = X[x]
            ch = x * 4 + hd
            Yc = Ycs[x % 2]
            P.ts(Yc, Y[:, 0:S], cw[:, ch, 0:1], None, ALU.mult)
            for j in range(1, 4):
                P.stt(Yc, Y[:, j:S + j], cw[:, ch, j:j + 1], Yc, ALU.mult, ALU.add)
            P.act(Yc, Yc, AF.Silu)
            return Yc

        def postx(x, Yc):
            if x < 2:
                P.tt(sq, Yc, Yc, ALU.mult, eng="pool")
                dst = QT if x == 0 else KT
                for b in range(8):
                    cs_ = slice(b * 512, (b + 1) * 512)
                    ps = nb()
                    P.mm(ps, C.onesb, sq[:, cs_])
                    rn = rnp.next()
                    P.ts(rn, ps, RMS_EPS, None, ALU.add)
                    P.act(rn, rn, AF.Sqrt)
                    P.recip(rn, rn)
                    if x == 0:
                        P.stt(dst[:, cs_], Yc[:, cs_], 128 ** -0.5, rn, ALU.mult, ALU.mult)
                    else:
                        P.tt(dst[:, cs_], Yc[:, cs_], rn, ALU.mult)
            else:
                P.copy(VT, Yc, eng="act")

        y0 = convx(0)
        y1 = convx(1)
        postx(0, y0)
        y2 = convx(2)
        postx(1, y1)
        postx(2, y2)
        for src, dstm in ((KT, KTM), (VT, VTM)):
            for g8 in range(4):
                for c8 in range(8):
                    c = g8 * 8 + c8
                    P.tr(C.pst[:, c8 * 128:(c8 + 1) * 128], src[:, c * 128:(c + 1) * 128], C.identb)
                P.copy(dstm[:, g8 * 8:(g8 + 1) * 8, :], C.pst.re("p (c t) -> p c t", c=8), eng="act")
        P.memset(Sst, 0.0)
        P.memset(Sb, 0.0)
        if GM == "stageA":
            continue
        for cb in range(32 // NBATCH):
            chunks = list(range(cb * NBATCH, (cb + 1) * NBATCH))
            st = {}
            for c in chunks:
                d = st[c] = {}
                chx = c * 4 + hd
                col = lambda T_: T_[:, chx:chx + 1]
                cs_ = slice(c * 128, (c + 1) * 128)
                gb, bb = p_gb.next(), p_bb.next()
                P.ts(gb, ONESf, col(Gt), None, ALU.mult)
                P.ts(bb, ONESf, col(BETA), None, ALU.mult)
                gcr = nb()[:, 0:128]
                P.mm(gcr, gb, TRI)
                ber = nb()[:, 0:128]
                P.mm(ber, bb, IDF)
                DT, DL, tmp, tmp2 = p_DT.next(), p_DL.next(), p_tmp.next(), p_tmp2.next()
                P.stt(tmp, gcr, col(GC), NEGU, ALU.subtract, ALU.add)
                P.act(DT, tmp, AF.Exp)
                P.stt(tmp2, gcr, col(GC), POSL, ALU.subtract, ALU.add)
                P.act(DL, tmp2, AF.Exp, scale=-1.0)
                EGr = p_EGr.next()
                P.act(EGr, gcr, AF.Exp)
                qdT = d["qdT"] = p_qdT.next()
                P.tt(qdT, QT[:, cs_], EGr, ALU.mult, eng="pool")
                gps = nb()[:, 0:128]
                P.mm(gps, KT[:, cs_], KT[:, cs_])
                qkps = nb()[:, 0:128]
                P.mm(qkps, KT[:, cs_], QT[:, cs_])
                intra = d["intra"] = p_intra.next()
                P.tt(intra, qkps, DT, ALU.mult)
                P.tt(DT, DT, STRU, ALU.mult, eng="pool")
                P.tt(DL, DL, STRL, ALU.mult, eng="pool")
                A, AT = p_A.next(), p_AT.next()
                P.stt(A, gps, col(BETA), DT, ALU.mult, ALU.mult)
                P.tt(tmp, ber, DL, ALU.mult)
                P.tt(AT, gps, tmp, ALU.mult)
                Pm = p_P.next()
                P.tt(Pm, IDF, A, ALU.subtract, eng="pool")
                d["A"], d["AT"], d["P"] = A, AT, Pm
                Keg, Kdec = d["Keg"], d["Kdec"] = p_Keg.next(), p_Kdec.next()
                P.ts(Keg, KTM[:, c, :], col(EG), None, ALU.mult)
                P.ts(Kdec, KTM[:, c, :], col(EDL), None, ALU.mult)
            if GM == "s1":
                continue
            for lev in range(6):
                for c in chunks:
                    d = st[c]
                    psa, psb = nb()[:, 0:128], nb()[:, 0:128]
                    P.mm(psa, d["AT"], d["A"])
                    P.mm(psb, d["A"], d["AT"])
                    A2, AT2 = p_A.next(), p_AT.next()
                    P.copy(A2, psa, eng="act")
                    P.copy(AT2, psb, eng="dve")
                    d["A"], d["AT"] = A2, AT2
                for c in chunks:
                    d = st[c]
                    psc = nb()[:, 0:128]
                    P.mm(psc, d["AT"], d["P"])
                    P2 = p_P.next()
                    P.tt(P2, psc, d["P"], ALU.add)
                    d["P"] = P2
            if GM.startswith("neu"):
                continue
            for c in chunks:
                d = st[c]
                d["Pb"] = p_Pb.next()
                P.copy(d["Pb"], d["P"], eng="act")
            for c in chunks:
                d = st[c]
                chx = c * 4 + hd
                col = lambda T_: T_[:, chx:chx + 1]
                psu, psw = nb()[:, 0:128], nb()[:, 0:128]
                P.mm(psu, d["Pb"], VTM[:, c, :])
                P.mm(psw, d["Keg"], d["Pb"])
                ub = d["ub"] = p_ub.next()
                P.ts(ub, psu, col(BETA), None, ALU.mult)
                WT = d["WT"] = p_WT.next()
                P.copy(WT, psw, eng="act")
            if GM == "uw":
                continue
            for c in chunks:
                d = st[c]
                chx = c * 4 + hd
                col = lambda T_: T_[:, chx:chx + 1]
                cs_ = slice(c * 128, (c + 1) * 128)
                ps1 = nb()[:, 0:128]
                P.mm(ps1, d["WT"], Sb)
                vn = p_vn.next()
                P.stt(vn, ps1, col(NBETA), d["ub"], ALU.mult, ALU.add)
                pso = nb()[:, 0:128]
                P.mm(pso, d["qdT"], Sb, start=True, stop=False)
                P.mm(pso, d["intra"], vn, start=False, stop=True)
                ps3 = nb()[:, 0:128]
                P.mm(ps3, d["Kdec"], vn)
                P.stt(Sb, Sst, col(EGL), ps3, ALU.mult, ALU.add)
                P.stt(Sst, Sst, col(EGL), ps3, ALU.mult, ALU.add)
                hc = hcp.next()
                P.dma(hc, C.hT.v(hT3[:, :, cs_], ("t", c)))
                psz = nb()[:, 0:128]
                for kc in range(8):
                    P.mm(psz, hc[:, kc, :], Wz[:, kc, :], start=(kc == 0), stop=(kc == 7))
                zs = p_zs.next()
                P.act(zs, psz, AF.Silu)
                gst, gmv = p_st.next(), p_mv.next()
                P.add("dve", lambda e, o=gst, i_=pso: e.bn_stats(o.ap, i_.ap), [pso], [gst])
                P.add("dve", lambda e, o=gmv[:, 0:2], i_=gst: e.bn_aggr(o.ap, i_.ap), [gst], [gmv])
                P.stt(gmv[:, 2:3], gmv[:, 0:1], gmv[:, 0:1], gmv[:, 1:2], ALU.mult, ALU.add)
                P.ts(gmv[:, 2:3], gmv[:, 2:3], RMS_EPS, None, ALU.add)
                P.act(gmv[:, 3:4], gmv[:, 2:3], AF.Sqrt)
                P.recip(gmv[:, 4:5], gmv[:, 3:4])
                t_ = p_t.next()
                P.stt(t_, pso, gmv[:, 4:5], NG, ALU.mult, ALU.mult)
                ao = p_ao.next()
                P.tt(ao, t_, zs, ALU.mult, eng="pool")
                P.tr(C.pst[:, 0:128], ao, C.identb)
                aoT = p_aoT.next()
                P.copy(aoT, C.pst[:, 0:128], eng="act")
                P.dma(C.mixT.v(C.mixT.ap[hd * 128:(hd + 1) * 128, cs_], ("t", c)), aoT)
    P.pop()
```

```python
import numpy as np
from contextlib import ExitStack
import concourse.bass as bass
import concourse.mybir as mybir
from concourse.bass_utils import run_bass_kernel_spmd

F32 = mybir.dt.float32
BF16 = mybir.dt.bfloat16
I32 = mybir.dt.int32
AF = mybir.ActivationFunctionType
ALU = mybir.AluOpType
AX = mybir.AxisListType

NDMA = 16
NSDMA = 8


def is_dma(t):
    return t.startswith("dma") or t.startswith("sdma")


class Buf:
    __slots__ = ("name", "w", "r")

    def __init__(self, name=""):
        self.name = name
        self.w = None
        self.r = []


class V:
    def __init__(self, ap, bufs):
        self.ap = ap
        self.bufs = bufs if isinstance(bufs, tuple) else (bufs,)

    def __getitem__(self, k):
        return V(self.ap[k], self.bufs)

    def re(self, s, **kw):
        return V(self.ap.rearrange(s, **kw), self.bufs)

    def bc(self, dt):
        return V(self.ap.bitcast(dt), self.bufs)

    def bcast(self, shape):
        return V(self.ap.to_broadcast(shape), self.bufs)


class Op:
    __slots__ = ("eng", "fn", "deps", "track", "tidx", "inc", "val")


class Prog:
    ENG = ("pe", "act", "dve", "pool", "sp")
    BLK = {"pe": "tensor", "act": "scalar", "dve": "vector", "pool": "gpsimd", "sp": "sync"}

    def __init__(self, nc):
        self.nc = nc
        self.es = ExitStack()
        self.ops = {e: [] for e in self.ENG}
        self.tracks = {e: [] for e in self.ENG}
        for i in range(NDMA):
            self.tracks["dma%d" % i] = []
        for i in range(NSDMA):
            self.tracks["sdma%d" % i] = []
        self.dma_rr = 0
        self.sdma_rr = 0
        self.seen = {e: {} for e in self.ENG}
        self.nalloc = 0
        self.out_ops = []
        self.stack = [self.es]

    def sb(self, shape, dt, name=None):
        self.nalloc += 1
        name = name or ("t%d" % self.nalloc)
        t = self.stack[-1].enter_context(self.nc.sbuf_tensor(name + "_%d" % self.nalloc, list(shape), dt))
        return V(t.ap() if hasattr(t, "ap") and callable(t.ap) else t[:], Buf(name))

    def ps(self, shape, dt=F32, name=None):
        self.nalloc += 1
        name = name or ("p%d" % self.nalloc)
        t = self.stack[-1].enter_context(self.nc.psum_tensor(name + "_%d" % self.nalloc, list(shape), dt))
        return V(t.ap() if hasattr(t, "ap") and callable(t.ap) else t[:], Buf(name))

    def dram(self, name, shape, dt, kind="Internal"):
        t = self.nc.dram_tensor(name, list(shape), dt, kind=kind)
        return DT(t.ap(), name)


    def push(self):
        self.stack.append(ExitStack())

    def pop(self):
        self.barrier()
        self.stack.pop().close()

    def barrier(self):
        last = {t: len(l) - 1 for t, l in self.tracks.items() if l}
        for eng in self.ENG:
            op = Op()
            op.eng = eng
            op.fn = None
            op.inc = False
            op.val = None
            op.track = eng
            op.tidx = len(self.tracks[eng])
            op.deps = {}
            seen = self.seen[eng]
            for t, i in last.items():
                if seen.get(t, -1) < i:
                    seen[t] = i
                    op.deps[t] = i
                    self.tracks[t][i].inc = True
            self.ops[eng].append(op)

    def add(self, eng, fn, reads, writes, dma=False):
        op = Op()
        op.eng = eng
        op.fn = fn
        op.inc = dma
        op.val = None
        if dma and eng == "pool":
            tr = "sdma%d" % self.sdma_rr
            self.sdma_rr = (self.sdma_rr + 1) % NSDMA
        elif dma:
            tr = "dma%d" % self.dma_rr
            self.dma_rr = (self.dma_rr + 1) % NDMA
        else:
            tr = eng
        op.track = tr
        op.tidx = len(self.tracks[tr])
        deps = {}

        def need(t, i):
            if deps.get(t, -1) < i:
                deps[t] = i

        for v in reads:
            for b in v.bufs:
                if b.w is not None:
                    t, i = b.w
                    if t == tr and eng == "pe":
                        continue
                    need(t, i)
        for v in writes:
            for b in v.bufs:
                if b.w is not None:
                    t, i = b.w
                    if not (t == tr and eng == "pe"):
                        need(t, i)
                for (t, i) in b.r:
                    if not (t == tr and eng == "pe"):
                        need(t, i)
        if dma and op.tidx > 0:
            need(tr, op.tidx - 1)
        seen = self.seen[eng]
        op.deps = {}
        for t, i in deps.items():
            if seen.get(t, -1) < i:
                seen[t] = i
                op.deps[t] = i
                self.tracks[t][i].inc = True
        self.tracks[tr].append(op)
        self.ops[eng].append(op)
        for v in reads:
            for b in v.bufs:
                b.r.append((tr, op.tidx))
        for v in writes:
            for b in v.bufs:
                b.w = (tr, op.tidx)
                b.r = []
        return op

    def mm(self, out, lhsT, rhs, start=True, stop=True):
        return self.add("pe", lambda e: e.matmul(out.ap, lhsT.ap, rhs.ap, start=start, stop=stop),
                        [lhsT, rhs] + ([] if start else [out]), [out])

    def tr(self, out, in_, ident):
        return self.add("pe", lambda e: e.transpose(out.ap, in_.ap, ident.ap), [in_, ident], [out])

    def act(self, out, in_, func, bias=None, scale=None, accum=None, eng="act"):
        kw = {}
        rd = [in_]
        wr = [out]
        if bias is not None:
            if isinstance(bias, V):
                kw["bias"] = bias.ap
                rd.append(bias)
            else:
                kw["bias"] = bias
        if scale is not None:
            if isinstance(scale, V):
                kw["scale"] = scale.ap
                rd.append(scale)
            else:
                kw["scale"] = scale
        if accum is not None:
            kw["accum_out"] = accum.ap
            wr.append(accum)
        return self.add("act", lambda e: e.activation(out.ap, in_.ap, func, **kw), rd, wr)

    def tt(self, out, a, b, op, eng="dve"):
        return self.add(eng, lambda e: e.tensor_tensor(out.ap, a.ap, b.ap, op), [a, b], [out])

    def ts(self, out, a, s1, s2, op0, op1=None, accum=None, eng="dve"):
        rd = [a]
        wr = [out]
        s1a = s1.ap if isinstance(s1, V) else s1
        s2a = s2.ap if isinstance(s2, V) else s2
        if isinstance(s1, V):
            rd.append(s1)
        if isinstance(s2, V):
            rd.append(s2)
        kw = {}
        if op1 is not None:
            kw["op1"] = op1
        if accum is not None:
            kw["accum_out"] = accum.ap
            wr.append(accum)
        return self.add(eng, lambda e: e.tensor_scalar(out.ap, a.ap, s1a, s2a, op0, **kw), rd, wr)

    def stt(self, out, a, s, b, op0, op1, eng="dve"):
        rd = [a, b]
        sa = s.ap if isinstance(s, V) else s
        if isinstance(s, V):
            rd.append(s)
        return self.add(eng, lambda e: e.scalar_tensor_tensor(out.ap, a.ap, sa, b.ap, op0, op1), rd, [out])

    def copy(self, out, in_, eng="dve"):
        if eng == "act":
            return self.add("act", lambda e: e.copy(out.ap, in_.ap), [in_], [out])
        return self.add(eng, lambda e: e.tensor_copy(out.ap, in_.ap), [in_], [out])

    def memset(self, out, val, eng="pool"):
        return self.add(eng, lambda e: e.memset(out.ap, val), [], [out])

    def recip(self, out, in_):
        return self.add("dve", lambda e: e.reciprocal(out.ap, in_.ap), [in_], [out])

    def scan(self, out, d0, d1, init, op0, op1):
        rd = [d0, d1]
        ia = init.ap if isinstance(init, V) else init
        if isinstance(init, V):
            rd.append(init)
        return self.add("dve", lambda e: e.tensor_tensor_scan(out.ap, d0.ap, d1.ap, ia, op0, op1), rd, [out])

    def dma(self, out, in_, eng="sp", is_out=False, **kw):
        op = self.add(eng, lambda e: e.dma_start(out=out.ap, in_=in_.ap, **kw), [in_], [out], dma=True)
        if is_out:
            self.out_ops.append(op)
        return op

    def emit(self):
        nc = self.nc
        fin = Op()
        fin.eng = "sp"
        fin.fn = None
        fin.inc = False
        fin.track = "sp"
        fin.tidx = len(self.tracks["sp"])
        fin.deps = {}
        for tname, lst in self.tracks.items():
            if is_dma(tname) and lst:
                fin.deps[tname] = len(lst) - 1
        self.ops["sp"].append(fin)
        for tname, lst in self.tracks.items():
            c = 0
            for op in lst:
                if is_dma(tname):
                    c += 16
                    op.val = c
                else:
                    if op.inc:
                        c += 1
                    op.val = c
        sems = {}
        for tname, lst in self.tracks.items():
            if lst:
                sems[tname] = self.es.enter_context(nc.semaphore("s_" + tname))
        block = self.es.enter_context(nc.Block())
        tracks = self.tracks
        for eng in self.ENG:
            ops = self.ops[eng]
            if not ops:
                continue

            def body(e, ops=ops):
                for op in ops:
                    for t, i in op.deps.items():
                        e.wait_ge(sems[t], tracks[t][i].val)
                    if op.fn is None:
                        continue
                    ins = op.fn(e)
                    if is_dma(op.track):
                        ins.then_inc(sems[op.track], 16)
                    elif op.inc:
                        ins.then_inc(sems[op.track], 1)

            getattr(block, self.BLK[eng])(body)
        self.es.close()


class DT:
    def __init__(self, ap, name):
        self.ap = ap
        self.name = name
        self.bufs = {}

    def v(self, ap, keys=(0,)):
        if not isinstance(keys, (tuple, list)):
            keys = (keys,)
        bs = []
        for k in keys:
            if k not in self.bufs:
                self.bufs[k] = Buf("%s[%s]" % (self.name, k))
            bs.append(self.bufs[k])
        return V(ap, tuple(bs))


class Pool:
    def __init__(self, P, n, shape, dt, name, psum=False):
        self.tiles = [(P.ps if psum else P.sb)(shape, dt, name="%s%d" % (name, i)) for i in range(n)]
        self.i = 0

    def next(self):
        t = self.tiles[self.i]
        self.i = (self.i + 1) % len(self.tiles)
        return t


import math

S = 4096
D = 1024
NT = 32
FF = 2816
ALPHA = 8 ** 0.25
LN_EPS = 1e-5
RMS_EPS = 1e-6
TWO_PI = 2.0 * math.pi


def host_consts():
    c = {}
    i = np.arange(128)
    c["c_ident"] = np.eye(128, dtype=np.float32)
    c["c_tri"] = (i[:, None] <= i[None, :]).astype(np.float32)
    c["c_ones"] = np.ones((128, 128), np.float32)
    c["c_negU"] = np.where(i[None, :] >= i[:, None], 0.0, -30000.0).astype(np.float32)
    c["c_posL"] = np.where(i[None, :] <= i[:, None], 0.0, 30000.0).astype(np.float32)
    c["c_strU"] = (i[None, :] > i[:, None]).astype(np.float32)
    c["c_strL"] = (i[None, :] < i[:, None]).astype(np.float32)
    dm = np.zeros((128, 256), np.float32)
    dm[:, :128] = (i[:, None] <= i[None, :])
    dm[:, 128:] = (i[:, None] >= i[None, :])
    c["c_dmask"] = dm
    c["c_dneg"] = ((dm - 1.0) * 30000.0).astype(np.float32)
    invf = (10000.0 ** (-np.arange(0, 64, 2, dtype=np.float32) / 64)).astype(np.float32)
    r = np.arange(128)
    c["c_invf"] = (invf[r % 32] / np.float32(TWO_PI)).astype(np.float32).reshape(128, 1)
    c["c_sgn"] = np.where((r % 64) < 32, -1.0, 1.0).astype(np.float32).reshape(128, 1)
    c["c_iota"] = np.broadcast_to(np.arange(64, dtype=np.float32)[None, :], (128, 64)).copy()
    sel = np.zeros((128, 64), np.float32)
    sel[64 + np.arange(64), np.arange(64)] = 1.0
    c["c_sel"] = sel
    return c


def host_layer_inputs(inp):
    o = {}
    for l in range(2):
        a_re, a_im, ldt = inp["s5_a_re"][l], inp["s5_a_im"][l], inp["s5_log_dt"][l]
        col = np.stack([a_re, a_im, np.repeat(ldt[:, None], 64, 1)], 0)
        col = col.reshape(3, 32, 2, 64).transpose(2, 3, 0, 1).reshape(128, 3, 32)
        o["s5col%d" % l] = np.ascontiguousarray(col, np.float32)
        o["s5row%d" % l] = np.ascontiguousarray(np.stack([a_re.reshape(-1), a_im.reshape(-1), np.repeat(ldt, 64)], 0), np.float32)
        for nm, src in (("re", inp["s5_b_re"][l]), ("im", inp["s5_b_im"][l])):
            bp = np.zeros((4, 2, 16, 32, 2, 64), np.float32)
            sp = src.reshape(8, 4, 2, 64, 16)
            for q in range(4):
                for gl in range(2):
                    bp[q, gl, :, q::4, gl, :] = sp[:, q, gl].transpose(2, 0, 1)
            o["s5b%s%d" % (nm, l)] = bp.reshape(128, 32 * 128)
        for nm, src in (("re", inp["s5_c_re"][l]), ("im", inp["s5_c_im"][l])):
            cp = np.zeros((2, 64, 32, 4, 2, 16), np.float32)
            sp = src.reshape(8, 4, 2, 16, 64)
            for q in range(4):
                for gl in range(2):
                    cp[gl, :, q::4, q, gl, :] = sp[:, q, gl].transpose(2, 0, 1)
            o["s5c%s%d" % (nm, l)] = cp.reshape(128, 32 * 128)
        o["s5d%d" % l] = np.ascontiguousarray(inp["s5_d"][l].reshape(8, 128).T, np.float32)
        o["dncw%d" % l] = np.ascontiguousarray(inp["dn_conv_w"][l].T.reshape(12, 128, 4).transpose(1, 0, 2), np.float32)
    return o


class Ctx:
    pass


def load_w(P, dst, src_ap, KC, eng="pool"):
    for kc in range(KC):
        P.dma(dst[:, kc, :], src_ap(kc), eng=eng)


def load_h(P, C, tt, h_src):
    ht = C.p_h.next()
    P.dma(ht, h_src.v(h_src.ap[tt * 128:(tt + 1) * 128, :], tt))
    return ht


def ln_epilogue(P, C, lnp, tt, sub_halves, ht):
    z = C.p_z.next()
    st = C.p_st.next()
    for hf in range(2):
        P.stt(z[:, hf * 512:(hf + 1) * 512], ht[:, hf * 512:(hf + 1) * 512], ALPHA, sub_halves[hf], ALU.mult, ALU.add)
        zz = z[:, hf * 512:(hf + 1) * 512]
        P.add("dve", lambda e, o=st[:, hf, :], i=zz: e.bn_stats(o.ap, i.ap), [zz], [st])
    mv = C.p_mv.next()
    P.add("dve", lambda e, o=mv[:, 0:2], i=st.re("p a b -> p (a b)"): e.bn_aggr(o.ap, i.ap), [st], [mv])
    P.ts(mv[:, 2:3], mv[:, 1:2], LN_EPS, None, ALU.add)
    P.act(mv[:, 3:4], mv[:, 2:3], AF.Sqrt)
    P.recip(mv[:, 4:5], mv[:, 3:4])
    zn = C.p_zn.next()
    P.ts(zn, z, mv[:, 0:1], mv[:, 4:5], ALU.subtract, ALU.mult)
    hn = C.p_hn.next()
    P.tt(zn, zn, lnp[0], ALU.mult, eng="pool")
    P.tt(hn, zn, lnp[1], ALU.add, eng="dve")
    P.dma(C.out.v(C.out.ap[tt * 128:(tt + 1) * 128, :], tt), hn, is_out=True)
    hb = C.p_hb.next()
    P.copy(hb, hn, eng="act")
    return (tt, hb)


def ln_epilogue_b(P, C, pend):
    if pend is None:
        return
    tt, hb = pend
    pt = C.pst
    for c in range(8):
        P.tr(pt[:, c * 128:(c + 1) * 128], hb[:, c * 128:(c + 1) * 128], C.identb)
    hts = C.p_hts.next()
    P.copy(hts, pt.re("p (c t) -> p c t", c=8), eng="act")
    P.dma(C.hT.v(C.hT.ap.rearrange("(c p) t -> p c t", p=128)[:, :, tt * 128:(tt + 1) * 128], ("t", tt)), hts)


def ln_pools(P, C):
    C.p_h = Pool(P, 8, [128, 1024], F32, "h")
    C.p_z = Pool(P, 2, [128, 1024], F32, "z")
    C.p_zn = Pool(P, 2, [128, 1024], F32, "zn")
    C.p_hn = Pool(P, 3, [128, 1024], F32, "hn")
    C.p_hb = Pool(P, 5, [128, 1024], BF16, "hb")
    C.p_hts = Pool(P, 2, [128, 8, 128], BF16, "hts")
    C.p_st = Pool(P, 2, [128, 2, 6], F32, "st")
    C.p_mv = Pool(P, 2, [128, 8], F32, "mv")


def load_ln(P, C, g_ap, b_ap):
    G = P.sb([128, 1024], F32, "lng")
    B = P.sb([128, 1024], F32, "lnb")
    P.dma(G, g_ap)
    P.dma(B, b_ap)
    return (G, B)


def hT_keys(b):
    return tuple(("t", t) for t in range(b * 4, b * 4 + 4))


def phase_prep(P, C, I):
    import os
    MODE = os.environ.get("PREP_MODE", "all")
    P.push()
    xs = Pool(P, 2, [128, 1024], F32, "xs")
    xb = Pool(P, 2, [128, 1024], BF16, "xb")
    hts = Pool(P, 2, [128, 8, 128], BF16, "hts")
    for tt in range(NT if MODE in ('all', 'A') else 0):
        x_t = xs.next()
        P.dma(x_t, I["x"].v(I["x"].ap[tt * 128:(tt + 1) * 128, :], tt))
        b_t = xb.next()
        P.copy(b_t, x_t, eng="dve")
        for c in range(8):
            P.tr(C.pst[:, c * 128:(c + 1) * 128], b_t[:, c * 128:(c + 1) * 128], C.identb)
        h_t = hts.next()
        P.copy(h_t, C.pst.re("p (c t) -> p c t", c=8), eng="act")
        P.dma(C.hT.v(C.hT.ap.rearrange("(c p) t -> p c t", p=128)[:, :, tt * 128:(tt + 1) * 128], ("t", tt)), h_t)
    for mt in range(2 if MODE in ('all', 'A2') else 0):
        x_t = xs.next()
        P.dma(x_t, I["mem"].v(I["mem"].ap[mt * 128:(mt + 1) * 128, :], mt))
        b_t = xb.next()
        P.copy(b_t, x_t, eng="dve")
        for c in range(8):
            P.tr(C.pst[:, c * 128:(c + 1) * 128], b_t[:, c * 128:(c + 1) * 128], C.identb)
        h_t = hts.next()
        P.copy(h_t, C.pst.re("p (c t) -> p c t", c=8), eng="act")
        P.dma(C.memT.v(C.memT.ap.rearrange("(c p) t -> p c t", p=128)[:, :, mt * 128:(mt + 1) * 128], 0), h_t)
    invf = P.sb([128, 1], F32, "invf")
    sgn = P.sb([128, 1], F32, "sgn")
    P.dma(invf, I["c_invf"].v(I["c_invf"].ap))
    P.dma(sgn, I["c_sgn"].v(I["c_sgn"].ap))
    W = 1024
    pi_ = P.sb([128, W], I32, "posi")
    pf = P.sb([128, W], F32, "posf")
    tu = P.sb([128, W], F32, "tu")
    f = P.sb([128, W], F32, "f")
    tb = P.sb([128, W], F32, "tb")
    scr = frac_scratch(P, [128, W])
    for cq in range(S // W if MODE in ('all', 'B') else 0):
        P.dma(pi_, I["pos"].v(I["pos"].ap[:, cq * W:(cq + 1) * W].partition_broadcast(128)))
        P.copy(pf, pi_, eng="dve")
        P.ts(tu, pf, invf[:, 0:1], None, ALU.mult)
        for which, shift, dst in (("sin", 0.0, C.sinT), ("cos", 0.25, C.cosT)):
            frac_wrap(P, f, tu, shift, scr)
            P.act(tb, f, AF.Sin, scale=TWO_PI)
            if which == "sin":
                P.ts(tb, tb, sgn[:, 0:1], None, ALU.mult)
            P.dma(dst.v(dst.ap[:, cq * W:(cq + 1) * W], cq), tb)
    P.pop()


def frac_scratch(P, shape):
    return (P.sb(shape, I32, "fw_ki"), P.sb(shape, F32, "fw_kf"), P.sb(shape, F32, "fw_m"))


def frac_wrap(P, f, tu, shift, scr):
    ki, kf, m = scr
    if shift != 0.0:
        P.ts(f, tu, shift, None, ALU.add)
        src = f
    else:
        src = tu
    P.copy(ki, src, eng="dve")
    P.copy(kf, ki, eng="dve")
    P.tt(f, src, kf, ALU.subtract)
    P.ts(m, f, 0.5, None, ALU.is_gt)
    P.tt(f, f, m, ALU.subtract)
    P.ts(m, f, -0.5, None, ALU.is_lt)
    P.tt(f, f, m, ALU.add)


def phase_proj_ln(P, C, inT, K, W_ap, lnp_aps, h_src, gate_ap=None, Wpre=None):
    P.push()
    KC = K // 128
    if Wpre is not None:
        Wb = Wpre
    else:
        Wb = P.sb([128, KC, 1024], BF16, "Wb")
        load_w(P, Wb, W_ap, KC)
    Wg = None
    if gate_ap is not None:
        Wg = P.sb([128, KC, 1024], BF16, "Wg")
        load_w(P, Wg, gate_ap, KC)
    lnp = load_ln(P, C, *lnp_aps)
    ln_pools(P, C)
    blk = Pool(P, 2, [128, KC, 512], BF16, "inblk")
    sig = Pool(P, 2, [128, 1024], F32, "sig") if Wg is not None else None
    pendq = []
    DEPTH = 3 if KC <= 8 else 1

    def prefetch(b):
        ib_ = blk.next()
        P.dma(ib_, inT.v(inT.ap.rearrange("(c p) t -> p c t", p=128)[:, :, b * 512:(b + 1) * 512], hT_keys(b)))
        return ib_, [load_h(P, C, b * 4 + t4, h_src) for t4 in range(4)]

    nxt = prefetch(0)
    for b in range(8):
        ib, hts_ = nxt
        if b < 7:
            nxt = prefetch(b + 1)
        for t4 in range(4):
            tt = b * 4 + t4
            par = (tt % 2) * 2 if Wg is None else 0
            sub = [C.PS[par], C.PS[par + 1]]
            for hf in range(2):
                for kc in range(KC):
                    P.mm(sub[hf], ib[:, kc, t4 * 128:(t4 + 1) * 128], Wb[:, kc, hf * 512:(hf + 1) * 512], start=(kc == 0), stop=(kc == KC - 1))
            if Wg is not None:
                gp = [C.PS[2], C.PS[3]]
                sg = sig.next()
                for hf in range(2):
                    for kc in range(KC):
                        P.mm(gp[hf], ib[:, kc, t4 * 128:(t4 + 1) * 128], Wg[:, kc, hf * 512:(hf + 1) * 512], start=(kc == 0), stop=(kc == KC - 1))
                    P.act(sg[:, hf * 512:(hf + 1) * 512], gp[hf], AF.Sigmoid)
                    P.tt(sg[:, hf * 512:(hf + 1) * 512], sub[hf], sg[:, hf * 512:(hf + 1) * 512], ALU.mult)
                sub = [sg[:, 0:512], sg[:, 512:1024]]
            if len(pendq) >= DEPTH:
                ln_epilogue_b(P, C, pendq.pop(0))
            pendq.append(ln_epilogue(P, C, lnp, tt, sub, hts_[t4]))
    for pe_ in pendq:
        ln_epilogue_b(P, C, pe_)
    P.pop()


def phase_ffn_up(P, C, wg_ap, wu_ap, Wgpre=None, after_loads=None):
    P.push()
    if Wgpre is not None:
        Wg = Wgpre
    else:
        Wg = P.sb([128, 8, FF], BF16, "Wg")
        load_w(P, Wg, wg_ap, 8)
    Wu = P.sb([128, 8, FF], BF16, "Wu")
    load_w(P, Wu, wu_ap, 8)
    if after_loads is not None:
        after_loads()
    blk = Pool(P, 2, [128, 8, 512], BF16, "hblk")
    ab = Pool(P, 2, [128, 22, 512], BF16, "ablk")
    sgp = Pool(P, 2, [128, 512], F32, "sg")
    def prefetch(b):
        ib_ = blk.next()
        P.dma(ib_, C.hT.v(C.hT.ap.rearrange("(c p) t -> p c t", p=128)[:, :, b * 512:(b + 1) * 512], hT_keys(b)))
        return ib_

    nxt = prefetch(0)
    for b in range(8):
        ib = nxt
        if b < 7:
            nxt = prefetch(b + 1)
        a = ab.next()
        for j in range(22):
            pg = C.PS[(j % 2) * 2]
            pu = C.PS[(j % 2) * 2 + 1]
            for kc in range(8):
                P.mm(pg, Wg[:, kc, j * 128:(j + 1) * 128], ib[:, kc, :], start=(kc == 0), stop=(kc == 7))
            for kc in range(8):
                P.mm(pu, Wu[:, kc, j * 128:(j + 1) * 128], ib[:, kc, :], start=(kc == 0), stop=(kc == 7))
            sg = sgp.next()
            P.act(sg, pg, AF.Silu)
            P.tt(a[:, j, :], pu, sg, ALU.mult)
        P.dma(C.aT.v(C.aT.ap.rearrange("(c p) t -> p c t", p=128)[:, :, b * 512:(b + 1) * 512], hT_keys(b)), a)
    P.pop()


def phase_cross(P, C, I, layer, lnp_aps, h_src, after_loads=None):
    P.push()
    wq, wk, wv, wo = (I[n] for n in ("xq_w", "xk_w", "xv_w", "xo_w"))
    rows = lambda w: (lambda kc: w.v(w.ap[layer, kc * 128:(kc + 1) * 128, :]))
    Wq = P.sb([128, 8, 1024], BF16, "Wq")
    Wo = P.sb([128, 8, 1024], BF16, "Wo")
    Wk = P.sb([128, 8, 1024], BF16, "Wk")
    load_w(P, Wk, rows(wk), 8)
    mT = P.sb([128, 8, 256], BF16, "mT")
    P.dma(mT, C.memT.v(C.memT.ap.rearrange("(c p) t -> p c t", p=128), 0))
    kT = P.sb([128, 8, 256], BF16, "kT")
    import os
    if os.environ.get("CROSS_MODE", "all") == "pre0":
        dbg = P.sb([128, 1024], F32, "dbg")
        P.copy(dbg, Wk[:, 3, :])
        P.dma(C.out.v(C.out.ap[0:128, :], 0), dbg, is_out=True)
        P.pop()
        return
    for oc in range(8):
        ps = C.PS[oc % 2]
        for kc in range(8):
            P.mm(ps[:, 0:256], Wk[:, kc, oc * 128:(oc + 1) * 128], mT[:, kc, :], start=(kc == 0), stop=(kc == 7))
        P.copy(kT[:, oc, :], ps[:, 0:256], eng="act")
    load_w(P, Wk, rows(wv), 8)
    Vm = P.sb([128, 2, 1024], BF16, "Vm")
    for mc in range(2):
        for hf in range(2):
            ps = C.PS[hf]
            for kc in range(8):
                P.mm(ps, mT[:, kc, mc * 128:(mc + 1) * 128], Wk[:, kc, hf * 512:(hf + 1) * 512], start=(kc == 0), stop=(kc == 7))
            P.copy(Vm[:, mc, hf * 512:(hf + 1) * 512], ps, eng="act")
    import os
    CM = os.environ.get("CROSS_MODE", "all")
    if CM == "pre":
        P.pop()
        return
    load_w(P, Wq, rows(wq), 8)
    load_w(P, Wo, rows(wo), 8)
    if after_loads is not None:
        after_loads()
    lnp = load_ln(P, C, *lnp_aps)
    ln_pools(P, C)
    onesb = C.onesb
    blk = Pool(P, 2, [128, 8, 512], BF16, "hblk")
    qTp = Pool(P, 1, [128, 8, 512], BF16, "qT")
    pTp = Pool(P, 2, [128, 2, 512], BF16, "pT")
    rdp = Pool(P, 2, [128, 512], F32, "rden")
    oTp = Pool(P, 2, [128, 8, 512], BF16, "oT")
    pendq = []

    def prefetch(b):
        ib_ = blk.next()
        P.dma(ib_, C.hT.v(C.hT.ap.rearrange("(c p) t -> p c t", p=128)[:, :, b * 512:(b + 1) * 512], hT_keys(b)))
        return ib_, [load_h(P, C, b * 4 + t4, h_src) for t4 in range(4)]

    nxt = prefetch(0)
    for b in range(8):
        ib, hts_ = nxt
        if b < 7:
            nxt = prefetch(b + 1)
        qT = qTp.next()
        for oc in range(8):
            ps = C.PS[2 + oc % 2]
            for kc in range(8):
                P.mm(ps, Wq[:, kc, oc * 128:(oc + 1) * 128], ib[:, kc, :], start=(kc == 0), stop=(kc == 7))
            P.copy(qT[:, oc, :], ps, eng="act")
        oT = oTp.next()
        if CM == "q":
            continue
        for hh in range(4):
            pT = pTp.next()
            for mc in range(2):
                ps = C.PS[2 + mc]
                for c in range(2):
                    P.mm(ps, kT[:, hh * 2 + c, mc * 128:(mc + 1) * 128], qT[:, hh * 2 + c, :], start=(c == 0), stop=(c == 1))
                P.act(pT[:, mc, :], ps, AF.Exp, scale=1.0 / 16.0)
            pd = C.PS[4]
            for mc in range(2):
                P.mm(pd, onesb, pT[:, mc, :], start=(mc == 0), stop=(mc == 1))
            rd = rdp.next()
            P.recip(rd, pd)
            for c in range(2):
                po = C.PS[5 + c]
                for mc in range(2):
                    P.mm(po, Vm[:, mc, hh * 256 + c * 128: hh * 256 + (c + 1) * 128], pT[:, mc, :], start=(mc == 0), stop=(mc == 1))
                P.tt(oT[:, hh * 2 + c, :], po, rd, ALU.mult)
        if CM == "attn":
            continue
        for t4 in range(4):
            tt = b * 4 + t4
            sub = [C.PS[0], C.PS[1]]
            for hf in range(2):
                for kc in range(8):
                    P.mm(sub[hf], oT[:, kc, t4 * 128:(t4 + 1) * 128], Wo[:, kc, hf * 512:(hf + 1) * 512], start=(kc == 0), stop=(kc == 7))
            if len(pendq) >= 3:
                ln_epilogue_b(P, C, pendq.pop(0))
            pendq.append(ln_epilogue(P, C, lnp, tt, sub, hts_[t4]))
    for pe_ in pendq:
        ln_epilogue_b(P, C, pe_)
    P.pop()


WEIGHT_SPECS = [
    ("hyb_w_in", [2, 1024, 3592]), ("dn_conv_w", [2, 4, 1536]), ("hyb_w_out", [2, 1024, 1024]),
    ("s5_glu_wo", [2, 1024, 1024]), ("s5_glu_wg", [2, 1024, 1024]),
    ("ln_mix_g", [4, 1024]), ("ln_mix_b", [4, 1024]),
    ("xq_w", [4, 1024, 1024]), ("xk_w", [4, 1024, 1024]), ("xv_w", [4, 1024, 1024]), ("xo_w", [4, 1024, 1024]),
    ("ln_x_g", [4, 1024]), ("ln_x_b", [4, 1024]),
    ("ffn_wg", [4, 1024, 2816]), ("ffn_wu", [4, 1024, 2816]), ("ffn_wd", [4, 2816, 1024]),
    ("ln_ffn_g", [4, 1024]), ("ln_ffn_b", [4, 1024]),
]


def build_program(plan):
    nc = bass.Bass("TRN2", target_bir_lowering=False)
    P = Prog(nc)
    C = Ctx()
    I = {}

    def inp(name, shape, dt=F32):
        I[name] = P.dram(name, shape, dt, kind="ExternalInput")

    inp("x", [S, D])
    inp("mem", [256, D])
    inp("pos", [1, S], I32)
    for n, sh in WEIGHT_SPECS:
        inp(n, sh)
    hc = host_consts()
    for n, a in hc.items():
        inp(n, list(a.shape))
    for l in range(2):
        inp("s5col%d" % l, [128, 3, 32])
        inp("s5row%d" % l, [3, 4096])
        for nm in ("bre", "bim", "cre", "cim"):
            inp("s5%s%d" % (nm, l), [128, 4096])
        inp("s5d%d" % l, [128, 8])
        inp("dnp%d" % l, [1, 4 + 4 + 128])
        inp("dncw%d" % l, [128, 12, 4])
    C.out = P.dram("out", [S, D], F32, kind="ExternalOutput")
    C.hT = P.dram("hT", [D, S], BF16)
    C.mixT = P.dram("mixT", [D, S], BF16)
    C.aT = P.dram("aT", [FF, S], BF16)
    C.memT = P.dram("memT", [D, 256], BF16)
    C.cosT = P.dram("cosT", [128, S], F32)
    C.sinT = P.dram("sinT", [128, S], F32)
    C.vd = P.dram("vd", [S, 512], BF16)
    C.PS = [P.ps([128, 512], F32, "ps%d" % i) for i in range(7)]
    C.pst = P.ps([128, 1024], BF16, "pst")
    idf = P.sb([128, 128], F32, "idf")
    P.dma(idf, I["c_ident"].v(I["c_ident"].ap))
    C.identb = P.sb([128, 128], BF16, "identb")
    P.copy(C.identb, idf)
    C.identf = idf
    C.onesb = P.sb([128, 128], BF16, "onesb")
    P.memset(C.onesb, 1.0)

    bvec = lambda nm, l: I[nm].v(I[nm].ap[l:l + 1, :].partition_broadcast(128))
    h_src = I["x"]
    for ph in plan:
        kind = ph[0]
        if kind == "prep":
            phase_prep(P, C, I)
        elif kind == "cross":
            l = ph[1]
            phase_cross(P, C, I, l, (bvec("ln_x_g", l), bvec("ln_x_b", l)), h_src)
            h_src = C.out
        elif kind == "ffn":
            l = ph[1]
            wg, wu, wd = I["ffn_wg"], I["ffn_wu"], I["ffn_wd"]
            P.push()
            WdPre = P.sb([128, 22, 1024], BF16, "WdPre")
            pre = lambda: load_w(P, WdPre, lambda kc: wd.v(wd.ap[l, kc * 128:(kc + 1) * 128, :]), 22)
            phase_ffn_up(P, C, lambda kc: wg.v(wg.ap[l, kc * 128:(kc + 1) * 128, :]), lambda kc: wu.v(wu.ap[l, kc * 128:(kc + 1) * 128, :]), after_loads=pre)
            phase_proj_ln(P, C, C.aT, FF, None, (bvec("ln_ffn_g", l), bvec("ln_ffn_b", l)), h_src, Wpre=WdPre)
            P.pop()
            h_src = C.out
        elif kind == "s5":
            l = ph[1]
            i = l // 2
            phase_s5(P, C, I, i)
            wo, wg = I["s5_glu_wo"], I["s5_glu_wg"]
            phase_proj_ln(P, C, C.mixT, D, lambda kc: wo.v(wo.ap[i, kc * 128:(kc + 1) * 128, :]),
                          (bvec("ln_mix_g", l), bvec("ln_mix_b", l)), h_src,
                          gate_ap=lambda kc: wg.v(wg.ap[i, kc * 128:(kc + 1) * 128, :]))
            h_src = C.out
        elif kind == "hyb":
            l = ph[1]
            i = l // 2
            phase_dilated(P, C, I, i)
            phase_gdn(P, C, I, i)
            wo = I["hyb_w_out"]
            phase_proj_ln(P, C, C.mixT, D, lambda kc: wo.v(wo.ap[i, kc * 128:(kc + 1) * 128, :]),
                          (bvec("ln_mix_g", l), bvec("ln_mix_b", l)), h_src)
            h_src = C.out
    P.emit()
    return nc


def make_in_maps(inputs, cores):
    hc = host_consts()
    hl = host_layer_inputs(inputs)
    shared = {n: np.ascontiguousarray(inputs[n], np.float32) for n, _ in WEIGHT_SPECS}
    shared.update(hc)
    shared.update(hl)
    for l in range(2):
        shared["dnp%d" % l] = np.concatenate([inputs["dn_a_log"][l], inputs["dn_dt_bias"][l], inputs["dn_norm_g"][l]]).astype(np.float32).reshape(1, 136)
    maps = []
    for b in cores:
        m = dict(shared)
        m["x"] = np.ascontiguousarray(inputs["x"][b], np.float32)
        m["mem"] = np.ascontiguousarray(inputs["mem"][b], np.float32)
        m["pos"] = np.ascontiguousarray(inputs["positions"][b].reshape(1, S), np.int32)
        maps.append(m)
    return maps


FULL_PLAN = [("prep",)]
for _l in range(4):
    FULL_PLAN += [("hyb" if _l % 2 == 0 else "s5", _l), ("cross", _l), ("ffn", _l)]


def kernel(**inputs):
    inputs = {k: np.asarray(v) for k, v in inputs.items()}
    nc = build_program(FULL_PLAN)
    maps = make_in_maps(inputs, list(range(8)))
    res = run_bass_kernel_spmd(nc, maps, core_ids=list(range(8)))
    return np.stack([r["out"] for r in res.results], 0).astype(np.float32)


def b3(v, shape, axis):
    return V(v.ap.unsqueeze(axis).to_broadcast(shape), v.bufs)


def phase_s5(P, C, I, l):
    P.push()
    f32t = lambda shape, n: P.sb(shape, F32, n)
    col = f32t([128, 3, 32], "col")
    P.dma(col, I["s5col%d" % l].v(I["s5col%d" % l].ap))
    dtc = f32t([128, 32], "dtc")
    P.act(dtc, col[:, 2, :], AF.Exp)
    t0 = f32t([128, 32], "t0")
    P.tt(t0, col[:, 0, :], dtc, ALU.mult)
    rho = f32t([128, 32], "rho")
    P.act(rho, t0, AF.Exp)
    phi = f32t([128, 32], "phi")
    P.tt(phi, col[:, 1, :], dtc, ALU.mult)
    P.ts(phi, phi, 1.0 / TWO_PI, None, ALU.mult)
    iota = f32t([128, 64], "iota")
    P.dma(iota, I["c_iota"].v(I["c_iota"].ap))
    dcol = f32t([128, 8], "dcol")
    P.dma(dcol, I["s5d%d" % l].v(I["s5d%d" % l].ap))
    E = {k: f32t([128, 2048], "F" + k) for k in ("lo", "hi")}
    Bb = {k: P.sb([128, 4096], BF16, "Bb" + k) for k in ("re", "im")}
    CreB = P.sb([128, 4096], BF16, "CreB")
    CreN = P.sb([128, 4096], BF16, "CreN")
    CimN = P.sb([128, 4096], BF16, "CimN")
    P.dma(CreB, I["s5cre%d" % l].v(I["s5cre%d" % l].ap), eng="pool")
    P.dma(CimN, I["s5cim%d" % l].v(I["s5cim%d" % l].ap), eng="pool")
    P.ts(CreN, CreB, -1.0, None, ALU.mult)
    P.ts(CimN, CimN, -1.0, None, ALU.mult)
    P.push()
    scr = frac_scratch(P, [128, 2048])
    tu = f32t([128, 2048], "tu")
    tu3 = tu.re("p (a b) -> p a b", b=64)
    iota3 = b3(iota, [128, 32, 64], 1)
    P.tt(tu3, b3(phi, [128, 32, 64], 2), iota3, ALU.mult)
    frac_wrap(P, E["lo"], tu, 0.0, scr)
    p64 = f32t([128, 32], "p64")
    P.ts(p64, phi, 64.0, None, ALU.mult)
    scr2 = frac_scratch(P, [128, 32])
    f64 = f32t([128, 32], "f64")
    frac_wrap(P, f64, p64, 0.0, scr2)
    P.tt(tu3, b3(f64, [128, 32, 64], 2), iota3, ALU.mult)
    frac_wrap(P, E["hi"], tu, 0.0, scr)
    P.pop()
    P.push()
    row = I["s5row%d" % l]
    T = {k: f32t([128, 1024], "r" + k) for k in ("are", "aim", "dt", "er", "tu", "f", "sn", "cs", "abr", "abi", "nr", "den", "cr", "ci", "Bre", "Bim", "x1", "x2")}
    scr = frac_scratch(P, [128, 1024])
    for qq in range(4):
        cs_ = slice(qq * 1024, (qq + 1) * 1024)
        P.dma(T["are"], row.v(row.ap[0:1, cs_].partition_broadcast(128)))
        P.dma(T["aim"], row.v(row.ap[1:2, cs_].partition_broadcast(128)))
        P.dma(T["dt"], row.v(row.ap[2:3, cs_].partition_broadcast(128)))
        P.dma(T["Bre"], I["s5bre%d" % l].v(I["s5bre%d" % l].ap[:, cs_]))
        P.dma(T["Bim"], I["s5bim%d" % l].v(I["s5bim%d" % l].ap[:, cs_]))
        P.act(T["dt"], T["dt"], AF.Exp)
        P.tt(T["er"], T["are"], T["dt"], ALU.mult)
        P.act(T["er"], T["er"], AF.Exp)
        P.tt(T["tu"], T["aim"], T["dt"], ALU.mult)
        P.ts(T["tu"], T["tu"], 1.0 / TWO_PI, None, ALU.mult)
        frac_wrap(P, T["f"], T["tu"], 0.0, scr)
        P.act(T["sn"], T["f"], AF.Sin, scale=TWO_PI)
        frac_wrap(P, T["f"], T["tu"], 0.25, scr)
        P.act(T["cs"], T["f"], AF.Sin, scale=TWO_PI)
        P.tt(T["abr"], T["er"], T["cs"], ALU.mult)
        P.tt(T["abi"], T["er"], T["sn"], ALU.mult)
        P.ts(T["nr"], T["abr"], -1.0, None, ALU.add)
        P.tt(T["den"], T["are"], T["are"], ALU.mult)
        P.tt(T["x1"], T["aim"], T["aim"], ALU.mult)
        P.tt(T["den"], T["den"], T["x1"], ALU.add)
        P.recip(T["den"], T["den"])
        P.tt(T["x1"], T["nr"], T["are"], ALU.mult)
        P.tt(T["x2"], T["abi"], T["aim"], ALU.mult)
        P.tt(T["cr"], T["x1"], T["x2"], ALU.add)
        P.tt(T["cr"], T["cr"], T["den"], ALU.mult)
        P.tt(T["x1"], T["abi"], T["are"], ALU.mult)
        P.tt(T["x2"], T["nr"], T["aim"], ALU.mult)
        P.tt(T["ci"], T["x1"], T["x2"], ALU.subtract)
        P.tt(T["ci"], T["ci"], T["den"], ALU.mult)
        P.tt(T["x1"], T["cr"], T["Bre"], ALU.mult)
        P.tt(T["x2"], T["ci"], T["Bim"], ALU.mult)
        P.tt(Bb["re"][:, cs_], T["x1"], T["x2"], ALU.subtract)
        P.tt(T["x1"], T["cr"], T["Bim"], ALU.mult)
        P.tt(T["x2"], T["ci"], T["Bre"], ALU.mult)
        P.tt(Bb["im"][:, cs_], T["x1"], T["x2"], ALU.add)
    P.pop()
    BW = 512
    NBLK = S // BW
    HC = BW // 64
    ones = f32t([128, BW], "ones")
    P.memset(ones, 1.0)
    halfpi = f32t([128, 1], "halfpi")
    P.memset(halfpi, math.pi / 2)
    onec = f32t([128, 1], "onec")
    P.memset(onec, 1.0)
    stre = [f32t([128, 1], "stre%d" % i) for i in range(32)]
    stim = [f32t([128, 1], "stim%d" % i) for i in range(32)]
    for t_ in stre + stim:
        P.memset(t_, 0.0)
    uTp = Pool(P, 2, [128, S], BF16, "uT")
    RHO = [[f32t([128, BW], "RHO%d_%d" % (par, q)) for q in range(4)] for par in range(2)]
    mkp = lambda n, dt=F32, k=4: Pool(P, k, [128, BW], dt, n)
    p_fs, p_s1, p_sq, p_ei = mkp("fs", F32, 2), mkp("s1", F32, 2), mkp("sq"), mkp("ei")
    p_m = [mkp("m%d" % i, F32, 3) for i in range(4)]
    p_xr, p_xi = mkp("xr", F32, 3), mkp("xi", F32, 3)
    p_pp = [mkp("pp%d" % i, BF16, 3) for i in range(4)]
    hidp = Pool(P, 2, [128, BW], BF16, "hid")
    vtp = Pool(P, 2, [128, BW], F32, "vt")
    allk = tuple(("t", t) for t in range(NT))
    v3 = lambda t: t.re("p (a b) -> p a b", b=64)
    iters = [(cc, blk, q) for cc in range(8) for blk in range(NBLK) for q in range(4)]
    ctx = {}
    uTs = {}
    bub = [0]
    btb = [0]
    IDN = f32t([128, 128], "IDN")
    P.ts(IDN, C.identf, -1.0, None, ALU.mult)

    def prologue(n):
        cc, blk, q = iters[n]
        ctx[n] = {}
        if blk == 0 and q == 0:
            uTs[cc] = uTp.next()
            P.dma(uTs[cc], C.hT.v(C.hT.ap[cc * 128:(cc + 1) * 128, :], allk))
            for q2 in range(4):
                P.ts(RHO[cc % 2][q2], ones, rho[:, 4 * cc + q2:4 * cc + q2 + 1], None, ALU.mult)

    def sA(n):
        cc, blk, q = iters[n]
        pair = 4 * cc + q
        d = ctx[n]
        hs = slice(pair * 64 + blk * HC, pair * 64 + (blk + 1) * HC)
        ls = slice(pair * 64, pair * 64 + 64)
        d["fs"], d["s1"], d["sq"], d["ei"] = p_fs.next(), p_s1.next(), p_sq.next(), p_ei.next()
        P.tt(v3(d["fs"]), b3(E["hi"][:, hs], [128, HC, 64], 2), b3(E["lo"][:, ls], [128, HC, 64], 1), ALU.add, eng="pool")

    def sD(n):
        d = ctx[n]
        fs, s1, sq = d["fs"], d["s1"], d["sq"]
        P.act(s1, fs, AF.Sin, scale=math.pi)
        P.act(fs, fs, AF.Abs)
        P.act(fs, fs, AF.Sin, scale=-math.pi, bias=halfpi[:, 0:1])
        P.act(sq, s1, AF.Square)
        P.act(sq, sq, AF.Identity, scale=-2.0, bias=onec[:, 0:1])

    def sF(n):
        d = ctx[n]
        P.stt(d["ei"], d["s1"], 2.0, d["fs"], ALU.mult, ALU.mult)
        d["Ere"], d["Eim"] = d["sq"], d["ei"]

    def sB(n):
        cc, blk, q = iters[n]
        pair = 4 * cc + q
        d = ctx[n]
        tk = slice(blk * BW, (blk + 1) * BW)
        pc = slice(pair * 128, (pair + 1) * 128)
        d["bre"], d["bim"] = C.PS[0], C.PS[1]
        P.mm(d["bre"], Bb["re"][:, pc], uTs[cc][:, tk])
        P.mm(d["bim"], Bb["im"][:, pc], uTs[cc][:, tk])

    def sE(n):
        d = ctx[n]
        m = [p.next() for p in p_m]
        bre, bim = d["bre"], d["bim"]
        P.tt(m[0], bre, d["Ere"], ALU.mult)
        P.tt(m[1], bim, d["Eim"], ALU.mult)
        P.tt(m[2], bim, d["Ere"], ALU.mult)
        P.tt(m[3], bre, d["Eim"], ALU.mult)
        btr, bti = C.PS[2 + btb[0]], C.PS[3 + btb[0]]
        btb[0] = (btb[0] + 2) % 4
        P.mm(btr, C.identf, m[0], start=True, stop=False)
        P.mm(btr, C.identf, m[1], start=False, stop=True)
        P.mm(bti, C.identf, m[2], start=True, stop=False)
        P.mm(bti, IDN, m[3], start=False, stop=True)
        d["m"] = [btr, None, bti, None]

    def sC(n):
        cc, blk, q = iters[n]
        pair = 4 * cc + q
        d = ctx[n]
        m = d["m"]
        xr, xi = p_xr.next(), p_xi.next()
        P.scan(xr, RHO[cc % 2][q], m[0], stre[pair], ALU.mult, ALU.add)
        P.scan(xi, RHO[cc % 2][q], m[2], stim[pair], ALU.mult, ALU.add)
        P.copy(stre[pair], xr[:, BW - 1:BW], eng="act")
        P.copy(stim[pair], xi[:, BW - 1:BW], eng="act")
        pp = d["pp"] = [p.next() for p in p_pp]
        P.tt(pp[0], d["Ere"], xr, ALU.mult)
        P.tt(pp[1], d["Eim"], xi, ALU.mult, eng="pool")
        P.tt(pp[2], d["Ere"], xi, ALU.mult, eng="pool")
        P.tt(pp[3], d["Eim"], xr, ALU.mult, eng="pool")

    def sG(n):
        cc, blk, q = iters[n]
        pair = 4 * cc + q
        d = ctx.pop(n)
        uT = uTs[cc]
        tk = slice(blk * BW, (blk + 1) * BW)
        pc = slice(pair * 128, (pair + 1) * 128)
        yps = C.PS[6]
        pp = d["pp"]
        for k, (W_, p_) in enumerate(((CreB, pp[0]), (CreN, pp[1]), (CimN, pp[2]), (CimN, pp[3]))):
            P.mm(yps, W_[:, pc], p_, start=(q == 0 and k == 0), stop=(q == 3 and k == 3))
        if q == 3:
            vt = vtp.next()
            P.stt(vt, uT[:, tk], dcol[:, cc:cc + 1], yps, ALU.mult, ALU.add)
            hid = hidp.next()
            P.act(hid, vt, AF.Gelu)
            P.dma(C.mixT.v(C.mixT.ap[cc * 128:(cc + 1) * 128, tk], hT_keys(blk)), hid)

    N_ = len(iters)
    ok = lambda k: 0 <= k < N_
    for n in range(N_ + 2):
        if ok(n):
            prologue(n)
            sA(n)
        if ok(n - 1):
            sB(n - 1)
        if ok(n - 2):
            sC(n - 2)
        if ok(n):
            sD(n)
        if ok(n - 1):
            sE(n - 1)
        if ok(n):
            sF(n)
        if ok(n - 2):
            sG(n - 2)
    P.pop()


def _zero_mix_rows(P, C, r0, r1):
    P.push()
    z = P.sb([128, 1024], BF16, "zmix")
    P.memset(z, 0.0)
    for c in range(r0 // 128, r1 // 128):
        for blk in range(4):
            P.dma(C.mixT.v(C.mixT.ap[c * 128:(c + 1) * 128, blk * 1024:(blk + 1) * 1024],
                           tuple(("t", t) for t in range(blk * 8, blk * 8 + 8))), z)
    P.pop()


def phase_dilated(P, C, I, i):
    w_in = I["hyb_w_in"]
    w3 = w_in.ap[i].rearrange("(kc p) n -> p kc n", p=128)
    hT3 = C.hT.ap.rearrange("(c p) t -> p c t", p=128)
    QOFF, KOFF, VOFF = 2056, 2568, 3080
    P.push()
    Wv = P.sb([128, 8, 512], BF16, "Wv")
    P.dma(Wv, w_in.v(w3[:, :, VOFF:VOFF + 512]), eng="pool")
    blkp = Pool(P, 2, [128, 8, 512], BF16, "hblk")
    vtp = Pool(P, 2, [128, 512], BF16, "vt")
    for b in range(8):
        ib = blkp.next()
        P.dma(ib, C.hT.v(hT3[:, :, b * 512:(b + 1) * 512], hT_keys(b)))
        for t4 in range(4):
            tt = b * 4 + t4
            ps = C.PS[t4 % 2]
            for kc in range(8):
                P.mm(ps, ib[:, kc, t4 * 128:(t4 + 1) * 128], Wv[:, kc, :], start=(kc == 0), stop=(kc == 7))
            vt = vtp.next()
            P.copy(vt, ps, eng="act")
            P.dma(C.vd.v(C.vd.ap[tt * 128:(tt + 1) * 128, :], 0), vt)
    P.pop()
    import os
    DM = os.environ.get("DIL_MODE", "all")
    if DM == "v":
        return
    P.push()
    COS = P.sb([128, S], F32, "COS")
    SIN = P.sb([128, S], F32, "SIN")
    P.dma(COS, C.cosT.v(C.cosT.ap, (0, 1, 2, 3)))
    P.dma(SIN, C.sinT.v(C.sinT.ap, (0, 1, 2, 3)))
    dmask = P.sb([128, 256], F32, "dmask")
    P.dma(dmask, I["c_dmask"].v(I["c_dmask"].ap))
    sel = P.sb([128, 64], F32, "sel")
    P.dma(sel, I["c_sel"].v(I["c_sel"].ap))
    Wt = {k: P.sb([128, 8, 128], BF16, "W" + k) for k in ("q", "qs", "k", "ks")}
    blkp = Pool(P, 2, [128, 8, 512], BF16, "hblk")
    RT = {k: P.sb([128, S], BF16, "R" + k) for k in ("q", "k")}
    CM = {(k, r): P.sb([128, S], BF16, "C%s%d" % (k, r)) for k in ("q", "k") for r in (4, 16)}
    OT = [P.sb([128, S], F32, "OT%d" % h) for h in range(2)]
    t1p = Pool(P, 2, [128, 512], F32, "t1")
    t2p = Pool(P, 2, [128, 512], F32, "t2")
    vap = Pool(P, 4, [128, 8, 128], BF16, "vaug")
    for va in vap.tiles:
        P.memset(va, 1.0)
    pep = Pool(P, 3, [128, 256], BF16, "pe")
    ptp = Pool(P, 6, [128, 256], BF16, "pt")
    dnegf = P.sb([128, 256], F32, "dnegf")
    P.dma(dnegf, I["c_dneg"].v(I["c_dneg"].ap))
    dnegb = P.sb([128, 256], BF16, "dnegb")
    P.copy(dnegb, dnegf)
    sbank = [0]
    rdp = Pool(P, 2, [64, 512], F32, "rd")
    onp = Pool(P, 2, [64, 512], BF16, "on")
    slots = [[V(C.PS[3 + 2 * h + sidx].ap[:, 0:128], C.PS[3 + 2 * h + sidx].bufs) for sidx in range(2)] for h in range(2)]
    for hp in range(4):
        for k, off in (("q", QOFF), ("k", KOFF)):
            base = off + hp * 128
            P.dma(Wt[k], w_in.v(w3[:, :, base:base + 128]), eng="pool")
            for hh in range(2):
                P.dma(Wt[k + "s"][:, :, hh * 64:hh * 64 + 32], w_in.v(w3[:, :, base + hh * 64 + 32:base + hh * 64 + 64]), eng="pool")
                P.dma(Wt[k + "s"][:, :, hh * 64 + 32:hh * 64 + 64], w_in.v(w3[:, :, base + hh * 64:base + hh * 64 + 32]), eng="pool")
        for b in range(8):
            ib = blkp.next()
            P.dma(ib, C.hT.v(hT3[:, :, b * 512:(b + 1) * 512], hT_keys(b)))
            cs_ = slice(b * 512, (b + 1) * 512)
            for k in ("q", "k"):
                for kc in range(8):
                    P.mm(C.PS[0], Wt[k][:, kc, :], ib[:, kc, :], start=(kc == 0), stop=(kc == 7))
                for kc in range(8):
                    P.mm(C.PS[1], Wt[k + "s"][:, kc, :], ib[:, kc, :], start=(kc == 0), stop=(kc == 7))
                t1 = t1p.next()
                t2 = t2p.next()
                P.tt(t1, C.PS[0], COS[:, cs_], ALU.mult)
                P.tt(t2, C.PS[1], SIN[:, cs_], ALU.mult)
                P.tt(RT[k][:, cs_], t1, t2, ALU.add, eng="pool")
        if DM == "proj":
            continue
        for k in ("q", "k"):
            for r in (4, 16):
                P.copy(CM[(k, r)].re("p (r u) -> p r u", r=r), RT[k].re("p (u r) -> p r u", r=r), eng="pool")
        if DM == "cm":
            continue
        its = []
        for bi, r in enumerate((1, 4, 16)):
            L = S // r
            nb_ = L // 128
            for rho in range(r):
                for m in range(nb_):
                    for hh in range(2):
                        its.append((bi, r, rho, m, hh, nb_, L))
        ictx = {}

        def st_S(n):
            bi, r, rho, m, hh, nb_, L = its[n]
            d = ictx[n] = {}
            Qs = RT["q"] if r == 1 else CM[("q", r)]
            Ks = RT["k"] if r == 1 else CM[("k", r)]
            if hh == 0:
                va = vap.next()
                st = (128 * m) * r + rho
                P.dma(va[:, :, 0:64], C.vd.v(C.vd.ap[st:st + 127 * r + 1:r, :].rearrange("t (h d) -> t h d", h=8), 0))
                ictx["va"] = va
            d["va"] = ictx["va"]
            nq = d["nq"] = 256 if m < nb_ - 1 else 128
            c0 = rho * L + m * 128
            hb = hh * 64
            sps = d["sps"] = C.PS[sbank[0]]
            sbank[0] = (sbank[0] + 1) % 3
            P.mm(sps[:, 0:nq], Ks[hb:hb + 64, c0:c0 + 128], Qs[hb:hb + 64, c0:c0 + nq], start=True, stop=False)
            P.mm(sps[:, 0:nq], C.identb, dnegb[:, 0:nq], start=False, stop=True)

        def st_X(n):
            d = ictx[n]
            nq = d["nq"]
            pt = d["pt"] = ptp.next()
            P.act(pt[:, 0:nq], d["sps"][:, 0:nq], AF.Exp, scale=0.125)

        def st_V(n):
            bi, r, rho, m, hh, nb_, L = its[n]
            d = ictx.pop(n)
            pt, va, nq = d["pt"], d["va"], d["nq"]
            head = hp * 2 + hh
            sl = slots[hh][m % 2]
            P.mm(sl, va[:, head, :], pt[:, 0:128], start=(m == 0), stop=True)
            dst = OT[hh].re("p (u r) -> p r u", r=r)[:, rho, m * 128:(m + 1) * 128] if r > 1 else OT[hh][:, m * 128:(m + 1) * 128]
            if bi == 0:
                P.copy(dst, sl, eng="dve")
            else:
                P.tt(dst, sl, dst, ALU.add)
            if nq == 256:
                sl2 = slots[hh][(m + 1) % 2]
                P.mm(sl2, va[:, head, :], pt[:, 128:256], start=True, stop=False)

        NI = len(its)
        for n in range(NI + 2):
            if n < NI:
                st_S(n)
            if 0 <= n - 1 < NI:
                st_X(n - 1)
            if 0 <= n - 2 < NI:
                st_V(n - 2)
        if DM in ("br0a", "br0b", "br0b1", "br0b2"):
            continue
        for hh in range(2):
            head = hp * 2 + hh
            for b in range(8):
                cs_ = slice(b * 512, (b + 1) * 512)
                pd = C.PS[hh]
                P.mm(pd[0:64, :], sel, OT[hh][:, cs_])
                rd = rdp.next()
                P.recip(rd, pd[0:64, :])
                on = onp.next()
                P.tt(on, OT[hh][0:64, cs_], rd, ALU.mult)
                r0 = 512 + head * 64
                P.dma(C.mixT.v(C.mixT.ap[r0:r0 + 64, cs_], hT_keys(b)), on)
    P.pop()


def phase_gdn(P, C, I, i):
    w_in = I["hyb_w_in"]
    w3 = w_in.ap[i].rearrange("(kc p) n -> p kc n", p=128)
    hT3 = C.hT.ap.rearrange("(c p) t -> p c t", p=128)
    dnp = I["dnp%d" % i]
    P.push()
    f32t = lambda shape, n: P.sb(shape, F32, n)
    cst = {}
    for nm in ("c_tri", "c_ones", "c_negU", "c_posL", "c_strU", "c_strL"):
        cst[nm] = f32t([128, 128], nm)
        P.dma(cst[nm], I[nm].v(I[nm].ap))
    TRI, ONESf, NEGU, POSL, STRU, STRL = (cst[n] for n in ("c_tri", "c_ones", "c_negU", "c_posL", "c_strU", "c_strL"))
    IDF = C.identf
    onecol = f32t([128, 1], "onecol")
    P.memset(onecol, 1.0)
    NG = f32t([128, 128], "NG")
    P.dma(NG, dnp.v(dnp.ap[0:1, 8:136].partition_broadcast(128)))
    ab = f32t([128, 8], "ab")
    P.dma(ab, dnp.v(dnp.ap[0:1, 0:8].partition_broadcast(128)))
    nA = f32t([128, 4], "nA")
    P.act(nA, ab[:, 0:4], AF.Exp)
    P.ts(nA, nA, -1.0, None, ALU.mult)
    cw = f32t([128, 12, 4], "cw")
    P.dma(cw, I["dncw%d" % i].v(I["dncw%d" % i].ap))
    bank = [0]

    def nb():
        bank[0] = (bank[0] + 1) % 7
        return C.PS[bank[0]]

    Wg8 = P.sb([128, 8, 8], BF16, "Wg8")
    P.dma(Wg8, w_in.v(w3[:, :, 2048:2056]), eng="pool")
    blkp = Pool(P, 2, [128, 8, 512], BF16, "hblk")
    GR = f32t([128, 32, 8], "GR")
    for b in range(8):
        ib = blkp.next()
        P.dma(ib, C.hT.v(hT3[:, :, b * 512:(b + 1) * 512], hT_keys(b)))
        for t4 in range(4):
            ps = nb()
            for kc in range(8):
                P.mm(ps[:, 0:8], ib[:, kc, t4 * 128:(t4 + 1) * 128], Wg8[:, kc, :], start=(kc == 0), stop=(kc == 7))
            P.copy(GR[:, b * 4 + t4, :], ps[:, 0:8], eng="act")
    BETA = f32t([128, 128], "BETA")
    NBETA = f32t([128, 128], "NBETA")
    Gt = f32t([128, 128], "Gt")
    v34 = lambda t: t.re("p (c h) -> p c h", h=4)
    P.copy(v34(BETA), GR[:, :, 0:4], eng="dve")
    P.act(BETA, BETA, AF.Sigmoid)
    P.ts(NBETA, BETA, -1.0, None, ALU.mult)
    P.tt(v34(Gt), GR[:, :, 4:8], b3(ab[:, 4:8], [128, 32, 4], 1), ALU.add)
    P.act(Gt, Gt, AF.Exp)
    P.act(Gt, Gt, AF.Ln, bias=onecol[:, 0:1])
    P.tt(v34(Gt), v34(Gt), b3(nA, [128, 32, 4], 1), ALU.mult)
    GC, GL, EG, EDL, EGL = (f32t([128, 128], n) for n in ("GC", "GL", "EG", "EDL", "EGL"))
    ps = nb()
    P.mm(ps[:, 0:128], TRI, Gt)
    P.copy(GC, ps[:, 0:128], eng="dve")
    ps = nb()
    P.mm(ps[:, 0:128], ONESf, Gt)
    P.copy(GL, ps[:, 0:128], eng="dve")
    P.act(EG, GC, AF.Exp)
    P.act(EGL, GL, AF.Exp)
    P.tt(EDL, GL, GC, ALU.subtract)
    P.act(EDL, EDL, AF.Exp)
    import os
    GM = os.environ.get("GDN_MODE", "all")
    if GM == "gates":
        P.pop()
        return
    Wh = P.sb([128, 8, 3, 128], BF16, "Wh")
    Wz = P.sb([128, 8, 128], BF16, "Wz")
    X = [f32t([128, S + 3], "X%d" % x) for x in range(3)]
    for x in range(3):
        P.memset(X[x][:, 0:3], 0.0)
    sq = P.sb([128, S], BF16, "sq")
    Ycv = f32t([128, S], "Ycv")
    QT = P.sb([128, S], BF16, "QT")
    KT = P.sb([128, S], BF16, "KT")
    VT = P.sb([128, S], BF16, "VT")
    KTM = P.sb([128, 32, 128], BF16, "KTM")
    VTM = P.sb([128, 32, 128], BF16, "VTM")
    rnp = Pool(P, 2, [128, 512], F32, "rn")
    Sst = f32t([128, 128], "Sst")
    Sb = P.sb([128, 128], BF16, "Sb")
    NBATCH = 4
    mk = lambda n, dt=F32, k=NBATCH: Pool(P, k, [128, 128], dt, n)
    p_gb, p_bb, p_DT, p_DL, p_tmp, p_tmp2 = mk("gb"), mk("bb"), mk("DT"), mk("DL"), mk("tmpa"), mk("tmpb")
    p_A, p_AT, p_P = mk("A", F32, 2 * NBATCH), mk("AT", F32, 2 * NBATCH), mk("P", F32, 2 * NBATCH)
    p_intra, p_Pb, p_ub, p_Keg, p_WT, p_Kdec, p_qdT, p_EGr = (mk("intra", BF16), mk("Pb", BF16), mk("ub"), mk("Keg", BF16),
                                                              mk("WT", BF16), mk("Kdec", BF16), mk("qdT", BF16), mk("EGr"))
    p_vn, p_zs, p_t, p_ao, p_aoT = mk("vn", BF16, 2), mk("zs", F32, 2), mk("tt", F32, 2), mk("ao", BF16, 2), mk("aoT", BF16, 2)
    p_st = Pool(P, 2, [128, 6], F32, "gst")
    p_mv = Pool(P, 2, [128, 8], F32, "gmv")
    hcp = Pool(P, 4, [128, 8, 128], BF16, "hchunk")
    for hd in range(4):
        for x in range(3):
            P.dma(Wh[:, :, x, :], w_in.v(w3[:, :, x * 512 + hd * 128: x * 512 + (hd + 1) * 128]), eng="pool")
        P.dma(Wz, w_in.v(w3[:, :, 1536 + hd * 128:1536 + (hd + 1) * 128]), eng="pool")
        for b in range(8):
            ib = blkp.next()
            P.dma(ib, C.hT.v(hT3[:, :, b * 512:(b + 1) * 512], hT_keys(b)))
            for x in range(3):
                ps = nb()
                for kc in range(8):
                    P.mm(ps, Wh[:, kc, x, :], ib[:, kc, :], start=(kc == 0), stop=(kc == 7))
                P.copy(X[x][:, 3 + b * 512:3 + (b + 1) * 512], ps, eng="act")
        for x in range(3):
            Y = X[x]
            ch = x * 4 + hd
            Yc = Ycv
            P.ts(Yc, Y[:, 0:S], cw[:, ch, 0:1], None, ALU.mult)
            for j in range(1, 4):
                P.stt(Yc, Y[:, j:S + j], cw[:, ch, j:j + 1], Yc, ALU.mult, ALU.add)
            P.act(Yc, Yc, AF.Silu)
            if x < 2:
                P.tt(sq, Yc, Yc, ALU.mult, eng="pool")
                dst = QT if x == 0 else KT
                for b in range(8):
                    cs_ = slice(b * 512, (b + 1) * 512)
                    ps = nb()
                    P.mm(ps, C.onesb, sq[:, cs_])
                    rn = rnp.next()
                    P.ts(rn, ps, RMS_EPS, None, ALU.add)
                    P.act(rn, rn, AF.Sqrt)
                    P.recip(rn, rn)
                    if x == 0:
                        P.stt(dst[:, cs_], Yc[:, cs_], 128 ** -0.5, rn, ALU.mult, ALU.mult)
                    else:
                        P.tt(dst[:, cs_], Yc[:, cs_], rn, ALU.mult)
            else:
                P.copy(VT, Yc, eng="act")
        for src, dstm in ((KT, KTM), (VT, VTM)):
            for g8 in range(4):
                for c8 in range(8):
                    c = g8 * 8 + c8
                    P.tr(C.pst[:, c8 * 128:(c8 + 1) * 128], src[:, c * 128:(c + 1) * 128], C.identb)
                P.copy(dstm[:, g8 * 8:(g8 + 1) * 8, :], C.pst.re("p (c t) -> p c t", c=8), eng="act")
        P.memset(Sst, 0.0)
        P.memset(Sb, 0.0)
        if GM == "stageA":
            continue
        for cb in range(32 // NBATCH):
            chunks = list(range(cb * NBATCH, (cb + 1) * NBATCH))
            st = {}
            for c in chunks:
                d = st[c] = {}
                chx = c * 4 + hd
                col = lambda T_: T_[:, chx:chx + 1]
                cs_ = slice(c * 128, (c + 1) * 128)
                gb, bb = p_gb.next(), p_bb.next()
                P.ts(gb, ONESf, col(Gt), None, ALU.mult)
                P.ts(bb, ONESf, col(BETA), None, ALU.mult)
                gcr = nb()[:, 0:128]
                P.mm(gcr, gb, TRI)
                ber = nb()[:, 0:128]
                P.mm(ber, bb, IDF)
                DT, DL, tmp, tmp2 = p_DT.next(), p_DL.next(), p_tmp.next(), p_tmp2.next()
                P.stt(tmp, gcr, col(GC), NEGU, ALU.subtract, ALU.add)
                P.act(DT, tmp, AF.Exp)
                P.stt(tmp2, gcr, col(GC), POSL, ALU.subtract, ALU.add)
                P.act(DL, tmp2, AF.Exp, scale=-1.0)
                EGr = p_EGr.next()
                P.act(EGr, gcr, AF.Exp)
                qdT = d["qdT"] = p_qdT.next()
                P.tt(qdT, QT[:, cs_], EGr, ALU.mult, eng="pool")
                gps = nb()[:, 0:128]
                P.mm(gps, KT[:, cs_], KT[:, cs_])
                qkps = nb()[:, 0:128]
                P.mm(qkps, KT[:, cs_], QT[:, cs_])
                intra = d["intra"] = p_intra.next()
                P.tt(intra, qkps, DT, ALU.mult)
                P.tt(DT, DT, STRU, ALU.mult, eng="pool")
                P.tt(DL, DL, STRL, ALU.mult, eng="pool")
                A, AT = p_A.next(), p_AT.next()
                P.stt(A, gps, col(BETA), DT, ALU.mult, ALU.mult)
                P.tt(tmp, ber, DL, ALU.mult)
                P.tt(AT, gps, tmp, ALU.mult)
                Pm = p_P.next()
                P.tt(Pm, IDF, A, ALU.subtract, eng="pool")
                d["A"], d["AT"], d["P"] = A, AT, Pm
                Keg, Kdec = d["Keg"], d["Kdec"] = p_Keg.next(), p_Kdec.next()
                P.ts(Keg, KTM[:, c, :], col(EG), None, ALU.mult)
                P.ts(Kdec, KTM[:, c, :], col(EDL), None, ALU.mult)
            if GM == "s1":
                continue
            for lev in range(6):
                for c in chunks:
                    d = st[c]
                    psa, psb = nb()[:, 0:128], nb()[:, 0:128]
                    P.mm(psa, d["AT"], d["A"])
                    P.mm(psb, d["A"], d["AT"])
                    A2, AT2 = p_A.next(), p_AT.next()
                    P.copy(A2, psa, eng="act")
                    P.copy(AT2, psb, eng="dve")
                    d["A"], d["AT"] = A2, AT2
                for c in chunks:
                    d = st[c]
                    psc = nb()[:, 0:128]
                    P.mm(psc, d["AT"], d["P"])
                    P2 = p_P.next()
                    P.tt(P2, psc, d["P"], ALU.add)
                    d["P"] = P2
            if GM.startswith("neu"):
                continue
            for c in chunks:
                d = st[c]
                d["hc"] = hcp.next()
                P.dma(d["hc"], C.hT.v(hT3[:, :, c * 128:(c + 1) * 128], ("t", c)))
                d["Pb"] = p_Pb.next()
                P.copy(d["Pb"], d["P"], eng="act")
            for c in chunks:
                d = st[c]
                chx = c * 4 + hd
                col = lambda T_: T_[:, chx:chx + 1]
                psu, psw = nb()[:, 0:128], nb()[:, 0:128]
                P.mm(psu, d["Pb"], VTM[:, c, :])
                P.mm(psw, d["Keg"], d["Pb"])
                ub = d["ub"] = p_ub.next()
                P.ts(ub, psu, col(BETA), None, ALU.mult)
                WT = d["WT"] = p_WT.next()
                P.copy(WT, psw, eng="act")
            if GM == "uw":
                continue
            for c in chunks:
                d = st[c]
                chx = c * 4 + hd
                col = lambda T_: T_[:, chx:chx + 1]
                cs_ = slice(c * 128, (c + 1) * 128)
                ps1 = nb()[:, 0:128]
                P.mm(ps1, d["WT"], Sb)
                vn = p_vn.next()
                P.stt(vn, ps1, col(NBETA), d["ub"], ALU.mult, ALU.add)
                pso = nb()[:, 0:128]
                P.mm(pso, d["qdT"], Sb, start=True, stop=False)
                P.mm(pso, d["intra"], vn, start=False, stop=True)
                ps3 = nb()[:, 0:128]
                P.mm(ps3, d["Kdec"], vn)
                P.stt(Sb, Sst, col(EGL), ps3, ALU.mult, ALU.add)
                P.stt(Sst, Sst, col(EGL), ps3, ALU.mult, ALU.add)
                hc = d["hc"]
                psz = nb()[:, 0:128]
                for kc in range(8):
                    P.mm(psz, hc[:, kc, :], Wz[:, kc, :], start=(kc == 0), stop=(kc == 7))
                zs = p_zs.next()
                P.act(zs, psz, AF.Silu)
                gst, gmv = p_st.next(), p_mv.next()
                P.add("dve", lambda e, o=gst, i_=pso: e.bn_stats(o.ap, i_.ap), [pso], [gst])
                P.add("dve", lambda e, o=gmv[:, 0:2], i_=gst: e.bn_aggr(o.ap, i_.ap), [gst], [gmv])
                P.stt(gmv[:, 2:3], gmv[:, 0:1], gmv[:, 0:1], gmv[:, 1:2], ALU.mult, ALU.add)
                P.ts(gmv[:, 2:3], gmv[:, 2:3], RMS_EPS, None, ALU.add)
                P.act(gmv[:, 3:4], gmv[:, 2:3], AF.Sqrt)
                P.recip(gmv[:, 4:5], gmv[:, 3:4])
                t_ = p_t.next()
                P.stt(t_, pso, gmv[:, 4:5], NG, ALU.mult, ALU.mult)
                ao = p_ao.next()
                P.tt(ao, t_, zs, ALU.mult, eng="pool")
                P.tr(C.pst[:, 0:128], ao, C.identb)
                aoT = p_aoT.next()
                P.copy(aoT, C.pst[:, 0:128], eng="act")
                P.dma(C.mixT.v(C.mixT.ap[hd * 128:(hd + 1) * 128, cs_], ("t", c)), aoT)
    P.pop()
```
